# Optimizing a Trainium2 kernel written in Bass

```python
import jax
import jax.numpy as jnp
from jax import lax
import numpy as np

D_MODEL = 1024
BATCH = 4
SEQ = 4096
DEPTH = 2
DEC_BATCH = 32
DEC_SEQ = 8
PAST_LEN = 8192
PAGE_SIZE = 128

HEAD_DIM = 64
A_HEADS = D_MODEL // (2 * HEAD_DIM)
DIL_PAIRS = ((128, 1), (512, 4), (2048, 16))
DIL_MAX = 2048
RNN_WIDTH = D_MODEL // 2
RNN_BLOCKS = RNN_WIDTH // HEAD_DIM
CONV_W = 4
LRU_C = 8.0
C_HEADS = D_MODEL // HEAD_DIM
C_KV_HEADS = C_HEADS // 4
CMP_LEN = 32
CMP_STRIDE = 16
SEL_BLOCK = 64
SEL_TOP = 16
WIN = 512
FFN_HIDDEN = -(-8 * D_MODEL // (3 * 256)) * 256
N_EVEN = (DEPTH + 1) // 2
N_ODD = DEPTH // 2
A_WIDTH = A_HEADS * HEAD_DIM
AB_IN = 3 * A_WIDTH + 2 * RNN_WIDTH
AB_MIX = A_WIDTH + RNN_WIDTH
C_Q = C_HEADS * HEAD_DIM
C_KV = 2 * C_KV_HEADS * HEAD_DIM
C_IN = C_Q + 3 * C_KV + 3 * C_HEADS
DIL_Q_BLOCK = 128
SEL_Q_BLOCK = 64
WIN_Q_BLOCK = 128
NEG = -1e30
FORCE_BONUS = 1e3
EPS = 1e-6

kernel_name = 'hybrid_dilated_rglru_nsa_step'


def _rms_norm(x, g):
    xf = x.astype(jnp.float32)
    y = xf * lax.rsqrt(jnp.mean(xf * xf, axis=-1, keepdims=True) + EPS) * g.astype(jnp.float32)
    return y.astype(x.dtype)


def _alibi_slopes(n):
    return 2.0 ** (-8.0 * jnp.arange(1, n + 1, dtype=jnp.float32) / n)


def _masked_softmax(s, mask):
    s = jnp.where(mask, s, NEG)
    m = jnp.max(s, axis=-1, keepdims=True)
    p = jnp.where(mask, jnp.exp(s - m), 0.0)
    return p / jnp.maximum(jnp.sum(p, axis=-1, keepdims=True), 1e-30)


def _blocked_map(fn, q, pos, block):
    b, t = q.shape[0], q.shape[1]
    if t <= block or t % block:
        return fn((q, pos))
    nb = t // block
    qb = jnp.moveaxis(q.reshape((b, nb, block) + q.shape[2:]), 1, 0)
    out = lax.map(fn, (qb, pos.reshape(nb, block)))

    def unblock(o):
        o = jnp.moveaxis(o, 0, 1)
        return o.reshape((b, t) + o.shape[3:])
    return jax.tree_util.tree_map(unblock, out)


def _gather_pages(pool, page_table):
    rows = pool[page_table]
    return rows.reshape((page_table.shape[0], -1) + pool.shape[2:])


def _dilated_attention(q, kv_ctx, q_off):
    slopes = _alibi_slopes(A_HEADS)
    scale = HEAD_DIM ** -0.5

    def block_fn(args):
        qb, qc = args
        lses, outs = [], []
        for window, dil in DIL_PAIRS:
            dist = jnp.arange(window // dil + 1) * dil
            idx = qc[:, None] - dist[None, :]
            kvg = jnp.take(kv_ctx, jnp.maximum(idx, 0), axis=1)
            s = jnp.einsum('bthd,btkhd->bhtk', qb, kvg[:, :, :, 0], preferred_element_type=jnp.float32) * scale
            s = s - slopes[:, None, None] * dist.astype(jnp.float32)
            s = jnp.where(idx >= 0, s, NEG)
            m = jnp.max(s, axis=-1, keepdims=True)
            p = jnp.exp(s - m)
            l = jnp.sum(p, axis=-1, keepdims=True)
            outs.append(jnp.einsum('bhtk,btkhd->bthd', p / l, kvg[:, :, :, 1].astype(jnp.float32)))
            lses.append(jnp.swapaxes((m + jnp.log(l))[..., 0], 1, 2))
        wts = jax.nn.softmax(jnp.stack(lses), axis=0)[..., None]
        return jnp.sum(wts * jnp.stack(outs), axis=0).astype(qb.dtype)

    pos = q_off + jnp.arange(q.shape[1])
    return _blocked_map(block_fn, q, pos, DIL_Q_BLOCK)


def _rg_lru(xr, gate, conv_prev, h_prev, conv_w, conv_b, gate_a_w, gate_a_b, gate_x_w, gate_x_b, lru_lambda):
    b, t, _ = xr.shape
    xin = jnp.concatenate([conv_prev.astype(xr.dtype), xr], axis=1)
    xc = conv_b + sum(xin[:, k:k + t] * conv_w[k] for k in range(CONV_W))
    xb = xc.reshape(b, t, RNN_BLOCKS, RNN_WIDTH // RNN_BLOCKS)
    rg = jax.nn.sigmoid((jnp.einsum('btni,nij->btnj', xb, gate_a_w).reshape(b, t, RNN_WIDTH) + gate_a_b).astype(jnp.float32))
    ig = jax.nn.sigmoid((jnp.einsum('btni,nij->btnj', xb, gate_x_w).reshape(b, t, RNN_WIDTH) + gate_x_b).astype(jnp.float32))
    log_a = -LRU_C * rg * jax.nn.softplus(-lru_lambda.astype(jnp.float32))
    a = jnp.exp(log_a)
    u = jnp.sqrt(-jnp.expm1(2.0 * log_a)) * ig * xc.astype(jnp.float32)

    def step(hc, au):
        hc = au[0] * hc + au[1]
        return hc, hc
    h_last, hs = lax.scan(step, h_prev.astype(jnp.float32), (jnp.swapaxes(a, 0, 1), jnp.swapaxes(u, 0, 1)))
    y = jnp.swapaxes(hs, 0, 1) * jax.nn.gelu(gate.astype(jnp.float32))
    return y.astype(xr.dtype), xin[:, t:], h_last.astype(h_prev.dtype)


def _mixer_ab(h, w_in, w_out, conv_w, conv_b, gate_a_w, gate_a_b, gate_x_w, gate_x_b, lru_lambda, kv_buf, conv_prev, h_prev):
    b, t, _ = h.shape
    proj = h @ w_in
    q = proj[..., :A_WIDTH].reshape(b, t, A_HEADS, HEAD_DIM)
    kv = proj[..., A_WIDTH:3 * A_WIDTH].reshape(b, t, 2, A_HEADS, HEAD_DIM)
    xr = proj[..., 3 * A_WIDTH:3 * A_WIDTH + RNN_WIDTH]
    gate = proj[..., 3 * A_WIDTH + RNN_WIDTH:]
    if kv_buf is None:
        ctx, keep = kv, min(DIL_MAX, t)
    else:
        ctx, keep = jnp.concatenate([kv_buf.astype(kv.dtype), kv], axis=1), kv_buf.shape[1]
    o_att = _dilated_attention(q, ctx, ctx.shape[1] - t)
    o_rnn, new_conv, new_h = _rg_lru(xr, gate, conv_prev, h_prev, conv_w, conv_b, gate_a_w, gate_a_b, gate_x_w, gate_x_b, lru_lambda)
    y = jnp.concatenate([o_att.reshape(b, t, A_WIDTH), o_rnn], axis=-1) @ w_out
    return y, ctx[:, ctx.shape[1] - keep:], new_conv, new_h


def _compress(ctx, w_cmp, pe_cmp):
    b, tc = ctx.shape[:2]
    n_r = CMP_LEN // CMP_STRIDE
    n_ch = tc // CMP_STRIDE
    n_c = n_ch - n_r + 1
    chunks = ctx[:, :n_ch * CMP_STRIDE].reshape((b, n_ch, CMP_STRIDE) + ctx.shape[2:])
    w = w_cmp.reshape(n_r, CMP_STRIDE, 2, HEAD_DIM, HEAD_DIM)
    out = jnp.einsum('lcd,lcde->ce', pe_cmp, w_cmp)[None, None, :, None, :]
    for i in range(n_r):
        out = out + jnp.einsum('bnscgd,scde->bncge', chunks[:, i:i + n_c], w[i])
    end = jnp.arange(n_c) * CMP_STRIDE + CMP_LEN - 1
    return out.astype(ctx.dtype), end


def _cmp_to_sel(n_c, n_sel):
    cs = jnp.arange(n_c)[:, None] * CMP_STRIDE
    ss = jnp.arange(n_sel)[None, :] * SEL_BLOCK
    ov = jnp.minimum(cs + CMP_LEN, ss + SEL_BLOCK) - jnp.maximum(cs, ss)
    return jnp.maximum(ov, 0).astype(jnp.float32) / CMP_STRIDE


def _band_attention(q, kv_ctx, q_off, slopes):
    scale = HEAD_DIM ** -0.5
    kv_pad = jnp.pad(kv_ctx, ((0, 0), (WIN, 0), (0, 0), (0, 0), (0, 0)))

    def block_fn(args):
        qb, qc = args
        tb = qb.shape[1]
        kvb = lax.dynamic_slice_in_dim(kv_pad, qc[0], tb + WIN, axis=1)
        kc = qc[0] - WIN + jnp.arange(tb + WIN)
        dist = qc[:, None] - kc[None, :]
        ok = (dist >= 0) & (dist <= WIN) & (kc >= 0)[None, :]
        s = jnp.einsum('btgrd,bsgd->bgrts', qb, kvb[:, :, 0], preferred_element_type=jnp.float32) * scale
        s = s - slopes[:, :, None, None] * dist.astype(jnp.float32)
        p = _masked_softmax(s, ok)
        return jnp.einsum('bgrts,bsgd->btgrd', p, kvb[:, :, 1])

    pos = q_off + jnp.arange(q.shape[1])
    return _blocked_map(block_fn, q, pos, WIN_Q_BLOCK)


def _mixer_nsa(h, w_in, w_out, w_cmp, pe_cmp, past_cmp, past_sel, win_buf):
    b, t, _ = h.shape
    g, r = C_KV_HEADS, C_HEADS // C_KV_HEADS
    scale = HEAD_DIM ** -0.5
    slopes = _alibi_slopes(C_HEADS).reshape(g, r)
    proj = h @ w_in
    q = proj[..., :C_Q].reshape(b, t, g, r, HEAD_DIM)
    kv_cmp, kv_sel, kv_win = [proj[..., C_Q + i * C_KV:C_Q + (i + 1) * C_KV].reshape(b, t, 2, g, HEAD_DIM) for i in range(3)]
    gates = jax.nn.sigmoid(proj[..., C_Q + 3 * C_KV:].astype(jnp.float32)).reshape(b, t, g, r, 3)
    if past_cmp is None:
        ctx_cmp, ctx_sel = kv_cmp, kv_sel
    else:
        ctx_cmp = jnp.concatenate([past_cmp.astype(kv_cmp.dtype), kv_cmp], axis=1)
        ctx_sel = jnp.concatenate([past_sel.astype(kv_sel.dtype), kv_sel], axis=1)
    if win_buf is None:
        ctx_win, keep = kv_win, min(WIN, t)
    else:
        ctx_win, keep = jnp.concatenate([win_buf.astype(kv_win.dtype), kv_win], axis=1), win_buf.shape[1]
    tc = ctx_cmp.shape[1]
    pos = (tc - t) + jnp.arange(t)
    kvc, cmp_end = _compress(ctx_cmp, w_cmp, pe_cmp)
    kc, vc = kvc[:, :, 0], kvc[:, :, 1]
    n_c = kvc.shape[1]
    n_sel = -(-tc // SEL_BLOCK)
    n_top = min(SEL_TOP, n_sel)
    sel_blocks = jnp.pad(ctx_sel, ((0, 0), (0, n_sel * SEL_BLOCK - tc), (0, 0), (0, 0), (0, 0)))
    sel_blocks = sel_blocks.reshape(b, n_sel, SEL_BLOCK, 2, g, HEAD_DIM).transpose(0, 4, 1, 2, 3, 5)
    cmp2sel = _cmp_to_sel(n_c, n_sel)
    bi = jnp.arange(b)[:, None, None, None]
    gi = jnp.arange(g)[None, :, None, None]
    blk = jnp.arange(n_sel)
    tok_off = jnp.arange(SEL_BLOCK)

    def block_fn(args):
        qb, qp = args
        tb = qb.shape[1]
        s = jnp.einsum('btgrd,bngd->bgrtn', qb, kc, preferred_element_type=jnp.float32) * scale
        s = s - slopes[:, :, None, None] * (qp[:, None] - cmp_end[None, :]).astype(jnp.float32)
        p = _masked_softmax(s, cmp_end[None, :] <= qp[:, None])
        o_cmp = jnp.einsum('bgrtn,bngd->btgrd', p, vc)
        imp = jnp.einsum('bgrtn,nj->bgtj', p, cmp2sel)
        cb = qp // SEL_BLOCK
        blk_ok = blk[None, :] <= cb[:, None]
        forced = (blk[None, :] == 0) | (blk[None, :] == cb[:, None]) | (blk[None, :] == cb[:, None] - 1)
        score = jnp.where(blk_ok, imp + FORCE_BONUS * forced.astype(jnp.float32), NEG)
        top_s, top_i = lax.top_k(score, n_top)
        kvs = sel_blocks[bi, gi, top_i]
        tok = top_i[..., None] * SEL_BLOCK + tok_off
        ok = (top_s > 0.5 * NEG)[..., None] & (tok <= qp[None, None, :, None, None])
        ks = kvs[..., 0, :].reshape(b, g, tb, n_top * SEL_BLOCK, HEAD_DIM)
        vs = kvs[..., 1, :].reshape(b, g, tb, n_top * SEL_BLOCK, HEAD_DIM)
        dist = (qp[None, None, :, None, None] - tok).reshape(b, g, tb, n_top * SEL_BLOCK)
        s2 = jnp.einsum('btgrd,bgtkd->bgrtk', qb, ks, preferred_element_type=jnp.float32) * scale
        s2 = s2 - slopes[None, :, :, None, None] * dist[:, :, None].astype(jnp.float32)
        p2 = _masked_softmax(s2, ok.reshape(b, g, tb, n_top * SEL_BLOCK)[:, :, None])
        o_sel = jnp.einsum('bgrtk,bgtkd->btgrd', p2, vs)
        return o_cmp, o_sel

    o_cmp, o_sel = _blocked_map(block_fn, q, pos, SEL_Q_BLOCK)
    o_win = _band_attention(q, ctx_win, ctx_win.shape[1] - t, slopes)
    o = gates[..., 0:1] * o_cmp + gates[..., 1:2] * o_sel + gates[..., 2:3] * o_win
    y = o.astype(h.dtype).reshape(b, t, C_Q) @ w_out
    return y, kv_cmp, kv_sel, ctx_win[:, ctx_win.shape[1] - keep:]


def _swiglu(h, w1, w3, w2):
    return (jax.nn.silu(h @ w1) * (h @ w3)) @ w2


def setup_inputs(seed: int = 0) -> dict:
    key = jax.random.key(seed)
    ks = iter(jax.random.split(key, 40))

    def nrm(shape, s):
        return s * jax.random.normal(next(ks), shape, jnp.float32)
    n_pages = PAST_LEN // PAGE_SIZE
    n_used = DEC_BATCH * n_pages
    n_phys = n_used + n_used // 4
    l_dil = min(DIL_MAX, PAST_LEN)
    l_win = min(WIN, PAST_LEN)
    bw = RNN_WIDTH // RNN_BLOCKS
    a0 = jax.random.uniform(next(ks), (N_EVEN, RNN_WIDTH), jnp.float32, 0.9, 0.999) ** (1.0 / LRU_C)
    page_table = jax.random.permutation(next(ks), n_phys)[:n_used].reshape(DEC_BATCH, n_pages).astype(jnp.int32)
    return {
        'x_prompt': nrm((BATCH, SEQ, D_MODEL), 1.0),
        'x_sample': nrm((DEC_BATCH, DEC_SEQ, D_MODEL), 1.0),
        'cache_dil_kv': nrm((N_EVEN, DEC_BATCH, l_dil, 2, A_HEADS, HEAD_DIM), 1.0),
        'state_conv': nrm((N_EVEN, DEC_BATCH, CONV_W - 1, RNN_WIDTH), 1.0),
        'state_rnn': nrm((N_EVEN, DEC_BATCH, RNN_WIDTH), 0.5),
        'cache_win_kv': nrm((N_ODD, DEC_BATCH, l_win, 2, C_KV_HEADS, HEAD_DIM), 1.0),
        'cache_cmp_kv': nrm((N_ODD, n_phys, PAGE_SIZE, 2, C_KV_HEADS, HEAD_DIM), 1.0),
        'cache_sel_kv': nrm((N_ODD, n_phys, PAGE_SIZE, 2, C_KV_HEADS, HEAD_DIM), 1.0),
        'page_table': page_table,
        'norm_mix': 1.0 + nrm((DEPTH, D_MODEL), 0.1),
        'norm_ffn': 1.0 + nrm((DEPTH, D_MODEL), 0.1),
        'norm_out': 1.0 + nrm((D_MODEL,), 0.1),
        'w_in_ab': nrm((N_EVEN, D_MODEL, AB_IN), D_MODEL ** -0.5),
        'w_out_ab': nrm((N_EVEN, AB_MIX, D_MODEL), AB_MIX ** -0.5),
        'conv_w': nrm((N_EVEN, CONV_W, RNN_WIDTH), CONV_W ** -0.5),
        'conv_b': nrm((N_EVEN, RNN_WIDTH), 0.1),
        'gate_a_w': nrm((N_EVEN, RNN_BLOCKS, bw, bw), bw ** -0.5),
        'gate_a_b': nrm((N_EVEN, RNN_WIDTH), 0.1),
        'gate_x_w': nrm((N_EVEN, RNN_BLOCKS, bw, bw), bw ** -0.5),
        'gate_x_b': nrm((N_EVEN, RNN_WIDTH), 0.1),
        'lru_lambda': jnp.log(a0) - jnp.log1p(-a0),
        'w_in_c': nrm((N_ODD, D_MODEL, C_IN), D_MODEL ** -0.5),
        'w_out_c': nrm((N_ODD, C_Q, D_MODEL), C_Q ** -0.5),
        'w_cmp': nrm((N_ODD, CMP_LEN, 2, HEAD_DIM, HEAD_DIM), (CMP_LEN * HEAD_DIM) ** -0.5),
        'pe_cmp': nrm((N_ODD, CMP_LEN, 2, HEAD_DIM), 0.1),
        'ffn_w1': nrm((DEPTH, D_MODEL, FFN_HIDDEN), D_MODEL ** -0.5),
        'ffn_w3': nrm((DEPTH, D_MODEL, FFN_HIDDEN), D_MODEL ** -0.5),
        'ffn_w2': nrm((DEPTH, FFN_HIDDEN, D_MODEL), FFN_HIDDEN ** -0.5),
    }


def reference(x_prompt, x_sample, cache_dil_kv, state_conv, state_rnn, cache_win_kv, cache_cmp_kv, cache_sel_kv, page_table, norm_mix, norm_ffn, norm_out, w_in_ab, w_out_ab, conv_w, conv_b, gate_a_w, gate_a_b, gate_x_w, gate_x_b, lru_lambda, w_in_c, w_out_c, w_cmp, pe_cmp, ffn_w1, ffn_w3, ffn_w2):
    yp, ys = x_prompt, x_sample
    bp = x_prompt.shape[0]
    dil_p, dil_s, conv_p, conv_s, rnn_p, rnn_s = [], [], [], [], [], []
    win_p, win_s, cmp_p, cmp_s, sel_p, sel_s = [], [], [], [], [], []
    for layer in range(DEPTH):
        hp = _rms_norm(yp, norm_mix[layer])
        hs = _rms_norm(ys, norm_mix[layer])
        li = layer // 2
        if layer % 2 == 0:
            wts = (w_in_ab[li], w_out_ab[li], conv_w[li], conv_b[li], gate_a_w[li], gate_a_b[li], gate_x_w[li], gate_x_b[li], lru_lambda[li])
            zc = jnp.zeros((bp, CONV_W - 1, RNN_WIDTH), x_prompt.dtype)
            zh = jnp.zeros((bp, RNN_WIDTH), x_prompt.dtype)
            op, kvp, cvp, hpn = _mixer_ab(hp, *wts, None, zc, zh)
            osm, kvs, cvs, hsn = _mixer_ab(hs, *wts, cache_dil_kv[li], state_conv[li], state_rnn[li])
            dil_p.append(kvp)
            dil_s.append(kvs)
            conv_p.append(cvp)
            conv_s.append(cvs)
            rnn_p.append(hpn)
            rnn_s.append(hsn)
        else:
            wts = (w_in_c[li], w_out_c[li], w_cmp[li], pe_cmp[li])
            op, cp, sp, wp = _mixer_nsa(hp, *wts, None, None, None)
            past_cmp = _gather_pages(cache_cmp_kv[li], page_table)
            past_sel = _gather_pages(cache_sel_kv[li], page_table)
            osm, cs, ss, wsn = _mixer_nsa(hs, *wts, past_cmp, past_sel, cache_win_kv[li])
            cmp_p.append(cp)
            cmp_s.append(cs)
            sel_p.append(sp)
            sel_s.append(ss)
            win_p.append(wp)
            win_s.append(wsn)
        yp = yp + op
        ys = ys + osm
        yp = yp + _swiglu(_rms_norm(yp, norm_ffn[layer]), ffn_w1[layer], ffn_w3[layer], ffn_w2[layer])
        ys = ys + _swiglu(_rms_norm(ys, norm_ffn[layer]), ffn_w1[layer], ffn_w3[layer], ffn_w2[layer])
    y_prompt = _rms_norm(yp, norm_out)
    y_sample = _rms_norm(ys, norm_out)
    dil_kv_p = jnp.stack(dil_p)
    dil_kv_s = jnp.stack(dil_s)
    conv_pr = jnp.stack(conv_p)
    conv_sa = jnp.stack(conv_s)
    rnn_pr = jnp.stack(rnn_p)
    rnn_sa = jnp.stack(rnn_s)
    win_kv_p = jnp.stack(win_p)
    win_kv_s = jnp.stack(win_s)
    cmp_kv_p = jnp.stack(cmp_p)
    cmp_kv_s = jnp.stack(cmp_s)
    sel_kv_p = jnp.stack(sel_p)
    sel_kv_s = jnp.stack(sel_s)
    return (y_prompt, y_sample, dil_kv_p, dil_kv_s, conv_pr, conv_sa, rnn_pr, rnn_sa, win_kv_p, win_kv_s, cmp_kv_p, cmp_kv_s, sel_kv_p, sel_kv_s)
```

```python
import numpy as np
from contextlib import ExitStack
import ml_dtypes
import concourse.bass as bass
import concourse.mybir as mybir
from concourse.bass_utils import run_bass_kernel_spmd

F32 = mybir.dt.float32
BF16 = mybir.dt.bfloat16
I32 = mybir.dt.int32
AF = mybir.ActivationFunctionType
ALU = mybir.AluOpType
AX = mybir.AxisListType

D = 1024
SEQ = 4096
HD = 64
AH = 8
EPS = 1e-6
FFN = 2816
NAUG = 9
KR = HD + NAUG
NEGM = -30000.0


class Buf:
    __slots__ = ("name", "lws", "rd", "dsem", "dsem2", "multi", "isdma", "excl")

    def __init__(self, name, dsem=None, multi=False):
        self.name = name
        self.lws = {}
        self.rd = []
        self.dsem = dsem
        self.dsem2 = None
        self.isdma = dsem is not None
        self.multi = multi
        self.excl = False


class DmaSem:
    def __init__(self, sem):
        self.sem = sem
        self.count = 0


class Op:
    __slots__ = ("eng", "fn", "deps", "signal", "count", "dsem", "dval", "isdma", "key")

    def __init__(self, eng, fn, deps, isdma=False, dsem=None):
        self.eng = eng
        self.fn = fn
        self.deps = deps
        self.signal = False
        self.count = None
        self.isdma = isdma
        self.dsem = dsem
        self.dval = None
        self.key = ("d", id(dsem)) if isdma else ("e", eng)


class FW:
    ENGS = ("pe", "act", "dve", "pool", "sp")

    def __init__(self, nc, stack):
        self.nc = nc
        self.stack = stack
        self.ops = {e: [] for e in self.ENGS}
        self.ndsem = 0
        self._dsems = []
        self._free = []
        self._free_sw = []
        self._phase = []
        self.last = {e: None for e in self.ENGS}

    def new_dsem(self, sw=False):
        fl = self._free_sw if sw else self._free
        if fl:
            d = fl.pop()
        else:
            self.ndsem += 1
            d = DmaSem(self.stack.enter_context(self.nc.semaphore("ds_%d" % self.ndsem)))
            d.sw = sw
            self._dsems.append(d)
        self._phase.append(d)
        return d

    def buf(self, name, dma=False):
        return Buf(name, self.new_dsem() if dma else None)

    def dbuf(self, name):
        return Buf(name, None, multi=True)

    def _deps(self, key, r, w):
        deps = []
        for b in r:
            deps.extend(b.lws.values())
            if b.excl:
                deps.extend(d for d in b.rd if d.key != key)
        for b in w:
            if not b.multi:
                deps.extend(b.lws.values())
            deps.extend(b.rd)
        out = []
        seen = set()
        for d in deps:
            if id(d) in seen:
                continue
            seen.add(id(d))
            if d.key == key and (key[0] == "d" or key[1] == "pe"):
                continue
            out.append(d)
        return out

    def _commit(self, op, r, w):
        for b in r:
            b.rd.append(op)
        for b in w:
            if b.multi and not b.rd:
                b.lws[op.key] = op
            else:
                b.lws = {op.key: op}
            b.rd = []
        for d in op.deps:
            d.signal = True
        if not op.isdma:
            self.last[op.eng] = op

    def op(self, eng, fn, r=(), w=()):
        o = Op(eng, fn, None)
        o.deps = self._deps(o.key, r, w)
        self.ops[eng].append(o)
        self._commit(o, r, w)
        return o

    def dma(self, eng, out, in_, r=(), w=(), **kw):
        dsem = None
        for b in list(w) + list(r):
            if b.isdma:
                if eng != "sp":
                    if b.dsem2 is None:
                        b.dsem2 = self.new_dsem(sw=False)
                    dsem = b.dsem2
                else:
                    dsem = b.dsem
                break
        assert dsem is not None, "dma needs an sbuf-side buf with dma=True"
        o = Op(eng, (lambda e: e.dma_start(out=out, in_=in_, **kw)), None, isdma=True, dsem=dsem)
        o.deps = self._deps(o.key, r, w)
        dsem.count += 16
        o.dval = dsem.count
        self.ops[eng].append(o)
        self._commit(o, r, w)
        return o

    def barrier(self):
        lasts = [o for o in self.last.values() if o is not None]
        dm = []
        for ds in self._dsems:
            if ds.count > 0:
                f = Op("sp", None, [], isdma=True, dsem=ds)
                f.dval = ds.count
                dm.append(f)
        for e in self.ENGS:
            o = Op(e, None, [d for d in lasts if d.eng != e] + dm)
            for d in o.deps:
                d.signal = True
            self.ops[e].append(o)
        for d in self._phase:
            (self._free_sw if d.sw else self._free).append(d)
        self._phase = []

    def keep(self, buf):
        if buf.dsem in self._phase:
            self._phase.remove(buf.dsem)

    def emit(self):
        nc = self.nc
        for e in self.ENGS:
            c = 0
            for o in self.ops[e]:
                if o.fn is not None and not o.isdma and o.signal:
                    c += 1
                    o.count = c
        self.max_counts = {e: max([o.count or 0 for o in self.ops[e]] + [0]) for e in self.ENGS}
        EPOCH = 2000
        self.esems = {}
        for e in self.ENGS:
            n = (self.max_counts[e] + EPOCH - 1) // EPOCH
            self.esems[e] = [self.stack.enter_context(nc.semaphore("es_%s_%d" % (e, i))) for i in range(n)]

        def semval(op):
            c = op.count - 1
            return self.esems[op.eng][c // EPOCH], c % EPOCH + 1
        self.n_ops = {e: len(self.ops[e]) for e in self.ENGS}
        engmap = {"pe": "tensor", "act": "scalar", "dve": "vector", "pool": "gpsimd", "sp": "sync"}
        final = [(ds.sem, ds.count) for ds in self._dsems]
        with nc.Block() as block:
            for e in self.ENGS:
                ops = self.ops[e]

                def body(eng, ops=ops, e=e):
                    seen = {}
                    for o in ops:
                        for d in o.deps:
                            if d.isdma:
                                s, v = d.dsem.sem, d.dval
                            else:
                                s, v = semval(d)
                            k = id(s)
                            if seen.get(k, 0) >= v:
                                continue
                            seen[k] = v
                            eng.wait_ge(s, v)
                        if o.fn is None:
                            continue
                        ins = o.fn(eng)
                        if o.isdma:
                            ins.then_inc(o.dsem.sem, 16)
                        elif o.signal:
                            ins.then_inc(semval(o)[0], 1)
                    if e == "sp":
                        for s, v in final:
                            if v > 0:
                                eng.wait_ge(s, v)
                getattr(block, engmap[e])(body)


def _split3(x):
    x = np.asarray(x, np.float64)
    hi = x.astype(ml_dtypes.bfloat16).astype(np.float64)
    mid = (x - hi).astype(ml_dtypes.bfloat16).astype(np.float64)
    lo = (x - hi - mid).astype(ml_dtypes.bfloat16).astype(np.float64)
    return [hi, mid, lo]


def _slopes(n):
    return 2.0 ** (-8.0 * np.arange(1, n + 1, dtype=np.float64) / n)


def q_aug_rows(slope, tq):
    tq = np.asarray(tq, np.float64)
    rows = _split3(-8.0 * slope * tq)
    for v in _split3(8.0 * 128.0 * slope):
        rows.append(np.full(tq.shape, v))
    for v in _split3(8.0 * slope):
        rows.append(np.full(tq.shape, v))
    return np.stack(rows).astype(ml_dtypes.bfloat16)


def k_aug_rows(tk):
    tk = np.asarray(tk, np.int64)
    a = (tk // 128).astype(np.float64)
    b = (tk % 128).astype(np.float64)
    one = np.ones(tk.shape)
    return np.stack([one, one, one, a, a, a, b, b, b]).astype(ml_dtypes.bfloat16)


def host_consts():
    c = {}
    c["ident"] = np.eye(128, dtype=np.float32)
    ik = np.arange(128)[:, None]
    iq = np.arange(128)[None, :]
    cm = np.zeros((128, 256), np.float32)
    cm[:, 0:128] = np.where(ik >= iq, 0.0, NEGM)
    cm[:, 128:256] = np.where(ik <= iq, 0.0, NEGM)
    c["cmask"] = cm.astype(ml_dtypes.bfloat16)
    s8 = _slopes(AH)
    c["qaug0"] = np.stack([q_aug_rows(s8[h], np.arange(SEQ)) for h in range(AH)])
    c["kaug"] = k_aug_rows(np.arange(SEQ))
    s16 = _slopes(16)
    c["qaug1"] = np.stack([np.stack([q_aug_rows(s16[g * 4 + r], np.arange(SEQ)) for r in range(4)], axis=1) for g in range(4)])
    nn = np.arange(256)
    c["kaugc"] = k_aug_rows(16 * nn + 31)
    mc = np.zeros((128, 33, 128), np.float32)
    iq = np.arange(128)[None, :]
    for qt in range(17):
        n = np.arange(128)[:, None]
        mc[:, qt, :] = np.where(16 * n + 31 <= 128 * qt + iq, 0.0, NEGM)
    for qt in range(16, 32):
        n = 128 + np.arange(128)[:, None]
        mc[:, 17 + qt - 16, :] = np.where(16 * n + 31 <= 128 * qt + iq, 0.0, NEGM)
    c["mc"] = mc.astype(ml_dtypes.bfloat16)
    ts = np.zeros((128, 32, 64), np.float32)
    for qt in range(32):
        tq = 128 * qt + np.arange(128)[:, None]
        cb = tq // 64
        j = np.arange(64)[None, :]
        ok = j <= cb
        forced = (j == 0) | (j == cb) | (j == cb - 1)
        ts[:, qt, :] = np.where(ok, np.where(forced, 1000.0, 0.0), -1e30)
    c["tsel"] = ts
    n = np.arange(256)[:, None]
    j = np.arange(64)[None, :]
    ov = np.minimum(16 * n + 32, 64 * j + 64) - np.maximum(16 * n, 64 * j)
    c2s = (np.maximum(ov, 0) / 16.0).astype(np.float32)
    c2s[255] = 0
    c["c2s"] = np.ascontiguousarray(c2s.reshape(2, 128, 64).transpose(1, 0, 2)).astype(ml_dtypes.bfloat16)
    rows = np.arange(17 * 128)[:, None]
    qrow = 2048 + np.arange(8)[None, :]
    dist = qrow - rows
    cntm = np.zeros((17 * 128, 8), np.float32)
    for (wdw, dl) in ((128, 1), (512, 4), (2048, 16)):
        cntm += ((dist >= 0) & (dist <= wdw) & (dist % dl == 0) & (rows < 2056)).astype(np.float32)
    c["cnt_s"] = np.ascontiguousarray(cntm.reshape(17, 128, 8).transpose(1, 0, 2)).astype(ml_dtypes.bfloat16)
    c["kaug_s"] = k_aug_rows(np.arange(17 * 128))
    c["qaug0_s"] = np.stack([q_aug_rows(s8[h], 2048 + np.arange(8)) for h in range(AH)], axis=1)
    c["kaug_ss"] = k_aug_rows(np.arange(8320))
    c["kaug_sw"] = k_aug_rows(7680 + np.arange(640))
    c["kaugc_s"] = k_aug_rows(16 * np.arange(512) + 31)
    c["qaug1_s"] = np.stack([np.stack([q_aug_rows(s16[g * 4 + r], 8192 + np.arange(8)) for r in range(4)], axis=1) for g in range(4)])
    n5 = np.arange(512)[:, None]
    j5 = np.arange(129)[None, :]
    ov5 = np.minimum(16 * n5 + 32, 64 * j5 + 64) - np.maximum(16 * n5, 64 * j5)
    c2s5 = (np.maximum(ov5, 0) / 16.0).astype(np.float32)
    c2s5[511] = 0
    c["c2s_s"] = np.ascontiguousarray(c2s5.reshape(4, 128, 129).transpose(1, 0, 2)).astype(ml_dtypes.bfloat16)
    ts5 = np.zeros((8, 129), np.float32)
    ts5[:, [0, 127, 128]] = 1000.0
    c["tsel_s"] = ts5
    e30s = np.zeros((128, 8192), np.float32)
    for jj in range(128):
        e30s[jj, jj * 64:(jj + 1) * 64] = 30000.0
    c["e30s"] = e30s.astype(ml_dtypes.bfloat16)
    jj8 = np.arange(8)[:, None]
    ii8 = np.arange(8)[None, :]
    c["cm8"] = np.tile(np.where(jj8 <= ii8, 0.0, NEGM), (1, 4)).astype(ml_dtypes.bfloat16)
    j128 = np.arange(128)[:, None]
    c["wm0"] = np.tile(np.where(j128 >= ii8, 0.0, NEGM), (1, 4)).astype(ml_dtypes.bfloat16)
    c["iota_p"] = np.arange(128, dtype=np.float32).reshape(128, 1)
    e30 = np.zeros((64, SEQ), np.float32)
    for jj in range(64):
        e30[jj, jj * 64:(jj + 1) * 64] = 30000.0
    c["e30"] = e30.astype(ml_dtypes.bfloat16)
    return c


class KB:
    def __init__(self, stage):
        self.stage = stage
        self.nc = bass.Bass("TRN2", target_bir_lowering=False)
        self.ins = {}
        self.outs = {}

    def din(self, name, shape, dt=F32):
        ap = self.nc.dram_tensor(name, list(shape), dt, kind="ExternalInput").ap()
        self.ins[name] = ap
        return ap

    def dout(self, name, shape, dt=F32):
        ap = self.nc.dram_tensor(name, list(shape), dt, kind="ExternalOutput").ap()
        self.outs[name] = ap
        return ap

    def dscr(self, name, shape, dt):
        return self.nc.dram_tensor(name, list(shape), dt).ap()

    def salloc(self, cols, dt, parts=128):
        nb = cols * (2 if dt == BF16 else 4)
        nb = (nb + 63) // 64 * 64
        off = self.sp_off
        self.sp_off += nb
        assert self.sp_off <= self.arena_bytes, ("sbuf overflow", self.sp_off)
        v = self.arena[0:parts, off // 4:(off + nb) // 4]
        if dt != F32:
            v = v.bitcast(dt)
        return v[:, 0:cols]

    def build(self):
        nc = self.nc
        with ExitStack() as st:
            self.fw = fw = FW(nc, st)
            self.arena_bytes = 45056 * 4
            self.arena = st.enter_context(nc.sbuf_tensor("arena", [128, 45056], F32))
            self.sp_off = 0
            self.ps = [st.enter_context(nc.psum_tensor("ps%d" % i, [128, 512], F32)) for i in range(8)]
            self.bps = [fw.buf("ps%d" % i) for i in range(8)]
            for b in self.bps:
                b.excl = True
            self.body()
            fw.emit()
        return nc

    def body(self):
        nc, fw = self.nc, self.fw
        S = self.stage
        xp = self.din("xp", [SEQ, D])
        w_in_ab = self.din("w_in_ab", [D, 2560])
        w_out_ab = self.din("w_out_ab", [D, D])
        w1 = self.din("ffn_w1", [2, D, FFN])
        w3 = self.din("ffn_w3", [2, D, FFN])
        w2 = self.din("ffn_w2", [2, FFN, D])
        w_out_c = self.din("w_out_c", [D, D])
        w_in_c = self.din("w_in_c", [D, 2048])
        w_in_c_tm = self.din("w_in_c_tm", [D, 1536 + 256])
        gains = self.din("gains", [128, 5 * 8])
        rnnp = self.din("rnnp", [128, 4 * 9])
        gaw = self.din("gate_a_w", [8, 64, 64])
        gxw = self.din("gate_x_w", [8, 64, 64])
        c_ident = self.din("ident", [128, 128])
        c_cmask = self.din("cmask", [128, 256], BF16)
        c_qaug0 = self.din("qaug0", [AH, NAUG, SEQ], BF16)
        c_kaug = self.din("kaug", [NAUG, SEQ], BF16)
        c_qaug1 = self.din("qaug1", [4, NAUG, 4, SEQ], BF16)
        c_kaugc = self.din("kaugc", [NAUG, 256], BF16)
        c_mc = self.din("mc", [128, 33, 128], BF16)
        c_tsel = self.din("tsel", [128, 32, 64])
        c_c2s = self.din("c2s", [128, 2, 64], BF16)
        c_e30 = self.din("e30", [64, SEQ], BF16)
        w_cmp_t = self.din("w_cmp_t", [64, 32 * 2 * 64])
        pe_cmp_t = self.din("pe_cmp_t", [64, 64])

        xs = self.din("xs", [32, D])
        cache_dil = self.din("cache_dil", [4, 2048, 1024])
        st_conv = self.din("st_conv", [128, 4 * 4 * 3])
        st_rnn = self.din("st_rnn", [128, 4 * 4])
        cache_win = self.din("cache_win", [4, 512, 512])
        c_cnt_s = self.din("cnt_s", [128, 17, 8], BF16)
        c_kaug_s = self.din("kaug_s", [NAUG, 17 * 128], BF16)
        c_qaug0_s = self.din("qaug0_s", [NAUG, 8, 8], BF16)
        NPR = 2560 * 128
        pool_cmp = self.din("pool_cmp", [NPR, 512]) if S >= 6 else None
        pool_sel = self.din("pool_sel", [NPR, 512]) if S >= 6 else None
        ptab = self.din("ptab", [1, 256], I32) if S >= 6 else None
        if S >= 6:
            c_kaug_ss = self.din("kaug_ss", [NAUG, 8320], BF16)
            c_kaug_sw = self.din("kaug_sw", [NAUG, 640], BF16)
            c_kaugc_s = self.din("kaugc_s", [NAUG, 512], BF16)
            c_qaug1_s = self.din("qaug1_s", [4, NAUG, 4, 8], BF16)
            c_c2s_s = self.din("c2s_s", [128, 4, 129], BF16)
            c_tsel_s = self.din("tsel_s", [8, 129])
            c_e30s = self.din("e30s", [128, 8192], BF16)
            c_cm8 = self.din("cm8", [8, 32], BF16)
            c_wm0 = self.din("wm0", [128, 32], BF16)
            c_iota_p = self.din("iota_p", [128, 1])
        s_cT_cmp = self.dscr("s_cT_cmp", [4, 512, 8320], BF16)
        s_cT_sel = self.dscr("s_cT_sel", [4, 256, 8320], BF16)
        s_cV_sel = self.dscr("s_cV_sel", [4, 8320, 256], BF16)
        s_cT_win = self.dscr("s_cT_win", [4, 256, 640], BF16)
        s_cV_win = self.dscr("s_cV_win", [4, 640, 256], BF16)
        s_sQT1 = self.dscr("s_sQT1", [1024, 32], BF16)
        s_sG = self.dscr("s_sG", [32, 48], F32)
        s_sOT1 = self.dscr("s_sOT1", [1024, 32], BF16)
        o_y_s = self.dout("y_s", [32, D])
        o_dil_kv_s = self.dout("dil_kv_s", [4, 2048, 1024])
        o_conv_s = self.dout("conv_s", [4, 3, 512])
        o_rnn_s = self.dout("rnn_s", [4, 512])
        o_win_kv_s = self.dout("win_kv_s", [4, 512, 512])
        o_cmp_kv_s = self.dout("cmp_kv_s", [32, 512])
        o_sel_kv_s = self.dout("sel_kv_s", [32, 512])
        s_sQT0 = self.dscr("s_sQT0", [512, 32], BF16)
        s_sKV = self.dscr("s_sKV", [32, 1024], BF16)
        s_sXR = self.dscr("s_sXR", [512, 32], F32)
        s_sGT = self.dscr("s_sGT", [512, 32], F32)
        s_sOT0 = self.dscr("s_sOT0", [1024, 32], BF16)
        s_sX1 = self.dscr("s_sX1", [32, D], F32)
        o_dil_kv_p = self.dout("dil_kv_p", [2048, 1024])
        o_conv_p = self.dout("conv_p", [3, 512])
        o_rnn_p = self.dout("rnn_p", [512])
        o_cmp_kv_p = self.dout("cmp_kv_p", [SEQ, 512])
        o_sel_kv_p = self.dout("sel_kv_p", [SEQ, 512])
        o_win_kv_p = self.dout("win_kv_p", [512, 512])
        o_y_p = self.dout("y_p", [SEQ, D])

        def wscr(name, K, N):
            return self.dscr(name, [N // 256, 128, (K // 128) * 256], BF16)
        wb_in_ab = wscr("wb_in_ab", D, 2560)
        wb_out_ab = wscr("wb_out_ab", D, D)
        wb_w1 = [wscr("wb_w1_%d" % l, D, FFN) for l in range(2)]
        wb_w3 = [wscr("wb_w3_%d" % l, D, FFN) for l in range(2)]
        wb_w2 = [wscr("wb_w2_%d" % l, FFN, D) for l in range(2)]
        wb_out_c = wscr("wb_out_c", D, D)
        wb_in_c = wscr("wb_in_c", D, 2048)
        wb_in_c_tm = wscr("wb_in_c_tm", D, 1792)
        s_QT0 = self.dscr("s_QT0", [512, SEQ], BF16)
        s_KT0 = self.dscr("s_KT0", [512, SEQ], BF16)
        s_V0 = self.dscr("s_V0", [SEQ, 512], BF16)
        s_XR = self.dscr("s_XR", [512, SEQ], F32)
        s_GT = self.dscr("s_GT", [512, SEQ], F32)
        s_OT0 = self.dscr("s_OT0", [1024, SEQ], BF16)
        s_X1 = self.dscr("s_X1", [SEQ, D], F32)
        DBG = (S == 4.5)
        s_OT1 = (self.dout if DBG else self.dscr)("s_OT1", [1024, SEQ], BF16)
        s_QT1 = (self.dout if DBG else self.dscr)("s_QT1", [1024, SEQ], BF16)
        s_KC = self.dscr("s_KC", [512, SEQ], BF16)
        s_KS = self.dscr("s_KS", [256, SEQ], BF16)
        s_KW = self.dscr("s_KW", [256, SEQ], BF16)
        s_VS = self.dscr("s_VS", [SEQ, 256], BF16)
        s_VW = self.dscr("s_VW", [SEQ, 256], BF16)
        s_G = (self.dout if DBG else self.dscr)("s_G", [SEQ, 48], F32)
        norm_out = self.din("norm_out", [D])
        b_scr = {n: fw.dbuf(n) for n in ["w", "QT0", "KT0", "V0", "XR", "GT", "OT0", "X1", "OT1", "QT1", "KC", "KS", "KW", "VS", "VW", "G", "sQT0", "sKV", "sXR", "sGT", "sOT0", "sX1", "cT_cmp", "cT_sel", "cV_sel", "cT_win", "cV_win", "sQT1", "sG", "sOT1", "ocmp", "osel", "owin"]}

        ident_f = self.salloc(128, F32)
        ident_b = self.salloc(128, BF16)
        cmask = self.salloc(256, BF16)
        ones_b = self.salloc(128, BF16)
        gains_t = self.salloc(40, F32)
        b_const = fw.buf("const", dma=True)
        fw.dma("sp", ident_f, c_ident, w=[b_const])
        fw.dma("sp", cmask, c_cmask, w=[b_const])
        fw.dma("sp", gains_t, gains, w=[b_const])
        b_const2 = fw.buf("const2")
        fw.op("dve", lambda e: e.tensor_copy(ident_b, ident_f), r=[b_const], w=[b_const2])
        fw.op("dve", lambda e: e.memset(ones_b, 1.0), w=[b_const2])
        self.ident_b, self.ident_f, self.cmask, self.ones_b = ident_b, ident_f, cmask, ones_b
        self.b_consts = [b_const, b_const2]
        base_off = self.sp_off

        wmark = self.sp_off
        w_stg = [self.salloc(22 * 256, F32) for _ in range(2)]
        w_stb = [self.salloc(22 * 256, BF16) for _ in range(2)]
        w_bs = [fw.buf("wst%d" % i, dma=True) for i in range(2)]
        w_bb = [fw.buf("wsb%d" % i, dma=True) for i in range(2)]
        wcount = [0]

        def wprep(src, dst, K, N, tag):
            KC = K // 128
            srcv = src.rearrange("(kc p) n -> p kc n", p=128)
            for nb in range(N // 256):
                i = wcount[0] % 2
                wcount[0] += 1
                stg_i = w_stg[i][:, 0:KC * 256]
                stb_i = w_stb[i][:, 0:KC * 256]
                sv = stg_i.rearrange("p (kc n) -> p kc n", kc=KC)
                for k0 in range(0, KC, 8):
                    k1 = min(KC, k0 + 8)
                    fw.dma("sp", sv[:, k0:k1, :], srcv[:, k0:k1, nb * 256:(nb + 1) * 256], w=[w_bs[i]])
                eng = "dve" if i == 0 else "pool"
                fw.op(eng, lambda e, stb_i=stb_i, stg_i=stg_i: e.tensor_copy(stb_i, stg_i), r=[w_bs[i]], w=[w_bb[i]])
                fw.dma("act", dst[nb], stb_i, r=[w_bb[i]], w=[b_scr["w"]])

        wprep(w_in_ab, wb_in_ab, D, 2560, "inab")
        wprep(w_out_ab, wb_out_ab, D, D, "outab")
        if S >= 2.7:
            for l in range(2 if S >= 4 else 1):
                wprep(w1[l], wb_w1[l], D, FFN, "w1")
                wprep(w3[l], wb_w3[l], D, FFN, "w3")
                wprep(w2[l], wb_w2[l], FFN, D, "w2")
            wprep(w_in_c, wb_in_c, D, 2048, "inc")
            wprep(w_out_c, wb_out_c, D, D, "outc")
            wprep(w_in_c_tm, wb_in_c_tm, D, 1792, "inctm")
        self.sp_off = wmark
        fw.barrier()

        def load_w(wb, K, nblocks, name):
            KC = K // 128
            t = self.salloc(nblocks * KC * 256, BF16)
            b = fw.buf(name, dma=True)
            tv = t.rearrange("p (nb x) -> p nb x", nb=nblocks)
            for nb in range(nblocks):
                fw.dma("sp", tv[:, nb, :], wb[nb], r=[b_scr["w"]], w=[b])
            return t.rearrange("p (nb kc n) -> p nb kc n", nb=nblocks, kc=KC), b

        def rmsnorm_T(xt, b_xt, ntok, gidx, hT, b_hT, col0, tmp):
            sq, ss, xn, bt = tmp
            fw.op("act", lambda e: e.activation(out=sq[0:ntok, :], in_=xt[0:ntok, :], func=AF.Square, accum_out=ss[0:ntok, 0:1]), r=[b_xt], w=[bt])
            fw.op("dve", lambda e: e.tensor_scalar(out=ss[0:ntok, 1:2], in0=ss[0:ntok, 0:1], scalar1=1.0 / D, scalar2=EPS, op0=ALU.mult, op1=ALU.add), r=[bt], w=[bt])
            fw.op("act", lambda e: e.activation(out=ss[0:ntok, 2:3], in_=ss[0:ntok, 1:2], func=AF.Sqrt), r=[bt], w=[bt])
            fw.op("dve", lambda e: e.reciprocal(out=ss[0:ntok, 3:4], in_=ss[0:ntok, 2:3]), r=[bt], w=[bt])
            fw.op("dve", lambda e: e.tensor_scalar(out=xn[0:ntok, :], in0=xt[0:ntok, :], scalar1=ss[0:ntok, 3:4], scalar2=None, op0=ALU.mult), r=[b_xt, bt], w=[bt])
            pst = self.ps[7][:, :].bitcast(BF16)
            for kc in range(8):
                fw.op("pe", lambda e, kc=kc: e.transpose(pst[:, kc * 128:kc * 128 + ntok], xn[0:ntok, kc * 128:(kc + 1) * 128], ident_b[0:ntok, 0:ntok]), r=[bt, b_const2], w=[self.bps[7]])
            g = gains_t[:, gidx * 8:(gidx + 1) * 8]
            fw.op("dve", lambda e: e.tensor_tensor(out=hT[:, :, col0:col0 + ntok], in0=pst.rearrange("p (kc t) -> p kc t", kc=8)[:, :, 0:ntok],
                                                     in1=g.rearrange("p (kc o) -> p kc o", o=1).to_broadcast([128, 8, ntok]), op=ALU.mult),
                  r=[self.bps[7], b_const], w=[b_hT])

        self.load_w = load_w
        self.rmsnorm_T = rmsnorm_T

        if S == 9:
            self.sample_path(locals())
            return

        mark = self.sp_off
        win, b_win = load_w(wb_in_ab, D, 10, "win_ab")
        xts = [self.salloc(D, F32) for _ in range(2)]
        b_xts = [fw.buf("xt%d" % i, dma=True) for i in range(2)]
        tmp = (self.salloc(D, F32), self.salloc(4, F32), self.salloc(D, BF16), fw.buf("nrm_tmp"))
        hT = self.salloc(8 * 512, BF16).rearrange("p (kc t) -> p kc t", kc=8)
        b_hT = fw.buf("hT")
        stg_b = [self.salloc(512, BF16) for _ in range(2)]
        stg_f = [self.salloc(512, F32) for _ in range(2)]
        b_stg_b = [fw.buf("stgb%d" % i, dma=True) for i in range(2)]
        b_stg_f = [fw.buf("stgf%d" % i, dma=True) for i in range(2)]
        kv_f = [self.salloc(1024, F32) for _ in range(2)]
        kv_b = [self.salloc(512, BF16) for _ in range(2)]
        b_kv_f = [fw.buf("kvf%d" % i, dma=True) for i in range(2)]
        b_kv_b = [fw.buf("kvb%d" % i, dma=True) for i in range(2)]
        cnt = 0
        for blk in range(8):
            for j in range(4):
                t0 = blk * 512 + j * 128
                i = (blk * 4 + j) % 2
                fw.dma("sp", xts[i], xp[t0:t0 + 128, :], w=[b_xts[i]])
                rmsnorm_T(xts[i], b_xts[i], 128, 0, hT, b_hT, j * 128, tmp)
            for oc in list(range(0, 8)) + list(range(12, 20)):
                pi = cnt % 2
                cnt += 1
                pst_, bp = self.ps[pi], self.bps[pi]
                nb, sub = oc // 2, oc % 2
                for kc in range(8):
                    fw.op("pe", lambda e, kc=kc, nb=nb, sub=sub, pst_=pst_: e.matmul(pst_[:, :], lhsT=win[:, nb, kc, sub * 128:(sub + 1) * 128], rhs=hT[:, kc, :], start=(kc == 0), stop=(kc == 7)),
                          r=[b_win, b_hT], w=[bp])
                if oc < 8:
                    si = oc % 2
                    fw.op("act", lambda e, si=si, pst_=pst_: e.copy(stg_b[si], pst_[:, :]), r=[bp], w=[b_stg_b[si]])
                    dst = s_QT0 if oc < 4 else s_KT0
                    nm = "QT0" if oc < 4 else "KT0"
                    r0 = (oc % 4) * 128
                    fw.dma("act", dst[r0:r0 + 128, blk * 512:(blk + 1) * 512], stg_b[si], r=[b_stg_b[si]], w=[b_scr[nm]])
                else:
                    si = oc % 2
                    fw.op("dve", lambda e, si=si, pst_=pst_: e.tensor_copy(stg_f[si], pst_[:, :]), r=[bp], w=[b_stg_f[si]])
                    dst = s_XR if oc < 16 else s_GT
                    nm = "XR" if oc < 16 else "GT"
                    r0 = (oc % 4) * 128
                    fw.dma("act", dst[r0:r0 + 128, blk * 512:(blk + 1) * 512], stg_f[si], r=[b_stg_f[si]], w=[b_scr[nm]])
            for j in range(4):
                t0 = blk * 512 + j * 128
                i = j % 2
                for half in range(2):
                    pi = 2 + (j * 2 + half) % 2
                    pst_, bp = self.ps[pi], self.bps[pi]
                    for q2 in range(2):
                        nb = 2 + half * 2 + q2
                        for kc in range(8):
                            fw.op("pe", lambda e, kc=kc, nb=nb, q2=q2, pst_=pst_, j=j: e.matmul(pst_[:, q2 * 256:(q2 + 1) * 256], lhsT=hT[:, kc, j * 128:(j + 1) * 128], rhs=win[:, nb, kc, :], start=(kc == 0), stop=(kc == 7)),
                                  r=[b_win, b_hT], w=[bp])
                    if half == 0:
                        fw.op("act", lambda e, i=i, pst_=pst_: e.copy(kv_f[i][:, 0:512], pst_[:, :]), r=[bp], w=[b_kv_f[i]])
                    else:
                        fw.op("dve", lambda e, i=i, pst_=pst_: e.tensor_copy(kv_f[i][:, 512:1024], pst_[:, :]), r=[bp], w=[b_kv_f[i]])
                fw.op("pool", lambda e, i=i: e.tensor_copy(kv_b[i], kv_f[i][:, 512:1024]), r=[b_kv_f[i]], w=[b_kv_b[i]])
                fw.dma("act", s_V0[t0:t0 + 128, :], kv_b[i], r=[b_kv_b[i]], w=[b_scr["V0"]])
                if t0 >= 2048:
                    fw.dma("act", o_dil_kv_p[t0 - 2048:t0 - 2048 + 128, :], kv_f[i], r=[b_kv_f[i]])
        self.sp_off = mark
        fw.barrier()
        if S < 2:
            return

        mark = self.sp_off
        T = SEQ
        rp = self.salloc(36, F32)
        b_rp = fw.buf("rp", dma=True)
        fw.dma("sp", rp, rnnp, w=[b_rp])
        bd_f = self.salloc(256, F32)
        bd_b = self.salloc(256, BF16)
        b_bdf = fw.buf("bdf", dma=True)
        b_bdb = fw.buf("bdb")
        xr = self.salloc(T + 4, F32)
        xc = self.salloc(T, F32)
        xcb = self.salloc(T, BF16)
        t1 = self.salloc(T, F32)
        t2 = self.salloc(T, F32)
        t3 = self.salloc(T, F32)
        ob = self.salloc(T, BF16)
        sp_t = self.salloc(4, F32)
        b_xr = fw.buf("xr", dma=True)
        b_xc, b_xcb, b_t1, b_t3, b_spt = [fw.buf(n) for n in ["xc", "xcb", "t1", "t3", "spt"]]
        b_t2 = fw.buf("t2", dma=True)
        b_ob = fw.buf("ob", dma=True)
        b_gt = fw.buf("gt_in", dma=True)
        for c in range(4):
            pr = rp[:, c * 9:(c + 1) * 9]
            fw.op("dve", lambda e: e.memset(bd_f, 0.0), w=[b_bdf])
            for gi, gw in enumerate((gaw, gxw)):
                fw.dma("sp", bd_f[0:64, gi * 128:gi * 128 + 64], gw[2 * c], w=[b_bdf])
                fw.dma("sp", bd_f[64:128, gi * 128 + 64:gi * 128 + 128], gw[2 * c + 1], w=[b_bdf])
            fw.op("dve", lambda e: e.tensor_copy(bd_b, bd_f), r=[b_bdf], w=[b_bdb])
            fw.op("dve", lambda e: e.memset(xr[:, 0:4], 0.0), w=[b_xr])
            fw.dma("sp", xr[:, 4:4 + T], s_XR[c * 128:(c + 1) * 128, :], r=[b_scr["XR"]], w=[b_xr])
            fw.dma("act", o_conv_p[:, c * 128:(c + 1) * 128].rearrange("k p -> p k"), xr[:, 4 + T - 3:4 + T], r=[b_xr], allow_slow_non_contiguous=True)
            fw.op("dve", lambda e, pr=pr: e.tensor_scalar(out=xc, in0=xr[:, 1:1 + T], scalar1=pr[:, 0:1], scalar2=pr[:, 4:5], op0=ALU.mult, op1=ALU.add), r=[b_xr, b_rp], w=[b_xc])
            for k in range(1, 4):
                fw.op("dve", lambda e, pr=pr, k=k: e.scalar_tensor_tensor(out=xc, in0=xr[:, 1 + k:1 + k + T], scalar=pr[:, k:k + 1], in1=xc, op0=ALU.mult, op1=ALU.add), r=[b_xr, b_rp, b_xc], w=[b_xc])
            fw.op("pool", lambda e: e.tensor_copy(xcb, xc), r=[b_xc], w=[b_xcb])
            fw.op("act", lambda e, pr=pr: e.activation(out=sp_t[:, 0:1], in_=pr[:, 7:8], func=AF.Exp, scale=-1.0), r=[b_rp], w=[b_spt])
            fw.op("act", lambda e: e.activation(out=sp_t[:, 1:2], in_=sp_t[:, 0:1], func=AF.Ln, bias=1.0), r=[b_spt], w=[b_spt])
            fw.op("dve", lambda e: e.tensor_scalar(out=sp_t[:, 2:3], in0=sp_t[:, 1:2], scalar1=-8.0, scalar2=None, op0=ALU.mult), r=[b_spt], w=[b_spt])
            for tb in range(T // 512):
                for gi, (tt, bt_) in enumerate(((t1, b_t1), (t2, b_t2))):
                    pi = (tb * 2 + gi) % 4
                    fw.op("pe", lambda e, gi=gi, tb=tb, pi=pi: e.matmul(self.ps[pi][:, :], lhsT=bd_b[:, gi * 128:(gi + 1) * 128], rhs=xcb[:, tb * 512:(tb + 1) * 512], start=True, stop=True),
                          r=[b_bdb, b_xcb], w=[self.bps[pi]])
                    fw.op("act", lambda e, gi=gi, tb=tb, pi=pi, tt=tt, pr=pr: e.activation(out=tt[:, tb * 512:(tb + 1) * 512], in_=self.ps[pi][:, :], func=AF.Sigmoid, bias=pr[:, 5 + gi:6 + gi]),
                          r=[self.bps[pi], b_rp], w=[bt_])
            fw.op("dve", lambda e: e.tensor_scalar(out=sp_t[:, 3:4], in0=sp_t[:, 2:3], scalar1=2.0, scalar2=None, op0=ALU.mult), r=[b_spt], w=[b_spt])
            fw.op("act", lambda e: e.activation(out=t3, in_=t1, func=AF.Exp, scale=sp_t[:, 3:4]), r=[b_t1, b_spt], w=[b_t3])
            fw.op("act", lambda e: e.activation(out=t1, in_=t1, func=AF.Exp, scale=sp_t[:, 2:3]), r=[b_t1, b_spt], w=[b_t1])
            fw.op("dve", lambda e: e.tensor_scalar(out=t3, in0=t3, scalar1=-1.0, scalar2=1.0, op0=ALU.mult, op1=ALU.add), r=[b_t3], w=[b_t3])
            fw.op("dve", lambda e: e.tensor_scalar(out=t3, in0=t3, scalar1=0.0, scalar2=None, op0=ALU.max), r=[b_t3], w=[b_t3])
            fw.op("act", lambda e: e.activation(out=t3, in_=t3, func=AF.Sqrt), r=[b_t3], w=[b_t3])
            fw.op("dve", lambda e: e.tensor_tensor(out=t3, in0=t3, in1=t2, op=ALU.mult), r=[b_t3, b_t2], w=[b_t3])
            fw.op("dve", lambda e: e.tensor_tensor(out=t3, in0=t3, in1=xc, op=ALU.mult), r=[b_t3, b_xc], w=[b_t3])
            for sg in range(T // 512):
                init = 0.0 if sg == 0 else t2[:, sg * 512 - 1:sg * 512]
                fw.op("dve", lambda e, sg=sg, init=init: e.tensor_tensor_scan(out=t2[:, sg * 512:(sg + 1) * 512], data0=t1[:, sg * 512:(sg + 1) * 512], data1=t3[:, sg * 512:(sg + 1) * 512], initial=init, op0=ALU.mult, op1=ALU.add),
                      r=[b_t1, b_t3, b_t2], w=[b_t2])
            fw.dma("act", o_rnn_p[c * 128:(c + 1) * 128].rearrange("(p o) -> p o", o=1), t2[:, T - 1:T], r=[b_t2])
            fw.dma("sp", t1, s_GT[c * 128:(c + 1) * 128, :], r=[b_scr["GT"], b_t1], w=[b_gt, b_t1])
            fw.op("act", lambda e: e.activation(out=t1, in_=t1, func=AF.Gelu), r=[b_gt, b_t1], w=[b_t1])
            fw.op("dve", lambda e: e.tensor_tensor(out=ob, in0=t2, in1=t1, op=ALU.mult), r=[b_t1, b_t2], w=[b_ob])
            fw.dma("act", s_OT0[512 + c * 128:512 + (c + 1) * 128, :], ob, r=[b_ob], w=[b_scr["OT0"]])
        self.sp_off = mark
        fw.barrier()
        if S < 2.5:
            return

        mark = self.sp_off
        Qh = self.salloc(SEQ, BF16, parts=KR)
        Kh = self.salloc(SEQ, BF16, parts=KR)
        b_Qh = fw.buf("Qh", dma=True)
        b_Kh = fw.buf("Kh", dma=True)
        accL = self.salloc(2 * SEQ, F32, parts=64).rearrange("p (a t) -> p a t", a=2)
        b_acc = fw.buf("accL")
        obh = self.salloc(SEQ, BF16, parts=64)
        b_obh = fw.buf("obh", dma=True)
        NV = 4
        vts = [self.salloc(64, BF16) for _ in range(NV)]
        b_vts = [fw.buf("vt%d" % i, dma=True) for i in range(NV)]
        pts = [self.salloc(256, BF16) for _ in range(2)]
        b_pts = [fw.buf("pt%d" % i) for i in range(2)]
        unit = 0
        vcnt = 0
        for h in range(AH):
            fw.dma("sp", Qh[0:64, :], s_QT0[h * 64:(h + 1) * 64, :], r=[b_scr["QT0"]], w=[b_Qh])
            fw.dma("sp", Qh[64:KR, :], c_qaug0[h], w=[b_Qh])
            fw.dma("sp", Kh[0:64, :], s_KT0[h * 64:(h + 1) * 64, :], r=[b_scr["KT0"]], w=[b_Kh])
            fw.dma("sp", Kh[64:KR, :], c_kaug, w=[b_Kh])
            for dil in (1, 4, 16):
                ntile = SEQ // dil // 128
                for r_ in range(dil):
                    prev_v = None
                    for qi in range(ntile):
                        def tok(t):
                            a0 = r_ + dil * 128 * t
                            return slice(a0, a0 + dil * 127 + 1, dil)
                        vi = vcnt % NV
                        vcnt += 1
                        fw.dma("sp", vts[vi], s_V0[tok(qi), h * 64:(h + 1) * 64], r=[b_scr["V0"]], w=[b_vts[vi]])
                        si = unit % 2
                        unit += 1
                        Sp, bS = self.ps[si], self.bps[si]
                        Op_, bO = self.ps[2 + si], self.bps[2 + si]
                        pt, bpt = pts[si], b_pts[si]
                        kts = ([(qi - 1, 0, prev_v)] if qi >= 1 else []) + [(qi, 1, vi)]
                        for (kt, ci, _) in kts:
                            cs = slice(ci * 128, (ci + 1) * 128)
                            fw.op("pe", lambda e, kt=kt, cs=cs, Sp=Sp, tk=tok(kt), tq=tok(qi): e.matmul(Sp[:, cs], lhsT=Kh[:, tk], rhs=Qh[:, tq], start=True, stop=False), r=[b_Qh, b_Kh], w=[bS])
                            fw.op("pe", lambda e, cs=cs, Sp=Sp: e.matmul(Sp[:, cs], lhsT=ident_b, rhs=cmask[:, cs], start=False, stop=True), r=[b_const, b_const2], w=[bS])
                        c0 = 0 if qi >= 1 else 128
                        fw.op("act", lambda e, Sp=Sp, pt=pt, c0=c0: e.activation(out=pt[:, c0:256], in_=Sp[:, c0:256], func=AF.Exp, scale=0.125), r=[bS], w=[bpt])
                        for n_, (kt, ci, vslot) in enumerate(kts):
                            cs = slice(ci * 128, (ci + 1) * 128)
                            first, last_ = (n_ == 0), (n_ == len(kts) - 1)
                            fw.op("pe", lambda e, cs=cs, Op_=Op_, pt=pt, vslot=vslot, first=first, last_=last_: e.matmul(Op_[0:64, 0:128], lhsT=vts[vslot], rhs=pt[:, cs], start=first, stop=last_), r=[bpt, b_vts[vslot]], w=[bO])
                        for n_, (kt, ci, vslot) in enumerate(kts):
                            cs = slice(ci * 128, (ci + 1) * 128)
                            first, last_ = (n_ == 0), (n_ == len(kts) - 1)
                            fw.op("pe", lambda e, cs=cs, Op_=Op_, pt=pt, first=first, last_=last_: e.matmul(Op_[0:64, 128:256], lhsT=ones_b[:, 0:64], rhs=pt[:, cs], start=first, stop=last_), r=[bpt, b_const2], w=[bO])
                        dst = accL[:, :, tok(qi)]
                        src = Op_[0:64, 0:256].rearrange("p (a t) -> p a t", a=2)
                        if dil == 1:
                            fw.op("dve", lambda e, dst=dst, src=src: e.tensor_copy(dst, src), r=[bO], w=[b_acc])
                        else:
                            fw.op("dve", lambda e, dst=dst, src=src: e.tensor_tensor(out=dst, in0=src, in1=dst, op=ALU.add), r=[bO, b_acc], w=[b_acc])
                        prev_v = vi
            for ch in range(4):
                cs = slice(ch * 1024, (ch + 1) * 1024)
                fw.op("dve", lambda e, cs=cs: e.reciprocal(out=accL[:, 1, cs], in_=accL[:, 1, cs]), r=[b_acc], w=[b_acc])
                fw.op("dve", lambda e, cs=cs: e.tensor_tensor(out=obh[:, cs], in0=accL[:, 0, cs], in1=accL[:, 1, cs], op=ALU.mult), r=[b_acc], w=[b_obh])
            fw.dma("act", s_OT0[h * 64:(h + 1) * 64, :], obh, r=[b_obh], w=[b_scr["OT0"]])
        self.sp_off = mark
        fw.barrier()

        def phaseD(l):
            mark = self.sp_off
            s_OT = s_OT0 if l == 0 else s_OT1
            nm_OT = "OT0" if l == 0 else "OT1"
            wb_out = wb_out_ab if l == 0 else wb_out_c
            wout, b_wout = load_w(wb_out, D, 4, "wout")
            w2t, b_w2 = load_w(wb_w2[l], FFN, 4, "w2")
            ot = self.salloc(8 * 512, BF16).rearrange("p (kc t) -> p kc t", kc=8)
            b_ot = fw.buf("ot", dma=True)
            x1 = [self.salloc(D, F32) for _ in range(4)]
            b_x1 = [fw.buf("x1_%d" % j, dma=True) for j in range(4)]
            tmp = (self.salloc(D, F32), self.salloc(4, F32), self.salloc(D, BF16), fw.buf("nrm_tmp"))
            hT = self.salloc(8 * 512, BF16).rearrange("p (kc t) -> p kc t", kc=8)
            b_hT = fw.buf("hT")
            gT = self.salloc(22 * 512, BF16).rearrange("p (hc t) -> p hc t", hc=22)
            b_gT = fw.buf("gT")
            wst = [self.salloc(2 * 8 * 256, BF16).rearrange("p (w kc n) -> p w kc n", w=2, kc=8) for _ in range(2)]
            b_wst = [fw.buf("wst%d" % i, dma=True) for i in range(2)]
            stmp = [self.salloc(512, F32) for _ in range(2)]
            b_stmp = [fw.buf("stmp%d" % i) for i in range(2)]
            stg_b = [self.salloc(512, BF16) for _ in range(2)]
            b_stg_b = [fw.buf("stgb%d" % i, dma=True) for i in range(2)]
            tm_f = [self.salloc(512, F32) for _ in range(2)]
            b_tm_f = [fw.buf("tmf%d" % i, dma=True) for i in range(2)]
            tm_b = [self.salloc(256, BF16) for _ in range(2)]
            b_tm_b = [fw.buf("tmb%d" % i, dma=True) for i in range(2)]
            if l == 1:
                gout = self.salloc(D, F32)
                b_gout = fw.buf("gout", dma=True)
                fw.dma("sp", gout, norm_out.partition_broadcast(128), w=[b_gout])
                yt = [self.salloc(D, F32) for _ in range(2)]
                b_yt = [fw.buf("yt%d" % i, dma=True) for i in range(2)]
            wcnt = 0
            pcnt = 0
            for blk in range(8):
                fw.dma("sp", ot, s_OT[:, blk * 512:(blk + 1) * 512].rearrange("(kc p) t -> p kc t", p=128), r=[b_scr[nm_OT]], w=[b_ot])
                for j in range(4):
                    t0 = blk * 512 + j * 128
                    if l == 0:
                        fw.dma("sp", x1[j], xp[t0:t0 + 128, :], w=[b_x1[j]])
                    else:
                        fw.dma("sp", x1[j], s_X1[t0:t0 + 128, :], r=[b_scr["X1"]], w=[b_x1[j]])
                    for half in range(2):
                        pi = pcnt % 2
                        pcnt += 1
                        for q2 in range(2):
                            nb = half * 2 + q2
                            for kc in range(8):
                                fw.op("pe", lambda e, kc=kc, nb=nb, q2=q2, pi=pi, j=j: e.matmul(self.ps[pi][:, q2 * 256:(q2 + 1) * 256], lhsT=ot[:, kc, j * 128:(j + 1) * 128], rhs=wout[:, nb, kc, :], start=(kc == 0), stop=(kc == 7)),
                                      r=[b_ot, b_wout], w=[self.bps[pi]])
                        fw.op("dve", lambda e, pi=pi, j=j, half=half: e.tensor_tensor(out=x1[j][:, half * 512:(half + 1) * 512], in0=self.ps[pi][:, :], in1=x1[j][:, half * 512:(half + 1) * 512], op=ALU.add),
                              r=[self.bps[pi], b_x1[j]], w=[b_x1[j]])
                    rmsnorm_T(x1[j], b_x1[j], 128, 1 + 2 * l, hT, b_hT, j * 128, tmp)
                if S < 2.85:
                    continue
                for nb in range(11):
                    wi = wcnt % 2
                    wcnt += 1
                    fw.dma("sp", wst[wi][:, 0], wb_w1[l][nb].rearrange("p (kc n) -> p kc n", kc=8), r=[b_scr["w"]], w=[b_wst[wi]])
                    fw.dma("sp", wst[wi][:, 1], wb_w3[l][nb].rearrange("p (kc n) -> p kc n", kc=8), r=[b_scr["w"]], w=[b_wst[wi]])
                    for sub in range(2):
                        hc = nb * 2 + sub
                        p1, p3 = 2 + (hc % 2) * 2, 3 + (hc % 2) * 2
                        for (pp, wsel) in ((p1, 0), (p3, 1)):
                            for kc in range(8):
                                fw.op("pe", lambda e, kc=kc, pp=pp, wsel=wsel, wi=wi, sub=sub: e.matmul(self.ps[pp][:, :], lhsT=wst[wi][:, wsel, kc, sub * 128:(sub + 1) * 128], rhs=hT[:, kc, :], start=(kc == 0), stop=(kc == 7)),
                                      r=[b_wst[wi], b_hT], w=[self.bps[pp]])
                        si = hc % 2
                        fw.op("act", lambda e, si=si, p1=p1: e.activation(out=stmp[si], in_=self.ps[p1][:, :], func=AF.Silu), r=[self.bps[p1]], w=[b_stmp[si]])
                        fw.op("dve", lambda e, si=si, p3=p3, hc=hc: e.tensor_tensor(out=gT[:, hc, :], in0=self.ps[p3][:, :], in1=stmp[si], op=ALU.mult), r=[self.bps[p3], b_stmp[si]], w=[b_gT])
                if S < 2.95:
                    continue
                for j in range(4):
                    t0 = blk * 512 + j * 128
                    for half in range(2):
                        pi = pcnt % 2
                        pcnt += 1
                        for q2 in range(2):
                            nb = half * 2 + q2
                            for hc in range(22):
                                fw.op("pe", lambda e, hc=hc, nb=nb, q2=q2, pi=pi, j=j: e.matmul(self.ps[pi][:, q2 * 256:(q2 + 1) * 256], lhsT=gT[:, hc, j * 128:(j + 1) * 128], rhs=w2t[:, nb, hc, :], start=(hc == 0), stop=(hc == 21)),
                                      r=[b_gT, b_w2], w=[self.bps[pi]])
                        fw.op("dve", lambda e, pi=pi, j=j, half=half: e.tensor_tensor(out=x1[j][:, half * 512:(half + 1) * 512], in0=self.ps[pi][:, :], in1=x1[j][:, half * 512:(half + 1) * 512], op=ALU.add),
                              r=[self.bps[pi], b_x1[j]], w=[b_x1[j]])
                    if l == 0:
                        fw.dma("act", s_X1[t0:t0 + 128, :], x1[j], r=[b_x1[j]], w=[b_scr["X1"]])
                        rmsnorm_T(x1[j], b_x1[j], 128, 2, hT, b_hT, j * 128, tmp)
                    else:
                        sq, ss, xn, bt = tmp
                        yi = j % 2
                        fw.op("act", lambda e, j=j: e.activation(out=sq, in_=x1[j], func=AF.Square, accum_out=ss[:, 0:1]), r=[b_x1[j]], w=[bt])
                        fw.op("dve", lambda e: e.tensor_scalar(out=ss[:, 1:2], in0=ss[:, 0:1], scalar1=1.0 / D, scalar2=EPS, op0=ALU.mult, op1=ALU.add), r=[bt], w=[bt])
                        fw.op("act", lambda e: e.activation(out=ss[:, 2:3], in_=ss[:, 1:2], func=AF.Sqrt), r=[bt], w=[bt])
                        fw.op("dve", lambda e: e.reciprocal(out=ss[:, 3:4], in_=ss[:, 2:3]), r=[bt], w=[bt])
                        fw.op("dve", lambda e, j=j, yi=yi: e.scalar_tensor_tensor(out=yt[yi], in0=x1[j], scalar=ss[:, 3:4], in1=gout, op0=ALU.mult, op1=ALU.mult), r=[bt, b_x1[j], b_gout], w=[b_yt[yi]])
                        fw.dma("act", o_y_p[t0:t0 + 128, :], yt[yi], r=[b_yt[yi]])
                if l == 0 and S >= 3:
                    for nb in range(8):
                        wi = wcnt % 2
                        wcnt += 1
                        fw.dma("sp", wst[wi][:, 0], wb_in_c[nb].rearrange("p (kc n) -> p kc n", kc=8), r=[b_scr["w"]], w=[b_wst[wi]])
                        for sub in range(2):
                            oc = nb * 2 + sub
                            pp = 2 + oc % 2
                            for kc in range(8):
                                fw.op("pe", lambda e, kc=kc, pp=pp, wi=wi, sub=sub: e.matmul(self.ps[pp][:, :], lhsT=wst[wi][:, 0, kc, sub * 128:(sub + 1) * 128], rhs=hT[:, kc, :], start=(kc == 0), stop=(kc == 7)),
                                      r=[b_wst[wi], b_hT], w=[self.bps[pp]])
                            si = oc % 2
                            fw.op("act", lambda e, si=si, pp=pp: e.copy(stg_b[si], self.ps[pp][:, :]), r=[self.bps[pp]], w=[b_stg_b[si]])
                            if oc < 8:
                                dst, nm, r0 = s_QT1, "QT1", oc * 128
                            elif oc < 12:
                                dst, nm, r0 = s_KC, "KC", (oc - 8) * 128
                            elif oc < 14:
                                dst, nm, r0 = s_KS, "KS", (oc - 12) * 128
                            else:
                                dst, nm, r0 = s_KW, "KW", (oc - 14) * 128
                            fw.dma("act", dst[r0:r0 + 128, blk * 512:(blk + 1) * 512], stg_b[si], r=[b_stg_b[si]], w=[b_scr[nm]])
                    for nb2 in range(4 if S >= 3.2 else (3 if S >= 3.1 else 0)):
                        wi = wcnt % 2
                        wcnt += 1
                        nq = 2 if nb2 < 3 else 1
                        for q2 in range(nq):
                            fw.dma("sp", wst[wi][:, q2], wb_in_c_tm[nb2 * 2 + q2].rearrange("p (kc n) -> p kc n", kc=8), r=[b_scr["w"]], w=[b_wst[wi]])
                        for j in range(4):
                            t0 = blk * 512 + j * 128
                            pp = 2 + j % 2
                            for q2 in range(nq):
                                for kc in range(8):
                                    fw.op("pe", lambda e, kc=kc, pp=pp, wi=wi, q2=q2, j=j: e.matmul(self.ps[pp][:, q2 * 256:(q2 + 1) * 256], lhsT=hT[:, kc, j * 128:(j + 1) * 128], rhs=wst[wi][:, q2, kc, :], start=(kc == 0), stop=(kc == 7)),
                                          r=[b_wst[wi], b_hT], w=[self.bps[pp]])
                            fi = j % 2
                            if nb2 < 3:
                                fw.op("act", lambda e, fi=fi, pp=pp: e.copy(tm_f[fi], self.ps[pp][:, :]), r=[self.bps[pp]], w=[b_tm_f[fi]])
                                if nb2 == 0:
                                    fw.dma("act", o_cmp_kv_p[t0:t0 + 128, :], tm_f[fi], r=[b_tm_f[fi]])
                                elif nb2 == 1:
                                    fw.dma("act", o_sel_kv_p[t0:t0 + 128, :], tm_f[fi], r=[b_tm_f[fi]])
                                elif t0 >= SEQ - 512:
                                    fw.dma("act", o_win_kv_p[t0 - (SEQ - 512):t0 - (SEQ - 512) + 128, :], tm_f[fi], r=[b_tm_f[fi]])
                                if nb2 >= 1:
                                    fw.op("dve", lambda e, fi=fi: e.tensor_copy(tm_b[fi], tm_f[fi][:, 256:512]), r=[b_tm_f[fi]], w=[b_tm_b[fi]])
                                    dst, nm = (s_VS, "VS") if nb2 == 1 else (s_VW, "VW")
                                    fw.dma("act", dst[t0:t0 + 128, :], tm_b[fi], r=[b_tm_b[fi]], w=[b_scr[nm]])
                            else:
                                fw.op("act", lambda e, fi=fi, pp=pp: e.activation(out=tm_f[fi][:, 0:48], in_=self.ps[pp][:, 0:48], func=AF.Sigmoid), r=[self.bps[pp]], w=[b_tm_f[fi]])
                                fw.dma("act", s_G[t0:t0 + 128, :], tm_f[fi][:, 0:48], r=[b_tm_f[fi]], w=[b_scr["G"]])
            self.sp_off = mark
            fw.barrier()

        if S < 2.8:
            return
        phaseD(0)
        if S < 4:
            return

        mark = self.sp_off
        mc_t = self.salloc(33 * 128, BF16).rearrange("p (m q) -> p m q", m=33)
        tsel_t = self.salloc(32 * 64, F32).rearrange("p (a j) -> p a j", a=32)
        e30_t = self.salloc(SEQ, BF16, parts=64)
        wc_f = self.salloc(4096, F32, parts=64)
        wc_b = self.salloc(4096, BF16, parts=64).rearrange("p (l c e) -> p l c e", l=32, c=2)
        pe_f = self.salloc(64, F32, parts=64)
        pe_b = self.salloc(64, BF16, parts=64).rearrange("p (l c) -> p l c", c=2)
        b_fc = fw.buf("fconst", dma=True)
        b_fc2 = fw.buf("fconst2")
        fw.dma("sp", mc_t, c_mc, w=[b_fc])
        fw.dma("sp", tsel_t, c_tsel, w=[b_fc])
        fw.dma("sp", e30_t, c_e30, w=[b_fc])
        fw.dma("sp", wc_f, w_cmp_t, w=[b_fc])
        fw.dma("sp", pe_f, pe_cmp_t, w=[b_fc])
        fw.op("dve", lambda e: e.tensor_copy(wc_b.rearrange("p l c e -> p (l c e)"), wc_f), r=[b_fc], w=[b_fc2])
        fw.op("dve", lambda e: e.tensor_copy(pe_b.rearrange("p l c -> p (l c)"), pe_f), r=[b_fc], w=[b_fc2])
        VS = self.salloc(32 * 4 * 65, BF16).rearrange("p (k g d) -> p k g d", k=32, g=4)
        VW = self.salloc(32 * 4 * 65, BF16).rearrange("p (k g d) -> p k g d", k=32, g=4)
        b_VS = fw.buf("VSa", dma=True)
        b_VW = fw.buf("VWa", dma=True)
        for (Vt, bV, src, nm) in ((VS, b_VS, s_VS, "VS"), (VW, b_VW, s_VW, "VW")):
            fw.op("dve", lambda e, Vt=Vt: e.memset(Vt[:, :, :, 64:65], 1.0), w=[bV])
            for kt in range(32):
                fw.dma("sp", Vt[:, kt, :, 0:64], src[kt * 128:(kt + 1) * 128, :].rearrange("p (g d) -> p g d", g=4), r=[b_scr[nm]], w=[bV])
        XK = self.salloc(SEQ, BF16, parts=64)
        XV = self.salloc(SEQ, BF16, parts=64)
        b_XK = fw.buf("XK", dma=True)
        b_XV = fw.buf("XV", dma=True)
        Ksa = self.salloc(SEQ, BF16, parts=KR)
        Kwa = self.salloc(SEQ, BF16, parts=KR)
        Kca = self.salloc(256, BF16, parts=KR)
        b_Ksa = fw.buf("Ksa", dma=True)
        b_Kwa = fw.buf("Kwa", dma=True)
        b_Kca = fw.buf("Kca", dma=True)
        VCX = self.salloc(2 * 129, BF16).rearrange("p (t x) -> p t x", t=2)
        b_VCX = fw.buf("VCX", dma=True)
        pek = self.salloc(4, F32, parts=64)
        pev = self.salloc(64, BF16, parts=1)
        b_pe = fw.buf("pekv")
        qas = [self.salloc(4 * 128, BF16, parts=KR).rearrange("p (r q) -> p r q", r=4) for _ in range(2)]
        b_qas = [fw.buf("qa%d" % i, dma=True) for i in range(2)]
        pcs = [self.salloc(512, BF16) for _ in range(2)]
        b_pcs = [fw.buf("pc%d" % i) for i in range(2)]
        pss = [self.salloc(512, BF16) for _ in range(3)]
        b_pss = [fw.buf("ps_%d" % i) for i in range(3)]
        sc = self.salloc(64, F32)
        sc2 = self.salloc(64, F32)
        m8 = self.salloc(16, F32)
        small = self.salloc(64, F32)
        madd = self.salloc(64, BF16)
        msel = self.salloc(512, BF16, parts=64).rearrange("p (r q) -> p r q", r=4)
        b_sel = fw.buf("seltmp")
        b_msel = fw.buf("msel")
        gts = [self.salloc(48, F32) for _ in range(2)]
        b_gts = [fw.buf("gt%d" % i, dma=True) for i in range(2)]
        otile = self.salloc(256, F32)
        otb = self.salloc(256, BF16)
        b_ot_ = fw.buf("otile")
        ot1s = [self.salloc(256, BF16).rearrange("p (c t) -> p c t", c=2) for _ in range(2)]
        b_ot1s = [fw.buf("ot1_%d" % i, dma=True) for i in range(2)]
        if DBG:
            dbg_br = self.dout("dbg_br", [3, SEQ, 1024])
            dbgt = self.salloc(3 * 256, F32).rearrange("p (b f) -> p b f", b=3)
            b_dbgt = fw.buf("dbgt", dma=True)
        PS_S = (0, 1)
        PS_CA, PS_CB, PS_OS, PS_OW, PS_X, PS_T = 2, 3, 4, 5, 6, 7
        ucnt = 0
        for g in range(4):
            fw.dma("sp", XK, s_KC[g * 64:(g + 1) * 64, :], r=[b_scr["KC"]], w=[b_XK])
            fw.dma("sp", XV, s_KC[256 + g * 64:256 + (g + 1) * 64, :], r=[b_scr["KC"]], w=[b_XV])
            fw.dma("sp", Ksa[0:64, :], s_KS[g * 64:(g + 1) * 64, :], r=[b_scr["KS"]], w=[b_Ksa])
            fw.dma("sp", Ksa[64:KR, :], c_kaug, w=[b_Ksa])
            fw.dma("sp", Kwa[0:64, :], s_KW[g * 64:(g + 1) * 64, :], r=[b_scr["KW"]], w=[b_Kwa])
            fw.dma("sp", Kwa[64:KR, :], c_kaug, w=[b_Kwa])
            fw.dma("sp", Kca[64:KR, :], c_kaugc, w=[b_Kca])
            fw.dma("sp", VCX[:, :, 65:129], c_c2s, w=[b_VCX])
            fw.op("dve", lambda e: e.memset(VCX[:, :, 64:65], 1.0), w=[b_VCX])
            for l in range(32):
                fw.op("pe", lambda e, l=l: e.matmul(self.ps[PS_X][0:64, 0:1], lhsT=wc_b[:, l, 0, :], rhs=pe_b[:, l, 0:1], start=(l == 0), stop=(l == 31)), r=[b_fc2], w=[self.bps[PS_X]])
            for l in range(32):
                fw.op("pe", lambda e, l=l: e.matmul(self.ps[PS_X][0:1, 64:128], lhsT=pe_b[:, l, 1:2], rhs=wc_b[:, l, 1, :], start=(l == 0), stop=(l == 31)), r=[b_fc2], w=[self.bps[PS_X]])
            fw.op("dve", lambda e: e.tensor_copy(pek[:, 0:1], self.ps[PS_X][0:64, 0:1]), r=[self.bps[PS_X]], w=[b_pe])
            fw.op("dve", lambda e: e.tensor_copy(pev, self.ps[PS_X][0:1, 64:128]), r=[self.bps[PS_X]], w=[b_pe])
            for l in range(32):
                fw.op("pe", lambda e, l=l: e.matmul(self.ps[PS_X][0:64, 0:255], lhsT=wc_b[:, l, 0, :], rhs=XK[:, l:l + 16 * 254 + 1:16], start=(l == 0), stop=(l == 31)), r=[b_fc2, b_XK], w=[self.bps[PS_X]])
            fw.op("dve", lambda e: e.memset(Kca[0:64, 255:256], 0.0), w=[b_Kca])
            fw.op("dve", lambda e: e.tensor_scalar(out=Kca[0:64, 0:255], in0=self.ps[PS_X][0:64, 0:255], scalar1=pek[:, 0:1], scalar2=None, op0=ALU.add), r=[self.bps[PS_X], b_pe], w=[b_Kca])
            for nt_, nrows in ((0, 128), (1, 127)):
                for l in range(32):
                    a0 = 16 * 128 * nt_ + l
                    fw.op("pe", lambda e, l=l, a0=a0, nrows=nrows: e.matmul(self.ps[PS_X][0:nrows, 256:320], lhsT=XV[:, a0:a0 + 16 * (nrows - 1) + 1:16], rhs=wc_b[:, l, 1, :], start=(l == 0), stop=False), r=[b_fc2, b_XV], w=[self.bps[PS_X]])
                fw.op("pe", lambda e, nrows=nrows: e.matmul(self.ps[PS_X][0:nrows, 256:320], lhsT=ones_b[0:1, 0:nrows], rhs=pev, start=False, stop=True), r=[b_const2, b_pe], w=[self.bps[PS_X]])
                fw.op("dve", lambda e, nt_=nt_, nrows=nrows: e.tensor_copy(VCX[0:nrows, nt_, 0:64], self.ps[PS_X][0:nrows, 256:320]), r=[self.bps[PS_X]], w=[b_VCX])
            for qt in range(32):
                tq = slice(qt * 128, (qt + 1) * 128)
                qi_ = (g * 32 + qt) % 2
                qa, b_qa = qas[qi_], b_qas[qi_]
                fw.dma("sp", qa[0:64, :, :], s_QT1[g * 256:(g + 1) * 256, tq].rearrange("(r d) t -> d r t", d=64), r=[b_scr["QT1"]], w=[b_qa])
                fw.dma("sp", qa[64:KR, :, :], c_qaug1[g][:, :, tq], w=[b_qa])
                gt, b_gt = gts[qi_], b_gts[qi_]
                fw.dma("sp", gt, s_G[tq, :], r=[b_scr["G"]], w=[b_gt])
                ntl = [(0, 128)] + ([(1, 127)] if qt >= 16 else [])
                pcl = []
                for (nt_, nrows) in ntl:
                    si = ucnt % 2
                    ucnt += 1
                    Sb, bS = self.ps[PS_S[si]], self.bps[PS_S[si]]
                    mi = qt if nt_ == 0 else 17 + qt - 16
                    need_mask = (nt_ == 1) or (qt <= 16)
                    for r_ in range(4):
                        cs = slice(r_ * 128, (r_ + 1) * 128)
                        fw.op("pe", lambda e, Sb=Sb, cs=cs, nt_=nt_, nrows=nrows, r_=r_, qa=qa, need_mask=need_mask: e.matmul(Sb[0:nrows, cs], lhsT=Kca[:, nt_ * 128:nt_ * 128 + nrows], rhs=qa[:, r_, :], start=True, stop=not need_mask), r=[b_Kca, b_qa], w=[bS])
                        if need_mask:
                            fw.op("pe", lambda e, Sb=Sb, cs=cs, nrows=nrows, mi=mi: e.matmul(Sb[0:nrows, cs], lhsT=ident_b[0:nrows, 0:nrows], rhs=mc_t[0:nrows, mi, :], start=False, stop=True), r=[b_const2, b_fc], w=[bS])
                    pc, b_pc = pcs[nt_], b_pcs[nt_]
                    fw.op("act", lambda e, Sb=Sb, pc=pc, nrows=nrows: e.activation(out=pc[0:nrows, :], in_=Sb[0:nrows, :], func=AF.Exp, scale=0.125), r=[bS], w=[b_pc])
                    pcl.append((nt_, nrows, pc, b_pc))
                for r_ in range(4):
                    bank = PS_CA if r_ < 2 else PS_CB
                    c0 = (r_ % 2) * 129
                    for n_, (nt_, nrows, pc, b_pc) in enumerate(pcl):
                        fw.op("pe", lambda e, bank=bank, c0=c0, nt_=nt_, nrows=nrows, pc=pc, r_=r_, n_=n_: e.matmul(self.ps[bank][:, c0:c0 + 129], lhsT=pc[0:nrows, r_ * 128:(r_ + 1) * 128], rhs=VCX[0:nrows, nt_, :], start=(n_ == 0), stop=(n_ == len(pcl) - 1)), r=[b_pc, b_VCX], w=[self.bps[bank]])
                for r_ in range(4):
                    bank = PS_CA if r_ < 2 else PS_CB
                    c0 = (r_ % 2) * 129
                    fw.op("dve", lambda e, bank=bank, c0=c0, r_=r_: e.tensor_scalar(out=small[:, r_:r_ + 1], in0=self.ps[bank][:, c0 + 64:c0 + 65], scalar1=1e-30, scalar2=None, op0=ALU.max), r=[self.bps[bank]], w=[b_sel])
                fw.op("dve", lambda e: e.reciprocal(out=small[:, 16:20], in_=small[:, 0:4]), r=[b_sel], w=[b_sel])
                for r_ in range(4):
                    bank = PS_CA if r_ < 2 else PS_CB
                    c0 = (r_ % 2) * 129
                    if r_ == 0:
                        fw.op("dve", lambda e, bank=bank, c0=c0: e.tensor_scalar(out=sc, in0=self.ps[bank][:, c0 + 65:c0 + 129], scalar1=small[:, 16:17], scalar2=None, op0=ALU.mult), r=[self.bps[bank], b_sel], w=[b_sel])
                    else:
                        fw.op("dve", lambda e, bank=bank, c0=c0, r_=r_: e.scalar_tensor_tensor(out=sc, in0=self.ps[bank][:, c0 + 65:c0 + 129], scalar=small[:, 16 + r_:17 + r_], in1=sc, op0=ALU.mult, op1=ALU.add), r=[self.bps[bank], b_sel], w=[b_sel])
                fw.op("dve", lambda e, qt=qt: e.tensor_tensor(out=sc, in0=sc, in1=tsel_t[:, qt, :], op=ALU.add), r=[b_sel, b_fc], w=[b_sel])
                fw.op("dve", lambda e: e.max(out=m8[:, 0:8], in_=sc), r=[b_sel], w=[b_sel])
                fw.op("dve", lambda e: e.match_replace(out=sc2, in_to_replace=m8[:, 0:8], in_values=sc, imm_value=-1e30), r=[b_sel], w=[b_sel])
                fw.op("dve", lambda e: e.max(out=m8[:, 8:16], in_=sc2), r=[b_sel], w=[b_sel])
                fw.op("dve", lambda e: e.tensor_scalar(out=m8[:, 0:1], in0=m8[:, 15:16], scalar1=-1e29, scalar2=None, op0=ALU.max), r=[b_sel], w=[b_sel])
                fw.op("dve", lambda e: e.tensor_scalar(out=madd, in0=sc, scalar1=m8[:, 0:1], scalar2=1.0, op0=ALU.is_ge, op1=ALU.subtract), r=[b_sel], w=[b_sel])
                pstb = self.ps[PS_T][:, :].bitcast(BF16)
                fw.op("pe", lambda e: e.transpose(pstb[0:64, 0:128], madd, ident_b), r=[b_sel, b_const2], w=[self.bps[PS_T]])
                fw.op("dve", lambda e: e.tensor_copy(msel, pstb[0:64, 0:128].rearrange("p (o q) -> p o q", o=1).to_broadcast([64, 4, 128])), r=[self.bps[PS_T]], w=[b_msel])
                for (br, Ka, b_Ka, Vt, bV, bank, k0) in ((1, Ksa, b_Ksa, VS, b_VS, PS_OS, 0), (2, Kwa, b_Kwa, VW, b_VW, PS_OW, max(0, qt - 4))):
                    for kt in range(k0, qt + 1):
                        si = ucnt % 2
                        ucnt += 1
                        Sb, bS = self.ps[PS_S[si]], self.bps[PS_S[si]]
                        tk = slice(kt * 128, (kt + 1) * 128)
                        for r_ in range(4):
                            cs = slice(r_ * 128, (r_ + 1) * 128)
                            extra = []
                            if br == 1:
                                extra.append((e30_t[:, tk], msel[:, r_, :], [b_fc, b_msel]))
                            if kt == qt:
                                extra.append((ident_b, cmask[:, 128:256], [b_const, b_const2]))
                            if br == 2 and kt == qt - 4:
                                extra.append((ident_b, cmask[:, 0:128], [b_const, b_const2]))
                            fw.op("pe", lambda e, Sb=Sb, cs=cs, Ka=Ka, tk=tk, qa=qa, r_=r_, ne=len(extra): e.matmul(Sb[:, cs], lhsT=Ka[:, tk], rhs=qa[:, r_, :], start=True, stop=(ne == 0)), r=[b_Ka, b_qa], w=[bS])
                            for xi, (lh, rh, rb) in enumerate(extra):
                                fw.op("pe", lambda e, Sb=Sb, cs=cs, lh=lh, rh=rh, last_=(xi == len(extra) - 1): e.matmul(Sb[:, cs], lhsT=lh, rhs=rh, start=False, stop=last_), r=rb, w=[bS])
                        pi_ = ucnt % 3
                        pp_, b_pp = pss[pi_], b_pss[pi_]
                        fw.op("act", lambda e, Sb=Sb, pp_=pp_: e.activation(out=pp_, in_=Sb[:, :], func=AF.Exp, scale=0.125), r=[bS], w=[b_pp])
                        for r_ in range(4):
                            fw.op("pe", lambda e, bank=bank, r_=r_, pp_=pp_, Vt=Vt, kt=kt, k0=k0, g=g, qt=qt: e.matmul(self.ps[bank][:, r_ * 65:(r_ + 1) * 65], lhsT=pp_[:, r_ * 128:(r_ + 1) * 128], rhs=Vt[:, kt, g, :], start=(kt == k0 and r_ == 0), stop=(kt == qt), skip_group_check=True), r=[b_pp, bV], w=[self.bps[bank]])
                    fw.op("dve", lambda e, bank=bank, br=br: e.tensor_scalar(out=small[:, br * 4:(br + 1) * 4], in0=self.ps[bank][:, 0:260].rearrange("p (r x) -> p r x", r=4)[:, :, 64], scalar1=1e-30, scalar2=None, op0=ALU.max), r=[self.bps[bank]], w=[b_sel])
                fw.op("dve", lambda e: e.reciprocal(out=small[:, 16:28], in_=small[:, 0:12]), r=[b_sel], w=[b_sel])
                fw.op("dve", lambda e, gt=gt, g=g: e.tensor_tensor(out=small[:, 32:44].rearrange("p (b r) -> p b r", b=3), in0=small[:, 16:28].rearrange("p (b r) -> p b r", b=3), in1=gt[:, g * 12:(g + 1) * 12].rearrange("p (r b) -> p b r", b=3), op=ALU.mult), r=[b_sel, b_gt], w=[b_sel])
                for r_ in range(4):
                    bankc = PS_CA if r_ < 2 else PS_CB
                    c0 = (r_ % 2) * 129
                    od = otile[:, r_ * 64:(r_ + 1) * 64]
                    fw.op("dve", lambda e, bankc=bankc, c0=c0, od=od, r_=r_: e.tensor_scalar(out=od, in0=self.ps[bankc][:, c0:c0 + 64], scalar1=small[:, 32 + r_:33 + r_], scalar2=None, op0=ALU.mult), r=[self.bps[bankc], b_sel], w=[b_ot_])
                    fw.op("dve", lambda e, od=od, r_=r_: e.scalar_tensor_tensor(out=od, in0=self.ps[PS_OS][:, r_ * 65:r_ * 65 + 64], scalar=small[:, 36 + r_:37 + r_], in1=od, op0=ALU.mult, op1=ALU.add), r=[self.bps[PS_OS], b_sel, b_ot_], w=[b_ot_])
                    fw.op("dve", lambda e, od=od, r_=r_: e.scalar_tensor_tensor(out=od, in0=self.ps[PS_OW][:, r_ * 65:r_ * 65 + 64], scalar=small[:, 40 + r_:41 + r_], in1=od, op0=ALU.mult, op1=ALU.add), r=[self.bps[PS_OW], b_sel, b_ot_], w=[b_ot_])
                if DBG:
                    for b_i in range(3):
                        for r_ in range(4):
                            if b_i == 0:
                                bk = PS_CA if r_ < 2 else PS_CB
                                src = self.ps[bk][:, (r_ % 2) * 129:(r_ % 2) * 129 + 64]
                            else:
                                bk = PS_OS if b_i == 1 else PS_OW
                                src = self.ps[bk][:, r_ * 65:r_ * 65 + 64]
                            fw.op("dve", lambda e, src=src, b_i=b_i, r_=r_: e.tensor_scalar(out=dbgt[:, b_i, r_ * 64:(r_ + 1) * 64], in0=src, scalar1=small[:, 16 + b_i * 4 + r_:17 + b_i * 4 + r_], scalar2=None, op0=ALU.mult), r=[self.bps[bk], b_sel], w=[b_dbgt])
                    fw.dma("act", dbg_br[:, tq, g * 256:(g + 1) * 256].rearrange("b t f -> t b f"), dbgt, r=[b_dbgt])
                fw.op("act", lambda e: e.copy(otb, otile), r=[b_ot_], w=[b_ot_])
                for c_ in range(2):
                    fw.op("pe", lambda e, c_=c_: e.transpose(pstb[:, 256 + c_ * 128:256 + (c_ + 1) * 128], otb[:, c_ * 128:(c_ + 1) * 128], ident_b), r=[b_ot_, b_const2], w=[self.bps[PS_T]])
                oi = qt % 2
                fw.op("dve", lambda e, oi=oi: e.tensor_copy(ot1s[oi], pstb[:, 256:512].rearrange("p (c t) -> p c t", c=2)), r=[self.bps[PS_T]], w=[b_ot1s[oi]])
                fw.dma("act", s_OT1[g * 256:(g + 1) * 256, tq].rearrange("(c p) t -> p c t", p=128), ot1s[oi], r=[b_ot1s[oi]], w=[b_scr["OT1"]])
        self.sp_off = mark
        fw.barrier()
        phaseD(1)
        if S < 5:
            return
        self.sample_path(locals())

    def sample_path(self, L):
        fw = self.fw
        g_ = lambda n: L[n]
        xs, cache_dil, st_conv, st_rnn, cache_win = g_("xs"), g_("cache_dil"), g_("st_conv"), g_("st_rnn"), g_("cache_win")
        b_scr, load_w, rmsnorm_T = g_("b_scr"), g_("load_w"), g_("rmsnorm_T")
        ident_b, ones_b, b_const, b_const2 = g_("ident_b"), g_("ones_b"), g_("b_const"), g_("b_const2")
        NT = 32
        mark = self.sp_off
        win, b_win = load_w(g_("wb_in_ab"), D, 10, "win_ab_s")
        xt = self.salloc(D, F32)
        b_xt = fw.buf("xts", dma=True)
        tmp = (self.salloc(D, F32), self.salloc(4, F32), self.salloc(D, BF16), fw.buf("nrm_tmp_s"))
        hT = self.salloc(8 * NT, BF16).rearrange("p (kc t) -> p kc t", kc=8)
        b_hT = fw.buf("hTs")
        stg_b = [self.salloc(NT, BF16) for _ in range(2)]
        stg_f = [self.salloc(NT, F32) for _ in range(2)]
        b_stg_b = [fw.buf("sstgb%d" % i, dma=True) for i in range(2)]
        b_stg_f = [fw.buf("sstgf%d" % i, dma=True) for i in range(2)]
        kv_f = self.salloc(1024, F32)
        kv_b = self.salloc(1024, BF16)
        b_kv_f = fw.buf("skvf", dma=True)
        b_kv_b = fw.buf("skvb", dma=True)
        fw.dma("sp", xt[0:NT, :], xs, w=[b_xt])
        rmsnorm_T(xt, b_xt, NT, 0, hT, b_hT, 0, tmp)
        cnt = 0
        for oc in list(range(0, 4)) + list(range(12, 20)):
            pi = cnt % 2
            cnt += 1
            nb, sub = oc // 2, oc % 2
            for kc in range(8):
                fw.op("pe", lambda e, kc=kc, nb=nb, sub=sub, pi=pi: e.matmul(self.ps[pi][:, 0:NT], lhsT=win[:, nb, kc, sub * 128:(sub + 1) * 128], rhs=hT[:, kc, :], start=(kc == 0), stop=(kc == 7)), r=[b_win, b_hT], w=[self.bps[pi]])
            si = oc % 2
            r0 = (oc % 4) * 128
            if oc < 4:
                fw.op("act", lambda e, si=si, pi=pi: e.copy(stg_b[si], self.ps[pi][:, 0:NT]), r=[self.bps[pi]], w=[b_stg_b[si]])
                fw.dma("act", g_("s_sQT0")[r0:r0 + 128, :], stg_b[si], r=[b_stg_b[si]], w=[b_scr["sQT0"]])
            else:
                fw.op("act", lambda e, si=si, pi=pi: e.copy(stg_f[si], self.ps[pi][:, 0:NT]), r=[self.bps[pi]], w=[b_stg_f[si]])
                dst, nm = (g_("s_sXR"), "sXR") if oc < 16 else (g_("s_sGT"), "sGT")
                fw.dma("act", dst[r0:r0 + 128, :], stg_f[si], r=[b_stg_f[si]], w=[b_scr[nm]])
        for half in range(2):
            pi = 2 + half
            for q2 in range(2):
                nb = 2 + half * 2 + q2
                for kc in range(8):
                    fw.op("pe", lambda e, kc=kc, nb=nb, q2=q2, pi=pi: e.matmul(self.ps[pi][0:NT, q2 * 256:(q2 + 1) * 256], lhsT=hT[:, kc, :], rhs=win[:, nb, kc, :], start=(kc == 0), stop=(kc == 7)), r=[b_win, b_hT], w=[self.bps[pi]])
            fw.op("act", lambda e, half=half, pi=pi: e.copy(kv_f[0:NT, half * 512:(half + 1) * 512], self.ps[pi][0:NT, :]), r=[self.bps[pi]], w=[b_kv_f])
        fw.op("dve", lambda e: e.tensor_copy(kv_b[0:NT, :], kv_f[0:NT, :]), r=[b_kv_f], w=[b_kv_b])
        fw.dma("act", g_("s_sKV"), kv_b[0:NT, :], r=[b_kv_b], w=[b_scr["sKV"]])
        o_dil_kv_s = g_("o_dil_kv_s")
        b_d2d = fw.buf("d2d", dma=True)
        for sq in range(4):
            fw.dma("act", o_dil_kv_s[sq, 2040:2048, :], kv_f[sq * 8:(sq + 1) * 8, :], r=[b_kv_f])
            for c4 in range(4):
                fw.dma("sp", o_dil_kv_s[sq, c4 * 510:(c4 + 1) * 510, :], cache_dil[sq, 8 + c4 * 510:8 + (c4 + 1) * 510, :], w=[b_d2d])
        self.sp_off = mark
        fw.barrier()

        mark = self.sp_off
        rp = self.salloc(36, F32)
        stc = self.salloc(48, F32).rearrange("p (c s k) -> p c s k", c=4, s=4)
        strn = self.salloc(16, F32).rearrange("p (c s) -> p c s", c=4)
        b_rp = fw.buf("rps", dma=True)
        fw.dma("sp", rp, g_("rnnp"), w=[b_rp])
        fw.dma("sp", stc.rearrange("p c s k -> p (c s k)"), st_conv, w=[b_rp])
        fw.dma("sp", strn.rearrange("p c s -> p (c s)"), st_rnn, w=[b_rp])
        bd_f = self.salloc(256, F32)
        bd_b = self.salloc(256, BF16)
        b_bdf = fw.buf("bdfs", dma=True)
        b_bdb = fw.buf("bdbs")
        xr = self.salloc(4 * 11, F32).rearrange("p (s t) -> p s t", s=4)
        b_xr = fw.buf("xrs", dma=True)
        xc = self.salloc(NT, F32)
        xcb = self.salloc(NT, BF16)
        t1 = self.salloc(NT, F32)
        t2 = self.salloc(NT, F32)
        t3 = self.salloc(NT, F32)
        gtl = self.salloc(NT, F32)
        ob = self.salloc(NT, BF16)
        sp_t = self.salloc(4, F32)
        cvo = self.salloc(12, F32).rearrange("p (s k) -> p s k", s=4)
        hl = self.salloc(4, F32)
        b_w = fw.buf("rnn_s_work")
        b_gtl = fw.buf("gtls", dma=True)
        b_ob = fw.buf("obs", dma=True)
        b_cvo = fw.buf("cvos", dma=True)
        v3 = lambda a: a.rearrange("p (s t) -> p s t", s=4)
        for c in range(4):
            pr = rp[:, c * 9:(c + 1) * 9]
            fw.op("dve", lambda e: e.memset(bd_f, 0.0), w=[b_bdf])
            for gi, gw in enumerate((g_("gaw"), g_("gxw"))):
                fw.dma("sp", bd_f[0:64, gi * 128:gi * 128 + 64], gw[2 * c], w=[b_bdf])
                fw.dma("sp", bd_f[64:128, gi * 128 + 64:gi * 128 + 128], gw[2 * c + 1], w=[b_bdf])
            fw.op("dve", lambda e: e.tensor_copy(bd_b, bd_f), r=[b_bdf], w=[b_bdb])
            fw.dma("sp", xr[:, :, 3:11], g_("s_sXR")[c * 128:(c + 1) * 128, :].rearrange("p (s t) -> p s t", s=4), r=[b_scr["sXR"]], w=[b_xr])
            fw.op("dve", lambda e, c=c: e.tensor_copy(xr[:, :, 0:3], stc[:, c, :, :]), r=[b_rp, b_xr], w=[b_xr])
            fw.op("dve", lambda e: e.tensor_copy(cvo, xr[:, :, 8:11]), r=[b_xr], w=[b_cvo])
            fw.dma("act", g_("o_conv_s")[:, :, c * 128:(c + 1) * 128].rearrange("s k p -> p s k"), cvo, r=[b_cvo], allow_slow_non_contiguous=True)
            fw.op("dve", lambda e, pr=pr: e.tensor_scalar(out=v3(xc), in0=xr[:, :, 0:8], scalar1=pr[:, 0:1], scalar2=pr[:, 4:5], op0=ALU.mult, op1=ALU.add), r=[b_xr, b_rp], w=[b_w])
            for k in range(1, 4):
                fw.op("dve", lambda e, pr=pr, k=k: e.scalar_tensor_tensor(out=v3(xc), in0=xr[:, :, k:k + 8], scalar=pr[:, k:k + 1], in1=v3(xc), op0=ALU.mult, op1=ALU.add), r=[b_xr, b_rp, b_w], w=[b_w])
            fw.op("dve", lambda e: e.tensor_copy(xcb, xc), r=[b_w], w=[b_w])
            fw.op("act", lambda e, pr=pr: e.activation(out=sp_t[:, 0:1], in_=pr[:, 7:8], func=AF.Exp, scale=-1.0), r=[b_rp], w=[b_w])
            fw.op("act", lambda e: e.activation(out=sp_t[:, 1:2], in_=sp_t[:, 0:1], func=AF.Ln, bias=1.0), r=[b_w], w=[b_w])
            fw.op("dve", lambda e: e.tensor_scalar(out=sp_t[:, 2:3], in0=sp_t[:, 1:2], scalar1=-8.0, scalar2=None, op0=ALU.mult), r=[b_w], w=[b_w])
            fw.op("dve", lambda e: e.tensor_scalar(out=sp_t[:, 3:4], in0=sp_t[:, 1:2], scalar1=-16.0, scalar2=None, op0=ALU.mult), r=[b_w], w=[b_w])
            for gi, tt in enumerate((t1, t2)):
                fw.op("pe", lambda e, gi=gi: e.matmul(self.ps[gi][:, 0:NT], lhsT=bd_b[:, gi * 128:(gi + 1) * 128], rhs=xcb, start=True, stop=True), r=[b_bdb, b_w], w=[self.bps[gi]])
                fw.op("act", lambda e, gi=gi, tt=tt, pr=pr: e.activation(out=tt, in_=self.ps[gi][:, 0:NT], func=AF.Sigmoid, bias=pr[:, 5 + gi:6 + gi]), r=[self.bps[gi], b_rp], w=[b_w])
            fw.op("act", lambda e: e.activation(out=t3, in_=t1, func=AF.Exp, scale=sp_t[:, 3:4]), r=[b_w], w=[b_w])
            fw.op("act", lambda e: e.activation(out=t1, in_=t1, func=AF.Exp, scale=sp_t[:, 2:3]), r=[b_w], w=[b_w])
            fw.op("dve", lambda e: e.tensor_scalar(out=t3, in0=t3, scalar1=-1.0, scalar2=1.0, op0=ALU.mult, op1=ALU.add), r=[b_w], w=[b_w])
            fw.op("dve", lambda e: e.tensor_scalar(out=t3, in0=t3, scalar1=0.0, scalar2=None, op0=ALU.max), r=[b_w], w=[b_w])
            fw.op("act", lambda e: e.activation(out=t3, in_=t3, func=AF.Sqrt), r=[b_w], w=[b_w])
            fw.op("dve", lambda e: e.tensor_tensor(out=t3, in0=t3, in1=t2, op=ALU.mult), r=[b_w], w=[b_w])
            fw.op("dve", lambda e: e.tensor_tensor(out=t3, in0=t3, in1=xc, op=ALU.mult), r=[b_w], w=[b_w])
            for sq in range(4):
                fw.op("dve", lambda e, sq=sq, c=c: e.tensor_tensor_scan(out=t2[:, sq * 8:(sq + 1) * 8], data0=t1[:, sq * 8:(sq + 1) * 8], data1=t3[:, sq * 8:(sq + 1) * 8], initial=strn[:, c, sq:sq + 1], op0=ALU.mult, op1=ALU.add), r=[b_w, b_rp], w=[b_w])
            fw.op("dve", lambda e: e.tensor_copy(cvo[:, :, 0], v3(t2)[:, :, 7]), r=[b_w, b_cvo], w=[b_cvo])
            fw.dma("act", g_("o_rnn_s")[:, c * 128:(c + 1) * 128].rearrange("s p -> p s"), cvo[:, :, 0], r=[b_cvo], allow_slow_non_contiguous=True)
            fw.dma("sp", gtl, g_("s_sGT")[c * 128:(c + 1) * 128, :], r=[b_scr["sGT"]], w=[b_gtl])
            fw.op("act", lambda e: e.activation(out=gtl, in_=gtl, func=AF.Gelu), r=[b_gtl], w=[b_gtl])
            fw.op("dve", lambda e: e.tensor_tensor(out=ob, in0=t2, in1=gtl, op=ALU.mult), r=[b_w, b_gtl], w=[b_ob])
            fw.dma("act", g_("s_sOT0")[512 + c * 128:512 + (c + 1) * 128, :], ob, r=[b_ob], w=[b_scr["sOT0"]])
        self.sp_off = mark
        fw.barrier()

        mark = self.sp_off
        c_cnt_s, c_kaug_s, c_qaug0_s = g_("c_cnt_s"), g_("c_kaug_s"), g_("c_qaug0_s")
        cnt_t = self.salloc(17 * 8, BF16).rearrange("p (k q) -> p k q", k=17)
        kaug_t = self.salloc(17 * 128, BF16, parts=KR).rearrange("p (k t) -> p k t", k=17)
        b_cc = fw.buf("sc_const", dma=True)
        fw.dma("sp", cnt_t, c_cnt_s, w=[b_cc])
        fw.dma("sp", kaug_t[64:KR, :, :], c_kaug_s.rearrange("a (k t) -> a k t", k=17), w=[b_cc])
        cts = [self.salloc(1024, F32) for _ in range(2)]
        b_cts = [fw.buf("ct%d" % i, dma=True) for i in range(2)]
        ctbs = [self.salloc(1024, BF16) for _ in range(2)]
        b_ctbs = [fw.buf("ctb%d" % i, dma=True) for i in range(2)]
        KA = [self.salloc(8 * 128, BF16, parts=KR).rearrange("p (h t) -> p h t", h=8) for _ in range(2)]
        b_KA = [fw.buf("KA%d" % i) for i in range(2)]
        QA = self.salloc(64, BF16, parts=KR).rearrange("p (h q) -> p h q", h=8)
        b_QA = fw.buf("QAs", dma=True)
        Pf = [self.salloc(64, F32) for _ in range(2)]
        Pb = [self.salloc(64, BF16).rearrange("p (h q) -> p h q", h=8) for _ in range(2)]
        b_P = [fw.buf("Ps%d" % i) for i in range(2)]
        Va = [self.salloc(8 * 65, BF16).rearrange("p (h d) -> p h d", h=8) for _ in range(2)]
        b_Va = [fw.buf("Va%d" % i) for i in range(2)]
        for i in range(2):
            fw.op("dve", lambda e, i=i: e.memset(Va[i][:, :, 64:65], 1.0), w=[b_Va[i]])
        osm = self.salloc(16, F32)
        of = self.salloc(512, F32)
        obf = self.salloc(512, BF16)
        oT = self.salloc(32, BF16).rearrange("p (c q) -> p c q", c=4)
        b_o = fw.buf("so_work")
        b_oT = fw.buf("soT", dma=True)
        pstb = self.ps[7][:, :].bitcast(BF16)
        PA, PB_ = 4, 5
        u = 0
        for sq in range(4):
            fw.dma("sp", QA[0:64, :, :], g_("s_sQT0")[:, sq * 8:(sq + 1) * 8].rearrange("(h d) t -> d h t", d=64), r=[b_scr["sQT0"]], w=[b_QA])
            fw.dma("sp", QA[64:KR, :, :], c_qaug0_s, w=[b_QA])
            for kt in range(17):
                i = u % 2
                u += 1
                nk = 128 if kt < 16 else 8
                ctb, b_ctb = ctbs[i], b_ctbs[i]
                if kt < 16:
                    fw.dma("sp", cts[i], cache_dil[sq, kt * 128:(kt + 1) * 128, :], w=[b_cts[i]])
                    fw.op("dve", lambda e, i=i, ctb=ctb: e.tensor_copy(ctb[:, 0:512], cts[i][:, 0:512]), r=[b_cts[i]], w=[b_ctb])
                    fw.op("pool", lambda e, i=i, ctb=ctb: e.tensor_copy(ctb[:, 512:1024], cts[i][:, 512:1024]), r=[b_cts[i]], w=[b_ctb])
                else:
                    fw.dma("sp", ctb[0:8, :], g_("s_sKV")[sq * 8:(sq + 1) * 8, :], r=[b_scr["sKV"]], w=[b_ctb])
                for c4 in range(4):
                    fw.op("pe", lambda e, c4=c4, ctb=ctb, nk=nk: e.transpose(pstb[:, c4 * 128:c4 * 128 + nk], ctb[0:nk, c4 * 128:(c4 + 1) * 128], ident_b[0:nk, 0:nk]), r=[b_ctb, b_const2], w=[self.bps[7]])
                ka, b_ka = KA[i], b_KA[i]
                pv = pstb[:, 0:512].rearrange("p (c t) -> p c t", c=4)
                fw.op("dve", lambda e, ka=ka, pv=pv, nk=nk: e.tensor_copy(ka[0:64, 0:8:2, 0:nk], pv[0:64, :, 0:nk]), r=[self.bps[7]], w=[b_ka])
                fw.op("dve", lambda e, ka=ka, pv=pv, nk=nk: e.tensor_copy(ka[0:64, 1:8:2, 0:nk], pv[64:128, :, 0:nk]), r=[self.bps[7]], w=[b_ka])
                fw.op("dve", lambda e, ka=ka, kt=kt: e.tensor_copy(ka[64:KR, :, :], kaug_t[64:KR, kt:kt + 1, :].to_broadcast([NAUG, 8, 128])), r=[b_cc], w=[b_ka])
                va, b_va = Va[i], b_Va[i]
                fw.op("act", lambda e, va=va, ctb=ctb, nk=nk: e.copy(va[0:nk, :, 0:64], ctb[0:nk, 512:1024].rearrange("p (h d) -> p h d", h=8)), r=[b_ctb], w=[b_va])
                Sb, bS = self.ps[i], self.bps[i]
                for h in range(8):
                    fw.op("pe", lambda e, Sb=Sb, h=h, ka=ka, nk=nk: e.matmul(Sb[0:nk, h * 8:(h + 1) * 8], lhsT=ka[:, h, 0:nk], rhs=QA[:, h, :], start=True, stop=True), r=[b_ka, b_QA], w=[bS])
                fw.op("act", lambda e, Sb=Sb, i=i, nk=nk: e.activation(out=Pf[i][0:nk, :], in_=Sb[0:nk, 0:64], func=AF.Exp, scale=0.125), r=[bS], w=[b_P[i]])
                fw.op("dve", lambda e, i=i, kt=kt, nk=nk: e.tensor_tensor(out=Pb[i][0:nk, :, :], in0=Pf[i][0:nk, :].rearrange("p (h q) -> p h q", h=8), in1=cnt_t[0:nk, kt:kt + 1, :].to_broadcast([nk, 8, 8]), op=ALU.mult), r=[b_P[i], b_cc], w=[b_P[i]])
                for h in range(8):
                    bank = PA if h < 4 else PB_
                    hh = h % 4
                    fw.op("pe", lambda e, bank=bank, hh=hh, h=h, i=i, va=va, nk=nk, kt=kt: e.matmul(self.ps[bank][0:8, hh * 65:(hh + 1) * 65], lhsT=Pb[i][0:nk, h, :], rhs=va[0:nk, h, :], start=(kt == 0 and hh == 0), stop=(kt == 16), skip_group_check=True), r=[b_P[i], b_va], w=[self.bps[bank]])
            for half in range(2):
                bank = PA if half == 0 else PB_
                ov = self.ps[bank][0:8, 0:260].rearrange("p (h x) -> p h x", h=4)
                fw.op("dve", lambda e, ov=ov, half=half: e.reciprocal(out=osm[0:8, half * 4:(half + 1) * 4], in_=ov[:, :, 64]), r=[self.bps[bank]], w=[b_o])
                fw.op("dve", lambda e, ov=ov, half=half: e.tensor_tensor(out=of[0:8, half * 256:(half + 1) * 256].rearrange("p (h d) -> p h d", h=4), in0=ov[:, :, 0:64], in1=osm[0:8, half * 4:(half + 1) * 4].rearrange("p (h o) -> p h o", o=1).to_broadcast([8, 4, 64]), op=ALU.mult), r=[self.bps[bank], b_o], w=[b_o])
            fw.op("act", lambda e: e.copy(obf[0:8, :], of[0:8, :]), r=[b_o], w=[b_o])
            for c4 in range(4):
                fw.op("pe", lambda e, c4=c4: e.transpose(pstb[:, 512 + c4 * 8:512 + (c4 + 1) * 8], obf[0:8, c4 * 128:(c4 + 1) * 128], ident_b[0:8, 0:8]), r=[b_o, b_const2], w=[self.bps[7]])
            fw.op("dve", lambda e: e.tensor_copy(oT, pstb[:, 512:544].rearrange("p (c q) -> p c q", c=4)), r=[self.bps[7]], w=[b_oT])
            fw.dma("act", g_("s_sOT0")[0:512, sq * 8:(sq + 1) * 8].rearrange("(c p) t -> p c t", p=128), oT, r=[b_oT], w=[b_scr["sOT0"]])
        self.sp_off = mark
        fw.barrier()
        self.phaseD_s(L, 0)
        if L["S"] >= 6:
            self.sample_nsa(L)
            self.phaseD_s(L, 1)

    def sample_nsa(self, L):
        fw = self.fw
        g_ = lambda n: L[n]
        b_scr = g_("b_scr")
        ident_b, ones_b, b_const, b_const2 = g_("ident_b"), g_("ones_b"), g_("b_const"), g_("b_const2")
        pstb = self.ps[7][:, :].bitcast(BF16)

        def pool_gather(out, pool, idx_col, r, w, dsem):
            o = Op("pool", (lambda e: e.indirect_dma_start(out=out, out_offset=None, in_=pool, in_offset=bass.IndirectOffsetOnAxis(ap=idx_col, axis=0))), None, isdma=True, dsem=dsem)
            o.deps = fw._deps(o.key, r, w)
            dsem.count += 16
            o.dval = dsem.count
            fw.ops["pool"].append(o)
            fw._commit(o, r, w)

        mark = self.sp_off
        pti = self.salloc(256, I32)
        ptf = self.salloc(256, F32)
        iot = self.salloc(1, F32)
        idx = self.salloc(256, I32)
        b_pt = fw.buf("pt", dma=True)
        b_idx = fw.buf("idx")
        fw.dma("sp", pti, g_("ptab")[0:1, :].to_broadcast([128, 256]), w=[b_pt])
        fw.dma("sp", iot, g_("c_iota_p"), w=[b_pt])
        fw.op("dve", lambda e: e.tensor_copy(ptf, pti), r=[b_pt], w=[b_idx])
        fw.op("dve", lambda e: e.tensor_scalar(out=ptf, in0=ptf, scalar1=128.0, scalar2=iot[:, 0:1], op0=ALU.mult, op1=ALU.add), r=[b_idx, b_pt], w=[b_idx])
        fw.op("dve", lambda e: e.tensor_copy(idx, ptf), r=[b_idx], w=[b_idx])
        cts = [self.salloc(512, F32) for _ in range(2)]
        b_cts = [fw.buf("gct%d" % i, dma=True) for i in range(2)]
        gsems = [fw.new_dsem(sw=True) for _ in range(2)]
        ctbs = [self.salloc(512, BF16) for _ in range(2)]
        b_ctbs = [fw.buf("gctb%d" % i, dma=True) for i in range(2)]
        xts = [self.salloc(512, BF16).rearrange("p (c t) -> p c t", c=4) for _ in range(2)]
        b_xts = [fw.buf("gxt%d" % i, dma=True) for i in range(2)]
        u = 0
        for sq in range(4):
            for (cname, pool, nch, ntile, newsrc, nmnew) in (("cmp", g_("pool_cmp"), 4, 65, g_("o_cmp_kv_s"), "ocmp"), ("sel", g_("pool_sel"), 2, 65, g_("o_sel_kv_s"), "osel"), ("win", None, 2, 5, g_("o_win_kv_s"), "owin")):
                for kt in range(ntile):
                    i = u % 2
                    u += 1
                    last = (kt == ntile - 1)
                    nk = 8 if last else 128
                    if last:
                        src = newsrc[sq * 8:(sq + 1) * 8, :] if cname != "win" else newsrc[sq, 504:512, :]
                        fw.dma("sp", cts[i][0:8, :], src, r=[b_scr[nmnew]], w=[b_cts[i]])
                    elif cname == "win":
                        fw.dma("sp", cts[i], g_("cache_win")[sq, kt * 128:(kt + 1) * 128, :], w=[b_cts[i]])
                    else:
                        pool_gather(cts[i], pool, idx[:, sq * 64 + kt:sq * 64 + kt + 1], [b_idx], [b_cts[i]], gsems[i])
                    fw.op("dve", lambda e, i=i, nk=nk: e.tensor_copy(ctbs[i][0:nk, :], cts[i][0:nk, :]), r=[b_cts[i]], w=[b_ctbs[i]])
                    for c4 in range(nch):
                        fw.op("pe", lambda e, c4=c4, i=i, nk=nk: e.transpose(pstb[:, c4 * 128:c4 * 128 + nk], ctbs[i][0:nk, c4 * 128:(c4 + 1) * 128], ident_b[0:nk, 0:nk]), r=[b_ctbs[i], b_const2], w=[self.bps[7]])
                    fw.op("act", lambda e, i=i, nk=nk, nch=nch: e.copy(xts[i][:, 0:nch, 0:nk], pstb[:, 0:512].rearrange("p (c t) -> p c t", c=4)[:, 0:nch, 0:nk]), r=[self.bps[7]], w=[b_xts[i]])
                    dT = {"cmp": g_("s_cT_cmp"), "sel": g_("s_cT_sel"), "win": g_("s_cT_win")}[cname]
                    fw.dma("act", dT[sq, :, kt * 128:kt * 128 + nk].rearrange("(c p) t -> p c t", p=128), xts[i][:, 0:nch, 0:nk], r=[b_xts[i]], w=[b_scr["cT_" + cname]])
                    if cname != "cmp":
                        dV = g_("s_cV_sel") if cname == "sel" else g_("s_cV_win")
                        fw.dma("act", dV[sq, kt * 128:kt * 128 + nk, :], ctbs[i][0:nk, 256:512], r=[b_ctbs[i]], w=[b_scr["cV_" + cname]])
        self.sp_off = mark
        fw.barrier()

        mark = self.sp_off
        wc_f = self.salloc(4096, F32, parts=64)
        wc_b = self.salloc(4096, BF16, parts=64).rearrange("p (l c e) -> p l c e", l=32, c=2)
        pe_f = self.salloc(64, F32, parts=64)
        pe_b = self.salloc(64, BF16, parts=64).rearrange("p (l c) -> p l c", c=2)
        e30s = self.salloc(8192, BF16)
        c2s_t = self.salloc(4 * 129, BF16).rearrange("p (t x) -> p t x", t=4)
        tsel_t = self.salloc(129, F32)
        cm8 = self.salloc(32, BF16)
        wm0 = self.salloc(32, BF16)
        b_fc = fw.buf("sfconst", dma=True)
        b_fc2 = fw.buf("sfconst2")
        fw.dma("sp", wc_f, g_("w_cmp_t"), w=[b_fc])
        fw.dma("sp", pe_f, g_("pe_cmp_t"), w=[b_fc])
        fw.dma("sp", e30s, g_("c_e30s"), w=[b_fc])
        fw.dma("sp", c2s_t, g_("c_c2s_s"), w=[b_fc])
        fw.dma("sp", tsel_t[0:8, :], g_("c_tsel_s"), w=[b_fc])
        fw.dma("sp", cm8[0:8, :], g_("c_cm8"), w=[b_fc])
        fw.dma("sp", wm0, g_("c_wm0"), w=[b_fc])
        fw.op("dve", lambda e: e.tensor_copy(wc_b.rearrange("p l c e -> p (l c e)"), wc_f), r=[b_fc], w=[b_fc2])
        fw.op("dve", lambda e: e.tensor_copy(pe_b.rearrange("p l c -> p (l c)"), pe_f), r=[b_fc], w=[b_fc2])
        XK = self.salloc(8192, BF16, parts=64)
        XV = self.salloc(8192, BF16, parts=64)
        b_XK = fw.buf("sXK", dma=True)
        b_XV = fw.buf("sXV", dma=True)
        Ksa = self.salloc(8320, BF16, parts=KR)
        Kwa = self.salloc(640, BF16, parts=KR)
        Kca = self.salloc(512, BF16, parts=KR)
        b_Ksa, b_Kwa, b_Kca = fw.buf("sKsa", dma=True), fw.buf("sKwa", dma=True), fw.buf("sKca", dma=True)
        fw.dma("sp", Ksa[64:KR, :], g_("c_kaug_ss"), w=[b_Ksa])
        fw.dma("sp", Kwa[64:KR, :], g_("c_kaug_sw"), w=[b_Kwa])
        fw.dma("sp", Kca[64:KR, :], g_("c_kaugc_s"), w=[b_Kca])
        VSa = self.salloc(65 * 4 * 65, BF16).rearrange("p (k g d) -> p k g d", k=65, g=4)
        VWa = self.salloc(5 * 4 * 65, BF16).rearrange("p (k g d) -> p k g d", k=5, g=4)
        b_VSa, b_VWa = fw.buf("sVSa", dma=True), fw.buf("sVWa", dma=True)
        fw.op("dve", lambda e: e.memset(VSa[:, :, :, 64:65], 1.0), w=[b_VSa])
        fw.op("dve", lambda e: e.memset(VWa[:, :, :, 64:65], 1.0), w=[b_VWa])
        VCX = self.salloc(4 * 194, BF16).rearrange("p (t x) -> p t x", t=4)
        b_VCX = fw.buf("sVCX")
        fw.op("dve", lambda e: e.memset(VCX[:, :, 64:65], 1.0), w=[b_VCX])
        fw.op("dve", lambda e: e.tensor_copy(VCX[:, :, 65:194], c2s_t), r=[b_fc], w=[b_VCX])
        vcT = self.salloc(512, BF16, parts=64)
        pekv = self.salloc(4, F32, parts=64)
        b_pe = fw.buf("spekv")
        QA = self.salloc(32, BF16, parts=KR).rearrange("p (r q) -> p r q", r=4)
        b_QA = fw.buf("sQA1", dma=True)
        Pc = self.salloc(128, BF16).rearrange("p (t x) -> p t x", t=4)
        b_Pc = fw.buf("sPc")
        Pss = [self.salloc(32, BF16) for _ in range(3)]
        b_Pss = [fw.buf("sPs%d" % i) for i in range(3)]
        sc = self.salloc(129, F32)
        sc2 = self.salloc(129, F32)
        m8 = self.salloc(16, F32)
        small = self.salloc(64, F32)
        madd = self.salloc(128, BF16)
        msel = self.salloc(32, BF16).rearrange("p (r q) -> p r q", r=4)
        b_sel, b_msel = fw.buf("sseltmp"), fw.buf("smsel")
        gt = self.salloc(48, F32)
        b_gt = fw.buf("sgt", dma=True)
        otile = self.salloc(256, F32)
        otb = self.salloc(256, BF16)
        b_ot_ = fw.buf("sotile")
        ot1 = self.salloc(16, BF16).rearrange("p (c t) -> p c t", c=2)
        b_ot1 = fw.buf("sot1", dma=True)
        PS_S = (0, 1)
        PS_CA, PS_CB, PS_OS, PS_OW, PS_X, PS_T = 2, 3, 4, 5, 6, 7
        NROWS = (128, 128, 128, 127)
        ucnt = 0
        for sq in range(4):
            for kt in range(65):
                nk = 128 if kt < 64 else 8
                fw.dma("sp", VSa[0:nk, kt, :, 0:64], g_("s_cV_sel")[sq, kt * 128:kt * 128 + nk, :].rearrange("p (g d) -> p g d", g=4), r=[b_scr["cV_sel"]], w=[b_VSa])
            for kt in range(5):
                nk = 128 if kt < 4 else 8
                fw.dma("sp", VWa[0:nk, kt, :, 0:64], g_("s_cV_win")[sq, kt * 128:kt * 128 + nk, :].rearrange("p (g d) -> p g d", g=4), r=[b_scr["cV_win"]], w=[b_VWa])
            fw.dma("sp", gt[0:8, :], g_("s_sG")[sq * 8:(sq + 1) * 8, :], r=[b_scr["sG"]], w=[b_gt])
            for g in range(4):
                fw.dma("sp", XK, g_("s_cT_cmp")[sq, g * 64:(g + 1) * 64, 0:8192], r=[b_scr["cT_cmp"]], w=[b_XK])
                fw.dma("sp", XV, g_("s_cT_cmp")[sq, 256 + g * 64:256 + (g + 1) * 64, 0:8192], r=[b_scr["cT_cmp"]], w=[b_XV])
                fw.dma("sp", Ksa[0:64, 0:8200], g_("s_cT_sel")[sq, g * 64:(g + 1) * 64, 0:8200], r=[b_scr["cT_sel"]], w=[b_Ksa])
                fw.dma("sp", Kwa[0:64, 0:520], g_("s_cT_win")[sq, g * 64:(g + 1) * 64, 0:520], r=[b_scr["cT_win"]], w=[b_Kwa])
                fw.dma("sp", QA[0:64, :, :], g_("s_sQT1")[g * 256:(g + 1) * 256, sq * 8:(sq + 1) * 8].rearrange("(r d) t -> d r t", d=64), r=[b_scr["sQT1"]], w=[b_QA])
                fw.dma("sp", QA[64:KR, :, :], g_("c_qaug1_s")[g], w=[b_QA])
                QA32 = QA.rearrange("p r q -> p (r q)")
                for c_ in range(2):
                    for l in range(32):
                        fw.op("pe", lambda e, l=l, c_=c_: e.matmul(self.ps[PS_X][0:64, c_:c_ + 1], lhsT=wc_b[:, l, c_, :], rhs=pe_b[:, l, c_:c_ + 1], start=(l == 0), stop=(l == 31)), r=[b_fc2], w=[self.bps[PS_X]])
                fw.op("dve", lambda e: e.tensor_copy(pekv[:, 0:2], self.ps[PS_X][0:64, 0:2]), r=[self.bps[PS_X]], w=[b_pe])
                for c_, (X, bX, bank) in enumerate(((XK, b_XK, PS_X), (XV, b_XV, PS_T))):
                    for l in range(32):
                        fw.op("pe", lambda e, l=l, c_=c_, X=X, bank=bank: e.matmul(self.ps[bank][0:64, 0:511], lhsT=wc_b[:, l, c_, :], rhs=X[:, l:l + 16 * 510 + 1:16], start=(l == 0), stop=(l == 31)), r=[b_fc2, bX], w=[self.bps[bank]])
                fw.op("dve", lambda e: e.memset(Kca[0:64, 511:512], 0.0), w=[b_Kca])
                fw.op("dve", lambda e: e.tensor_scalar(out=Kca[0:64, 0:511], in0=self.ps[PS_X][0:64, 0:511], scalar1=pekv[:, 0:1], scalar2=None, op0=ALU.add), r=[self.bps[PS_X], b_pe], w=[b_Kca])
                fw.op("dve", lambda e: e.memset(vcT[:, 511:512], 0.0), w=[b_pe])
                fw.op("dve", lambda e: e.tensor_scalar(out=vcT[:, 0:511], in0=self.ps[PS_T][0:64, 0:511], scalar1=pekv[:, 1:2], scalar2=None, op0=ALU.add), r=[self.bps[PS_T], b_pe], w=[b_pe])
                for nt_ in range(4):
                    fw.op("pe", lambda e, nt_=nt_: e.transpose(pstb[:, nt_ * 64:(nt_ + 1) * 64], vcT[:, nt_ * 128:(nt_ + 1) * 128], ident_b[0:64, 0:64]), r=[b_pe, b_const2], w=[self.bps[PS_T]])
                fw.op("dve", lambda e: e.tensor_copy(VCX[:, :, 0:64], pstb[:, 0:256].rearrange("p (t d) -> p t d", t=4)), r=[self.bps[PS_T]], w=[b_VCX])
                Sb, bS = self.ps[PS_S[0]], self.bps[PS_S[0]]
                for nt_ in range(4):
                    nr = NROWS[nt_]
                    fw.op("pe", lambda e, nt_=nt_, nr=nr, Sb=Sb: e.matmul(Sb[0:nr, nt_ * 32:(nt_ + 1) * 32], lhsT=Kca[:, nt_ * 128:nt_ * 128 + nr], rhs=QA32, start=True, stop=True), r=[b_Kca, b_QA], w=[bS])
                fw.op("act", lambda e, Sb=Sb: e.activation(out=Pc[:, 0:3, :], in_=Sb[:, 0:96].rearrange("p (t x) -> p t x", t=3), func=AF.Exp, scale=0.125), r=[bS], w=[b_Pc])
                fw.op("act", lambda e, Sb=Sb: e.activation(out=Pc[0:127, 3, :], in_=Sb[0:127, 96:128], func=AF.Exp, scale=0.125), r=[bS], w=[b_Pc])
                for r_ in range(4):
                    bank = PS_CA if r_ < 2 else PS_CB
                    c0 = (r_ % 2) * 194
                    for nt_ in range(4):
                        nr = NROWS[nt_]
                        fw.op("pe", lambda e, bank=bank, c0=c0, nt_=nt_, nr=nr, r_=r_: e.matmul(self.ps[bank][0:8, c0:c0 + 194], lhsT=Pc[0:nr, nt_, r_ * 8:(r_ + 1) * 8], rhs=VCX[0:nr, nt_, :], start=(nt_ == 0), stop=(nt_ == 3)), r=[b_Pc, b_VCX], w=[self.bps[bank]])
                for r_ in range(4):
                    bank = PS_CA if r_ < 2 else PS_CB
                    c0 = (r_ % 2) * 194
                    fw.op("dve", lambda e, bank=bank, c0=c0, r_=r_: e.tensor_scalar(out=small[0:8, r_:r_ + 1], in0=self.ps[bank][0:8, c0 + 64:c0 + 65], scalar1=1e-30, scalar2=None, op0=ALU.max), r=[self.bps[bank]], w=[b_sel])
                fw.op("dve", lambda e: e.reciprocal(out=small[0:8, 16:20], in_=small[0:8, 0:4]), r=[b_sel], w=[b_sel])
                for r_ in range(4):
                    bank = PS_CA if r_ < 2 else PS_CB
                    c0 = (r_ % 2) * 194
                    if r_ == 0:
                        fw.op("dve", lambda e, bank=bank, c0=c0: e.tensor_scalar(out=sc[0:8, :], in0=self.ps[bank][0:8, c0 + 65:c0 + 194], scalar1=small[0:8, 16:17], scalar2=None, op0=ALU.mult), r=[self.bps[bank], b_sel], w=[b_sel])
                    else:
                        fw.op("dve", lambda e, bank=bank, c0=c0, r_=r_: e.scalar_tensor_tensor(out=sc[0:8, :], in0=self.ps[bank][0:8, c0 + 65:c0 + 194], scalar=small[0:8, 16 + r_:17 + r_], in1=sc[0:8, :], op0=ALU.mult, op1=ALU.add), r=[self.bps[bank], b_sel], w=[b_sel])
                fw.op("dve", lambda e: e.tensor_tensor(out=sc[0:8, :], in0=sc[0:8, :], in1=tsel_t[0:8, :], op=ALU.add), r=[b_sel, b_fc], w=[b_sel])
                fw.op("dve", lambda e: e.max(out=m8[0:8, 0:8], in_=sc[0:8, :]), r=[b_sel], w=[b_sel])
                fw.op("dve", lambda e: e.match_replace(out=sc2[0:8, :], in_to_replace=m8[0:8, 0:8], in_values=sc[0:8, :], imm_value=-1e30), r=[b_sel], w=[b_sel])
                fw.op("dve", lambda e: e.max(out=m8[0:8, 8:16], in_=sc2[0:8, :]), r=[b_sel], w=[b_sel])
                fw.op("dve", lambda e: e.tensor_scalar(out=madd[0:8, :], in0=sc[0:8, 0:128], scalar1=m8[0:8, 15:16], scalar2=1.0, op0=ALU.is_ge, op1=ALU.subtract), r=[b_sel], w=[b_sel])
                fw.op("pe", lambda e: e.transpose(pstb[:, 512:520], madd[0:8, :], ident_b[0:8, 0:8]), r=[b_sel, b_const2], w=[self.bps[PS_T]])
                fw.op("dve", lambda e: e.tensor_copy(msel, pstb[:, 512:520].rearrange("p (o q) -> p o q", o=1).to_broadcast([128, 4, 8])), r=[self.bps[PS_T]], w=[b_msel])
                msel32 = msel.rearrange("p r q -> p (r q)")
                for (br, Ka, b_Ka, Vt, bV, bank, ntile) in ((1, Ksa, b_Ksa, VSa, b_VSa, PS_OS, 65), (2, Kwa, b_Kwa, VWa, b_VWa, PS_OW, 5)):
                    for kt in range(ntile):
                        si = ucnt % 2
                        ucnt += 1
                        last = (kt == ntile - 1)
                        nk = 8 if last else 128
                        Sb, bS = self.ps[PS_S[si]], self.bps[PS_S[si]]
                        extra = []
                        if last:
                            extra.append((ident_b[0:8, 0:8], cm8[0:8, :], [b_const2, b_fc]))
                        elif br == 1:
                            extra.append((e30s[:, kt * 128:(kt + 1) * 128], msel32, [b_fc, b_msel]))
                        elif kt == 0:
                            extra.append((ident_b, wm0, [b_const2, b_fc]))
                        fw.op("pe", lambda e, Sb=Sb, Ka=Ka, kt=kt, nk=nk, ne=len(extra): e.matmul(Sb[0:nk, 0:32], lhsT=Ka[:, kt * 128:kt * 128 + nk], rhs=QA32, start=True, stop=(ne == 0)), r=[b_Ka, b_QA], w=[bS])
                        for (lh, rh, rb) in extra:
                            fw.op("pe", lambda e, Sb=Sb, lh=lh, rh=rh, nk=nk: e.matmul(Sb[0:nk, 0:32], lhsT=lh, rhs=rh, start=False, stop=True), r=rb, w=[bS])
                        pi_ = ucnt % 3
                        pp_, b_pp = Pss[pi_], b_Pss[pi_]
                        fw.op("act", lambda e, Sb=Sb, pp_=pp_, nk=nk: e.activation(out=pp_[0:nk, :], in_=Sb[0:nk, 0:32], func=AF.Exp, scale=0.125), r=[bS], w=[b_pp])
                        for r_ in range(4):
                            fw.op("pe", lambda e, bank=bank, r_=r_, pp_=pp_, Vt=Vt, kt=kt, g=g, nk=nk, ntile=ntile: e.matmul(self.ps[bank][0:8, r_ * 65:(r_ + 1) * 65], lhsT=pp_[0:nk, r_ * 8:(r_ + 1) * 8], rhs=Vt[0:nk, kt, g, :], start=(kt == 0 and r_ == 0), stop=(kt == ntile - 1), skip_group_check=True), r=[b_pp, bV], w=[self.bps[bank]])
                    fw.op("dve", lambda e, bank=bank, br=br: e.tensor_scalar(out=small[0:8, br * 4:(br + 1) * 4], in0=self.ps[bank][0:8, 0:260].rearrange("p (r x) -> p r x", r=4)[:, :, 64], scalar1=1e-30, scalar2=None, op0=ALU.max), r=[self.bps[bank]], w=[b_sel])
                fw.op("dve", lambda e: e.reciprocal(out=small[0:8, 16:28], in_=small[0:8, 0:12]), r=[b_sel], w=[b_sel])
                fw.op("dve", lambda e, g=g: e.tensor_tensor(out=small[0:8, 32:44].rearrange("p (b r) -> p b r", b=3), in0=small[0:8, 16:28].rearrange("p (b r) -> p b r", b=3), in1=gt[0:8, g * 12:(g + 1) * 12].rearrange("p (r b) -> p b r", b=3), op=ALU.mult), r=[b_sel, b_gt], w=[b_sel])
                for r_ in range(4):
                    bankc = PS_CA if r_ < 2 else PS_CB
                    c0 = (r_ % 2) * 194
                    od = otile[0:8, r_ * 64:(r_ + 1) * 64]
                    fw.op("dve", lambda e, bankc=bankc, c0=c0, od=od, r_=r_: e.tensor_scalar(out=od, in0=self.ps[bankc][0:8, c0:c0 + 64], scalar1=small[0:8, 32 + r_:33 + r_], scalar2=None, op0=ALU.mult), r=[self.bps[bankc], b_sel], w=[b_ot_])
                    fw.op("dve", lambda e, od=od, r_=r_: e.scalar_tensor_tensor(out=od, in0=self.ps[PS_OS][0:8, r_ * 65:r_ * 65 + 64], scalar=small[0:8, 36 + r_:37 + r_], in1=od, op0=ALU.mult, op1=ALU.add), r=[self.bps[PS_OS], b_sel, b_ot_], w=[b_ot_])
                    fw.op("dve", lambda e, od=od, r_=r_: e.scalar_tensor_tensor(out=od, in0=self.ps[PS_OW][0:8, r_ * 65:r_ * 65 + 64], scalar=small[0:8, 40 + r_:41 + r_], in1=od, op0=ALU.mult, op1=ALU.add), r=[self.bps[PS_OW], b_sel, b_ot_], w=[b_ot_])
                fw.op("act", lambda e: e.copy(otb[0:8, :], otile[0:8, :]), r=[b_ot_], w=[b_ot_])
                for c_ in range(2):
                    fw.op("pe", lambda e, c_=c_: e.transpose(pstb[:, 528 + c_ * 8:528 + (c_ + 1) * 8], otb[0:8, c_ * 128:(c_ + 1) * 128], ident_b[0:8, 0:8]), r=[b_ot_, b_const2], w=[self.bps[PS_T]])
                fw.op("dve", lambda e: e.tensor_copy(ot1, pstb[:, 528:544].rearrange("p (c t) -> p c t", c=2)), r=[self.bps[PS_T]], w=[b_ot1])
                fw.dma("act", g_("s_sOT1")[g * 256:(g + 1) * 256, sq * 8:(sq + 1) * 8].rearrange("(c p) t -> p c t", p=128), ot1, r=[b_ot1], w=[b_scr["sOT1"]])
        self.sp_off = mark
        fw.barrier()

    def phaseD_s(self, L, l):
        fw = self.fw
        g_ = lambda n: L[n]
        b_scr, load_w, rmsnorm_T = g_("b_scr"), g_("load_w"), g_("rmsnorm_T")
        NT = 32
        mark = self.sp_off
        s_OT = g_("s_sOT0") if l == 0 else g_("s_sOT1")
        nm_OT = "sOT0" if l == 0 else "sOT1"
        wout, b_wout = load_w(g_("wb_out_ab") if l == 0 else g_("wb_out_c"), D, 4, "wout_s")
        w2t, b_w2 = load_w(g_("wb_w2")[l], FFN, 4, "w2_s")
        ot = self.salloc(8 * NT, BF16).rearrange("p (kc t) -> p kc t", kc=8)
        b_ot = fw.buf("ot_s", dma=True)
        x1 = self.salloc(D, F32)
        b_x1 = fw.buf("x1_s", dma=True)
        tmp = (self.salloc(D, F32), self.salloc(4, F32), self.salloc(D, BF16), fw.buf("nrm_tmp_s2"))
        hT = self.salloc(8 * NT, BF16).rearrange("p (kc t) -> p kc t", kc=8)
        b_hT = fw.buf("hT_s2")
        gT = self.salloc(22 * NT, BF16).rearrange("p (hc t) -> p hc t", hc=22)
        b_gT = fw.buf("gT_s")
        wst = [self.salloc(2 * 8 * 256, BF16).rearrange("p (w kc n) -> p w kc n", w=2, kc=8) for _ in range(2)]
        b_wst = [fw.buf("wst_s%d" % i, dma=True) for i in range(2)]
        stmp = [self.salloc(NT, F32) for _ in range(2)]
        b_stmp = [fw.buf("stmp_s%d" % i) for i in range(2)]
        tm_f = [self.salloc(512, F32) for _ in range(2)]
        b_tm_f = [fw.buf("tmf_s%d" % i, dma=True) for i in range(2)]
        fw.dma("sp", ot, s_OT.rearrange("(kc p) t -> p kc t", p=128), r=[b_scr[nm_OT]], w=[b_ot])
        if l == 0:
            fw.dma("sp", x1[0:NT, :], g_("xs"), w=[b_x1])
        else:
            fw.dma("sp", x1[0:NT, :], g_("s_sX1"), r=[b_scr["sX1"]], w=[b_x1])
        for half in range(2):
            pi = half
            for q2 in range(2):
                nb = half * 2 + q2
                for kc in range(8):
                    fw.op("pe", lambda e, kc=kc, nb=nb, q2=q2, pi=pi: e.matmul(self.ps[pi][0:NT, q2 * 256:(q2 + 1) * 256], lhsT=ot[:, kc, :], rhs=wout[:, nb, kc, :], start=(kc == 0), stop=(kc == 7)), r=[b_ot, b_wout], w=[self.bps[pi]])
            fw.op("dve", lambda e, pi=pi, half=half: e.tensor_tensor(out=x1[0:NT, half * 512:(half + 1) * 512], in0=self.ps[pi][0:NT, :], in1=x1[0:NT, half * 512:(half + 1) * 512], op=ALU.add), r=[self.bps[pi], b_x1], w=[b_x1])
        rmsnorm_T(x1, b_x1, NT, 1 + 2 * l, hT, b_hT, 0, tmp)
        wcnt = 0
        for nb in range(11):
            wi = wcnt % 2
            wcnt += 1
            fw.dma("sp", wst[wi][:, 0], g_("wb_w1")[l][nb].rearrange("p (kc n) -> p kc n", kc=8), r=[b_scr["w"]], w=[b_wst[wi]])
            fw.dma("sp", wst[wi][:, 1], g_("wb_w3")[l][nb].rearrange("p (kc n) -> p kc n", kc=8), r=[b_scr["w"]], w=[b_wst[wi]])
            for sub in range(2):
                hc = nb * 2 + sub
                p1, p3 = 2 + (hc % 2) * 2, 3 + (hc % 2) * 2
                for (pp, wsel) in ((p1, 0), (p3, 1)):
                    for kc in range(8):
                        fw.op("pe", lambda e, kc=kc, pp=pp, wsel=wsel, wi=wi, sub=sub: e.matmul(self.ps[pp][:, 0:NT], lhsT=wst[wi][:, wsel, kc, sub * 128:(sub + 1) * 128], rhs=hT[:, kc, :], start=(kc == 0), stop=(kc == 7)), r=[b_wst[wi], b_hT], w=[self.bps[pp]])
                si = hc % 2
                fw.op("act", lambda e, si=si, p1=p1: e.activation(out=stmp[si], in_=self.ps[p1][:, 0:NT], func=AF.Silu), r=[self.bps[p1]], w=[b_stmp[si]])
                fw.op("dve", lambda e, si=si, p3=p3, hc=hc: e.tensor_tensor(out=gT[:, hc, :], in0=self.ps[p3][:, 0:NT], in1=stmp[si], op=ALU.mult), r=[self.bps[p3], b_stmp[si]], w=[b_gT])
        for half in range(2):
            pi = half
            for q2 in range(2):
                nb = half * 2 + q2
                for hc in range(22):
                    fw.op("pe", lambda e, hc=hc, nb=nb, q2=q2, pi=pi: e.matmul(self.ps[pi][0:NT, q2 * 256:(q2 + 1) * 256], lhsT=gT[:, hc, :], rhs=w2t[:, nb, hc, :], start=(hc == 0), stop=(hc == 21)), r=[b_gT, b_w2], w=[self.bps[pi]])
            fw.op("dve", lambda e, pi=pi, half=half: e.tensor_tensor(out=x1[0:NT, half * 512:(half + 1) * 512], in0=self.ps[pi][0:NT, :], in1=x1[0:NT, half * 512:(half + 1) * 512], op=ALU.add), r=[self.bps[pi], b_x1], w=[b_x1])
        if l == 1:
            gout = self.salloc(D, F32)
            b_gout = fw.buf("gout_s", dma=True)
            fw.dma("sp", gout[0:NT, :], g_("norm_out").partition_broadcast(NT), w=[b_gout])
            sq_, ss, xn, bt = tmp
            yt = self.salloc(D, F32)
            b_yt = fw.buf("yt_s", dma=True)
            fw.op("act", lambda e: e.activation(out=sq_[0:NT, :], in_=x1[0:NT, :], func=AF.Square, accum_out=ss[0:NT, 0:1]), r=[b_x1], w=[bt])
            fw.op("dve", lambda e: e.tensor_scalar(out=ss[0:NT, 1:2], in0=ss[0:NT, 0:1], scalar1=1.0 / D, scalar2=EPS, op0=ALU.mult, op1=ALU.add), r=[bt], w=[bt])
            fw.op("act", lambda e: e.activation(out=ss[0:NT, 2:3], in_=ss[0:NT, 1:2], func=AF.Sqrt), r=[bt], w=[bt])
            fw.op("dve", lambda e: e.reciprocal(out=ss[0:NT, 3:4], in_=ss[0:NT, 2:3]), r=[bt], w=[bt])
            fw.op("dve", lambda e: e.scalar_tensor_tensor(out=yt[0:NT, :], in0=x1[0:NT, :], scalar=ss[0:NT, 3:4], in1=gout[0:NT, :], op0=ALU.mult, op1=ALU.mult), r=[bt, b_x1, b_gout], w=[b_yt])
            fw.dma("act", g_("o_y_s"), yt[0:NT, :], r=[b_yt])
            self.sp_off = mark
            fw.barrier()
            return
        fw.dma("act", g_("s_sX1"), x1[0:NT, :], r=[b_x1], w=[b_scr["sX1"]])
        rmsnorm_T(x1, b_x1, NT, 2, hT, b_hT, 0, tmp)
        stq = [self.salloc(NT, BF16) for _ in range(2)]
        b_stq = [fw.buf("stq%d" % i, dma=True) for i in range(2)]
        for nb in range(4):
            wi = wcnt % 2
            wcnt += 1
            fw.dma("sp", wst[wi][:, 0], g_("wb_in_c")[nb].rearrange("p (kc n) -> p kc n", kc=8), r=[b_scr["w"]], w=[b_wst[wi]])
            for sub in range(2):
                oc = nb * 2 + sub
                pp = 4 + oc % 2
                for kc in range(8):
                    fw.op("pe", lambda e, kc=kc, pp=pp, wi=wi, sub=sub: e.matmul(self.ps[pp][:, 0:NT], lhsT=wst[wi][:, 0, kc, sub * 128:(sub + 1) * 128], rhs=hT[:, kc, :], start=(kc == 0), stop=(kc == 7)), r=[b_wst[wi], b_hT], w=[self.bps[pp]])
                si = oc % 2
                fw.op("act", lambda e, si=si, pp=pp: e.copy(stq[si], self.ps[pp][:, 0:NT]), r=[self.bps[pp]], w=[b_stq[si]])
                fw.dma("act", g_("s_sQT1")[oc * 128:(oc + 1) * 128, :], stq[si], r=[b_stq[si]], w=[b_scr["sQT1"]])
        wi = wcnt % 2
        wcnt += 1
        fw.dma("sp", wst[wi][:, 0], g_("wb_in_c_tm")[6].rearrange("p (kc n) -> p kc n", kc=8), r=[b_scr["w"]], w=[b_wst[wi]])
        for kc in range(8):
            fw.op("pe", lambda e, kc=kc, wi=wi: e.matmul(self.ps[4][0:NT, 0:256], lhsT=hT[:, kc, :], rhs=wst[wi][:, 0, kc, :], start=(kc == 0), stop=(kc == 7)), r=[b_wst[wi], b_hT], w=[self.bps[4]])
        fw.op("act", lambda e: e.activation(out=tm_f[0][0:NT, 0:48], in_=self.ps[4][0:NT, 0:48], func=AF.Sigmoid), r=[self.bps[4]], w=[b_tm_f[0]])
        fw.dma("act", g_("s_sG"), tm_f[0][0:NT, 0:48], r=[b_tm_f[0]], w=[b_scr["sG"]])
        o_win_kv_s, cache_win = g_("o_win_kv_s"), g_("cache_win")
        b_d2d = fw.buf("d2d_s", dma=True)
        for sq in range(4):
            fw.dma("sp", o_win_kv_s[sq, 0:504, :], cache_win[sq, 8:512, :], w=[b_d2d])
        for nb2 in range(3):
            wi = wcnt % 2
            wcnt += 1
            for q2 in range(2):
                fw.dma("sp", wst[wi][:, q2], g_("wb_in_c_tm")[nb2 * 2 + q2].rearrange("p (kc n) -> p kc n", kc=8), r=[b_scr["w"]], w=[b_wst[wi]])
            pp = 2 + nb2 % 2
            for q2 in range(2):
                for kc in range(8):
                    fw.op("pe", lambda e, kc=kc, pp=pp, wi=wi, q2=q2: e.matmul(self.ps[pp][0:NT, q2 * 256:(q2 + 1) * 256], lhsT=hT[:, kc, :], rhs=wst[wi][:, q2, kc, :], start=(kc == 0), stop=(kc == 7)), r=[b_wst[wi], b_hT], w=[self.bps[pp]])
            fi = nb2 % 2
            fw.op("act", lambda e, fi=fi, pp=pp: e.copy(tm_f[fi][0:NT, :], self.ps[pp][0:NT, :]), r=[self.bps[pp]], w=[b_tm_f[fi]])
            if nb2 == 0:
                fw.dma("act", g_("o_cmp_kv_s"), tm_f[fi][0:NT, :], r=[b_tm_f[fi]], w=[b_scr["ocmp"]])
            elif nb2 == 1:
                fw.dma("act", g_("o_sel_kv_s"), tm_f[fi][0:NT, :], r=[b_tm_f[fi]], w=[b_scr["osel"]])
            else:
                for sq in range(4):
                    fw.dma("act", o_win_kv_s[sq, 504:512, :], tm_f[fi][sq * 8:(sq + 1) * 8, :], r=[b_tm_f[fi]], w=[b_scr["owin"]])
        self.sp_off = mark
        fw.barrier()


def make_inputs(inputs, core):
    f = lambda a: np.ascontiguousarray(a, dtype=np.float32)
    inputs = {k: np.asarray(v) for k, v in inputs.items()}
    m = {}
    m["xp"] = f(inputs["x_prompt"][core % 4])
    m["w_in_ab"] = f(inputs["w_in_ab"][0])
    m["w_out_ab"] = f(inputs["w_out_ab"][0])
    m["w_out_c"] = f(inputs["w_out_c"][0])
    m["ffn_w1"] = f(inputs["ffn_w1"])
    m["ffn_w3"] = f(inputs["ffn_w3"])
    m["ffn_w2"] = f(inputs["ffn_w2"])
    wc = np.asarray(inputs["w_in_c"][0], np.float32)
    m["w_in_c"] = f(np.concatenate([wc[:, 0:1536], wc[:, 1536:1792], wc[:, 2048:2304]], axis=1))
    wt = np.zeros((D, 1792), np.float32)
    wt[:, 0:1536] = wc[:, 1024:2560]
    wt[:, 1536:1584] = wc[:, 2560:2608]
    m["w_in_c_tm"] = f(wt)
    m["norm_out"] = f(inputs["norm_out"])
    g = np.stack([inputs["norm_mix"][0], inputs["norm_ffn"][0], inputs["norm_mix"][1], inputs["norm_ffn"][1], inputs["norm_out"]])
    m["gains"] = f(np.asarray(g, np.float32).reshape(5, 8, 128).transpose(2, 0, 1).reshape(128, 40))
    rp = np.zeros((9, 512), np.float32)
    rp[0:4] = inputs["conv_w"][0]
    rp[4] = inputs["conv_b"][0]
    rp[5] = inputs["gate_a_b"][0]
    rp[6] = inputs["gate_x_b"][0]
    rp[7] = inputs["lru_lambda"][0]
    m["rnnp"] = f(rp.reshape(9, 4, 128).transpose(2, 1, 0).reshape(128, 36))
    m["gate_a_w"] = f(inputs["gate_a_w"][0])
    m["gate_x_w"] = f(inputs["gate_x_w"][0])
    sl = slice(4 * core, 4 * core + 4)
    m["xs"] = f(inputs["x_sample"][sl].reshape(32, D))
    m["cache_dil"] = f(inputs["cache_dil_kv"][0, sl].reshape(4, 2048, 1024))
    m["st_conv"] = f(np.asarray(inputs["state_conv"][0, sl], np.float32).reshape(4, 3, 4, 128).transpose(3, 2, 0, 1).reshape(128, 48))
    m["st_rnn"] = f(np.asarray(inputs["state_rnn"][0, sl], np.float32).reshape(4, 4, 128).transpose(2, 1, 0).reshape(128, 16))
    m["cache_win"] = f(inputs["cache_win_kv"][0, sl].reshape(4, 512, 512))
    m["pool_cmp"] = np.asarray(inputs["cache_cmp_kv"][0], np.float32).reshape(-1, 512)
    m["pool_sel"] = np.asarray(inputs["cache_sel_kv"][0], np.float32).reshape(-1, 512)
    m["ptab"] = np.ascontiguousarray(np.asarray(inputs["page_table"][sl], np.int32).reshape(1, 256))
    m["w_cmp_t"] = f(np.asarray(inputs["w_cmp"][0], np.float32).transpose(2, 0, 1, 3).reshape(64, 4096))
    m["pe_cmp_t"] = f(np.asarray(inputs["pe_cmp"][0], np.float32).transpose(2, 0, 1).reshape(64, 64))
    for k, v in host_consts().items():
        m[k] = v
    return m


_CACHE = {}


def run(inputs, stage):
    if stage not in _CACHE:
        kb = KB(stage)
        kb.build()
        _CACHE[stage] = kb
    kb = _CACHE[stage]
    in_maps = []
    for c in range(8):
        m = make_inputs(inputs, c)
        in_maps.append({k: m[k] for k in kb.ins})
    res = run_bass_kernel_spmd(kb.nc, in_maps, core_ids=list(range(8)))
    return kb, res


STAGE = 6


def kernel(**inputs):
    inputs = {k: np.asarray(v) for k, v in inputs.items()}
    kb, res = run(inputs, STAGE)
    R = res.results
    z = lambda *sh: np.zeros(sh, np.float32)

    def getp(name, shape):
        if name in R[0]:
            return np.stack([np.asarray(R[b][name], np.float32).reshape(shape) for b in range(4)])
        return np.zeros((4,) + tuple(shape), np.float32)
    y_prompt = getp("y_p", (SEQ, D))
    def gets(name, shape):
        if name in R[0]:
            return np.concatenate([np.asarray(R[c][name], np.float32).reshape(shape) for c in range(8)], axis=0)
        return None
    y_sample = gets("y_s", (4, 8, D))
    dil_kv_p = getp("dil_kv_p", (2048, 2, 8, 64))[None]
    dil_kv_s = gets("dil_kv_s", (4, 2048, 2, 8, 64))[None]
    conv_p = getp("conv_p", (3, 512))[None]
    conv_s = gets("conv_s", (4, 3, 512))[None]
    rnn_p = getp("rnn_p", (512,))[None]
    rnn_s = gets("rnn_s", (4, 512))[None]
    win_kv_p = getp("win_kv_p", (512, 2, 4, 64))[None]
    win_kv_s = gets("win_kv_s", (4, 512, 2, 4, 64))[None]
    cmp_kv_p = getp("cmp_kv_p", (SEQ, 2, 4, 64))[None]
    cmp_kv_s = gets("cmp_kv_s", (4, 8, 2, 4, 64))[None]
    sel_kv_p = getp("sel_kv_p", (SEQ, 2, 4, 64))[None]
    sel_kv_s = gets("sel_kv_s", (4, 8, 2, 4, 64))[None]
    return (y_prompt, y_sample, dil_kv_p, dil_kv_s, conv_p, conv_s, rnn_p, rnn_s, win_kv_p, win_kv_s, cmp_kv_p, cmp_kv_s, sel_kv_p, sel_kv_s)
```

```python
import numpy as np
from contextlib import ExitStack
import ml_dtypes
import concourse.bass as bass
import concourse.mybir as mybir
from concourse.bass_utils import run_bass_kernel_spmd

F32 = mybir.dt.float32
BF16 = mybir.dt.bfloat16
I32 = mybir.dt.int32
AF = mybir.ActivationFunctionType
ALU = mybir.AluOpType
AX = mybir.AxisListType

D = 1024
SEQ = 4096
HD = 64
AH = 8
EPS = 1e-6
FFN = 2816
NAUG = 9
KR = HD + NAUG
NEGM = -30000.0


class Buf:
    __slots__ = ("name", "lws", "rd", "dsem", "dsem2", "multi", "isdma", "excl")

    def __init__(self, name, dsem=None, multi=False):
        self.name = name
        self.lws = {}
        self.rd = []
        self.dsem = dsem
        self.dsem2 = None
        self.isdma = dsem is not None
        self.multi = multi
        self.excl = False


class DmaSem:
    def __init__(self, sem):
        self.sem = sem
        self.count = 0


class Op:
    __slots__ = ("eng", "fn", "deps", "signal", "count", "dsem", "dval", "isdma", "key")

    def __init__(self, eng, fn, deps, isdma=False, dsem=None):
        self.eng = eng
        self.fn = fn
        self.deps = deps
        self.signal = False
        self.count = None
        self.isdma = isdma
        self.dsem = dsem
        self.dval = None
        self.key = ("d", id(dsem)) if isdma else ("e", eng)


class FW:
    ENGS = ("pe", "act", "dve", "pool", "sp")

    def __init__(self, nc, stack):
        self.nc = nc
        self.stack = stack
        self.ops = {e: [] for e in self.ENGS}
        self.ndsem = 0
        self._dsems = []
        self._free = []
        self._free_sw = []
        self._phase = []
        self.last = {e: None for e in self.ENGS}

    def new_dsem(self, sw=False):
        fl = self._free_sw if sw else self._free
        if fl:
            d = fl.pop()
        else:
            self.ndsem += 1
            d = DmaSem(self.stack.enter_context(self.nc.semaphore("ds_%d" % self.ndsem)))
            d.sw = sw
            self._dsems.append(d)
        self._phase.append(d)
        return d

    def buf(self, name, dma=False):
        return Buf(name, self.new_dsem() if dma else None)

    def dbuf(self, name):
        return Buf(name, None, multi=True)

    def _deps(self, key, r, w):
        deps = []
        for b in r:
            deps.extend(b.lws.values())
            if b.excl:
                deps.extend(d for d in b.rd if d.key != key)
        for b in w:
            if not b.multi:
                deps.extend(b.lws.values())
            deps.extend(b.rd)
        out = []
        seen = set()
        for d in deps:
            if id(d) in seen:
                continue
            seen.add(id(d))
            if d.key == key and (key[0] == "d" or key[1] == "pe"):
                continue
            out.append(d)
        return out

    def _commit(self, op, r, w):
        for b in r:
            b.rd.append(op)
        for b in w:
            if b.multi and not b.rd:
                b.lws[op.key] = op
            else:
                b.lws = {op.key: op}
            b.rd = []
        for d in op.deps:
            d.signal = True
        if not op.isdma:
            self.last[op.eng] = op

    def op(self, eng, fn, r=(), w=()):
        o = Op(eng, fn, None)
        o.deps = self._deps(o.key, r, w)
        self.ops[eng].append(o)
        self._commit(o, r, w)
        return o

    def dma(self, eng, out, in_, r=(), w=(), **kw):
        dsem = None
        for b in list(w) + list(r):
            if b.isdma:
                if eng != "sp":
                    if b.dsem2 is None:
                        b.dsem2 = self.new_dsem(sw=False)
                    dsem = b.dsem2
                else:
                    dsem = b.dsem
                break
        assert dsem is not None, "dma needs an sbuf-side buf with dma=True"
        o = Op(eng, (lambda e: e.dma_start(out=out, in_=in_, **kw)), None, isdma=True, dsem=dsem)
        o.deps = self._deps(o.key, r, w)
        dsem.count += 16
        o.dval = dsem.count
        self.ops[eng].append(o)
        self._commit(o, r, w)
        return o

    def barrier(self):
        lasts = [o for o in self.last.values() if o is not None]
        dm = []
        for ds in self._dsems:
            if ds.count > 0:
                f = Op("sp", None, [], isdma=True, dsem=ds)
                f.dval = ds.count
                dm.append(f)
        for e in self.ENGS:
            o = Op(e, None, [d for d in lasts if d.eng != e] + dm)
            for d in o.deps:
                d.signal = True
            self.ops[e].append(o)
        for d in self._phase:
            (self._free_sw if d.sw else self._free).append(d)
        self._phase = []

    def keep(self, buf):
        if buf.dsem in self._phase:
            self._phase.remove(buf.dsem)

    def emit(self):
        nc = self.nc
        for e in self.ENGS:
            c = 0
            for o in self.ops[e]:
                if o.fn is not None and not o.isdma and o.signal:
                    c += 1
                    o.count = c
        self.max_counts = {e: max([o.count or 0 for o in self.ops[e]] + [0]) for e in self.ENGS}
        EPOCH = 2000
        self.esems = {}
        for e in self.ENGS:
            n = (self.max_counts[e] + EPOCH - 1) // EPOCH
            self.esems[e] = [self.stack.enter_context(nc.semaphore("es_%s_%d" % (e, i))) for i in range(n)]

        def semval(op):
            c = op.count - 1
            return self.esems[op.eng][c // EPOCH], c % EPOCH + 1
        self.n_ops = {e: len(self.ops[e]) for e in self.ENGS}
        engmap = {"pe": "tensor", "act": "scalar", "dve": "vector", "pool": "gpsimd", "sp": "sync"}
        final = [(ds.sem, ds.count) for ds in self._dsems]
        with nc.Block() as block:
            for e in self.ENGS:
                ops = self.ops[e]

                def body(eng, ops=ops, e=e):
                    seen = {}
                    for o in ops:
                        for d in o.deps:
                            if d.isdma:
                                s, v = d.dsem.sem, d.dval
                            else:
                                s, v = semval(d)
                            k = id(s)
                            if seen.get(k, 0) >= v:
                                continue
                            seen[k] = v
                            eng.wait_ge(s, v)
                        if o.fn is None:
                            continue
                        ins = o.fn(eng)
                        if o.isdma:
                            ins.then_inc(o.dsem.sem, 16)
                        elif o.signal:
                            ins.then_inc(semval(o)[0], 1)
                    if e == "sp":
                        for s, v in final:
                            if v > 0:
                                eng.wait_ge(s, v)
                getattr(block, engmap[e])(body)


def _split3(x):
    x = np.asarray(x, np.float64)
    hi = x.astype(ml_dtypes.bfloat16).astype(np.float64)
    mid = (x - hi).astype(ml_dtypes.bfloat16).astype(np.float64)
    lo = (x - hi - mid).astype(ml_dtypes.bfloat16).astype(np.float64)
    return [hi, mid, lo]


def _slopes(n):
    return 2.0 ** (-8.0 * np.arange(1, n + 1, dtype=np.float64) / n)


def q_aug_rows(slope, tq):
    tq = np.asarray(tq, np.float64)
    rows = _split3(-8.0 * slope * tq)
    for v in _split3(8.0 * 128.0 * slope):
        rows.append(np.full(tq.shape, v))
    for v in _split3(8.0 * slope):
        rows.append(np.full(tq.shape, v))
    return np.stack(rows).astype(ml_dtypes.bfloat16)


def k_aug_rows(tk):
    tk = np.asarray(tk, np.int64)
    a = (tk // 128).astype(np.float64)
    b = (tk % 128).astype(np.float64)
    one = np.ones(tk.shape)
    return np.stack([one, one, one, a, a, a, b, b, b]).astype(ml_dtypes.bfloat16)


def host_consts():
    c = {}
    c["ident"] = np.eye(128, dtype=np.float32)
    ik = np.arange(128)[:, None]
    iq = np.arange(128)[None, :]
    cm = np.zeros((128, 256), np.float32)
    cm[:, 0:128] = np.where(ik >= iq, 0.0, NEGM)
    cm[:, 128:256] = np.where(ik <= iq, 0.0, NEGM)
    c["cmask"] = cm.astype(ml_dtypes.bfloat16)
    c["cmask4"] = np.concatenate([np.tile(cm[:, 0:128], (1, 4)), np.tile(cm[:, 128:256], (1, 4))], axis=1).astype(ml_dtypes.bfloat16)
    s8 = _slopes(AH)
    c["qaug0"] = np.stack([q_aug_rows(s8[h], np.arange(SEQ)) for h in range(AH)])
    c["kaug"] = k_aug_rows(np.arange(SEQ))
    s16 = _slopes(16)
    c["qaug1"] = np.stack([np.stack([q_aug_rows(s16[g * 4 + r], np.arange(SEQ)) for r in range(4)], axis=1) for g in range(4)])
    nn = np.arange(256)
    c["kaugc"] = k_aug_rows(16 * nn + 31)
    mc = np.zeros((128, 33, 128), np.float32)
    iq = np.arange(128)[None, :]
    for qt in range(17):
        n = np.arange(128)[:, None]
        mc[:, qt, :] = np.where(16 * n + 31 <= 128 * qt + iq, 0.0, NEGM)
    for qt in range(16, 32):
        n = 128 + np.arange(128)[:, None]
        mc[:, 17 + qt - 16, :] = np.where(16 * n + 31 <= 128 * qt + iq, 0.0, NEGM)
    c["mc"] = mc.astype(ml_dtypes.bfloat16)
    ts = np.zeros((128, 32, 64), np.float32)
    for qt in range(32):
        tq = 128 * qt + np.arange(128)[:, None]
        cb = tq // 64
        j = np.arange(64)[None, :]
        ok = j <= cb
        forced = (j == 0) | (j == cb) | (j == cb - 1)
        ts[:, qt, :] = np.where(ok, np.where(forced, 1000.0, 0.0), -1e30)
    c["tsel"] = ts
    n = np.arange(256)[:, None]
    j = np.arange(64)[None, :]
    ov = np.minimum(16 * n + 32, 64 * j + 64) - np.maximum(16 * n, 64 * j)
    c2s = (np.maximum(ov, 0) / 16.0).astype(np.float32)
    c2s[255] = 0
    c["c2s"] = np.ascontiguousarray(c2s.reshape(2, 128, 64).transpose(1, 0, 2)).astype(ml_dtypes.bfloat16)
    rows = np.arange(17 * 128)[:, None]
    qrow = 2048 + np.arange(8)[None, :]
    dist = qrow - rows
    cntm = np.zeros((17 * 128, 8), np.float32)
    for (wdw, dl) in ((128, 1), (512, 4), (2048, 16)):
        cntm += ((dist >= 0) & (dist <= wdw) & (dist % dl == 0) & (rows < 2056)).astype(np.float32)
    c["cnt_s"] = np.ascontiguousarray(cntm.reshape(17, 128, 8).transpose(1, 0, 2)).astype(ml_dtypes.bfloat16)
    c["kaug_s"] = k_aug_rows(np.arange(17 * 128))
    c["qaug0_s"] = np.stack([q_aug_rows(s8[h], 2048 + np.arange(8)) for h in range(AH)], axis=1)
    c["kaug_ss"] = k_aug_rows(np.arange(8320))
    c["kaug_sw"] = k_aug_rows(7680 + np.arange(640))
    c["kaugc_s"] = k_aug_rows(16 * np.arange(512) + 31)
    c["qaug1_s"] = np.stack([np.stack([q_aug_rows(s16[g * 4 + r], 8192 + np.arange(8)) for r in range(4)], axis=1) for g in range(4)])
    n5 = np.arange(512)[:, None]
    j5 = np.arange(129)[None, :]
    ov5 = np.minimum(16 * n5 + 32, 64 * j5 + 64) - np.maximum(16 * n5, 64 * j5)
    c2s5 = (np.maximum(ov5, 0) / 16.0).astype(np.float32)
    c2s5[511] = 0
    c["c2s_s"] = np.ascontiguousarray(c2s5.reshape(4, 128, 129).transpose(1, 0, 2)).astype(ml_dtypes.bfloat16)
    ts5 = np.zeros((8, 129), np.float32)
    ts5[:, [0, 127, 128]] = 1000.0
    c["tsel_s"] = ts5
    e30s = np.zeros((128, 8192), np.float32)
    for jj in range(128):
        e30s[jj, jj * 64:(jj + 1) * 64] = 30000.0
    c["e30s"] = e30s.astype(ml_dtypes.bfloat16)
    jj8 = np.arange(8)[:, None]
    ii8 = np.arange(8)[None, :]
    c["cm8"] = np.tile(np.where(jj8 <= ii8, 0.0, NEGM), (1, 4)).astype(ml_dtypes.bfloat16)
    j128 = np.arange(128)[:, None]
    c["wm0"] = np.tile(np.where(j128 >= ii8, 0.0, NEGM), (1, 4)).astype(ml_dtypes.bfloat16)
    c["iota_p"] = np.arange(128, dtype=np.float32).reshape(128, 1)
    e30 = np.zeros((64, SEQ), np.float32)
    for jj in range(64):
        e30[jj, jj * 64:(jj + 1) * 64] = 30000.0
    c["e30"] = e30.astype(ml_dtypes.bfloat16)
    return c


class KB:
    def __init__(self, stage):
        self.stage = stage
        self.nc = bass.Bass("TRN2", target_bir_lowering=False)
        self.ins = {}
        self.outs = {}

    def din(self, name, shape, dt=F32):
        ap = self.nc.dram_tensor(name, list(shape), dt, kind="ExternalInput").ap()
        self.ins[name] = ap
        return ap

    def dout(self, name, shape, dt=F32):
        ap = self.nc.dram_tensor(name, list(shape), dt, kind="ExternalOutput").ap()
        self.outs[name] = ap
        return ap

    def dscr(self, name, shape, dt):
        return self.nc.dram_tensor(name, list(shape), dt).ap()

    def salloc(self, cols, dt, parts=128):
        nb = cols * (2 if dt == BF16 else 4)
        nb = (nb + 63) // 64 * 64
        off = self.sp_off
        self.sp_off += nb
        assert self.sp_off <= self.arena_bytes, ("sbuf overflow", self.sp_off)
        v = self.arena[0:parts, off // 4:(off + nb) // 4]
        if dt != F32:
            v = v.bitcast(dt)
        return v[:, 0:cols]

    def build(self):
        nc = self.nc
        with ExitStack() as st:
            self.fw = fw = FW(nc, st)
            self.arena_bytes = 45056 * 4
            self.arena = st.enter_context(nc.sbuf_tensor("arena", [128, 45056], F32))
            self.sp_off = 0
            self.ps = [st.enter_context(nc.psum_tensor("ps%d" % i, [128, 512], F32)) for i in range(8)]
            self.bps = [fw.buf("ps%d" % i) for i in range(8)]
            for b in self.bps:
                b.excl = True
            self.body()
            fw.emit()
        return nc

    def body(self):
        nc, fw = self.nc, self.fw
        S = self.stage
        xp = self.din("xp", [SEQ, D])
        w_in_ab = self.din("w_in_ab", [D, 2560])
        w_out_ab = self.din("w_out_ab", [D, D])
        w1 = self.din("ffn_w1", [2, D, FFN])
        w3 = self.din("ffn_w3", [2, D, FFN])
        w2 = self.din("ffn_w2", [2, FFN, D])
        w_out_c = self.din("w_out_c", [D, D])
        w_in_c = self.din("w_in_c", [D, 2048])
        w_in_c_tm = self.din("w_in_c_tm", [D, 1536 + 256])
        gains = self.din("gains", [128, 5 * 8])
        rnnp = self.din("rnnp", [128, 4 * 9])
        gaw = self.din("gate_a_w", [8, 64, 64])
        gxw = self.din("gate_x_w", [8, 64, 64])
        c_ident = self.din("ident", [128, 128])
        c_cmask = self.din("cmask", [128, 256], BF16)
        c_cmask4 = self.din("cmask4", [128, 1024], BF16)
        c_qaug0 = self.din("qaug0", [AH, NAUG, SEQ], BF16)
        c_kaug = self.din("kaug", [NAUG, SEQ], BF16)
        c_qaug1 = self.din("qaug1", [4, NAUG, 4, SEQ], BF16)
        c_kaugc = self.din("kaugc", [NAUG, 256], BF16)
        c_mc = self.din("mc", [128, 33, 128], BF16)
        c_tsel = self.din("tsel", [128, 32, 64])
        c_c2s = self.din("c2s", [128, 2, 64], BF16)
        c_e30 = self.din("e30", [64, SEQ], BF16)
        w_cmp_t = self.din("w_cmp_t", [64, 32 * 2 * 64])
        pe_cmp_t = self.din("pe_cmp_t", [64, 64])

        xs = self.din("xs", [32, D])
        cache_dil = self.din("cache_dil", [4, 2048, 1024])
        st_conv = self.din("st_conv", [128, 4 * 4 * 3])
        st_rnn = self.din("st_rnn", [128, 4 * 4])
        cache_win = self.din("cache_win", [4, 512, 512])
        c_cnt_s = self.din("cnt_s", [128, 17, 8], BF16)
        c_kaug_s = self.din("kaug_s", [NAUG, 17 * 128], BF16)
        c_qaug0_s = self.din("qaug0_s", [NAUG, 8, 8], BF16)
        NPR = 2560 * 128
        pool_cmp = self.din("pool_cmp", [NPR, 512]) if S >= 6 else None
        pool_sel = self.din("pool_sel", [NPR, 512]) if S >= 6 else None
        ptab = self.din("ptab", [1, 256], I32) if S >= 6 else None
        if S >= 6:
            c_kaug_ss = self.din("kaug_ss", [NAUG, 8320], BF16)
            c_kaug_sw = self.din("kaug_sw", [NAUG, 640], BF16)
            c_kaugc_s = self.din("kaugc_s", [NAUG, 512], BF16)
            c_qaug1_s = self.din("qaug1_s", [4, NAUG, 4, 8], BF16)
            c_c2s_s = self.din("c2s_s", [128, 4, 129], BF16)
            c_tsel_s = self.din("tsel_s", [8, 129])
            c_e30s = self.din("e30s", [128, 8192], BF16)
            c_cm8 = self.din("cm8", [8, 32], BF16)
            c_wm0 = self.din("wm0", [128, 32], BF16)
            c_iota_p = self.din("iota_p", [128, 1])
        s_cT_cmp = self.dscr("s_cT_cmp", [4, 512, 8320], BF16)
        s_cT_sel = self.dscr("s_cT_sel", [4, 256, 8320], BF16)
        s_cV_sel = self.dscr("s_cV_sel", [4, 8320, 256], BF16)
        s_cT_win = self.dscr("s_cT_win", [4, 256, 640], BF16)
        s_cV_win = self.dscr("s_cV_win", [4, 640, 256], BF16)
        s_sQT1 = self.dscr("s_sQT1", [1024, 32], BF16)
        s_sG = self.dscr("s_sG", [32, 48], F32)
        s_sOT1 = self.dscr("s_sOT1", [1024, 32], BF16)
        o_y_s = self.dout("y_s", [32, D])
        o_dil_kv_s = self.dout("dil_kv_s", [4, 2048, 1024])
        o_conv_s = self.dout("conv_s", [4, 3, 512])
        o_rnn_s = self.dout("rnn_s", [4, 512])
        o_win_kv_s = self.dout("win_kv_s", [4, 512, 512])
        o_cmp_kv_s = self.dout("cmp_kv_s", [32, 512])
        o_sel_kv_s = self.dout("sel_kv_s", [32, 512])
        s_sQT0 = self.dscr("s_sQT0", [512, 32], BF16)
        s_sKV = self.dscr("s_sKV", [32, 1024], BF16)
        s_sXR = self.dscr("s_sXR", [512, 32], F32)
        s_sGT = self.dscr("s_sGT", [512, 32], F32)
        s_sOT0 = self.dscr("s_sOT0", [1024, 32], BF16)
        s_sX1 = self.dscr("s_sX1", [32, D], F32)
        o_dil_kv_p = self.dout("dil_kv_p", [2048, 1024])
        o_conv_p = self.dout("conv_p", [3, 512])
        o_rnn_p = self.dout("rnn_p", [512])
        o_cmp_kv_p = self.dout("cmp_kv_p", [SEQ, 512])
        o_sel_kv_p = self.dout("sel_kv_p", [SEQ, 512])
        o_win_kv_p = self.dout("win_kv_p", [512, 512])
        o_y_p = self.dout("y_p", [SEQ, D])

        def wscr(name, K, N):
            return self.dscr(name, [N // 256, 128, (K // 128) * 256], BF16)
        wb_in_ab = wscr("wb_in_ab", D, 2560)
        wb_out_ab = wscr("wb_out_ab", D, D)
        wb_w1 = [wscr("wb_w1_%d" % l, D, FFN) for l in range(2)]
        wb_w3 = [wscr("wb_w3_%d" % l, D, FFN) for l in range(2)]
        wb_w2 = [wscr("wb_w2_%d" % l, FFN, D) for l in range(2)]
        wb_out_c = wscr("wb_out_c", D, D)
        wb_in_c = wscr("wb_in_c", D, 2048)
        wb_in_c_tm = wscr("wb_in_c_tm", D, 1792)
        s_QT0 = self.dscr("s_QT0", [512, SEQ], BF16)
        s_KT0 = self.dscr("s_KT0", [512, SEQ], BF16)
        s_V0 = self.dscr("s_V0", [SEQ, 512], BF16)
        s_XR = self.dscr("s_XR", [512, SEQ], F32)
        s_GT = self.dscr("s_GT", [512, SEQ], F32)
        s_OT0 = self.dscr("s_OT0", [1024, SEQ], BF16)
        s_X1 = self.dscr("s_X1", [SEQ, D], F32)
        DBG = (S == 4.5)
        s_OT1 = (self.dout if DBG else self.dscr)("s_OT1", [1024, SEQ], BF16)
        s_QT1 = (self.dout if DBG else self.dscr)("s_QT1", [1024, SEQ], BF16)
        s_KC = self.dscr("s_KC", [512, SEQ], BF16)
        s_KS = self.dscr("s_KS", [256, SEQ], BF16)
        s_KW = self.dscr("s_KW", [256, SEQ], BF16)
        s_VS = self.dscr("s_VS", [SEQ, 256], BF16)
        s_VW = self.dscr("s_VW", [SEQ, 256], BF16)
        s_G = (self.dout if DBG else self.dscr)("s_G", [SEQ, 48], F32)
        norm_out = self.din("norm_out", [D])
        b_scr = {n: fw.dbuf(n) for n in ["w", "QT0", "KT0", "V0", "XR", "GT", "OT0", "X1", "OT1", "QT1", "KC", "KS", "KW", "VS", "VW", "G", "sQT0", "sKV", "sXR", "sGT", "sOT0", "sX1", "cT_cmp", "cT_sel", "cV_sel", "cT_win", "cV_win", "sQT1", "sG", "sOT1", "ocmp", "osel", "owin"]}

        ident_f = self.salloc(128, F32)
        ident_b = self.salloc(128, BF16)
        cmask = self.salloc(256, BF16)
        cmask4 = self.salloc(1024, BF16)
        ones_b = self.salloc(128, BF16)
        gains_t = self.salloc(40, F32)
        b_const = fw.buf("const", dma=True)
        fw.dma("sp", ident_f, c_ident, w=[b_const])
        fw.dma("sp", cmask, c_cmask, w=[b_const])
        fw.dma("sp", cmask4, c_cmask4, w=[b_const])
        fw.dma("sp", gains_t, gains, w=[b_const])
        b_const2 = fw.buf("const2")
        fw.op("dve", lambda e: e.tensor_copy(ident_b, ident_f), r=[b_const], w=[b_const2])
        fw.op("dve", lambda e: e.memset(ones_b, 1.0), w=[b_const2])
        self.ident_b, self.ident_f, self.cmask, self.ones_b = ident_b, ident_f, cmask, ones_b
        self.b_consts = [b_const, b_const2]
        base_off = self.sp_off

        wmark = self.sp_off
        w_stg = [self.salloc(22 * 256, F32) for _ in range(2)]
        w_stb = [self.salloc(22 * 256, BF16) for _ in range(2)]
        w_bs = [fw.buf("wst%d" % i, dma=True) for i in range(2)]
        w_bb = [fw.buf("wsb%d" % i, dma=True) for i in range(2)]
        wcount = [0]

        def wprep(src, dst, K, N, tag):
            KC = K // 128
            srcv = src.rearrange("(kc p) n -> p kc n", p=128)
            for nb in range(N // 256):
                i = wcount[0] % 2
                wcount[0] += 1
                stg_i = w_stg[i][:, 0:KC * 256]
                stb_i = w_stb[i][:, 0:KC * 256]
                sv = stg_i.rearrange("p (kc n) -> p kc n", kc=KC)
                for k0 in range(0, KC, 8):
                    k1 = min(KC, k0 + 8)
                    fw.dma("sp", sv[:, k0:k1, :], srcv[:, k0:k1, nb * 256:(nb + 1) * 256], w=[w_bs[i]])
                if i == 0:
                    fw.op("dve", lambda e, stb_i=stb_i, stg_i=stg_i: e.tensor_copy(stb_i, stg_i), r=[w_bs[i]], w=[w_bb[i]])
                else:
                    fw.op("act", lambda e, stb_i=stb_i, stg_i=stg_i: e.copy(stb_i, stg_i), r=[w_bs[i]], w=[w_bb[i]])
                fw.dma("act", dst[nb], stb_i, r=[w_bb[i]], w=[b_scr["w"]])

        wprep(w_in_ab, wb_in_ab, D, 2560, "inab")
        wprep(w_out_ab, wb_out_ab, D, D, "outab")
        if S >= 2.7:
            for l in range(2 if S >= 4 else 1):
                wprep(w1[l], wb_w1[l], D, FFN, "w1")
                wprep(w3[l], wb_w3[l], D, FFN, "w3")
                wprep(w2[l], wb_w2[l], FFN, D, "w2")
            wprep(w_in_c, wb_in_c, D, 2048, "inc")
            wprep(w_out_c, wb_out_c, D, D, "outc")
            wprep(w_in_c_tm, wb_in_c_tm, D, 1792, "inctm")
        self.sp_off = wmark
        fw.barrier()

        def load_w(wb, K, nblocks, name):
            KC = K // 128
            t = self.salloc(nblocks * KC * 256, BF16)
            b = fw.buf(name, dma=True)
            tv = t.rearrange("p (nb x) -> p nb x", nb=nblocks)
            for nb in range(nblocks):
                fw.dma("sp", tv[:, nb, :], wb[nb], r=[b_scr["w"]], w=[b])
            return t.rearrange("p (nb kc n) -> p nb kc n", nb=nblocks, kc=KC), b

        def rmsnorm_T(xt, b_xt, ntok, gidx, hT, b_hT, col0, tmp):
            sq, ss, xn, bt = tmp
            fw.op("act", lambda e: e.activation(out=sq[0:ntok, :], in_=xt[0:ntok, :], func=AF.Square, accum_out=ss[0:ntok, 0:1]), r=[b_xt], w=[bt])
            fw.op("dve", lambda e: e.tensor_scalar(out=ss[0:ntok, 1:2], in0=ss[0:ntok, 0:1], scalar1=1.0 / D, scalar2=EPS, op0=ALU.mult, op1=ALU.add), r=[bt], w=[bt])
            fw.op("act", lambda e: e.activation(out=ss[0:ntok, 2:3], in_=ss[0:ntok, 1:2], func=AF.Sqrt), r=[bt], w=[bt])
            fw.op("dve", lambda e: e.reciprocal(out=ss[0:ntok, 3:4], in_=ss[0:ntok, 2:3]), r=[bt], w=[bt])
            fw.op("dve", lambda e: e.tensor_scalar(out=xn[0:ntok, :], in0=xt[0:ntok, :], scalar1=ss[0:ntok, 3:4], scalar2=None, op0=ALU.mult), r=[b_xt, bt], w=[bt])
            pst = self.ps[7][:, :].bitcast(BF16)
            for kc in range(8):
                fw.op("pe", lambda e, kc=kc: e.transpose(pst[:, kc * 128:kc * 128 + ntok], xn[0:ntok, kc * 128:(kc + 1) * 128], ident_b[0:ntok, 0:ntok]), r=[bt, b_const2], w=[self.bps[7]])
            g = gains_t[:, gidx * 8:(gidx + 1) * 8]
            fw.op("dve", lambda e: e.tensor_tensor(out=hT[:, :, col0:col0 + ntok], in0=pst.rearrange("p (kc t) -> p kc t", kc=8)[:, :, 0:ntok],
                                                     in1=g.rearrange("p (kc o) -> p kc o", o=1).to_broadcast([128, 8, ntok]), op=ALU.mult),
                  r=[self.bps[7], b_const], w=[b_hT])

        self.load_w = load_w
        self.rmsnorm_T = rmsnorm_T

        if S == 9:
            self.sample_path(locals())
            return

        mark = self.sp_off
        win, b_win = load_w(wb_in_ab, D, 10, "win_ab")
        xts = [self.salloc(D, F32) for _ in range(2)]
        b_xts = [fw.buf("xt%d" % i, dma=True) for i in range(2)]
        tmp = (self.salloc(D, F32), self.salloc(4, F32), self.salloc(D, BF16), fw.buf("nrm_tmp"))
        hT = self.salloc(8 * 512, BF16).rearrange("p (kc t) -> p kc t", kc=8)
        b_hT = fw.buf("hT")
        stg_b = [self.salloc(512, BF16) for _ in range(2)]
        stg_f = [self.salloc(512, F32) for _ in range(2)]
        b_stg_b = [fw.buf("stgb%d" % i, dma=True) for i in range(2)]
        b_stg_f = [fw.buf("stgf%d" % i, dma=True) for i in range(2)]
        kv_f = [self.salloc(1024, F32) for _ in range(2)]
        kv_b = [self.salloc(512, BF16) for _ in range(2)]
        b_kv_f = [fw.buf("kvf%d" % i, dma=True) for i in range(2)]
        b_kv_b = [fw.buf("kvb%d" % i, dma=True) for i in range(2)]
        cnt = 0
        for blk in range(8):
            for j in range(4):
                t0 = blk * 512 + j * 128
                i = (blk * 4 + j) % 2
                fw.dma("sp", xts[i], xp[t0:t0 + 128, :], w=[b_xts[i]])
                rmsnorm_T(xts[i], b_xts[i], 128, 0, hT, b_hT, j * 128, tmp)
            for oc in list(range(0, 8)) + list(range(12, 20)):
                pi = cnt % 2
                cnt += 1
                pst_, bp = self.ps[pi], self.bps[pi]
                nb, sub = oc // 2, oc % 2
                for kc in range(8):
                    fw.op("pe", lambda e, kc=kc, nb=nb, sub=sub, pst_=pst_: e.matmul(pst_[:, :], lhsT=win[:, nb, kc, sub * 128:(sub + 1) * 128], rhs=hT[:, kc, :], start=(kc == 0), stop=(kc == 7)),
                          r=[b_win, b_hT], w=[bp])
                if oc < 8:
                    si = oc % 2
                    fw.op("act", lambda e, si=si, pst_=pst_: e.copy(stg_b[si], pst_[:, :]), r=[bp], w=[b_stg_b[si]])
                    dst = s_QT0 if oc < 4 else s_KT0
                    nm = "QT0" if oc < 4 else "KT0"
                    r0 = (oc % 4) * 128
                    fw.dma("act", dst[r0:r0 + 128, blk * 512:(blk + 1) * 512], stg_b[si], r=[b_stg_b[si]], w=[b_scr[nm]])
                else:
                    si = oc % 2
                    fw.op("dve", lambda e, si=si, pst_=pst_: e.tensor_copy(stg_f[si], pst_[:, :]), r=[bp], w=[b_stg_f[si]])
                    dst = s_XR if oc < 16 else s_GT
                    nm = "XR" if oc < 16 else "GT"
                    r0 = (oc % 4) * 128
                    fw.dma("act", dst[r0:r0 + 128, blk * 512:(blk + 1) * 512], stg_f[si], r=[b_stg_f[si]], w=[b_scr[nm]])
            for j in range(4):
                t0 = blk * 512 + j * 128
                i = j % 2
                for half in range(2):
                    pi = 2 + (j * 2 + half) % 2
                    pst_, bp = self.ps[pi], self.bps[pi]
                    for q2 in range(2):
                        nb = 2 + half * 2 + q2
                        for kc in range(8):
                            fw.op("pe", lambda e, kc=kc, nb=nb, q2=q2, pst_=pst_, j=j: e.matmul(pst_[:, q2 * 256:(q2 + 1) * 256], lhsT=hT[:, kc, j * 128:(j + 1) * 128], rhs=win[:, nb, kc, :], start=(kc == 0), stop=(kc == 7)),
                                  r=[b_win, b_hT], w=[bp])
                    if half == 0:
                        fw.op("act", lambda e, i=i, pst_=pst_: e.copy(kv_f[i][:, 0:512], pst_[:, :]), r=[bp], w=[b_kv_f[i]])
                    else:
                        fw.op("dve", lambda e, i=i, pst_=pst_: e.tensor_copy(kv_f[i][:, 512:1024], pst_[:, :]), r=[bp], w=[b_kv_f[i]])
                fw.op("pool", lambda e, i=i: e.tensor_copy(kv_b[i], kv_f[i][:, 512:1024]), r=[b_kv_f[i]], w=[b_kv_b[i]])
                fw.dma("act", s_V0[t0:t0 + 128, :], kv_b[i], r=[b_kv_b[i]], w=[b_scr["V0"]])
                if t0 >= 2048:
                    fw.dma("act", o_dil_kv_p[t0 - 2048:t0 - 2048 + 128, :], kv_f[i], r=[b_kv_f[i]])
        self.sp_off = mark
        fw.barrier()
        if S < 2:
            return

        mark = self.sp_off
        T = SEQ
        rp = self.salloc(36, F32)
        b_rp = fw.buf("rp", dma=True)
        fw.dma("sp", rp, rnnp, w=[b_rp])
        bd_f = self.salloc(256, F32)
        bd_b = self.salloc(256, BF16)
        b_bdf = fw.buf("bdf", dma=True)
        b_bdb = fw.buf("bdb")
        xr = self.salloc(T + 4, F32)
        xc = self.salloc(T, F32)
        xcb = self.salloc(T, BF16)
        t1 = self.salloc(T, F32)
        t2 = self.salloc(T, F32)
        t3 = self.salloc(T, F32)
        ob = self.salloc(T, BF16)
        sp_t = self.salloc(4, F32)
        b_xr = fw.buf("xr", dma=True)
        b_xc, b_xcb, b_t1, b_t3, b_spt = [fw.buf(n) for n in ["xc", "xcb", "t1", "t3", "spt"]]
        b_t2 = fw.buf("t2", dma=True)
        b_ob = fw.buf("ob", dma=True)
        b_gt = fw.buf("gt_in", dma=True)
        for c in range(4):
            pr = rp[:, c * 9:(c + 1) * 9]
            fw.op("dve", lambda e: e.memset(bd_f, 0.0), w=[b_bdf])
            for gi, gw in enumerate((gaw, gxw)):
                fw.dma("sp", bd_f[0:64, gi * 128:gi * 128 + 64], gw[2 * c], w=[b_bdf])
                fw.dma("sp", bd_f[64:128, gi * 128 + 64:gi * 128 + 128], gw[2 * c + 1], w=[b_bdf])
            fw.op("dve", lambda e: e.tensor_copy(bd_b, bd_f), r=[b_bdf], w=[b_bdb])
            fw.op("dve", lambda e: e.memset(xr[:, 0:4], 0.0), w=[b_xr])
            fw.dma("sp", xr[:, 4:4 + T], s_XR[c * 128:(c + 1) * 128, :], r=[b_scr["XR"]], w=[b_xr])
            fw.dma("act", o_conv_p[:, c * 128:(c + 1) * 128].rearrange("k p -> p k"), xr[:, 4 + T - 3:4 + T], r=[b_xr], allow_slow_non_contiguous=True)
            fw.op("dve", lambda e, pr=pr: e.tensor_scalar(out=xc, in0=xr[:, 1:1 + T], scalar1=pr[:, 0:1], scalar2=pr[:, 4:5], op0=ALU.mult, op1=ALU.add), r=[b_xr, b_rp], w=[b_xc])
            for k in range(1, 4):
                fw.op("dve", lambda e, pr=pr, k=k: e.scalar_tensor_tensor(out=xc, in0=xr[:, 1 + k:1 + k + T], scalar=pr[:, k:k + 1], in1=xc, op0=ALU.mult, op1=ALU.add), r=[b_xr, b_rp, b_xc], w=[b_xc])
            fw.op("pool", lambda e: e.tensor_copy(xcb, xc), r=[b_xc], w=[b_xcb])
            fw.op("act", lambda e, pr=pr: e.activation(out=sp_t[:, 0:1], in_=pr[:, 7:8], func=AF.Exp, scale=-1.0), r=[b_rp], w=[b_spt])
            fw.op("act", lambda e: e.activation(out=sp_t[:, 1:2], in_=sp_t[:, 0:1], func=AF.Ln, bias=1.0), r=[b_spt], w=[b_spt])
            fw.op("dve", lambda e: e.tensor_scalar(out=sp_t[:, 2:3], in0=sp_t[:, 1:2], scalar1=-8.0, scalar2=None, op0=ALU.mult), r=[b_spt], w=[b_spt])
            for tb in range(T // 512):
                for gi, (tt, bt_) in enumerate(((t1, b_t1), (t2, b_t2))):
                    pi = (tb * 2 + gi) % 4
                    fw.op("pe", lambda e, gi=gi, tb=tb, pi=pi: e.matmul(self.ps[pi][:, :], lhsT=bd_b[:, gi * 128:(gi + 1) * 128], rhs=xcb[:, tb * 512:(tb + 1) * 512], start=True, stop=True),
                          r=[b_bdb, b_xcb], w=[self.bps[pi]])
                    fw.op("act", lambda e, gi=gi, tb=tb, pi=pi, tt=tt, pr=pr: e.activation(out=tt[:, tb * 512:(tb + 1) * 512], in_=self.ps[pi][:, :], func=AF.Sigmoid, bias=pr[:, 5 + gi:6 + gi]),
                          r=[self.bps[pi], b_rp], w=[bt_])
            fw.op("dve", lambda e: e.tensor_scalar(out=sp_t[:, 3:4], in0=sp_t[:, 2:3], scalar1=2.0, scalar2=None, op0=ALU.mult), r=[b_spt], w=[b_spt])
            fw.op("act", lambda e: e.activation(out=t3, in_=t1, func=AF.Exp, scale=sp_t[:, 3:4]), r=[b_t1, b_spt], w=[b_t3])
            fw.op("act", lambda e: e.activation(out=t1, in_=t1, func=AF.Exp, scale=sp_t[:, 2:3]), r=[b_t1, b_spt], w=[b_t1])
            fw.op("dve", lambda e: e.tensor_scalar(out=t3, in0=t3, scalar1=-1.0, scalar2=1.0, op0=ALU.mult, op1=ALU.add), r=[b_t3], w=[b_t3])
            fw.op("dve", lambda e: e.tensor_scalar(out=t3, in0=t3, scalar1=0.0, scalar2=None, op0=ALU.max), r=[b_t3], w=[b_t3])
            fw.op("act", lambda e: e.activation(out=t3, in_=t3, func=AF.Sqrt), r=[b_t3], w=[b_t3])
            fw.op("dve", lambda e: e.tensor_tensor(out=t3, in0=t3, in1=t2, op=ALU.mult), r=[b_t3, b_t2], w=[b_t3])
            fw.op("dve", lambda e: e.tensor_tensor(out=t3, in0=t3, in1=xc, op=ALU.mult), r=[b_t3, b_xc], w=[b_t3])
            for sg in range(T // 512):
                init = 0.0 if sg == 0 else t2[:, sg * 512 - 1:sg * 512]
                fw.op("dve", lambda e, sg=sg, init=init: e.tensor_tensor_scan(out=t2[:, sg * 512:(sg + 1) * 512], data0=t1[:, sg * 512:(sg + 1) * 512], data1=t3[:, sg * 512:(sg + 1) * 512], initial=init, op0=ALU.mult, op1=ALU.add),
                      r=[b_t1, b_t3, b_t2], w=[b_t2])
            fw.dma("act", o_rnn_p[c * 128:(c + 1) * 128].rearrange("(p o) -> p o", o=1), t2[:, T - 1:T], r=[b_t2])
            fw.dma("sp", t1, s_GT[c * 128:(c + 1) * 128, :], r=[b_scr["GT"], b_t1], w=[b_gt, b_t1])
            fw.op("act", lambda e: e.activation(out=t1, in_=t1, func=AF.Gelu), r=[b_gt, b_t1], w=[b_t1])
            fw.op("dve", lambda e: e.tensor_tensor(out=ob, in0=t2, in1=t1, op=ALU.mult), r=[b_t1, b_t2], w=[b_ob])
            fw.dma("act", s_OT0[512 + c * 128:512 + (c + 1) * 128, :], ob, r=[b_ob], w=[b_scr["OT0"]])
        self.sp_off = mark
        fw.barrier()
        if S < 2.5:
            return

        mark = self.sp_off
        Qh = self.salloc(SEQ, BF16, parts=KR)
        Kh = self.salloc(SEQ, BF16, parts=KR)
        b_Qh = fw.buf("Qh", dma=True)
        b_Kh = fw.buf("Kh", dma=True)
        accL = self.salloc(2 * SEQ, F32, parts=64).rearrange("p (a t) -> p a t", a=2)
        b_acc = fw.buf("accL")
        obh = self.salloc(SEQ, BF16, parts=64)
        b_obh = fw.buf("obh", dma=True)
        NV = 4
        vts = [self.salloc(64, BF16) for _ in range(NV)]
        b_vts = [fw.buf("vt%d" % i, dma=True) for i in range(NV)]
        pts = [self.salloc(256, BF16) for _ in range(2)]
        b_pts = [fw.buf("pt%d" % i) for i in range(2)]
        unit = 0
        vcnt = 0
        for h in range(AH):
            fw.dma("sp", Qh[0:64, :], s_QT0[h * 64:(h + 1) * 64, :], r=[b_scr["QT0"]], w=[b_Qh])
            fw.dma("sp", Qh[64:KR, :], c_qaug0[h], w=[b_Qh])
            fw.dma("sp", Kh[0:64, :], s_KT0[h * 64:(h + 1) * 64, :], r=[b_scr["KT0"]], w=[b_Kh])
            fw.dma("sp", Kh[64:KR, :], c_kaug, w=[b_Kh])
            for dil in (1, 4, 16):
                ntile = SEQ // dil // 128
                for r_ in range(dil):
                    prev_v = None
                    for qi in range(ntile):
                        def tok(t):
                            a0 = r_ + dil * 128 * t
                            return slice(a0, a0 + dil * 127 + 1, dil)
                        vi = vcnt % NV
                        vcnt += 1
                        fw.dma("sp", vts[vi], s_V0[tok(qi), h * 64:(h + 1) * 64], r=[b_scr["V0"]], w=[b_vts[vi]])
                        si = unit % 2
                        unit += 1
                        Sp, bS = self.ps[si], self.bps[si]
                        Op_, bO = self.ps[2 + si], self.bps[2 + si]
                        pt, bpt = pts[si], b_pts[si]
                        kts = ([(qi - 1, 0, prev_v)] if qi >= 1 else []) + [(qi, 1, vi)]
                        for (kt, ci, _) in kts:
                            cs = slice(ci * 128, (ci + 1) * 128)
                            fw.op("pe", lambda e, kt=kt, cs=cs, Sp=Sp, tk=tok(kt), tq=tok(qi): e.matmul(Sp[:, cs], lhsT=Kh[:, tk], rhs=Qh[:, tq], start=True, stop=False), r=[b_Qh, b_Kh], w=[bS])
                            fw.op("pe", lambda e, cs=cs, Sp=Sp: e.matmul(Sp[:, cs], lhsT=ident_b, rhs=cmask[:, cs], start=False, stop=True), r=[b_const, b_const2], w=[bS])
                        c0 = 0 if qi >= 1 else 128
                        fw.op("act", lambda e, Sp=Sp, pt=pt, c0=c0: e.activation(out=pt[:, c0:256], in_=Sp[:, c0:256], func=AF.Exp, scale=0.125), r=[bS], w=[bpt])
                        for n_, (kt, ci, vslot) in enumerate(kts):
                            cs = slice(ci * 128, (ci + 1) * 128)
                            first, last_ = (n_ == 0), (n_ == len(kts) - 1)
                            fw.op("pe", lambda e, cs=cs, Op_=Op_, pt=pt, vslot=vslot, first=first, last_=last_: e.matmul(Op_[0:64, 0:128], lhsT=vts[vslot], rhs=pt[:, cs], start=first, stop=last_), r=[bpt, b_vts[vslot]], w=[bO])
                        for n_, (kt, ci, vslot) in enumerate(kts):
                            cs = slice(ci * 128, (ci + 1) * 128)
                            first, last_ = (n_ == 0), (n_ == len(kts) - 1)
                            fw.op("pe", lambda e, cs=cs, Op_=Op_, pt=pt, first=first, last_=last_: e.matmul(Op_[0:64, 128:256], lhsT=ones_b[:, 0:64], rhs=pt[:, cs], start=first, stop=last_), r=[bpt, b_const2], w=[bO])
                        dst = accL[:, :, tok(qi)]
                        src = Op_[0:64, 0:256].rearrange("p (a t) -> p a t", a=2)
                        if dil == 1:
                            fw.op("dve", lambda e, dst=dst, src=src: e.tensor_copy(dst, src), r=[bO], w=[b_acc])
                        else:
                            fw.op("dve", lambda e, dst=dst, src=src: e.tensor_tensor(out=dst, in0=src, in1=dst, op=ALU.add), r=[bO, b_acc], w=[b_acc])
                        prev_v = vi
            for ch in range(4):
                cs = slice(ch * 1024, (ch + 1) * 1024)
                fw.op("dve", lambda e, cs=cs: e.reciprocal(out=accL[:, 1, cs], in_=accL[:, 1, cs]), r=[b_acc], w=[b_acc])
                fw.op("dve", lambda e, cs=cs: e.tensor_tensor(out=obh[:, cs], in0=accL[:, 0, cs], in1=accL[:, 1, cs], op=ALU.mult), r=[b_acc], w=[b_obh])
            fw.dma("act", s_OT0[h * 64:(h + 1) * 64, :], obh, r=[b_obh], w=[b_scr["OT0"]])
        self.sp_off = mark
        fw.barrier()

        def phaseD(l):
            mark = self.sp_off
            s_OT = s_OT0 if l == 0 else s_OT1
            nm_OT = "OT0" if l == 0 else "OT1"
            wb_out = wb_out_ab if l == 0 else wb_out_c
            wout, b_wout = load_w(wb_out, D, 4, "wout")
            w2t, b_w2 = load_w(wb_w2[l], FFN, 4, "w2")
            ot = self.salloc(8 * 512, BF16).rearrange("p (kc t) -> p kc t", kc=8)
            b_ot = fw.buf("ot", dma=True)
            x1 = [self.salloc(D, F32) for _ in range(4)]
            b_x1 = [fw.buf("x1_%d" % j, dma=True) for j in range(4)]
            tmp = (self.salloc(D, F32), self.salloc(4, F32), self.salloc(D, BF16), fw.buf("nrm_tmp"))
            hT = self.salloc(8 * 512, BF16).rearrange("p (kc t) -> p kc t", kc=8)
            b_hT = fw.buf("hT")
            gT = self.salloc(22 * 512, BF16).rearrange("p (hc t) -> p hc t", hc=22)
            b_gT = fw.buf("gT")
            wst = [self.salloc(2 * 8 * 256, BF16).rearrange("p (w kc n) -> p w kc n", w=2, kc=8) for _ in range(2)]
            b_wst = [fw.buf("wst%d" % i, dma=True) for i in range(2)]
            stmp = [self.salloc(512, F32) for _ in range(2)]
            b_stmp = [fw.buf("stmp%d" % i) for i in range(2)]
            stg_b = [self.salloc(512, BF16) for _ in range(2)]
            b_stg_b = [fw.buf("stgb%d" % i, dma=True) for i in range(2)]
            tm_f = [self.salloc(512, F32) for _ in range(2)]
            b_tm_f = [fw.buf("tmf%d" % i, dma=True) for i in range(2)]
            tm_b = [self.salloc(256, BF16) for _ in range(2)]
            b_tm_b = [fw.buf("tmb%d" % i, dma=True) for i in range(2)]
            if l == 1:
                gout = self.salloc(D, F32)
                b_gout = fw.buf("gout", dma=True)
                fw.dma("sp", gout, norm_out.partition_broadcast(128), w=[b_gout])
                yt = [self.salloc(D, F32) for _ in range(2)]
                b_yt = [fw.buf("yt%d" % i, dma=True) for i in range(2)]
            wcnt = 0
            pcnt = 0
            for blk in range(8):
                fw.dma("sp", ot, s_OT[:, blk * 512:(blk + 1) * 512].rearrange("(kc p) t -> p kc t", p=128), r=[b_scr[nm_OT]], w=[b_ot])
                for j in range(4):
                    t0 = blk * 512 + j * 128
                    if l == 0:
                        fw.dma("sp", x1[j], xp[t0:t0 + 128, :], w=[b_x1[j]])
                    else:
                        fw.dma("sp", x1[j], s_X1[t0:t0 + 128, :], r=[b_scr["X1"]], w=[b_x1[j]])
                    for half in range(2):
                        pi = pcnt % 2
                        pcnt += 1
                        for q2 in range(2):
                            nb = half * 2 + q2
                            for kc in range(8):
                                fw.op("pe", lambda e, kc=kc, nb=nb, q2=q2, pi=pi, j=j: e.matmul(self.ps[pi][:, q2 * 256:(q2 + 1) * 256], lhsT=ot[:, kc, j * 128:(j + 1) * 128], rhs=wout[:, nb, kc, :], start=(kc == 0), stop=(kc == 7)),
                                      r=[b_ot, b_wout], w=[self.bps[pi]])
                        fw.op("dve", lambda e, pi=pi, j=j, half=half: e.tensor_tensor(out=x1[j][:, half * 512:(half + 1) * 512], in0=self.ps[pi][:, :], in1=x1[j][:, half * 512:(half + 1) * 512], op=ALU.add),
                              r=[self.bps[pi], b_x1[j]], w=[b_x1[j]])
                    rmsnorm_T(x1[j], b_x1[j], 128, 1 + 2 * l, hT, b_hT, j * 128, tmp)
                if S < 2.85:
                    continue
                for nb in range(11):
                    wi = wcnt % 2
                    wcnt += 1
                    fw.dma("sp", wst[wi][:, 0], wb_w1[l][nb].rearrange("p (kc n) -> p kc n", kc=8), r=[b_scr["w"]], w=[b_wst[wi]])
                    fw.dma("sp", wst[wi][:, 1], wb_w3[l][nb].rearrange("p (kc n) -> p kc n", kc=8), r=[b_scr["w"]], w=[b_wst[wi]])
                    for sub in range(2):
                        hc = nb * 2 + sub
                        p1, p3 = 2 + (hc % 2) * 2, 3 + (hc % 2) * 2
                        for (pp, wsel) in ((p1, 0), (p3, 1)):
                            for kc in range(8):
                                fw.op("pe", lambda e, kc=kc, pp=pp, wsel=wsel, wi=wi, sub=sub: e.matmul(self.ps[pp][:, :], lhsT=wst[wi][:, wsel, kc, sub * 128:(sub + 1) * 128], rhs=hT[:, kc, :], start=(kc == 0), stop=(kc == 7)),
                                      r=[b_wst[wi], b_hT], w=[self.bps[pp]])
                        si = hc % 2
                        fw.op("act", lambda e, si=si, p1=p1: e.activation(out=stmp[si], in_=self.ps[p1][:, :], func=AF.Silu), r=[self.bps[p1]], w=[b_stmp[si]])
                        fw.op("dve", lambda e, si=si, p3=p3, hc=hc: e.tensor_tensor(out=gT[:, hc, :], in0=self.ps[p3][:, :], in1=stmp[si], op=ALU.mult), r=[self.bps[p3], b_stmp[si]], w=[b_gT])
                if S < 2.95:
                    continue
                for j in range(4):
                    t0 = blk * 512 + j * 128
                    for half in range(2):
                        pi = pcnt % 2
                        pcnt += 1
                        for q2 in range(2):
                            nb = half * 2 + q2
                            for hc in range(22):
                                fw.op("pe", lambda e, hc=hc, nb=nb, q2=q2, pi=pi, j=j: e.matmul(self.ps[pi][:, q2 * 256:(q2 + 1) * 256], lhsT=gT[:, hc, j * 128:(j + 1) * 128], rhs=w2t[:, nb, hc, :], start=(hc == 0), stop=(hc == 21)),
                                      r=[b_gT, b_w2], w=[self.bps[pi]])
                        fw.op("dve", lambda e, pi=pi, j=j, half=half: e.tensor_tensor(out=x1[j][:, half * 512:(half + 1) * 512], in0=self.ps[pi][:, :], in1=x1[j][:, half * 512:(half + 1) * 512], op=ALU.add),
                              r=[self.bps[pi], b_x1[j]], w=[b_x1[j]])
                    if l == 0:
                        fw.dma("act", s_X1[t0:t0 + 128, :], x1[j], r=[b_x1[j]], w=[b_scr["X1"]])
                        rmsnorm_T(x1[j], b_x1[j], 128, 2, hT, b_hT, j * 128, tmp)
                    else:
                        sq, ss, xn, bt = tmp
                        yi = j % 2
                        fw.op("act", lambda e, j=j: e.activation(out=sq, in_=x1[j], func=AF.Square, accum_out=ss[:, 0:1]), r=[b_x1[j]], w=[bt])
                        fw.op("dve", lambda e: e.tensor_scalar(out=ss[:, 1:2], in0=ss[:, 0:1], scalar1=1.0 / D, scalar2=EPS, op0=ALU.mult, op1=ALU.add), r=[bt], w=[bt])
                        fw.op("act", lambda e: e.activation(out=ss[:, 2:3], in_=ss[:, 1:2], func=AF.Sqrt), r=[bt], w=[bt])
                        fw.op("dve", lambda e: e.reciprocal(out=ss[:, 3:4], in_=ss[:, 2:3]), r=[bt], w=[bt])
                        fw.op("dve", lambda e, j=j, yi=yi: e.scalar_tensor_tensor(out=yt[yi], in0=x1[j], scalar=ss[:, 3:4], in1=gout, op0=ALU.mult, op1=ALU.mult), r=[bt, b_x1[j], b_gout], w=[b_yt[yi]])
                        fw.dma("act", o_y_p[t0:t0 + 128, :], yt[yi], r=[b_yt[yi]])
                if l == 0 and S >= 3:
                    for nb in range(8):
                        wi = wcnt % 2
                        wcnt += 1
                        fw.dma("sp", wst[wi][:, 0], wb_in_c[nb].rearrange("p (kc n) -> p kc n", kc=8), r=[b_scr["w"]], w=[b_wst[wi]])
                        for sub in range(2):
                            oc = nb * 2 + sub
                            pp = 2 + oc % 2
                            for kc in range(8):
                                fw.op("pe", lambda e, kc=kc, pp=pp, wi=wi, sub=sub: e.matmul(self.ps[pp][:, :], lhsT=wst[wi][:, 0, kc, sub * 128:(sub + 1) * 128], rhs=hT[:, kc, :], start=(kc == 0), stop=(kc == 7)),
                                      r=[b_wst[wi], b_hT], w=[self.bps[pp]])
                            si = oc % 2
                            fw.op("act", lambda e, si=si, pp=pp: e.copy(stg_b[si], self.ps[pp][:, :]), r=[self.bps[pp]], w=[b_stg_b[si]])
                            if oc < 8:
                                dst, nm, r0 = s_QT1, "QT1", oc * 128
                            elif oc < 12:
                                dst, nm, r0 = s_KC, "KC", (oc - 8) * 128
                            elif oc < 14:
                                dst, nm, r0 = s_KS, "KS", (oc - 12) * 128
                            else:
                                dst, nm, r0 = s_KW, "KW", (oc - 14) * 128
                            fw.dma("act", dst[r0:r0 + 128, blk * 512:(blk + 1) * 512], stg_b[si], r=[b_stg_b[si]], w=[b_scr[nm]])
                    for nb2 in range(4 if S >= 3.2 else (3 if S >= 3.1 else 0)):
                        wi = wcnt % 2
                        wcnt += 1
                        nq = 2 if nb2 < 3 else 1
                        for q2 in range(nq):
                            fw.dma("sp", wst[wi][:, q2], wb_in_c_tm[nb2 * 2 + q2].rearrange("p (kc n) -> p kc n", kc=8), r=[b_scr["w"]], w=[b_wst[wi]])
                        for j in range(4):
                            t0 = blk * 512 + j * 128
                            pp = 2 + j % 2
                            for q2 in range(nq):
                                for kc in range(8):
                                    fw.op("pe", lambda e, kc=kc, pp=pp, wi=wi, q2=q2, j=j: e.matmul(self.ps[pp][:, q2 * 256:(q2 + 1) * 256], lhsT=hT[:, kc, j * 128:(j + 1) * 128], rhs=wst[wi][:, q2, kc, :], start=(kc == 0), stop=(kc == 7)),
                                          r=[b_wst[wi], b_hT], w=[self.bps[pp]])
                            fi = j % 2
                            if nb2 < 3:
                                fw.op("act", lambda e, fi=fi, pp=pp: e.copy(tm_f[fi], self.ps[pp][:, :]), r=[self.bps[pp]], w=[b_tm_f[fi]])
                                if nb2 == 0:
                                    fw.dma("act", o_cmp_kv_p[t0:t0 + 128, :], tm_f[fi], r=[b_tm_f[fi]])
                                elif nb2 == 1:
                                    fw.dma("act", o_sel_kv_p[t0:t0 + 128, :], tm_f[fi], r=[b_tm_f[fi]])
                                elif t0 >= SEQ - 512:
                                    fw.dma("act", o_win_kv_p[t0 - (SEQ - 512):t0 - (SEQ - 512) + 128, :], tm_f[fi], r=[b_tm_f[fi]])
                                if nb2 >= 1:
                                    fw.op("dve", lambda e, fi=fi: e.tensor_copy(tm_b[fi], tm_f[fi][:, 256:512]), r=[b_tm_f[fi]], w=[b_tm_b[fi]])
                                    dst, nm = (s_VS, "VS") if nb2 == 1 else (s_VW, "VW")
                                    fw.dma("act", dst[t0:t0 + 128, :], tm_b[fi], r=[b_tm_b[fi]], w=[b_scr[nm]])
                            else:
                                fw.op("act", lambda e, fi=fi, pp=pp: e.activation(out=tm_f[fi][:, 0:48], in_=self.ps[pp][:, 0:48], func=AF.Sigmoid), r=[self.bps[pp]], w=[b_tm_f[fi]])
                                fw.dma("act", s_G[t0:t0 + 128, :], tm_f[fi][:, 0:48], r=[b_tm_f[fi]], w=[b_scr["G"]])
            self.sp_off = mark
            fw.barrier()

        if S < 2.8:
            return
        phaseD(0)
        if S < 4:
            return

        mark = self.sp_off
        mc_t = self.salloc(33 * 128, BF16).rearrange("p (m q) -> p m q", m=33)
        tsel_t = self.salloc(32 * 64, F32).rearrange("p (a j) -> p a j", a=32)
        e30_t = self.salloc(SEQ, BF16, parts=64)
        wc_f = self.salloc(4096, F32, parts=64)
        wc_b = self.salloc(4096, BF16, parts=64).rearrange("p (l c e) -> p l c e", l=32, c=2)
        pe_f = self.salloc(64, F32, parts=64)
        pe_b = self.salloc(64, BF16, parts=64).rearrange("p (l c) -> p l c", c=2)
        b_fc = fw.buf("fconst", dma=True)
        b_fc2 = fw.buf("fconst2")
        fw.dma("sp", mc_t, c_mc, w=[b_fc])
        fw.dma("sp", tsel_t, c_tsel, w=[b_fc])
        fw.dma("sp", e30_t, c_e30, w=[b_fc])
        fw.dma("sp", wc_f, w_cmp_t, w=[b_fc])
        fw.dma("sp", pe_f, pe_cmp_t, w=[b_fc])
        fw.op("dve", lambda e: e.tensor_copy(wc_b.rearrange("p l c e -> p (l c e)"), wc_f), r=[b_fc], w=[b_fc2])
        fw.op("dve", lambda e: e.tensor_copy(pe_b.rearrange("p l c -> p (l c)"), pe_f), r=[b_fc], w=[b_fc2])
        VS = self.salloc(32 * 4 * 65, BF16).rearrange("p (k g d) -> p k g d", k=32, g=4)
        VW = self.salloc(32 * 4 * 65, BF16).rearrange("p (k g d) -> p k g d", k=32, g=4)
        b_VS = fw.buf("VSa", dma=True)
        b_VW = fw.buf("VWa", dma=True)
        for (Vt, bV, src, nm) in ((VS, b_VS, s_VS, "VS"), (VW, b_VW, s_VW, "VW")):
            fw.op("dve", lambda e, Vt=Vt: e.memset(Vt[:, :, :, 64:65], 1.0), w=[bV])
            for kt in range(32):
                fw.dma("sp", Vt[:, kt, :, 0:64], src[kt * 128:(kt + 1) * 128, :].rearrange("p (g d) -> p g d", g=4), r=[b_scr[nm]], w=[bV])
        XK = self.salloc(SEQ, BF16, parts=64)
        XV = self.salloc(SEQ, BF16, parts=64)
        b_XK = fw.buf("XK", dma=True)
        b_XV = fw.buf("XV", dma=True)
        Ksa = self.salloc(SEQ, BF16, parts=KR)
        Kwa = self.salloc(SEQ, BF16, parts=KR)
        Kca = self.salloc(256, BF16, parts=KR)
        b_Ksa = fw.buf("Ksa", dma=True)
        b_Kwa = fw.buf("Kwa", dma=True)
        b_Kca = fw.buf("Kca", dma=True)
        VCX = self.salloc(2 * 129, BF16).rearrange("p (t x) -> p t x", t=2)
        b_VCX = fw.buf("VCX", dma=True)
        pek = self.salloc(4, F32, parts=64)
        pev = self.salloc(64, BF16, parts=1)
        b_pe = fw.buf("pekv")
        qas = [self.salloc(4 * 128, BF16, parts=KR).rearrange("p (r q) -> p r q", r=4) for _ in range(2)]
        b_qas = [fw.buf("qa%d" % i, dma=True) for i in range(2)]
        pcs = [self.salloc(512, BF16) for _ in range(2)]
        b_pcs = [fw.buf("pc%d" % i) for i in range(2)]
        pss = [self.salloc(512, BF16) for _ in range(3)]
        b_pss = [fw.buf("ps_%d" % i) for i in range(3)]
        sc = self.salloc(64, F32)
        sc2 = self.salloc(64, F32)
        m8 = self.salloc(16, F32)
        small = self.salloc(64, F32)
        madd = self.salloc(64, BF16)
        msel = self.salloc(512, BF16, parts=64).rearrange("p (r q) -> p r q", r=4)
        b_sel = fw.buf("seltmp")
        b_msel = fw.buf("msel")
        gts = [self.salloc(48, F32) for _ in range(2)]
        b_gts = [fw.buf("gt%d" % i, dma=True) for i in range(2)]
        otile = self.salloc(256, F32)
        otb = self.salloc(256, BF16)
        b_ot_ = fw.buf("otile")
        ot1s = [self.salloc(256, BF16).rearrange("p (c t) -> p c t", c=2) for _ in range(2)]
        b_ot1s = [fw.buf("ot1_%d" % i, dma=True) for i in range(2)]
        if DBG:
            dbg_br = self.dout("dbg_br", [3, SEQ, 1024])
            dbgt = self.salloc(3 * 256, F32).rearrange("p (b f) -> p b f", b=3)
            b_dbgt = fw.buf("dbgt", dma=True)
        PS_S = (0, 1)
        PS_CA, PS_CB, PS_OS, PS_OW, PS_X, PS_T = 2, 3, 4, 5, 6, 7
        ucnt = 0
        for g in range(4):
            fw.dma("sp", XK, s_KC[g * 64:(g + 1) * 64, :], r=[b_scr["KC"]], w=[b_XK])
            fw.dma("sp", XV, s_KC[256 + g * 64:256 + (g + 1) * 64, :], r=[b_scr["KC"]], w=[b_XV])
            fw.dma("sp", Ksa[0:64, :], s_KS[g * 64:(g + 1) * 64, :], r=[b_scr["KS"]], w=[b_Ksa])
            fw.dma("sp", Ksa[64:KR, :], c_kaug, w=[b_Ksa])
            fw.dma("sp", Kwa[0:64, :], s_KW[g * 64:(g + 1) * 64, :], r=[b_scr["KW"]], w=[b_Kwa])
            fw.dma("sp", Kwa[64:KR, :], c_kaug, w=[b_Kwa])
            fw.dma("sp", Kca[64:KR, :], c_kaugc, w=[b_Kca])
            fw.dma("sp", VCX[:, :, 65:129], c_c2s, w=[b_VCX])
            fw.op("dve", lambda e: e.memset(VCX[:, :, 64:65], 1.0), w=[b_VCX])
            for l in range(32):
                fw.op("pe", lambda e, l=l: e.matmul(self.ps[PS_X][0:64, 0:1], lhsT=wc_b[:, l, 0, :], rhs=pe_b[:, l, 0:1], start=(l == 0), stop=(l == 31)), r=[b_fc2], w=[self.bps[PS_X]])
            for l in range(32):
                fw.op("pe", lambda e, l=l: e.matmul(self.ps[PS_X][0:1, 64:128], lhsT=pe_b[:, l, 1:2], rhs=wc_b[:, l, 1, :], start=(l == 0), stop=(l == 31)), r=[b_fc2], w=[self.bps[PS_X]])
            fw.op("dve", lambda e: e.tensor_copy(pek[:, 0:1], self.ps[PS_X][0:64, 0:1]), r=[self.bps[PS_X]], w=[b_pe])
            fw.op("dve", lambda e: e.tensor_copy(pev, self.ps[PS_X][0:1, 64:128]), r=[self.bps[PS_X]], w=[b_pe])
            for l in range(32):
                fw.op("pe", lambda e, l=l: e.matmul(self.ps[PS_X][0:64, 0:255], lhsT=wc_b[:, l, 0, :], rhs=XK[:, l:l + 16 * 254 + 1:16], start=(l == 0), stop=(l == 31)), r=[b_fc2, b_XK], w=[self.bps[PS_X]])
            fw.op("dve", lambda e: e.memset(Kca[0:64, 255:256], 0.0), w=[b_Kca])
            fw.op("dve", lambda e: e.tensor_scalar(out=Kca[0:64, 0:255], in0=self.ps[PS_X][0:64, 0:255], scalar1=pek[:, 0:1], scalar2=None, op0=ALU.add), r=[self.bps[PS_X], b_pe], w=[b_Kca])
            for nt_, nrows in ((0, 128), (1, 127)):
                for l in range(32):
                    a0 = 16 * 128 * nt_ + l
                    fw.op("pe", lambda e, l=l, a0=a0, nrows=nrows: e.matmul(self.ps[PS_X][0:nrows, 256:320], lhsT=XV[:, a0:a0 + 16 * (nrows - 1) + 1:16], rhs=wc_b[:, l, 1, :], start=(l == 0), stop=False), r=[b_fc2, b_XV], w=[self.bps[PS_X]])
                fw.op("pe", lambda e, nrows=nrows: e.matmul(self.ps[PS_X][0:nrows, 256:320], lhsT=ones_b[0:1, 0:nrows], rhs=pev, start=False, stop=True), r=[b_const2, b_pe], w=[self.bps[PS_X]])
                fw.op("dve", lambda e, nt_=nt_, nrows=nrows: e.tensor_copy(VCX[0:nrows, nt_, 0:64], self.ps[PS_X][0:nrows, 256:320]), r=[self.bps[PS_X]], w=[b_VCX])
            for qt in range(32):
                tq = slice(qt * 128, (qt + 1) * 128)
                qi_ = (g * 32 + qt) % 2
                qa, b_qa = qas[qi_], b_qas[qi_]
                fw.dma("sp", qa[0:64, :, :], s_QT1[g * 256:(g + 1) * 256, tq].rearrange("(r d) t -> d r t", d=64), r=[b_scr["QT1"]], w=[b_qa])
                fw.dma("sp", qa[64:KR, :, :], c_qaug1[g][:, :, tq], w=[b_qa])
                gt, b_gt = gts[qi_], b_gts[qi_]
                fw.dma("sp", gt, s_G[tq, :], r=[b_scr["G"]], w=[b_gt])
                ntl = [(0, 128)] + ([(1, 127)] if qt >= 16 else [])
                pcl = []
                for (nt_, nrows) in ntl:
                    si = ucnt % 2
                    ucnt += 1
                    Sb, bS = self.ps[PS_S[si]], self.bps[PS_S[si]]
                    mi = qt if nt_ == 0 else 17 + qt - 16
                    need_mask = (nt_ == 1) or (qt <= 16)
                    fw.op("pe", lambda e, Sb=Sb, nt_=nt_, nrows=nrows, qa=qa, need_mask=need_mask: e.matmul(Sb[0:nrows, :], lhsT=Kca[:, nt_ * 128:nt_ * 128 + nrows], rhs=qa.rearrange("p r q -> p (r q)"), start=True, stop=not need_mask, skip_group_check=True), r=[b_Kca, b_qa], w=[bS])
                    if need_mask:
                        for r_ in range(4):
                            cs = slice(r_ * 128, (r_ + 1) * 128)
                            fw.op("pe", lambda e, Sb=Sb, cs=cs, nrows=nrows, mi=mi, r_=r_: e.matmul(Sb[0:nrows, cs], lhsT=ident_b[0:nrows, 0:nrows], rhs=mc_t[0:nrows, mi, :], start=False, stop=(r_ == 3), skip_group_check=True), r=[b_const2, b_fc], w=[bS])
                    pc, b_pc = pcs[nt_], b_pcs[nt_]
                    fw.op("act", lambda e, Sb=Sb, pc=pc, nrows=nrows: e.activation(out=pc[0:nrows, :], in_=Sb[0:nrows, :], func=AF.Exp, scale=0.125), r=[bS], w=[b_pc])
                    pcl.append((nt_, nrows, pc, b_pc))
                for r_ in range(4):
                    bank = PS_CA if r_ < 2 else PS_CB
                    c0 = (r_ % 2) * 129
                    for n_, (nt_, nrows, pc, b_pc) in enumerate(pcl):
                        fw.op("pe", lambda e, bank=bank, c0=c0, nt_=nt_, nrows=nrows, pc=pc, r_=r_, n_=n_: e.matmul(self.ps[bank][:, c0:c0 + 129], lhsT=pc[0:nrows, r_ * 128:(r_ + 1) * 128], rhs=VCX[0:nrows, nt_, :], start=(n_ == 0), stop=(n_ == len(pcl) - 1)), r=[b_pc, b_VCX], w=[self.bps[bank]])
                for r_ in range(4):
                    bank = PS_CA if r_ < 2 else PS_CB
                    c0 = (r_ % 2) * 129
                    fw.op("dve", lambda e, bank=bank, c0=c0, r_=r_: e.tensor_scalar(out=small[:, r_:r_ + 1], in0=self.ps[bank][:, c0 + 64:c0 + 65], scalar1=1e-30, scalar2=None, op0=ALU.max), r=[self.bps[bank]], w=[b_sel])
                fw.op("dve", lambda e: e.reciprocal(out=small[:, 16:20], in_=small[:, 0:4]), r=[b_sel], w=[b_sel])
                for r_ in range(4):
                    bank = PS_CA if r_ < 2 else PS_CB
                    c0 = (r_ % 2) * 129
                    if r_ == 0:
                        fw.op("dve", lambda e, bank=bank, c0=c0: e.tensor_scalar(out=sc, in0=self.ps[bank][:, c0 + 65:c0 + 129], scalar1=small[:, 16:17], scalar2=None, op0=ALU.mult), r=[self.bps[bank], b_sel], w=[b_sel])
                    else:
                        fw.op("dve", lambda e, bank=bank, c0=c0, r_=r_: e.scalar_tensor_tensor(out=sc, in0=self.ps[bank][:, c0 + 65:c0 + 129], scalar=small[:, 16 + r_:17 + r_], in1=sc, op0=ALU.mult, op1=ALU.add), r=[self.bps[bank], b_sel], w=[b_sel])
                fw.op("dve", lambda e, qt=qt: e.tensor_tensor(out=sc, in0=sc, in1=tsel_t[:, qt, :], op=ALU.add), r=[b_sel, b_fc], w=[b_sel])
                fw.op("dve", lambda e: e.max(out=m8[:, 0:8], in_=sc), r=[b_sel], w=[b_sel])
                fw.op("dve", lambda e: e.match_replace(out=sc2, in_to_replace=m8[:, 0:8], in_values=sc, imm_value=-1e30), r=[b_sel], w=[b_sel])
                fw.op("dve", lambda e: e.max(out=m8[:, 8:16], in_=sc2), r=[b_sel], w=[b_sel])
                fw.op("dve", lambda e: e.tensor_scalar(out=m8[:, 0:1], in0=m8[:, 15:16], scalar1=-1e29, scalar2=None, op0=ALU.max), r=[b_sel], w=[b_sel])
                fw.op("dve", lambda e: e.tensor_scalar(out=madd, in0=sc, scalar1=m8[:, 0:1], scalar2=1.0, op0=ALU.is_ge, op1=ALU.subtract), r=[b_sel], w=[b_sel])
                pstb = self.ps[PS_T][:, :].bitcast(BF16)
                fw.op("pe", lambda e: e.transpose(pstb[0:64, 0:128], madd, ident_b), r=[b_sel, b_const2], w=[self.bps[PS_T]])
                fw.op("dve", lambda e: e.tensor_copy(msel, pstb[0:64, 0:128].rearrange("p (o q) -> p o q", o=1).to_broadcast([64, 4, 128])), r=[self.bps[PS_T]], w=[b_msel])
                for (br, Ka, b_Ka, Vt, bV, bank, k0) in ((1, Ksa, b_Ksa, VS, b_VS, PS_OS, 0), (2, Kwa, b_Kwa, VW, b_VW, PS_OW, max(0, qt - 4))):
                    for kt in range(k0, qt + 1):
                        si = ucnt % 2
                        ucnt += 1
                        Sb, bS = self.ps[PS_S[si]], self.bps[PS_S[si]]
                        tk = slice(kt * 128, (kt + 1) * 128)
                        extra = []
                        if br == 1:
                            extra.append((e30_t[:, tk], msel.rearrange("p r q -> p (r q)"), [b_fc, b_msel]))
                        if kt == qt:
                            extra.append((ident_b, cmask4[:, 512:1024], [b_const, b_const2]))
                        if br == 2 and kt == qt - 4:
                            extra.append((ident_b, cmask4[:, 0:512], [b_const, b_const2]))
                        fw.op("pe", lambda e, Sb=Sb, Ka=Ka, tk=tk, qa=qa, ne=len(extra): e.matmul(Sb[:, :], lhsT=Ka[:, tk], rhs=qa.rearrange("p r q -> p (r q)"), start=True, stop=(ne == 0)), r=[b_Ka, b_qa], w=[bS])
                        for xi, (lh, rh, rb) in enumerate(extra):
                            fw.op("pe", lambda e, Sb=Sb, lh=lh, rh=rh, last_=(xi == len(extra) - 1): e.matmul(Sb[:, :], lhsT=lh, rhs=rh, start=False, stop=last_), r=rb, w=[bS])
                        pi_ = ucnt % 3
                        pp_, b_pp = pss[pi_], b_pss[pi_]
                        fw.op("act", lambda e, Sb=Sb, pp_=pp_: e.activation(out=pp_, in_=Sb[:, :], func=AF.Exp, scale=0.125), r=[bS], w=[b_pp])
                        for r_ in range(4):
                            fw.op("pe", lambda e, bank=bank, r_=r_, pp_=pp_, Vt=Vt, kt=kt, k0=k0, g=g, qt=qt: e.matmul(self.ps[bank][:, r_ * 65:(r_ + 1) * 65], lhsT=pp_[:, r_ * 128:(r_ + 1) * 128], rhs=Vt[:, kt, g, :], start=(kt == k0 and r_ == 0), stop=(kt == qt), skip_group_check=True), r=[b_pp, bV], w=[self.bps[bank]])
                    fw.op("dve", lambda e, bank=bank, br=br: e.tensor_scalar(out=small[:, br * 4:(br + 1) * 4], in0=self.ps[bank][:, 0:260].rearrange("p (r x) -> p r x", r=4)[:, :, 64], scalar1=1e-30, scalar2=None, op0=ALU.max), r=[self.bps[bank]], w=[b_sel])
                fw.op("dve", lambda e: e.reciprocal(out=small[:, 16:28], in_=small[:, 0:12]), r=[b_sel], w=[b_sel])
                fw.op("dve", lambda e, gt=gt, g=g: e.tensor_tensor(out=small[:, 32:44].rearrange("p (b r) -> p b r", b=3), in0=small[:, 16:28].rearrange("p (b r) -> p b r", b=3), in1=gt[:, g * 12:(g + 1) * 12].rearrange("p (r b) -> p b r", b=3), op=ALU.mult), r=[b_sel, b_gt], w=[b_sel])
                for r_ in range(4):
                    bankc = PS_CA if r_ < 2 else PS_CB
                    c0 = (r_ % 2) * 129
                    od = otile[:, r_ * 64:(r_ + 1) * 64]
                    fw.op("dve", lambda e, bankc=bankc, c0=c0, od=od, r_=r_: e.tensor_scalar(out=od, in0=self.ps[bankc][:, c0:c0 + 64], scalar1=small[:, 32 + r_:33 + r_], scalar2=None, op0=ALU.mult), r=[self.bps[bankc], b_sel], w=[b_ot_])
                    fw.op("dve", lambda e, od=od, r_=r_: e.scalar_tensor_tensor(out=od, in0=self.ps[PS_OS][:, r_ * 65:r_ * 65 + 64], scalar=small[:, 36 + r_:37 + r_], in1=od, op0=ALU.mult, op1=ALU.add), r=[self.bps[PS_OS], b_sel, b_ot_], w=[b_ot_])
                    fw.op("dve", lambda e, od=od, r_=r_: e.scalar_tensor_tensor(out=od, in0=self.ps[PS_OW][:, r_ * 65:r_ * 65 + 64], scalar=small[:, 40 + r_:41 + r_], in1=od, op0=ALU.mult, op1=ALU.add), r=[self.bps[PS_OW], b_sel, b_ot_], w=[b_ot_])
                if DBG:
                    for b_i in range(3):
                        for r_ in range(4):
                            if b_i == 0:
                                bk = PS_CA if r_ < 2 else PS_CB
                                src = self.ps[bk][:, (r_ % 2) * 129:(r_ % 2) * 129 + 64]
                            else:
                                bk = PS_OS if b_i == 1 else PS_OW
                                src = self.ps[bk][:, r_ * 65:r_ * 65 + 64]
                            fw.op("dve", lambda e, src=src, b_i=b_i, r_=r_: e.tensor_scalar(out=dbgt[:, b_i, r_ * 64:(r_ + 1) * 64], in0=src, scalar1=small[:, 16 + b_i * 4 + r_:17 + b_i * 4 + r_], scalar2=None, op0=ALU.mult), r=[self.bps[bk], b_sel], w=[b_dbgt])
                    fw.dma("act", dbg_br[:, tq, g * 256:(g + 1) * 256].rearrange("b t f -> t b f"), dbgt, r=[b_dbgt])
                fw.op("act", lambda e: e.copy(otb, otile), r=[b_ot_], w=[b_ot_])
                for c_ in range(2):
                    fw.op("pe", lambda e, c_=c_: e.transpose(pstb[:, 256 + c_ * 128:256 + (c_ + 1) * 128], otb[:, c_ * 128:(c_ + 1) * 128], ident_b), r=[b_ot_, b_const2], w=[self.bps[PS_T]])
                oi = qt % 2
                fw.op("dve", lambda e, oi=oi: e.tensor_copy(ot1s[oi], pstb[:, 256:512].rearrange("p (c t) -> p c t", c=2)), r=[self.bps[PS_T]], w=[b_ot1s[oi]])
                fw.dma("act", s_OT1[g * 256:(g + 1) * 256, tq].rearrange("(c p) t -> p c t", p=128), ot1s[oi], r=[b_ot1s[oi]], w=[b_scr["OT1"]])
        self.sp_off = mark
        fw.barrier()
        phaseD(1)
        if S < 5:
            return
        self.sample_path(locals())

    def sample_path(self, L):
        fw = self.fw
        g_ = lambda n: L[n]
        xs, cache_dil, st_conv, st_rnn, cache_win = g_("xs"), g_("cache_dil"), g_("st_conv"), g_("st_rnn"), g_("cache_win")
        b_scr, load_w, rmsnorm_T = g_("b_scr"), g_("load_w"), g_("rmsnorm_T")
        ident_b, ones_b, b_const, b_const2 = g_("ident_b"), g_("ones_b"), g_("b_const"), g_("b_const2")
        NT = 32
        mark = self.sp_off
        win, b_win = load_w(g_("wb_in_ab"), D, 10, "win_ab_s")
        xt = self.salloc(D, F32)
        b_xt = fw.buf("xts", dma=True)
        tmp = (self.salloc(D, F32), self.salloc(4, F32), self.salloc(D, BF16), fw.buf("nrm_tmp_s"))
        hT = self.salloc(8 * NT, BF16).rearrange("p (kc t) -> p kc t", kc=8)
        b_hT = fw.buf("hTs")
        stg_b = [self.salloc(NT, BF16) for _ in range(2)]
        stg_f = [self.salloc(NT, F32) for _ in range(2)]
        b_stg_b = [fw.buf("sstgb%d" % i, dma=True) for i in range(2)]
        b_stg_f = [fw.buf("sstgf%d" % i, dma=True) for i in range(2)]
        kv_f = self.salloc(1024, F32)
        kv_b = self.salloc(1024, BF16)
        b_kv_f = fw.buf("skvf", dma=True)
        b_kv_b = fw.buf("skvb", dma=True)
        fw.dma("sp", xt[0:NT, :], xs, w=[b_xt])
        rmsnorm_T(xt, b_xt, NT, 0, hT, b_hT, 0, tmp)
        cnt = 0
        for oc in list(range(0, 4)) + list(range(12, 20)):
            pi = cnt % 2
            cnt += 1
            nb, sub = oc // 2, oc % 2
            for kc in range(8):
                fw.op("pe", lambda e, kc=kc, nb=nb, sub=sub, pi=pi: e.matmul(self.ps[pi][:, 0:NT], lhsT=win[:, nb, kc, sub * 128:(sub + 1) * 128], rhs=hT[:, kc, :], start=(kc == 0), stop=(kc == 7)), r=[b_win, b_hT], w=[self.bps[pi]])
            si = oc % 2
            r0 = (oc % 4) * 128
            if oc < 4:
                fw.op("act", lambda e, si=si, pi=pi: e.copy(stg_b[si], self.ps[pi][:, 0:NT]), r=[self.bps[pi]], w=[b_stg_b[si]])
                fw.dma("act", g_("s_sQT0")[r0:r0 + 128, :], stg_b[si], r=[b_stg_b[si]], w=[b_scr["sQT0"]])
            else:
                fw.op("act", lambda e, si=si, pi=pi: e.copy(stg_f[si], self.ps[pi][:, 0:NT]), r=[self.bps[pi]], w=[b_stg_f[si]])
                dst, nm = (g_("s_sXR"), "sXR") if oc < 16 else (g_("s_sGT"), "sGT")
                fw.dma("act", dst[r0:r0 + 128, :], stg_f[si], r=[b_stg_f[si]], w=[b_scr[nm]])
        for half in range(2):
            pi = 2 + half
            for q2 in range(2):
                nb = 2 + half * 2 + q2
                for kc in range(8):
                    fw.op("pe", lambda e, kc=kc, nb=nb, q2=q2, pi=pi: e.matmul(self.ps[pi][0:NT, q2 * 256:(q2 + 1) * 256], lhsT=hT[:, kc, :], rhs=win[:, nb, kc, :], start=(kc == 0), stop=(kc == 7)), r=[b_win, b_hT], w=[self.bps[pi]])
            fw.op("act", lambda e, half=half, pi=pi: e.copy(kv_f[0:NT, half * 512:(half + 1) * 512], self.ps[pi][0:NT, :]), r=[self.bps[pi]], w=[b_kv_f])
        fw.op("dve", lambda e: e.tensor_copy(kv_b[0:NT, :], kv_f[0:NT, :]), r=[b_kv_f], w=[b_kv_b])
        fw.dma("act", g_("s_sKV"), kv_b[0:NT, :], r=[b_kv_b], w=[b_scr["sKV"]])
        o_dil_kv_s = g_("o_dil_kv_s")
        b_d2d = fw.buf("d2d", dma=True)
        for sq in range(4):
            fw.dma("act", o_dil_kv_s[sq, 2040:2048, :], kv_f[sq * 8:(sq + 1) * 8, :], r=[b_kv_f])
            for c4 in range(4):
                fw.dma("sp", o_dil_kv_s[sq, c4 * 510:(c4 + 1) * 510, :], cache_dil[sq, 8 + c4 * 510:8 + (c4 + 1) * 510, :], w=[b_d2d])
        self.sp_off = mark
        fw.barrier()

        mark = self.sp_off
        rp = self.salloc(36, F32)
        stc = self.salloc(48, F32).rearrange("p (c s k) -> p c s k", c=4, s=4)
        strn = self.salloc(16, F32).rearrange("p (c s) -> p c s", c=4)
        b_rp = fw.buf("rps", dma=True)
        fw.dma("sp", rp, g_("rnnp"), w=[b_rp])
        fw.dma("sp", stc.rearrange("p c s k -> p (c s k)"), st_conv, w=[b_rp])
        fw.dma("sp", strn.rearrange("p c s -> p (c s)"), st_rnn, w=[b_rp])
        bd_f = self.salloc(256, F32)
        bd_b = self.salloc(256, BF16)
        b_bdf = fw.buf("bdfs", dma=True)
        b_bdb = fw.buf("bdbs")
        xr = self.salloc(4 * 11, F32).rearrange("p (s t) -> p s t", s=4)
        b_xr = fw.buf("xrs", dma=True)
        xc = self.salloc(NT, F32)
        xcb = self.salloc(NT, BF16)
        t1 = self.salloc(NT, F32)
        t2 = self.salloc(NT, F32)
        t3 = self.salloc(NT, F32)
        gtl = self.salloc(NT, F32)
        ob = self.salloc(NT, BF16)
        sp_t = self.salloc(4, F32)
        cvo = self.salloc(12, F32).rearrange("p (s k) -> p s k", s=4)
        hl = self.salloc(4, F32)
        b_w = fw.buf("rnn_s_work")
        b_gtl = fw.buf("gtls", dma=True)
        b_ob = fw.buf("obs", dma=True)
        b_cvo = fw.buf("cvos", dma=True)
        v3 = lambda a: a.rearrange("p (s t) -> p s t", s=4)
        for c in range(4):
            pr = rp[:, c * 9:(c + 1) * 9]
            fw.op("dve", lambda e: e.memset(bd_f, 0.0), w=[b_bdf])
            for gi, gw in enumerate((g_("gaw"), g_("gxw"))):
                fw.dma("sp", bd_f[0:64, gi * 128:gi * 128 + 64], gw[2 * c], w=[b_bdf])
                fw.dma("sp", bd_f[64:128, gi * 128 + 64:gi * 128 + 128], gw[2 * c + 1], w=[b_bdf])
            fw.op("dve", lambda e: e.tensor_copy(bd_b, bd_f), r=[b_bdf], w=[b_bdb])
            fw.dma("sp", xr[:, :, 3:11], g_("s_sXR")[c * 128:(c + 1) * 128, :].rearrange("p (s t) -> p s t", s=4), r=[b_scr["sXR"]], w=[b_xr])
            fw.op("dve", lambda e, c=c: e.tensor_copy(xr[:, :, 0:3], stc[:, c, :, :]), r=[b_rp, b_xr], w=[b_xr])
            fw.op("dve", lambda e: e.tensor_copy(cvo, xr[:, :, 8:11]), r=[b_xr], w=[b_cvo])
            fw.dma("act", g_("o_conv_s")[:, :, c * 128:(c + 1) * 128].rearrange("s k p -> p s k"), cvo, r=[b_cvo], allow_slow_non_contiguous=True)
            fw.op("dve", lambda e, pr=pr: e.tensor_scalar(out=v3(xc), in0=xr[:, :, 0:8], scalar1=pr[:, 0:1], scalar2=pr[:, 4:5], op0=ALU.mult, op1=ALU.add), r=[b_xr, b_rp], w=[b_w])
            for k in range(1, 4):
                fw.op("dve", lambda e, pr=pr, k=k: e.scalar_tensor_tensor(out=v3(xc), in0=xr[:, :, k:k + 8], scalar=pr[:, k:k + 1], in1=v3(xc), op0=ALU.mult, op1=ALU.add), r=[b_xr, b_rp, b_w], w=[b_w])
            fw.op("dve", lambda e: e.tensor_copy(xcb, xc), r=[b_w], w=[b_w])
            fw.op("act", lambda e, pr=pr: e.activation(out=sp_t[:, 0:1], in_=pr[:, 7:8], func=AF.Exp, scale=-1.0), r=[b_rp], w=[b_w])
            fw.op("act", lambda e: e.activation(out=sp_t[:, 1:2], in_=sp_t[:, 0:1], func=AF.Ln, bias=1.0), r=[b_w], w=[b_w])
            fw.op("dve", lambda e: e.tensor_scalar(out=sp_t[:, 2:3], in0=sp_t[:, 1:2], scalar1=-8.0, scalar2=None, op0=ALU.mult), r=[b_w], w=[b_w])
            fw.op("dve", lambda e: e.tensor_scalar(out=sp_t[:, 3:4], in0=sp_t[:, 1:2], scalar1=-16.0, scalar2=None, op0=ALU.mult), r=[b_w], w=[b_w])
            for gi, tt in enumerate((t1, t2)):
                fw.op("pe", lambda e, gi=gi: e.matmul(self.ps[gi][:, 0:NT], lhsT=bd_b[:, gi * 128:(gi + 1) * 128], rhs=xcb, start=True, stop=True), r=[b_bdb, b_w], w=[self.bps[gi]])
                fw.op("act", lambda e, gi=gi, tt=tt, pr=pr: e.activation(out=tt, in_=self.ps[gi][:, 0:NT], func=AF.Sigmoid, bias=pr[:, 5 + gi:6 + gi]), r=[self.bps[gi], b_rp], w=[b_w])
            fw.op("act", lambda e: e.activation(out=t3, in_=t1, func=AF.Exp, scale=sp_t[:, 3:4]), r=[b_w], w=[b_w])
            fw.op("act", lambda e: e.activation(out=t1, in_=t1, func=AF.Exp, scale=sp_t[:, 2:3]), r=[b_w], w=[b_w])
            fw.op("dve", lambda e: e.tensor_scalar(out=t3, in0=t3, scalar1=-1.0, scalar2=1.0, op0=ALU.mult, op1=ALU.add), r=[b_w], w=[b_w])
            fw.op("dve", lambda e: e.tensor_scalar(out=t3, in0=t3, scalar1=0.0, scalar2=None, op0=ALU.max), r=[b_w], w=[b_w])
            fw.op("act", lambda e: e.activation(out=t3, in_=t3, func=AF.Sqrt), r=[b_w], w=[b_w])
            fw.op("dve", lambda e: e.tensor_tensor(out=t3, in0=t3, in1=t2, op=ALU.mult), r=[b_w], w=[b_w])
            fw.op("dve", lambda e: e.tensor_tensor(out=t3, in0=t3, in1=xc, op=ALU.mult), r=[b_w], w=[b_w])
            for sq in range(4):
                fw.op("dve", lambda e, sq=sq, c=c: e.tensor_tensor_scan(out=t2[:, sq * 8:(sq + 1) * 8], data0=t1[:, sq * 8:(sq + 1) * 8], data1=t3[:, sq * 8:(sq + 1) * 8], initial=strn[:, c, sq:sq + 1], op0=ALU.mult, op1=ALU.add), r=[b_w, b_rp], w=[b_w])
            fw.op("dve", lambda e: e.tensor_copy(cvo[:, :, 0], v3(t2)[:, :, 7]), r=[b_w, b_cvo], w=[b_cvo])
            fw.dma("act", g_("o_rnn_s")[:, c * 128:(c + 1) * 128].rearrange("s p -> p s"), cvo[:, :, 0], r=[b_cvo], allow_slow_non_contiguous=True)
            fw.dma("sp", gtl, g_("s_sGT")[c * 128:(c + 1) * 128, :], r=[b_scr["sGT"]], w=[b_gtl])
            fw.op("act", lambda e: e.activation(out=gtl, in_=gtl, func=AF.Gelu), r=[b_gtl], w=[b_gtl])
            fw.op("dve", lambda e: e.tensor_tensor(out=ob, in0=t2, in1=gtl, op=ALU.mult), r=[b_w, b_gtl], w=[b_ob])
            fw.dma("act", g_("s_sOT0")[512 + c * 128:512 + (c + 1) * 128, :], ob, r=[b_ob], w=[b_scr["sOT0"]])
        self.sp_off = mark
        fw.barrier()

        mark = self.sp_off
        c_cnt_s, c_kaug_s, c_qaug0_s = g_("c_cnt_s"), g_("c_kaug_s"), g_("c_qaug0_s")
        cnt_t = self.salloc(17 * 8, BF16).rearrange("p (k q) -> p k q", k=17)
        kaug_t = self.salloc(17 * 128, BF16, parts=KR).rearrange("p (k t) -> p k t", k=17)
        b_cc = fw.buf("sc_const", dma=True)
        fw.dma("sp", cnt_t, c_cnt_s, w=[b_cc])
        fw.dma("sp", kaug_t[64:KR, :, :], c_kaug_s.rearrange("a (k t) -> a k t", k=17), w=[b_cc])
        cts = [self.salloc(1024, F32) for _ in range(2)]
        b_cts = [fw.buf("ct%d" % i, dma=True) for i in range(2)]
        ctbs = [self.salloc(1024, BF16) for _ in range(2)]
        b_ctbs = [fw.buf("ctb%d" % i, dma=True) for i in range(2)]
        KA = [self.salloc(8 * 128, BF16, parts=KR).rearrange("p (h t) -> p h t", h=8) for _ in range(2)]
        b_KA = [fw.buf("KA%d" % i) for i in range(2)]
        QA = self.salloc(64, BF16, parts=KR).rearrange("p (h q) -> p h q", h=8)
        b_QA = fw.buf("QAs", dma=True)
        Pf = [self.salloc(64, F32) for _ in range(2)]
        Pb = [self.salloc(64, BF16).rearrange("p (h q) -> p h q", h=8) for _ in range(2)]
        b_P = [fw.buf("Ps%d" % i) for i in range(2)]
        Va = [self.salloc(8 * 65, BF16).rearrange("p (h d) -> p h d", h=8) for _ in range(2)]
        b_Va = [fw.buf("Va%d" % i) for i in range(2)]
        for i in range(2):
            fw.op("dve", lambda e, i=i: e.memset(Va[i][:, :, 64:65], 1.0), w=[b_Va[i]])
        osm = self.salloc(16, F32)
        of = self.salloc(512, F32)
        obf = self.salloc(512, BF16)
        oT = self.salloc(32, BF16).rearrange("p (c q) -> p c q", c=4)
        b_o = fw.buf("so_work")
        b_oT = fw.buf("soT", dma=True)
        pstb = self.ps[7][:, :].bitcast(BF16)
        PA, PB_ = 4, 5
        u = 0
        for sq in range(4):
            fw.dma("sp", QA[0:64, :, :], g_("s_sQT0")[:, sq * 8:(sq + 1) * 8].rearrange("(h d) t -> d h t", d=64), r=[b_scr["sQT0"]], w=[b_QA])
            fw.dma("sp", QA[64:KR, :, :], c_qaug0_s, w=[b_QA])
            for kt in range(17):
                i = u % 2
                u += 1
                nk = 128 if kt < 16 else 8
                ctb, b_ctb = ctbs[i], b_ctbs[i]
                if kt < 16:
                    fw.dma("sp", cts[i], cache_dil[sq, kt * 128:(kt + 1) * 128, :], w=[b_cts[i]])
                    fw.op("dve", lambda e, i=i, ctb=ctb: e.tensor_copy(ctb[:, 0:512], cts[i][:, 0:512]), r=[b_cts[i]], w=[b_ctb])
                    fw.op("pool", lambda e, i=i, ctb=ctb: e.tensor_copy(ctb[:, 512:1024], cts[i][:, 512:1024]), r=[b_cts[i]], w=[b_ctb])
                else:
                    fw.dma("sp", ctb[0:8, :], g_("s_sKV")[sq * 8:(sq + 1) * 8, :], r=[b_scr["sKV"]], w=[b_ctb])
                for c4 in range(4):
                    fw.op("pe", lambda e, c4=c4, ctb=ctb, nk=nk: e.transpose(pstb[:, c4 * 128:c4 * 128 + nk], ctb[0:nk, c4 * 128:(c4 + 1) * 128], ident_b[0:nk, 0:nk]), r=[b_ctb, b_const2], w=[self.bps[7]])
                ka, b_ka = KA[i], b_KA[i]
                pv = pstb[:, 0:512].rearrange("p (c t) -> p c t", c=4)
                fw.op("dve", lambda e, ka=ka, pv=pv, nk=nk: e.tensor_copy(ka[0:64, 0:8:2, 0:nk], pv[0:64, :, 0:nk]), r=[self.bps[7]], w=[b_ka])
                fw.op("dve", lambda e, ka=ka, pv=pv, nk=nk: e.tensor_copy(ka[0:64, 1:8:2, 0:nk], pv[64:128, :, 0:nk]), r=[self.bps[7]], w=[b_ka])
                fw.op("dve", lambda e, ka=ka, kt=kt: e.tensor_copy(ka[64:KR, :, :], kaug_t[64:KR, kt:kt + 1, :].to_broadcast([NAUG, 8, 128])), r=[b_cc], w=[b_ka])
                va, b_va = Va[i], b_Va[i]
                fw.op("act", lambda e, va=va, ctb=ctb, nk=nk: e.copy(va[0:nk, :, 0:64], ctb[0:nk, 512:1024].rearrange("p (h d) -> p h d", h=8)), r=[b_ctb], w=[b_va])
                Sb, bS = self.ps[i], self.bps[i]
                for h in range(8):
                    fw.op("pe", lambda e, Sb=Sb, h=h, ka=ka, nk=nk: e.matmul(Sb[0:nk, h * 8:(h + 1) * 8], lhsT=ka[:, h, 0:nk], rhs=QA[:, h, :], start=True, stop=True), r=[b_ka, b_QA], w=[bS])
                fw.op("act", lambda e, Sb=Sb, i=i, nk=nk: e.activation(out=Pf[i][0:nk, :], in_=Sb[0:nk, 0:64], func=AF.Exp, scale=0.125), r=[bS], w=[b_P[i]])
                fw.op("dve", lambda e, i=i, kt=kt, nk=nk: e.tensor_tensor(out=Pb[i][0:nk, :, :], in0=Pf[i][0:nk, :].rearrange("p (h q) -> p h q", h=8), in1=cnt_t[0:nk, kt:kt + 1, :].to_broadcast([nk, 8, 8]), op=ALU.mult), r=[b_P[i], b_cc], w=[b_P[i]])
                for h in range(8):
                    bank = PA if h < 4 else PB_
                    hh = h % 4
                    fw.op("pe", lambda e, bank=bank, hh=hh, h=h, i=i, va=va, nk=nk, kt=kt: e.matmul(self.ps[bank][0:8, hh * 65:(hh + 1) * 65], lhsT=Pb[i][0:nk, h, :], rhs=va[0:nk, h, :], start=(kt == 0 and hh == 0), stop=(kt == 16), skip_group_check=True), r=[b_P[i], b_va], w=[self.bps[bank]])
            for half in range(2):
                bank = PA if half == 0 else PB_
                ov = self.ps[bank][0:8, 0:260].rearrange("p (h x) -> p h x", h=4)
                fw.op("dve", lambda e, ov=ov, half=half: e.reciprocal(out=osm[0:8, half * 4:(half + 1) * 4], in_=ov[:, :, 64]), r=[self.bps[bank]], w=[b_o])
                fw.op("dve", lambda e, ov=ov, half=half: e.tensor_tensor(out=of[0:8, half * 256:(half + 1) * 256].rearrange("p (h d) -> p h d", h=4), in0=ov[:, :, 0:64], in1=osm[0:8, half * 4:(half + 1) * 4].rearrange("p (h o) -> p h o", o=1).to_broadcast([8, 4, 64]), op=ALU.mult), r=[self.bps[bank], b_o], w=[b_o])
            fw.op("act", lambda e: e.copy(obf[0:8, :], of[0:8, :]), r=[b_o], w=[b_o])
            for c4 in range(4):
                fw.op("pe", lambda e, c4=c4: e.transpose(pstb[:, 512 + c4 * 8:512 + (c4 + 1) * 8], obf[0:8, c4 * 128:(c4 + 1) * 128], ident_b[0:8, 0:8]), r=[b_o, b_const2], w=[self.bps[7]])
            fw.op("dve", lambda e: e.tensor_copy(oT, pstb[:, 512:544].rearrange("p (c q) -> p c q", c=4)), r=[self.bps[7]], w=[b_oT])
            fw.dma("act", g_("s_sOT0")[0:512, sq * 8:(sq + 1) * 8].rearrange("(c p) t -> p c t", p=128), oT, r=[b_oT], w=[b_scr["sOT0"]])
        self.sp_off = mark
        fw.barrier()
        self.phaseD_s(L, 0)
        if L["S"] >= 6:
            self.sample_nsa(L)
            self.phaseD_s(L, 1)

    def sample_nsa(self, L):
        fw = self.fw
        g_ = lambda n: L[n]
        b_scr = g_("b_scr")
        ident_b, ones_b, b_const, b_const2 = g_("ident_b"), g_("ones_b"), g_("b_const"), g_("b_const2")
        pstb = self.ps[7][:, :].bitcast(BF16)

        def pool_gather(out, pool, idx_col, r, w, dsem):
            o = Op("pool", (lambda e: e.indirect_dma_start(out=out, out_offset=None, in_=pool, in_offset=bass.IndirectOffsetOnAxis(ap=idx_col, axis=0))), None, isdma=True, dsem=dsem)
            o.deps = fw._deps(o.key, r, w)
            dsem.count += 16
            o.dval = dsem.count
            fw.ops["pool"].append(o)
            fw._commit(o, r, w)

        mark = self.sp_off
        pti = self.salloc(256, I32)
        ptf = self.salloc(256, F32)
        iot = self.salloc(1, F32)
        idx = self.salloc(256, I32)
        b_pt = fw.buf("pt", dma=True)
        b_idx = fw.buf("idx")
        fw.dma("sp", pti, g_("ptab")[0:1, :].to_broadcast([128, 256]), w=[b_pt])
        fw.dma("sp", iot, g_("c_iota_p"), w=[b_pt])
        fw.op("dve", lambda e: e.tensor_copy(ptf, pti), r=[b_pt], w=[b_idx])
        fw.op("dve", lambda e: e.tensor_scalar(out=ptf, in0=ptf, scalar1=128.0, scalar2=iot[:, 0:1], op0=ALU.mult, op1=ALU.add), r=[b_idx, b_pt], w=[b_idx])
        fw.op("dve", lambda e: e.tensor_copy(idx, ptf), r=[b_idx], w=[b_idx])
        cts = [self.salloc(512, F32) for _ in range(2)]
        b_cts = [fw.buf("gct%d" % i, dma=True) for i in range(2)]
        gsems = [fw.new_dsem(sw=True) for _ in range(2)]
        ctbs = [self.salloc(512, BF16) for _ in range(2)]
        b_ctbs = [fw.buf("gctb%d" % i, dma=True) for i in range(2)]
        xts = [self.salloc(512, BF16).rearrange("p (c t) -> p c t", c=4) for _ in range(2)]
        b_xts = [fw.buf("gxt%d" % i, dma=True) for i in range(2)]
        u = 0
        for sq in range(4):
            for (cname, pool, nch, ntile, newsrc, nmnew) in (("cmp", g_("pool_cmp"), 4, 65, g_("o_cmp_kv_s"), "ocmp"), ("sel", g_("pool_sel"), 2, 65, g_("o_sel_kv_s"), "osel"), ("win", None, 2, 5, g_("o_win_kv_s"), "owin")):
                for kt in range(ntile):
                    i = u % 2
                    u += 1
                    last = (kt == ntile - 1)
                    nk = 8 if last else 128
                    if last:
                        src = newsrc[sq * 8:(sq + 1) * 8, :] if cname != "win" else newsrc[sq, 504:512, :]
                        fw.dma("sp", cts[i][0:8, :], src, r=[b_scr[nmnew]], w=[b_cts[i]])
                    elif cname == "win":
                        fw.dma("sp", cts[i], g_("cache_win")[sq, kt * 128:(kt + 1) * 128, :], w=[b_cts[i]])
                    else:
                        pool_gather(cts[i], pool, idx[:, sq * 64 + kt:sq * 64 + kt + 1], [b_idx], [b_cts[i]], gsems[i])
                    fw.op("dve", lambda e, i=i, nk=nk: e.tensor_copy(ctbs[i][0:nk, :], cts[i][0:nk, :]), r=[b_cts[i]], w=[b_ctbs[i]])
                    for c4 in range(nch):
                        fw.op("pe", lambda e, c4=c4, i=i, nk=nk: e.transpose(pstb[:, c4 * 128:c4 * 128 + nk], ctbs[i][0:nk, c4 * 128:(c4 + 1) * 128], ident_b[0:nk, 0:nk]), r=[b_ctbs[i], b_const2], w=[self.bps[7]])
                    fw.op("act", lambda e, i=i, nk=nk, nch=nch: e.copy(xts[i][:, 0:nch, 0:nk], pstb[:, 0:512].rearrange("p (c t) -> p c t", c=4)[:, 0:nch, 0:nk]), r=[self.bps[7]], w=[b_xts[i]])
                    dT = {"cmp": g_("s_cT_cmp"), "sel": g_("s_cT_sel"), "win": g_("s_cT_win")}[cname]
                    fw.dma("act", dT[sq, :, kt * 128:kt * 128 + nk].rearrange("(c p) t -> p c t", p=128), xts[i][:, 0:nch, 0:nk], r=[b_xts[i]], w=[b_scr["cT_" + cname]])
                    if cname != "cmp":
                        dV = g_("s_cV_sel") if cname == "sel" else g_("s_cV_win")
                        fw.dma("act", dV[sq, kt * 128:kt * 128 + nk, :], ctbs[i][0:nk, 256:512], r=[b_ctbs[i]], w=[b_scr["cV_" + cname]])
        self.sp_off = mark
        fw.barrier()

        mark = self.sp_off
        wc_f = self.salloc(4096, F32, parts=64)
        wc_b = self.salloc(4096, BF16, parts=64).rearrange("p (l c e) -> p l c e", l=32, c=2)
        pe_f = self.salloc(64, F32, parts=64)
        pe_b = self.salloc(64, BF16, parts=64).rearrange("p (l c) -> p l c", c=2)
        e30s = self.salloc(8192, BF16)
        c2s_t = self.salloc(4 * 129, BF16).rearrange("p (t x) -> p t x", t=4)
        tsel_t = self.salloc(129, F32)
        cm8 = self.salloc(32, BF16)
        wm0 = self.salloc(32, BF16)
        b_fc = fw.buf("sfconst", dma=True)
        b_fc2 = fw.buf("sfconst2")
        fw.dma("sp", wc_f, g_("w_cmp_t"), w=[b_fc])
        fw.dma("sp", pe_f, g_("pe_cmp_t"), w=[b_fc])
        fw.dma("sp", e30s, g_("c_e30s"), w=[b_fc])
        fw.dma("sp", c2s_t, g_("c_c2s_s"), w=[b_fc])
        fw.dma("sp", tsel_t[0:8, :], g_("c_tsel_s"), w=[b_fc])
        fw.dma("sp", cm8[0:8, :], g_("c_cm8"), w=[b_fc])
        fw.dma("sp", wm0, g_("c_wm0"), w=[b_fc])
        fw.op("dve", lambda e: e.tensor_copy(wc_b.rearrange("p l c e -> p (l c e)"), wc_f), r=[b_fc], w=[b_fc2])
        fw.op("dve", lambda e: e.tensor_copy(pe_b.rearrange("p l c -> p (l c)"), pe_f), r=[b_fc], w=[b_fc2])
        XK = self.salloc(8192, BF16, parts=64)
        XV = self.salloc(8192, BF16, parts=64)
        b_XK = fw.buf("sXK", dma=True)
        b_XV = fw.buf("sXV", dma=True)
        Ksa = self.salloc(8320, BF16, parts=KR)
        Kwa = self.salloc(640, BF16, parts=KR)
        Kca = self.salloc(512, BF16, parts=KR)
        b_Ksa, b_Kwa, b_Kca = fw.buf("sKsa", dma=True), fw.buf("sKwa", dma=True), fw.buf("sKca", dma=True)
        fw.dma("sp", Ksa[64:KR, :], g_("c_kaug_ss"), w=[b_Ksa])
        fw.dma("sp", Kwa[64:KR, :], g_("c_kaug_sw"), w=[b_Kwa])
        fw.dma("sp", Kca[64:KR, :], g_("c_kaugc_s"), w=[b_Kca])
        VSa = self.salloc(65 * 4 * 65, BF16).rearrange("p (k g d) -> p k g d", k=65, g=4)
        VWa = self.salloc(5 * 4 * 65, BF16).rearrange("p (k g d) -> p k g d", k=5, g=4)
        b_VSa, b_VWa = fw.buf("sVSa", dma=True), fw.buf("sVWa", dma=True)
        fw.op("dve", lambda e: e.memset(VSa[:, :, :, 64:65], 1.0), w=[b_VSa])
        fw.op("dve", lambda e: e.memset(VWa[:, :, :, 64:65], 1.0), w=[b_VWa])
        VCX = self.salloc(4 * 194, BF16).rearrange("p (t x) -> p t x", t=4)
        b_VCX = fw.buf("sVCX")
        fw.op("dve", lambda e: e.memset(VCX[:, :, 64:65], 1.0), w=[b_VCX])
        fw.op("dve", lambda e: e.tensor_copy(VCX[:, :, 65:194], c2s_t), r=[b_fc], w=[b_VCX])
        vcT = self.salloc(512, BF16, parts=64)
        pekv = self.salloc(4, F32, parts=64)
        b_pe = fw.buf("spekv")
        QA = self.salloc(32, BF16, parts=KR).rearrange("p (r q) -> p r q", r=4)
        b_QA = fw.buf("sQA1", dma=True)
        Pc = self.salloc(128, BF16).rearrange("p (t x) -> p t x", t=4)
        b_Pc = fw.buf("sPc")
        Pss = [self.salloc(32, BF16) for _ in range(3)]
        b_Pss = [fw.buf("sPs%d" % i) for i in range(3)]
        sc = self.salloc(129, F32)
        sc2 = self.salloc(129, F32)
        m8 = self.salloc(16, F32)
        small = self.salloc(64, F32)
        madd = self.salloc(128, BF16)
        msel = self.salloc(32, BF16).rearrange("p (r q) -> p r q", r=4)
        b_sel, b_msel = fw.buf("sseltmp"), fw.buf("smsel")
        gt = self.salloc(48, F32)
        b_gt = fw.buf("sgt", dma=True)
        otile = self.salloc(256, F32)
        otb = self.salloc(256, BF16)
        b_ot_ = fw.buf("sotile")
        ot1 = self.salloc(16, BF16).rearrange("p (c t) -> p c t", c=2)
        b_ot1 = fw.buf("sot1", dma=True)
        PS_S = (0, 1)
        PS_CA, PS_CB, PS_OS, PS_OW, PS_X, PS_T = 2, 3, 4, 5, 6, 7
        NROWS = (128, 128, 128, 127)
        ucnt = 0
        for sq in range(4):
            for kt in range(65):
                nk = 128 if kt < 64 else 8
                fw.dma("sp", VSa[0:nk, kt, :, 0:64], g_("s_cV_sel")[sq, kt * 128:kt * 128 + nk, :].rearrange("p (g d) -> p g d", g=4), r=[b_scr["cV_sel"]], w=[b_VSa])
            for kt in range(5):
                nk = 128 if kt < 4 else 8
                fw.dma("sp", VWa[0:nk, kt, :, 0:64], g_("s_cV_win")[sq, kt * 128:kt * 128 + nk, :].rearrange("p (g d) -> p g d", g=4), r=[b_scr["cV_win"]], w=[b_VWa])
            fw.dma("sp", gt[0:8, :], g_("s_sG")[sq * 8:(sq + 1) * 8, :], r=[b_scr["sG"]], w=[b_gt])
            for g in range(4):
                fw.dma("sp", XK, g_("s_cT_cmp")[sq, g * 64:(g + 1) * 64, 0:8192], r=[b_scr["cT_cmp"]], w=[b_XK])
                fw.dma("sp", XV, g_("s_cT_cmp")[sq, 256 + g * 64:256 + (g + 1) * 64, 0:8192], r=[b_scr["cT_cmp"]], w=[b_XV])
                fw.dma("sp", Ksa[0:64, 0:8200], g_("s_cT_sel")[sq, g * 64:(g + 1) * 64, 0:8200], r=[b_scr["cT_sel"]], w=[b_Ksa])
                fw.dma("sp", Kwa[0:64, 0:520], g_("s_cT_win")[sq, g * 64:(g + 1) * 64, 0:520], r=[b_scr["cT_win"]], w=[b_Kwa])
                fw.dma("sp", QA[0:64, :, :], g_("s_sQT1")[g * 256:(g + 1) * 256, sq * 8:(sq + 1) * 8].rearrange("(r d) t -> d r t", d=64), r=[b_scr["sQT1"]], w=[b_QA])
                fw.dma("sp", QA[64:KR, :, :], g_("c_qaug1_s")[g], w=[b_QA])
                QA32 = QA.rearrange("p r q -> p (r q)")
                for c_ in range(2):
                    for l in range(32):
                        fw.op("pe", lambda e, l=l, c_=c_: e.matmul(self.ps[PS_X][0:64, c_:c_ + 1], lhsT=wc_b[:, l, c_, :], rhs=pe_b[:, l, c_:c_ + 1], start=(l == 0), stop=(l == 31)), r=[b_fc2], w=[self.bps[PS_X]])
                fw.op("dve", lambda e: e.tensor_copy(pekv[:, 0:2], self.ps[PS_X][0:64, 0:2]), r=[self.bps[PS_X]], w=[b_pe])
                for c_, (X, bX, bank) in enumerate(((XK, b_XK, PS_X), (XV, b_XV, PS_T))):
                    for l in range(32):
                        fw.op("pe", lambda e, l=l, c_=c_, X=X, bank=bank: e.matmul(self.ps[bank][0:64, 0:511], lhsT=wc_b[:, l, c_, :], rhs=X[:, l:l + 16 * 510 + 1:16], start=(l == 0), stop=(l == 31)), r=[b_fc2, bX], w=[self.bps[bank]])
                fw.op("dve", lambda e: e.memset(Kca[0:64, 511:512], 0.0), w=[b_Kca])
                fw.op("dve", lambda e: e.tensor_scalar(out=Kca[0:64, 0:511], in0=self.ps[PS_X][0:64, 0:511], scalar1=pekv[:, 0:1], scalar2=None, op0=ALU.add), r=[self.bps[PS_X], b_pe], w=[b_Kca])
                fw.op("dve", lambda e: e.memset(vcT[:, 511:512], 0.0), w=[b_pe])
                fw.op("dve", lambda e: e.tensor_scalar(out=vcT[:, 0:511], in0=self.ps[PS_T][0:64, 0:511], scalar1=pekv[:, 1:2], scalar2=None, op0=ALU.add), r=[self.bps[PS_T], b_pe], w=[b_pe])
                for nt_ in range(4):
                    fw.op("pe", lambda e, nt_=nt_: e.transpose(pstb[:, nt_ * 64:(nt_ + 1) * 64], vcT[:, nt_ * 128:(nt_ + 1) * 128], ident_b[0:64, 0:64]), r=[b_pe, b_const2], w=[self.bps[PS_T]])
                fw.op("dve", lambda e: e.tensor_copy(VCX[:, :, 0:64], pstb[:, 0:256].rearrange("p (t d) -> p t d", t=4)), r=[self.bps[PS_T]], w=[b_VCX])
                Sb, bS = self.ps[PS_S[0]], self.bps[PS_S[0]]
                for nt_ in range(4):
                    nr = NROWS[nt_]
                    fw.op("pe", lambda e, nt_=nt_, nr=nr, Sb=Sb: e.matmul(Sb[0:nr, nt_ * 32:(nt_ + 1) * 32], lhsT=Kca[:, nt_ * 128:nt_ * 128 + nr], rhs=QA32, start=True, stop=True), r=[b_Kca, b_QA], w=[bS])
                fw.op("act", lambda e, Sb=Sb: e.activation(out=Pc[:, 0:3, :], in_=Sb[:, 0:96].rearrange("p (t x) -> p t x", t=3), func=AF.Exp, scale=0.125), r=[bS], w=[b_Pc])
                fw.op("act", lambda e, Sb=Sb: e.activation(out=Pc[0:127, 3, :], in_=Sb[0:127, 96:128], func=AF.Exp, scale=0.125), r=[bS], w=[b_Pc])
                for r_ in range(4):
                    bank = PS_CA if r_ < 2 else PS_CB
                    c0 = (r_ % 2) * 194
                    for nt_ in range(4):
                        nr = NROWS[nt_]
                        fw.op("pe", lambda e, bank=bank, c0=c0, nt_=nt_, nr=nr, r_=r_: e.matmul(self.ps[bank][0:8, c0:c0 + 194], lhsT=Pc[0:nr, nt_, r_ * 8:(r_ + 1) * 8], rhs=VCX[0:nr, nt_, :], start=(nt_ == 0), stop=(nt_ == 3)), r=[b_Pc, b_VCX], w=[self.bps[bank]])
                for r_ in range(4):
                    bank = PS_CA if r_ < 2 else PS_CB
                    c0 = (r_ % 2) * 194
                    fw.op("dve", lambda e, bank=bank, c0=c0, r_=r_: e.tensor_scalar(out=small[0:8, r_:r_ + 1], in0=self.ps[bank][0:8, c0 + 64:c0 + 65], scalar1=1e-30, scalar2=None, op0=ALU.max), r=[self.bps[bank]], w=[b_sel])
                fw.op("dve", lambda e: e.reciprocal(out=small[0:8, 16:20], in_=small[0:8, 0:4]), r=[b_sel], w=[b_sel])
                for r_ in range(4):
                    bank = PS_CA if r_ < 2 else PS_CB
                    c0 = (r_ % 2) * 194
                    if r_ == 0:
                        fw.op("dve", lambda e, bank=bank, c0=c0: e.tensor_scalar(out=sc[0:8, :], in0=self.ps[bank][0:8, c0 + 65:c0 + 194], scalar1=small[0:8, 16:17], scalar2=None, op0=ALU.mult), r=[self.bps[bank], b_sel], w=[b_sel])
                    else:
                        fw.op("dve", lambda e, bank=bank, c0=c0, r_=r_: e.scalar_tensor_tensor(out=sc[0:8, :], in0=self.ps[bank][0:8, c0 + 65:c0 + 194], scalar=small[0:8, 16 + r_:17 + r_], in1=sc[0:8, :], op0=ALU.mult, op1=ALU.add), r=[self.bps[bank], b_sel], w=[b_sel])
                fw.op("dve", lambda e: e.tensor_tensor(out=sc[0:8, :], in0=sc[0:8, :], in1=tsel_t[0:8, :], op=ALU.add), r=[b_sel, b_fc], w=[b_sel])
                fw.op("dve", lambda e: e.max(out=m8[0:8, 0:8], in_=sc[0:8, :]), r=[b_sel], w=[b_sel])
                fw.op("dve", lambda e: e.match_replace(out=sc2[0:8, :], in_to_replace=m8[0:8, 0:8], in_values=sc[0:8, :], imm_value=-1e30), r=[b_sel], w=[b_sel])
                fw.op("dve", lambda e: e.max(out=m8[0:8, 8:16], in_=sc2[0:8, :]), r=[b_sel], w=[b_sel])
                fw.op("dve", lambda e: e.tensor_scalar(out=madd[0:8, :], in0=sc[0:8, 0:128], scalar1=m8[0:8, 15:16], scalar2=1.0, op0=ALU.is_ge, op1=ALU.subtract), r=[b_sel], w=[b_sel])
                fw.op("pe", lambda e: e.transpose(pstb[:, 512:520], madd[0:8, :], ident_b[0:8, 0:8]), r=[b_sel, b_const2], w=[self.bps[PS_T]])
                fw.op("dve", lambda e: e.tensor_copy(msel, pstb[:, 512:520].rearrange("p (o q) -> p o q", o=1).to_broadcast([128, 4, 8])), r=[self.bps[PS_T]], w=[b_msel])
                msel32 = msel.rearrange("p r q -> p (r q)")
                for (br, Ka, b_Ka, Vt, bV, bank, ntile) in ((1, Ksa, b_Ksa, VSa, b_VSa, PS_OS, 65), (2, Kwa, b_Kwa, VWa, b_VWa, PS_OW, 5)):
                    for kt in range(ntile):
                        si = ucnt % 2
                        ucnt += 1
                        last = (kt == ntile - 1)
                        nk = 8 if last else 128
                        Sb, bS = self.ps[PS_S[si]], self.bps[PS_S[si]]
                        extra = []
                        if last:
                            extra.append((ident_b[0:8, 0:8], cm8[0:8, :], [b_const2, b_fc]))
                        elif br == 1:
                            extra.append((e30s[:, kt * 128:(kt + 1) * 128], msel32, [b_fc, b_msel]))
                        elif kt == 0:
                            extra.append((ident_b, wm0, [b_const2, b_fc]))
                        fw.op("pe", lambda e, Sb=Sb, Ka=Ka, kt=kt, nk=nk, ne=len(extra): e.matmul(Sb[0:nk, 0:32], lhsT=Ka[:, kt * 128:kt * 128 + nk], rhs=QA32, start=True, stop=(ne == 0)), r=[b_Ka, b_QA], w=[bS])
                        for (lh, rh, rb) in extra:
                            fw.op("pe", lambda e, Sb=Sb, lh=lh, rh=rh, nk=nk: e.matmul(Sb[0:nk, 0:32], lhsT=lh, rhs=rh, start=False, stop=True), r=rb, w=[bS])
                        pi_ = ucnt % 3
                        pp_, b_pp = Pss[pi_], b_Pss[pi_]
                        fw.op("act", lambda e, Sb=Sb, pp_=pp_, nk=nk: e.activation(out=pp_[0:nk, :], in_=Sb[0:nk, 0:32], func=AF.Exp, scale=0.125), r=[bS], w=[b_pp])
                        for r_ in range(4):
                            fw.op("pe", lambda e, bank=bank, r_=r_, pp_=pp_, Vt=Vt, kt=kt, g=g, nk=nk, ntile=ntile: e.matmul(self.ps[bank][0:8, r_ * 65:(r_ + 1) * 65], lhsT=pp_[0:nk, r_ * 8:(r_ + 1) * 8], rhs=Vt[0:nk, kt, g, :], start=(kt == 0 and r_ == 0), stop=(kt == ntile - 1), skip_group_check=True), r=[b_pp, bV], w=[self.bps[bank]])
                    fw.op("dve", lambda e, bank=bank, br=br: e.tensor_scalar(out=small[0:8, br * 4:(br + 1) * 4], in0=self.ps[bank][0:8, 0:260].rearrange("p (r x) -> p r x", r=4)[:, :, 64], scalar1=1e-30, scalar2=None, op0=ALU.max), r=[self.bps[bank]], w=[b_sel])
                fw.op("dve", lambda e: e.reciprocal(out=small[0:8, 16:28], in_=small[0:8, 0:12]), r=[b_sel], w=[b_sel])
                fw.op("dve", lambda e, g=g: e.tensor_tensor(out=small[0:8, 32:44].rearrange("p (b r) -> p b r", b=3), in0=small[0:8, 16:28].rearrange("p (b r) -> p b r", b=3), in1=gt[0:8, g * 12:(g + 1) * 12].rearrange("p (r b) -> p b r", b=3), op=ALU.mult), r=[b_sel, b_gt], w=[b_sel])
                for r_ in range(4):
                    bankc = PS_CA if r_ < 2 else PS_CB
                    c0 = (r_ % 2) * 194
                    od = otile[0:8, r_ * 64:(r_ + 1) * 64]
                    fw.op("dve", lambda e, bankc=bankc, c0=c0, od=od, r_=r_: e.tensor_scalar(out=od, in0=self.ps[bankc][0:8, c0:c0 + 64], scalar1=small[0:8, 32 + r_:33 + r_], scalar2=None, op0=ALU.mult), r=[self.bps[bankc], b_sel], w=[b_ot_])
                    fw.op("dve", lambda e, od=od, r_=r_: e.scalar_tensor_tensor(out=od, in0=self.ps[PS_OS][0:8, r_ * 65:r_ * 65 + 64], scalar=small[0:8, 36 + r_:37 + r_], in1=od, op0=ALU.mult, op1=ALU.add), r=[self.bps[PS_OS], b_sel, b_ot_], w=[b_ot_])
                    fw.op("dve", lambda e, od=od, r_=r_: e.scalar_tensor_tensor(out=od, in0=self.ps[PS_OW][0:8, r_ * 65:r_ * 65 + 64], scalar=small[0:8, 40 + r_:41 + r_], in1=od, op0=ALU.mult, op1=ALU.add), r=[self.bps[PS_OW], b_sel, b_ot_], w=[b_ot_])
                fw.op("act", lambda e: e.copy(otb[0:8, :], otile[0:8, :]), r=[b_ot_], w=[b_ot_])
                for c_ in range(2):
                    fw.op("pe", lambda e, c_=c_: e.transpose(pstb[:, 528 + c_ * 8:528 + (c_ + 1) * 8], otb[0:8, c_ * 128:(c_ + 1) * 128], ident_b[0:8, 0:8]), r=[b_ot_, b_const2], w=[self.bps[PS_T]])
                fw.op("dve", lambda e: e.tensor_copy(ot1, pstb[:, 528:544].rearrange("p (c t) -> p c t", c=2)), r=[self.bps[PS_T]], w=[b_ot1])
                fw.dma("act", g_("s_sOT1")[g * 256:(g + 1) * 256, sq * 8:(sq + 1) * 8].rearrange("(c p) t -> p c t", p=128), ot1, r=[b_ot1], w=[b_scr["sOT1"]])
        self.sp_off = mark
        fw.barrier()

    def phaseD_s(self, L, l):
        fw = self.fw
        g_ = lambda n: L[n]
        b_scr, load_w, rmsnorm_T = g_("b_scr"), g_("load_w"), g_("rmsnorm_T")
        NT = 32
        mark = self.sp_off
        s_OT = g_("s_sOT0") if l == 0 else g_("s_sOT1")
        nm_OT = "sOT0" if l == 0 else "sOT1"
        wout, b_wout = load_w(g_("wb_out_ab") if l == 0 else g_("wb_out_c"), D, 4, "wout_s")
        w2t, b_w2 = load_w(g_("wb_w2")[l], FFN, 4, "w2_s")
        ot = self.salloc(8 * NT, BF16).rearrange("p (kc t) -> p kc t", kc=8)
        b_ot = fw.buf("ot_s", dma=True)
        x1 = self.salloc(D, F32)
        b_x1 = fw.buf("x1_s", dma=True)
        tmp = (self.salloc(D, F32), self.salloc(4, F32), self.salloc(D, BF16), fw.buf("nrm_tmp_s2"))
        hT = self.salloc(8 * NT, BF16).rearrange("p (kc t) -> p kc t", kc=8)
        b_hT = fw.buf("hT_s2")
        gT = self.salloc(22 * NT, BF16).rearrange("p (hc t) -> p hc t", hc=22)
        b_gT = fw.buf("gT_s")
        wst = [self.salloc(2 * 8 * 256, BF16).rearrange("p (w kc n) -> p w kc n", w=2, kc=8) for _ in range(2)]
        b_wst = [fw.buf("wst_s%d" % i, dma=True) for i in range(2)]
        stmp = [self.salloc(NT, F32) for _ in range(2)]
        b_stmp = [fw.buf("stmp_s%d" % i) for i in range(2)]
        tm_f = [self.salloc(512, F32) for _ in range(2)]
        b_tm_f = [fw.buf("tmf_s%d" % i, dma=True) for i in range(2)]
        fw.dma("sp", ot, s_OT.rearrange("(kc p) t -> p kc t", p=128), r=[b_scr[nm_OT]], w=[b_ot])
        if l == 0:
            fw.dma("sp", x1[0:NT, :], g_("xs"), w=[b_x1])
        else:
            fw.dma("sp", x1[0:NT, :], g_("s_sX1"), r=[b_scr["sX1"]], w=[b_x1])
        for half in range(2):
            pi = half
            for q2 in range(2):
                nb = half * 2 + q2
                for kc in range(8):
                    fw.op("pe", lambda e, kc=kc, nb=nb, q2=q2, pi=pi: e.matmul(self.ps[pi][0:NT, q2 * 256:(q2 + 1) * 256], lhsT=ot[:, kc, :], rhs=wout[:, nb, kc, :], start=(kc == 0), stop=(kc == 7)), r=[b_ot, b_wout], w=[self.bps[pi]])
            fw.op("dve", lambda e, pi=pi, half=half: e.tensor_tensor(out=x1[0:NT, half * 512:(half + 1) * 512], in0=self.ps[pi][0:NT, :], in1=x1[0:NT, half * 512:(half + 1) * 512], op=ALU.add), r=[self.bps[pi], b_x1], w=[b_x1])
        rmsnorm_T(x1, b_x1, NT, 1 + 2 * l, hT, b_hT, 0, tmp)
        wcnt = 0
        for nb in range(11):
            wi = wcnt % 2
            wcnt += 1
            fw.dma("sp", wst[wi][:, 0], g_("wb_w1")[l][nb].rearrange("p (kc n) -> p kc n", kc=8), r=[b_scr["w"]], w=[b_wst[wi]])
            fw.dma("sp", wst[wi][:, 1], g_("wb_w3")[l][nb].rearrange("p (kc n) -> p kc n", kc=8), r=[b_scr["w"]], w=[b_wst[wi]])
            for sub in range(2):
                hc = nb * 2 + sub
                p1, p3 = 2 + (hc % 2) * 2, 3 + (hc % 2) * 2
                for (pp, wsel) in ((p1, 0), (p3, 1)):
                    for kc in range(8):
                        fw.op("pe", lambda e, kc=kc, pp=pp, wsel=wsel, wi=wi, sub=sub: e.matmul(self.ps[pp][:, 0:NT], lhsT=wst[wi][:, wsel, kc, sub * 128:(sub + 1) * 128], rhs=hT[:, kc, :], start=(kc == 0), stop=(kc == 7)), r=[b_wst[wi], b_hT], w=[self.bps[pp]])
                si = hc % 2
                fw.op("act", lambda e, si=si, p1=p1: e.activation(out=stmp[si], in_=self.ps[p1][:, 0:NT], func=AF.Silu), r=[self.bps[p1]], w=[b_stmp[si]])
                fw.op("dve", lambda e, si=si, p3=p3, hc=hc: e.tensor_tensor(out=gT[:, hc, :], in0=self.ps[p3][:, 0:NT], in1=stmp[si], op=ALU.mult), r=[self.bps[p3], b_stmp[si]], w=[b_gT])
        for half in range(2):
            pi = half
            for q2 in range(2):
                nb = half * 2 + q2
                for hc in range(22):
                    fw.op("pe", lambda e, hc=hc, nb=nb, q2=q2, pi=pi: e.matmul(self.ps[pi][0:NT, q2 * 256:(q2 + 1) * 256], lhsT=gT[:, hc, :], rhs=w2t[:, nb, hc, :], start=(hc == 0), stop=(hc == 21)), r=[b_gT, b_w2], w=[self.bps[pi]])
            fw.op("dve", lambda e, pi=pi, half=half: e.tensor_tensor(out=x1[0:NT, half * 512:(half + 1) * 512], in0=self.ps[pi][0:NT, :], in1=x1[0:NT, half * 512:(half + 1) * 512], op=ALU.add), r=[self.bps[pi], b_x1], w=[b_x1])
        if l == 1:
            gout = self.salloc(D, F32)
            b_gout = fw.buf("gout_s", dma=True)
            fw.dma("sp", gout[0:NT, :], g_("norm_out").partition_broadcast(NT), w=[b_gout])
            sq_, ss, xn, bt = tmp
            yt = self.salloc(D, F32)
            b_yt = fw.buf("yt_s", dma=True)
            fw.op("act", lambda e: e.activation(out=sq_[0:NT, :], in_=x1[0:NT, :], func=AF.Square, accum_out=ss[0:NT, 0:1]), r=[b_x1], w=[bt])
            fw.op("dve", lambda e: e.tensor_scalar(out=ss[0:NT, 1:2], in0=ss[0:NT, 0:1], scalar1=1.0 / D, scalar2=EPS, op0=ALU.mult, op1=ALU.add), r=[bt], w=[bt])
            fw.op("act", lambda e: e.activation(out=ss[0:NT, 2:3], in_=ss[0:NT, 1:2], func=AF.Sqrt), r=[bt], w=[bt])
            fw.op("dve", lambda e: e.reciprocal(out=ss[0:NT, 3:4], in_=ss[0:NT, 2:3]), r=[bt], w=[bt])
            fw.op("dve", lambda e: e.scalar_tensor_tensor(out=yt[0:NT, :], in0=x1[0:NT, :], scalar=ss[0:NT, 3:4], in1=gout[0:NT, :], op0=ALU.mult, op1=ALU.mult), r=[bt, b_x1, b_gout], w=[b_yt])
            fw.dma("act", g_("o_y_s"), yt[0:NT, :], r=[b_yt])
            self.sp_off = mark
            fw.barrier()
            return
        fw.dma("act", g_("s_sX1"), x1[0:NT, :], r=[b_x1], w=[b_scr["sX1"]])
        rmsnorm_T(x1, b_x1, NT, 2, hT, b_hT, 0, tmp)
        stq = [self.salloc(NT, BF16) for _ in range(2)]
        b_stq = [fw.buf("stq%d" % i, dma=True) for i in range(2)]
        for nb in range(4):
            wi = wcnt % 2
            wcnt += 1
            fw.dma("sp", wst[wi][:, 0], g_("wb_in_c")[nb].rearrange("p (kc n) -> p kc n", kc=8), r=[b_scr["w"]], w=[b_wst[wi]])
            for sub in range(2):
                oc = nb * 2 + sub
                pp = 4 + oc % 2
                for kc in range(8):
                    fw.op("pe", lambda e, kc=kc, pp=pp, wi=wi, sub=sub: e.matmul(self.ps[pp][:, 0:NT], lhsT=wst[wi][:, 0, kc, sub * 128:(sub + 1) * 128], rhs=hT[:, kc, :], start=(kc == 0), stop=(kc == 7)), r=[b_wst[wi], b_hT], w=[self.bps[pp]])
                si = oc % 2
                fw.op("act", lambda e, si=si, pp=pp: e.copy(stq[si], self.ps[pp][:, 0:NT]), r=[self.bps[pp]], w=[b_stq[si]])
                fw.dma("act", g_("s_sQT1")[oc * 128:(oc + 1) * 128, :], stq[si], r=[b_stq[si]], w=[b_scr["sQT1"]])
        wi = wcnt % 2
        wcnt += 1
        fw.dma("sp", wst[wi][:, 0], g_("wb_in_c_tm")[6].rearrange("p (kc n) -> p kc n", kc=8), r=[b_scr["w"]], w=[b_wst[wi]])
        for kc in range(8):
            fw.op("pe", lambda e, kc=kc, wi=wi: e.matmul(self.ps[4][0:NT, 0:256], lhsT=hT[:, kc, :], rhs=wst[wi][:, 0, kc, :], start=(kc == 0), stop=(kc == 7)), r=[b_wst[wi], b_hT], w=[self.bps[4]])
        fw.op("act", lambda e: e.activation(out=tm_f[0][0:NT, 0:48], in_=self.ps[4][0:NT, 0:48], func=AF.Sigmoid), r=[self.bps[4]], w=[b_tm_f[0]])
        fw.dma("act", g_("s_sG"), tm_f[0][0:NT, 0:48], r=[b_tm_f[0]], w=[b_scr["sG"]])
        o_win_kv_s, cache_win = g_("o_win_kv_s"), g_("cache_win")
        b_d2d = fw.buf("d2d_s", dma=True)
        for sq in range(4):
            fw.dma("sp", o_win_kv_s[sq, 0:504, :], cache_win[sq, 8:512, :], w=[b_d2d])
        for nb2 in range(3):
            wi = wcnt % 2
            wcnt += 1
            for q2 in range(2):
                fw.dma("sp", wst[wi][:, q2], g_("wb_in_c_tm")[nb2 * 2 + q2].rearrange("p (kc n) -> p kc n", kc=8), r=[b_scr["w"]], w=[b_wst[wi]])
            pp = 2 + nb2 % 2
            for q2 in range(2):
                for kc in range(8):
                    fw.op("pe", lambda e, kc=kc, pp=pp, wi=wi, q2=q2: e.matmul(self.ps[pp][0:NT, q2 * 256:(q2 + 1) * 256], lhsT=hT[:, kc, :], rhs=wst[wi][:, q2, kc, :], start=(kc == 0), stop=(kc == 7)), r=[b_wst[wi], b_hT], w=[self.bps[pp]])
            fi = nb2 % 2
            fw.op("act", lambda e, fi=fi, pp=pp: e.copy(tm_f[fi][0:NT, :], self.ps[pp][0:NT, :]), r=[self.bps[pp]], w=[b_tm_f[fi]])
            if nb2 == 0:
                fw.dma("act", g_("o_cmp_kv_s"), tm_f[fi][0:NT, :], r=[b_tm_f[fi]], w=[b_scr["ocmp"]])
            elif nb2 == 1:
                fw.dma("act", g_("o_sel_kv_s"), tm_f[fi][0:NT, :], r=[b_tm_f[fi]], w=[b_scr["osel"]])
            else:
                for sq in range(4):
                    fw.dma("act", o_win_kv_s[sq, 504:512, :], tm_f[fi][sq * 8:(sq + 1) * 8, :], r=[b_tm_f[fi]], w=[b_scr["owin"]])
        self.sp_off = mark
        fw.barrier()


def make_inputs(inputs, core):
    f = lambda a: np.ascontiguousarray(a, dtype=np.float32)
    inputs = {k: np.asarray(v) for k, v in inputs.items()}
    m = {}
    m["xp"] = f(inputs["x_prompt"][core % 4])
    m["w_in_ab"] = f(inputs["w_in_ab"][0])
    m["w_out_ab"] = f(inputs["w_out_ab"][0])
    m["w_out_c"] = f(inputs["w_out_c"][0])
    m["ffn_w1"] = f(inputs["ffn_w1"])
    m["ffn_w3"] = f(inputs["ffn_w3"])
    m["ffn_w2"] = f(inputs["ffn_w2"])
    wc = np.asarray(inputs["w_in_c"][0], np.float32)
    m["w_in_c"] = f(np.concatenate([wc[:, 0:1536], wc[:, 1536:1792], wc[:, 2048:2304]], axis=1))
    wt = np.zeros((D, 1792), np.float32)
    wt[:, 0:1536] = wc[:, 1024:2560]
    wt[:, 1536:1584] = wc[:, 2560:2608]
    m["w_in_c_tm"] = f(wt)
    m["norm_out"] = f(inputs["norm_out"])
    g = np.stack([inputs["norm_mix"][0], inputs["norm_ffn"][0], inputs["norm_mix"][1], inputs["norm_ffn"][1], inputs["norm_out"]])
    m["gains"] = f(np.asarray(g, np.float32).reshape(5, 8, 128).transpose(2, 0, 1).reshape(128, 40))
    rp = np.zeros((9, 512), np.float32)
    rp[0:4] = inputs["conv_w"][0]
    rp[4] = inputs["conv_b"][0]
    rp[5] = inputs["gate_a_b"][0]
    rp[6] = inputs["gate_x_b"][0]
    rp[7] = inputs["lru_lambda"][0]
    m["rnnp"] = f(rp.reshape(9, 4, 128).transpose(2, 1, 0).reshape(128, 36))
    m["gate_a_w"] = f(inputs["gate_a_w"][0])
    m["gate_x_w"] = f(inputs["gate_x_w"][0])
    sl = slice(4 * core, 4 * core + 4)
    m["xs"] = f(inputs["x_sample"][sl].reshape(32, D))
    m["cache_dil"] = f(inputs["cache_dil_kv"][0, sl].reshape(4, 2048, 1024))
    m["st_conv"] = f(np.asarray(inputs["state_conv"][0, sl], np.float32).reshape(4, 3, 4, 128).transpose(3, 2, 0, 1).reshape(128, 48))
    m["st_rnn"] = f(np.asarray(inputs["state_rnn"][0, sl], np.float32).reshape(4, 4, 128).transpose(2, 1, 0).reshape(128, 16))
    m["cache_win"] = f(inputs["cache_win_kv"][0, sl].reshape(4, 512, 512))
    m["pool_cmp"] = np.asarray(inputs["cache_cmp_kv"][0], np.float32).reshape(-1, 512)
    m["pool_sel"] = np.asarray(inputs["cache_sel_kv"][0], np.float32).reshape(-1, 512)
    m["ptab"] = np.ascontiguousarray(np.asarray(inputs["page_table"][sl], np.int32).reshape(1, 256))
    m["w_cmp_t"] = f(np.asarray(inputs["w_cmp"][0], np.float32).transpose(2, 0, 1, 3).reshape(64, 4096))
    m["pe_cmp_t"] = f(np.asarray(inputs["pe_cmp"][0], np.float32).transpose(2, 0, 1).reshape(64, 64))
    for k, v in host_consts().items():
        m[k] = v
    return m


_CACHE = {}


def run(inputs, stage):
    if stage not in _CACHE:
        kb = KB(stage)
        kb.build()
        _CACHE[stage] = kb
    kb = _CACHE[stage]
    in_maps = []
    for c in range(8):
        m = make_inputs(inputs, c)
        in_maps.append({k: m[k] for k in kb.ins})
    res = run_bass_kernel_spmd(kb.nc, in_maps, core_ids=list(range(8)))
    return kb, res


STAGE = 6


def kernel(**inputs):
    inputs = {k: np.asarray(v) for k, v in inputs.items()}
    kb, res = run(inputs, STAGE)
    R = res.results
    z = lambda *sh: np.zeros(sh, np.float32)

    def getp(name, shape):
        if name in R[0]:
            return np.stack([np.asarray(R[b][name], np.float32).reshape(shape) for b in range(4)])
        return np.zeros((4,) + tuple(shape), np.float32)
    y_prompt = getp("y_p", (SEQ, D))
    def gets(name, shape):
        if name in R[0]:
            return np.concatenate([np.asarray(R[c][name], np.float32).reshape(shape) for c in range(8)], axis=0)
        return None
    y_sample = gets("y_s", (4, 8, D))
    dil_kv_p = getp("dil_kv_p", (2048, 2, 8, 64))[None]
    dil_kv_s = gets("dil_kv_s", (4, 2048, 2, 8, 64))[None]
    conv_p = getp("conv_p", (3, 512))[None]
    conv_s = gets("conv_s", (4, 3, 512))[None]
    rnn_p = getp("rnn_p", (512,))[None]
    rnn_s = gets("rnn_s", (4, 512))[None]
    win_kv_p = getp("win_kv_p", (512, 2, 4, 64))[None]
    win_kv_s = gets("win_kv_s", (4, 512, 2, 4, 64))[None]
    cmp_kv_p = getp("cmp_kv_p", (SEQ, 2, 4, 64))[None]
    cmp_kv_s = gets("cmp_kv_s", (4, 8, 2, 4, 64))[None]
    sel_kv_p = getp("sel_kv_p", (SEQ, 2, 4, 64))[None]
    sel_kv_s = gets("sel_kv_s", (4, 8, 2, 4, 64))[None]
    return (y_prompt, y_sample, dil_kv_p, dil_kv_s, conv_p, conv_s, rnn_p, rnn_s, win_kv_p, win_kv_s, cmp_kv_p, cmp_kv_s, sel_kv_p, sel_kv_s)
```

```python
import numpy as np
from contextlib import ExitStack
import ml_dtypes
import concourse.bass as bass
import concourse.mybir as mybir
from concourse.bass_utils import run_bass_kernel_spmd

F32 = mybir.dt.float32
BF16 = mybir.dt.bfloat16
I32 = mybir.dt.int32
AF = mybir.ActivationFunctionType
ALU = mybir.AluOpType
AX = mybir.AxisListType

D = 1024
SEQ = 4096
HD = 64
AH = 8
EPS = 1e-6
FFN = 2816
NAUG = 9
KR = HD + NAUG
NEGM = -30000.0


class Buf:
    __slots__ = ("name", "lws", "rd", "dsem", "dsem2", "multi", "isdma", "excl")

    def __init__(self, name, dsem=None, multi=False):
        self.name = name
        self.lws = {}
        self.rd = []
        self.dsem = dsem
        self.dsem2 = None
        self.isdma = dsem is not None
        self.multi = multi
        self.excl = False


class DmaSem:
    def __init__(self, sem):
        self.sem = sem
        self.count = 0


class Op:
    __slots__ = ("eng", "fn", "deps", "signal", "count", "dsem", "dval", "isdma", "key")

    def __init__(self, eng, fn, deps, isdma=False, dsem=None):
        self.eng = eng
        self.fn = fn
        self.deps = deps
        self.signal = False
        self.count = None
        self.isdma = isdma
        self.dsem = dsem
        self.dval = None
        self.key = ("d", id(dsem)) if isdma else ("e", eng)


class FW:
    ENGS = ("pe", "act", "dve", "pool", "sp")

    def __init__(self, nc, stack):
        self.nc = nc
        self.stack = stack
        self.ops = {e: [] for e in self.ENGS}
        self.ndsem = 0
        self._dsems = []
        self._free = []
        self._free_sw = []
        self._phase = []
        self.last = {e: None for e in self.ENGS}

    def new_dsem(self, sw=False):
        fl = self._free_sw if sw else self._free
        if fl:
            d = fl.pop()
        else:
            self.ndsem += 1
            d = DmaSem(self.stack.enter_context(self.nc.semaphore("ds_%d" % self.ndsem)))
            d.sw = sw
            self._dsems.append(d)
        self._phase.append(d)
        return d

    def buf(self, name, dma=False):
        return Buf(name, self.new_dsem() if dma else None)

    def dbuf(self, name):
        return Buf(name, None, multi=True)

    def _deps(self, key, r, w):
        deps = []
        for b in r:
            deps.extend(b.lws.values())
            if b.excl:
                deps.extend(d for d in b.rd if d.key != key)
        for b in w:
            if not b.multi:
                deps.extend(b.lws.values())
            deps.extend(b.rd)
        out = []
        seen = set()
        for d in deps:
            if id(d) in seen:
                continue
            seen.add(id(d))
            if d.key == key and (key[0] == "d" or key[1] == "pe"):
                continue
            out.append(d)
        return out

    def _commit(self, op, r, w):
        for b in r:
            b.rd.append(op)
        for b in w:
            if b.multi and not b.rd:
                b.lws[op.key] = op
            else:
                b.lws = {op.key: op}
            b.rd = []
        for d in op.deps:
            d.signal = True
        if not op.isdma:
            self.last[op.eng] = op

    def op(self, eng, fn, r=(), w=()):
        o = Op(eng, fn, None)
        o.deps = self._deps(o.key, r, w)
        self.ops[eng].append(o)
        self._commit(o, r, w)
        return o

    def dma(self, eng, out, in_, r=(), w=(), **kw):
        dsem = None
        for b in list(w) + list(r):
            if b.isdma:
                if eng != "sp":
                    if b.dsem2 is None:
                        b.dsem2 = self.new_dsem(sw=False)
                    dsem = b.dsem2
                else:
                    dsem = b.dsem
                break
        assert dsem is not None, "dma needs an sbuf-side buf with dma=True"
        o = Op(eng, (lambda e: e.dma_start(out=out, in_=in_, **kw)), None, isdma=True, dsem=dsem)
        o.deps = self._deps(o.key, r, w)
        dsem.count += 16
        o.dval = dsem.count
        self.ops[eng].append(o)
        self._commit(o, r, w)
        return o

    def barrier(self):
        lasts = [o for o in self.last.values() if o is not None]
        dm = []
        for ds in self._dsems:
            if ds.count > 0:
                f = Op("sp", None, [], isdma=True, dsem=ds)
                f.dval = ds.count
                dm.append(f)
        for e in self.ENGS:
            o = Op(e, None, [d for d in lasts if d.eng != e] + dm)
            for d in o.deps:
                d.signal = True
            self.ops[e].append(o)
        for d in self._phase:
            (self._free_sw if d.sw else self._free).append(d)
        self._phase = []

    def keep(self, buf):
        if buf.dsem in self._phase:
            self._phase.remove(buf.dsem)

    def emit(self):
        nc = self.nc
        for e in self.ENGS:
            c = 0
            for o in self.ops[e]:
                if o.fn is not None and not o.isdma and o.signal:
                    c += 1
                    o.count = c
        self.max_counts = {e: max([o.count or 0 for o in self.ops[e]] + [0]) for e in self.ENGS}
        EPOCH = 2000
        self.esems = {}
        for e in self.ENGS:
            n = (self.max_counts[e] + EPOCH - 1) // EPOCH
            self.esems[e] = [self.stack.enter_context(nc.semaphore("es_%s_%d" % (e, i))) for i in range(n)]

        def semval(op):
            c = op.count - 1
            return self.esems[op.eng][c // EPOCH], c % EPOCH + 1
        self.n_ops = {e: len(self.ops[e]) for e in self.ENGS}
        engmap = {"pe": "tensor", "act": "scalar", "dve": "vector", "pool": "gpsimd", "sp": "sync"}
        final = [(ds.sem, ds.count) for ds in self._dsems]
        with nc.Block() as block:
            for e in self.ENGS:
                ops = self.ops[e]

                def body(eng, ops=ops, e=e):
                    seen = {}
                    for o in ops:
                        for d in o.deps:
                            if d.isdma:
                                s, v = d.dsem.sem, d.dval
                            else:
                                s, v = semval(d)
                            k = id(s)
                            if seen.get(k, 0) >= v:
                                continue
                            seen[k] = v
                            eng.wait_ge(s, v)
                        if o.fn is None:
                            continue
                        ins = o.fn(eng)
                        if o.isdma:
                            ins.then_inc(o.dsem.sem, 16)
                        elif o.signal:
                            ins.then_inc(semval(o)[0], 1)
                    if e == "sp":
                        for s, v in final:
                            if v > 0:
                                eng.wait_ge(s, v)
                getattr(block, engmap[e])(body)


def _split3(x):
    x = np.asarray(x, np.float64)
    hi = x.astype(ml_dtypes.bfloat16).astype(np.float64)
    mid = (x - hi).astype(ml_dtypes.bfloat16).astype(np.float64)
    lo = (x - hi - mid).astype(ml_dtypes.bfloat16).astype(np.float64)
    return [hi, mid, lo]


def _slopes(n):
    return 2.0 ** (-8.0 * np.arange(1, n + 1, dtype=np.float64) / n)


def q_aug_rows(slope, tq):
    tq = np.asarray(tq, np.float64)
    rows = _split3(-8.0 * slope * tq)
    for v in _split3(8.0 * 128.0 * slope):
        rows.append(np.full(tq.shape, v))
    for v in _split3(8.0 * slope):
        rows.append(np.full(tq.shape, v))
    return np.stack(rows).astype(ml_dtypes.bfloat16)


def k_aug_rows(tk):
    tk = np.asarray(tk, np.int64)
    a = (tk // 128).astype(np.float64)
    b = (tk % 128).astype(np.float64)
    one = np.ones(tk.shape)
    return np.stack([one, one, one, a, a, a, b, b, b]).astype(ml_dtypes.bfloat16)


def host_consts():
    c = {}
    c["ident"] = np.eye(128, dtype=np.float32)
    ik = np.arange(128)[:, None]
    iq = np.arange(128)[None, :]
    cm = np.zeros((128, 256), np.float32)
    cm[:, 0:128] = np.where(ik >= iq, 0.0, NEGM)
    cm[:, 128:256] = np.where(ik <= iq, 0.0, NEGM)
    c["cmask"] = cm.astype(ml_dtypes.bfloat16)
    c["cmask4"] = np.concatenate([np.tile(cm[:, 0:128], (1, 4)), np.tile(cm[:, 128:256], (1, 4))], axis=1).astype(ml_dtypes.bfloat16)
    s8 = _slopes(AH)
    c["qaug0"] = np.stack([q_aug_rows(s8[h], np.arange(SEQ)) for h in range(AH)])
    c["kaug"] = k_aug_rows(np.arange(SEQ))
    s16 = _slopes(16)
    c["qaug1"] = np.stack([np.stack([q_aug_rows(s16[g * 4 + r], np.arange(SEQ)) for r in range(4)], axis=1) for g in range(4)])
    nn = np.arange(256)
    c["kaugc"] = k_aug_rows(16 * nn + 31)
    mc = np.zeros((128, 33, 128), np.float32)
    iq = np.arange(128)[None, :]
    for qt in range(17):
        n = np.arange(128)[:, None]
        mc[:, qt, :] = np.where(16 * n + 31 <= 128 * qt + iq, 0.0, NEGM)
    for qt in range(16, 32):
        n = 128 + np.arange(128)[:, None]
        mc[:, 17 + qt - 16, :] = np.where(16 * n + 31 <= 128 * qt + iq, 0.0, NEGM)
    c["mc"] = mc.astype(ml_dtypes.bfloat16)
    ts = np.zeros((128, 32, 64), np.float32)
    for qt in range(32):
        tq = 128 * qt + np.arange(128)[:, None]
        cb = tq // 64
        j = np.arange(64)[None, :]
        ok = j <= cb
        forced = (j == 0) | (j == cb) | (j == cb - 1)
        ts[:, qt, :] = np.where(ok, np.where(forced, 1000.0, 0.0), -1e30)
    c["tsel"] = ts
    n = np.arange(256)[:, None]
    j = np.arange(64)[None, :]
    ov = np.minimum(16 * n + 32, 64 * j + 64) - np.maximum(16 * n, 64 * j)
    c2s = (np.maximum(ov, 0) / 16.0).astype(np.float32)
    c2s[255] = 0
    c["c2s"] = np.ascontiguousarray(c2s.reshape(2, 128, 64).transpose(1, 0, 2)).astype(ml_dtypes.bfloat16)
    rows = np.arange(17 * 128)[:, None]
    qrow = 2048 + np.arange(8)[None, :]
    dist = qrow - rows
    cntm = np.zeros((17 * 128, 8), np.float32)
    for (wdw, dl) in ((128, 1), (512, 4), (2048, 16)):
        cntm += ((dist >= 0) & (dist <= wdw) & (dist % dl == 0) & (rows < 2056)).astype(np.float32)
    c["cnt_s"] = np.ascontiguousarray(cntm.reshape(17, 128, 8).transpose(1, 0, 2)).astype(ml_dtypes.bfloat16)
    c["kaug_s"] = k_aug_rows(np.arange(17 * 128))
    c["qaug0_s"] = np.stack([q_aug_rows(s8[h], 2048 + np.arange(8)) for h in range(AH)], axis=1)
    c["kaug_ss"] = k_aug_rows(np.arange(8320))
    c["kaug_sw"] = k_aug_rows(7680 + np.arange(640))
    c["kaugc_s"] = k_aug_rows(16 * np.arange(512) + 31)
    c["qaug1_s"] = np.stack([np.stack([q_aug_rows(s16[g * 4 + r], 8192 + np.arange(8)) for r in range(4)], axis=1) for g in range(4)])
    n5 = np.arange(512)[:, None]
    j5 = np.arange(129)[None, :]
    ov5 = np.minimum(16 * n5 + 32, 64 * j5 + 64) - np.maximum(16 * n5, 64 * j5)
    c2s5 = (np.maximum(ov5, 0) / 16.0).astype(np.float32)
    c2s5[511] = 0
    c["c2s_s"] = np.ascontiguousarray(c2s5.reshape(4, 128, 129).transpose(1, 0, 2)).astype(ml_dtypes.bfloat16)
    ts5 = np.zeros((8, 129), np.float32)
    ts5[:, [0, 127, 128]] = 1000.0
    c["tsel_s"] = ts5
    e30s = np.zeros((128, 8192), np.float32)
    for jj in range(128):
        e30s[jj, jj * 64:(jj + 1) * 64] = 30000.0
    c["e30s"] = e30s.astype(ml_dtypes.bfloat16)
    jj8 = np.arange(8)[:, None]
    ii8 = np.arange(8)[None, :]
    c["cm8"] = np.tile(np.where(jj8 <= ii8, 0.0, NEGM), (1, 4)).astype(ml_dtypes.bfloat16)
    j128 = np.arange(128)[:, None]
    c["wm0"] = np.tile(np.where(j128 >= ii8, 0.0, NEGM), (1, 4)).astype(ml_dtypes.bfloat16)
    c["iota_p"] = np.arange(128, dtype=np.float32).reshape(128, 1)
    e30 = np.zeros((64, SEQ), np.float32)
    for jj in range(64):
        e30[jj, jj * 64:(jj + 1) * 64] = 30000.0
    c["e30"] = e30.astype(ml_dtypes.bfloat16)
    return c


class KB:
    def __init__(self, stage):
        self.stage = stage
        self.nc = bass.Bass("TRN2", target_bir_lowering=False)
        self.ins = {}
        self.outs = {}

    def din(self, name, shape, dt=F32):
        ap = self.nc.dram_tensor(name, list(shape), dt, kind="ExternalInput").ap()
        self.ins[name] = ap
        return ap

    def dout(self, name, shape, dt=F32):
        ap = self.nc.dram_tensor(name, list(shape), dt, kind="ExternalOutput").ap()
        self.outs[name] = ap
        return ap

    def dscr(self, name, shape, dt):
        return self.nc.dram_tensor(name, list(shape), dt).ap()

    def salloc(self, cols, dt, parts=128):
        nb = cols * (2 if dt == BF16 else 4)
        nb = (nb + 63) // 64 * 64
        off = self.sp_off
        self.sp_off += nb
        assert self.sp_off <= self.arena_bytes, ("sbuf overflow", self.sp_off)
        v = self.arena[0:parts, off // 4:(off + nb) // 4]
        if dt != F32:
            v = v.bitcast(dt)
        return v[:, 0:cols]

    def build(self):
        nc = self.nc
        with ExitStack() as st:
            self.fw = fw = FW(nc, st)
            self.arena_bytes = 45056 * 4
            self.arena = st.enter_context(nc.sbuf_tensor("arena", [128, 45056], F32))
            self.sp_off = 0
            self.ps = [st.enter_context(nc.psum_tensor("ps%d" % i, [128, 512], F32)) for i in range(8)]
            self.bps = [fw.buf("ps%d" % i) for i in range(8)]
            for b in self.bps:
                b.excl = True
            self.body()
            fw.emit()
        return nc

    def body(self):
        nc, fw = self.nc, self.fw
        S = self.stage
        xp = self.din("xp", [SEQ, D])
        w_in_ab = self.din("w_in_ab", [D, 2560])
        w_out_ab = self.din("w_out_ab", [D, D])
        w1 = self.din("ffn_w1", [2, D, FFN])
        w3 = self.din("ffn_w3", [2, D, FFN])
        w2 = self.din("ffn_w2", [2, FFN, D])
        w_out_c = self.din("w_out_c", [D, D])
        w_in_c = self.din("w_in_c", [D, 2048])
        w_in_c_tm = self.din("w_in_c_tm", [D, 1536 + 256])
        gains = self.din("gains", [128, 5 * 8])
        rnnp = self.din("rnnp", [128, 4 * 9])
        gaw = self.din("gate_a_w", [8, 64, 64])
        gxw = self.din("gate_x_w", [8, 64, 64])
        c_ident = self.din("ident", [128, 128])
        c_cmask = self.din("cmask", [128, 256], BF16)
        c_cmask4 = self.din("cmask4", [128, 1024], BF16)
        c_qaug0 = self.din("qaug0", [AH, NAUG, SEQ], BF16)
        c_kaug = self.din("kaug", [NAUG, SEQ], BF16)
        c_qaug1 = self.din("qaug1", [4, NAUG, 4, SEQ], BF16)
        c_kaugc = self.din("kaugc", [NAUG, 256], BF16)
        c_mc = self.din("mc", [128, 33, 128], BF16)
        c_tsel = self.din("tsel", [128, 32, 64])
        c_c2s = self.din("c2s", [128, 2, 64], BF16)
        c_e30 = self.din("e30", [64, SEQ], BF16)
        w_cmp_t = self.din("w_cmp_t", [64, 32 * 2 * 64])
        pe_cmp_t = self.din("pe_cmp_t", [64, 64])

        xs = self.din("xs", [32, D])
        cache_dil = self.din("cache_dil", [4, 2048, 1024])
        st_conv = self.din("st_conv", [128, 4 * 4 * 3])
        st_rnn = self.din("st_rnn", [128, 4 * 4])
        cache_win = self.din("cache_win", [4, 512, 512])
        c_cnt_s = self.din("cnt_s", [128, 17, 8], BF16)
        c_kaug_s = self.din("kaug_s", [NAUG, 17 * 128], BF16)
        c_qaug0_s = self.din("qaug0_s", [NAUG, 8, 8], BF16)
        NPR = 2560 * 128
        pool_cmp = self.din("pool_cmp", [NPR, 512]) if S >= 6 else None
        pool_sel = self.din("pool_sel", [NPR, 512]) if S >= 6 else None
        ptab = self.din("ptab", [1, 256], I32) if S >= 6 else None
        if S >= 6:
            c_kaug_ss = self.din("kaug_ss", [NAUG, 8320], BF16)
            c_kaug_sw = self.din("kaug_sw", [NAUG, 640], BF16)
            c_kaugc_s = self.din("kaugc_s", [NAUG, 512], BF16)
            c_qaug1_s = self.din("qaug1_s", [4, NAUG, 4, 8], BF16)
            c_c2s_s = self.din("c2s_s", [128, 4, 129], BF16)
            c_tsel_s = self.din("tsel_s", [8, 129])
            c_e30s = self.din("e30s", [128, 8192], BF16)
            c_cm8 = self.din("cm8", [8, 32], BF16)
            c_wm0 = self.din("wm0", [128, 32], BF16)
            c_iota_p = self.din("iota_p", [128, 1])
        s_cT_cmp = self.dscr("s_cT_cmp", [4, 512, 8320], BF16)
        s_cT_sel = self.dscr("s_cT_sel", [4, 256, 8320], BF16)
        s_cV_sel = self.dscr("s_cV_sel", [4, 8320, 256], BF16)
        s_cT_win = self.dscr("s_cT_win", [4, 256, 640], BF16)
        s_cV_win = self.dscr("s_cV_win", [4, 640, 256], BF16)
        s_sQT1 = self.dscr("s_sQT1", [1024, 32], BF16)
        s_sG = self.dscr("s_sG", [32, 48], F32)
        s_sOT1 = self.dscr("s_sOT1", [1024, 32], BF16)
        o_y_s = self.dout("y_s", [32, D])
        o_dil_kv_s = self.dout("dil_kv_s", [4, 2048, 1024])
        o_conv_s = self.dout("conv_s", [4, 3, 512])
        o_rnn_s = self.dout("rnn_s", [4, 512])
        o_win_kv_s = self.dout("win_kv_s", [4, 512, 512])
        o_cmp_kv_s = self.dout("cmp_kv_s", [32, 512])
        o_sel_kv_s = self.dout("sel_kv_s", [32, 512])
        s_sQT0 = self.dscr("s_sQT0", [512, 32], BF16)
        s_sKV = self.dscr("s_sKV", [32, 1024], BF16)
        s_sXR = self.dscr("s_sXR", [512, 32], F32)
        s_sGT = self.dscr("s_sGT", [512, 32], F32)
        s_sOT0 = self.dscr("s_sOT0", [1024, 32], BF16)
        s_sX1 = self.dscr("s_sX1", [32, D], F32)
        o_dil_kv_p = self.dout("dil_kv_p", [2048, 1024])
        o_conv_p = self.dout("conv_p", [3, 512])
        o_rnn_p = self.dout("rnn_p", [512])
        o_cmp_kv_p = self.dout("cmp_kv_p", [SEQ, 512])
        o_sel_kv_p = self.dout("sel_kv_p", [SEQ, 512])
        o_win_kv_p = self.dout("win_kv_p", [512, 512])
        o_y_p = self.dout("y_p", [SEQ, D])

        def wscr(name, K, N):
            return self.dscr(name, [N // 256, 128, (K // 128) * 256], BF16)
        wb_in_ab = wscr("wb_in_ab", D, 2560)
        wb_out_ab = wscr("wb_out_ab", D, D)
        wb_w1 = [wscr("wb_w1_%d" % l, D, FFN) for l in range(2)]
        wb_w3 = [wscr("wb_w3_%d" % l, D, FFN) for l in range(2)]
        wb_w2 = [wscr("wb_w2_%d" % l, FFN, D) for l in range(2)]
        wb_out_c = wscr("wb_out_c", D, D)
        wb_in_c = wscr("wb_in_c", D, 2048)
        wb_in_c_tm = wscr("wb_in_c_tm", D, 1792)
        s_QT0 = self.dscr("s_QT0", [512, SEQ], BF16)
        s_KT0 = self.dscr("s_KT0", [512, SEQ], BF16)
        s_V0 = self.dscr("s_V0", [SEQ, 512], BF16)
        s_XR = self.dscr("s_XR", [512, SEQ], F32)
        s_GT = self.dscr("s_GT", [512, SEQ], F32)
        s_OT0 = self.dscr("s_OT0", [1024, SEQ], BF16)
        s_X1 = self.dscr("s_X1", [SEQ, D], F32)
        DBG = (S == 4.5)
        s_OT1 = (self.dout if DBG else self.dscr)("s_OT1", [1024, SEQ], BF16)
        s_QT1 = (self.dout if DBG else self.dscr)("s_QT1", [1024, SEQ], BF16)
        s_KC = self.dscr("s_KC", [512, SEQ], BF16)
        s_KS = self.dscr("s_KS", [256, SEQ], BF16)
        s_KW = self.dscr("s_KW", [256, SEQ], BF16)
        s_VS = self.dscr("s_VS", [SEQ, 256], BF16)
        s_VW = self.dscr("s_VW", [SEQ, 256], BF16)
        s_G = (self.dout if DBG else self.dscr)("s_G", [SEQ, 48], F32)
        norm_out = self.din("norm_out", [D])
        b_scr = {n: fw.dbuf(n) for n in ["w", "QT0", "KT0", "V0", "XR", "GT", "OT0", "X1", "OT1", "QT1", "KC", "KS", "KW", "VS", "VW", "G", "sQT0", "sKV", "sXR", "sGT", "sOT0", "sX1", "cT_cmp", "cT_sel", "cV_sel", "cT_win", "cV_win", "sQT1", "sG", "sOT1", "ocmp", "osel", "owin"]}

        ident_f = self.salloc(128, F32)
        ident_b = self.salloc(128, BF16)
        cmask = self.salloc(256, BF16)
        cmask4 = self.salloc(1024, BF16)
        ones_b = self.salloc(128, BF16)
        gains_t = self.salloc(40, F32)
        b_const = fw.buf("const", dma=True)
        fw.dma("sp", ident_f, c_ident, w=[b_const])
        fw.dma("sp", cmask, c_cmask, w=[b_const])
        fw.dma("sp", cmask4, c_cmask4, w=[b_const])
        fw.dma("sp", gains_t, gains, w=[b_const])
        b_const2 = fw.buf("const2")
        fw.op("dve", lambda e: e.tensor_copy(ident_b, ident_f), r=[b_const], w=[b_const2])
        fw.op("dve", lambda e: e.memset(ones_b, 1.0), w=[b_const2])
        self.ident_b, self.ident_f, self.cmask, self.ones_b = ident_b, ident_f, cmask, ones_b
        self.b_consts = [b_const, b_const2]
        base_off = self.sp_off

        wmark = self.sp_off
        w_stg = [self.salloc(22 * 256, F32) for _ in range(2)]
        w_stb = [self.salloc(22 * 256, BF16) for _ in range(2)]
        w_bs = [fw.buf("wst%d" % i, dma=True) for i in range(2)]
        w_bb = [fw.buf("wsb%d" % i, dma=True) for i in range(2)]
        wcount = [0]

        def wprep(src, dst, K, N, tag):
            KC = K // 128
            srcv = src.rearrange("(kc p) n -> p kc n", p=128)
            for nb in range(N // 256):
                i = wcount[0] % 2
                wcount[0] += 1
                stg_i = w_stg[i][:, 0:KC * 256]
                stb_i = w_stb[i][:, 0:KC * 256]
                sv = stg_i.rearrange("p (kc n) -> p kc n", kc=KC)
                for k0 in range(0, KC, 8):
                    k1 = min(KC, k0 + 8)
                    fw.dma("sp", sv[:, k0:k1, :], srcv[:, k0:k1, nb * 256:(nb + 1) * 256], w=[w_bs[i]])
                if i == 0:
                    fw.op("dve", lambda e, stb_i=stb_i, stg_i=stg_i: e.tensor_copy(stb_i, stg_i), r=[w_bs[i]], w=[w_bb[i]])
                else:
                    fw.op("act", lambda e, stb_i=stb_i, stg_i=stg_i: e.copy(stb_i, stg_i), r=[w_bs[i]], w=[w_bb[i]])
                fw.dma("act", dst[nb], stb_i, r=[w_bb[i]], w=[b_scr["w"]])

        wprep(w_in_ab, wb_in_ab, D, 2560, "inab")
        wprep(w_out_ab, wb_out_ab, D, D, "outab")
        if S >= 2.7:
            for l in range(2 if S >= 4 else 1):
                wprep(w1[l], wb_w1[l], D, FFN, "w1")
                wprep(w3[l], wb_w3[l], D, FFN, "w3")
                wprep(w2[l], wb_w2[l], FFN, D, "w2")
            wprep(w_in_c, wb_in_c, D, 2048, "inc")
            wprep(w_out_c, wb_out_c, D, D, "outc")
            wprep(w_in_c_tm, wb_in_c_tm, D, 1792, "inctm")
        self.sp_off = wmark
        fw.barrier()

        def load_w(wb, K, nblocks, name):
            KC = K // 128
            t = self.salloc(nblocks * KC * 256, BF16)
            b = fw.buf(name, dma=True)
            tv = t.rearrange("p (nb x) -> p nb x", nb=nblocks)
            for nb in range(nblocks):
                fw.dma("sp", tv[:, nb, :], wb[nb], r=[b_scr["w"]], w=[b])
            return t.rearrange("p (nb kc n) -> p nb kc n", nb=nblocks, kc=KC), b

        def rmsnorm_T(xt, b_xt, ntok, gidx, hT, b_hT, col0, tmp):
            sq, ss, xn, bt = tmp
            fw.op("act", lambda e: e.activation(out=sq[0:ntok, :], in_=xt[0:ntok, :], func=AF.Square, accum_out=ss[0:ntok, 0:1]), r=[b_xt], w=[bt])
            fw.op("dve", lambda e: e.tensor_scalar(out=ss[0:ntok, 1:2], in0=ss[0:ntok, 0:1], scalar1=1.0 / D, scalar2=EPS, op0=ALU.mult, op1=ALU.add), r=[bt], w=[bt])
            fw.op("act", lambda e: e.activation(out=ss[0:ntok, 2:3], in_=ss[0:ntok, 1:2], func=AF.Sqrt), r=[bt], w=[bt])
            fw.op("dve", lambda e: e.reciprocal(out=ss[0:ntok, 3:4], in_=ss[0:ntok, 2:3]), r=[bt], w=[bt])
            fw.op("dve", lambda e: e.tensor_scalar(out=xn[0:ntok, :], in0=xt[0:ntok, :], scalar1=ss[0:ntok, 3:4], scalar2=None, op0=ALU.mult), r=[b_xt, bt], w=[bt])
            pst = self.ps[7][:, :].bitcast(BF16)
            for kc in range(8):
                fw.op("pe", lambda e, kc=kc: e.transpose(pst[:, kc * 128:kc * 128 + ntok], xn[0:ntok, kc * 128:(kc + 1) * 128], ident_b[0:ntok, 0:ntok]), r=[bt, b_const2], w=[self.bps[7]])
            g = gains_t[:, gidx * 8:(gidx + 1) * 8]
            fw.op("dve", lambda e: e.tensor_tensor(out=hT[:, :, col0:col0 + ntok], in0=pst.rearrange("p (kc t) -> p kc t", kc=8)[:, :, 0:ntok],
                                                     in1=g.rearrange("p (kc o) -> p kc o", o=1).to_broadcast([128, 8, ntok]), op=ALU.mult),
                  r=[self.bps[7], b_const], w=[b_hT])

        self.load_w = load_w
        self.rmsnorm_T = rmsnorm_T

        if S == 9:
            self.sample_path(locals())
            return

        mark = self.sp_off
        win, b_win = load_w(wb_in_ab, D, 10, "win_ab")
        xts = [self.salloc(D, F32) for _ in range(2)]
        b_xts = [fw.buf("xt%d" % i, dma=True) for i in range(2)]
        tmp = (self.salloc(D, F32), self.salloc(4, F32), self.salloc(D, BF16), fw.buf("nrm_tmp"))
        hT = self.salloc(8 * 512, BF16).rearrange("p (kc t) -> p kc t", kc=8)
        b_hT = fw.buf("hT")
        stg_b = [self.salloc(512, BF16) for _ in range(2)]
        stg_f = [self.salloc(512, F32) for _ in range(2)]
        b_stg_b = [fw.buf("stgb%d" % i, dma=True) for i in range(2)]
        b_stg_f = [fw.buf("stgf%d" % i, dma=True) for i in range(2)]
        kv_f = [self.salloc(1024, F32) for _ in range(2)]
        kv_b = [self.salloc(512, BF16) for _ in range(2)]
        b_kv_f = [fw.buf("kvf%d" % i, dma=True) for i in range(2)]
        b_kv_b = [fw.buf("kvb%d" % i, dma=True) for i in range(2)]
        cnt = 0
        for blk in range(8):
            for j in range(4):
                t0 = blk * 512 + j * 128
                i = (blk * 4 + j) % 2
                fw.dma("sp", xts[i], xp[t0:t0 + 128, :], w=[b_xts[i]])
                rmsnorm_T(xts[i], b_xts[i], 128, 0, hT, b_hT, j * 128, tmp)
            for oc in list(range(0, 8)) + list(range(12, 20)):
                pi = cnt % 2
                cnt += 1
                pst_, bp = self.ps[pi], self.bps[pi]
                nb, sub = oc // 2, oc % 2
                for kc in range(8):
                    fw.op("pe", lambda e, kc=kc, nb=nb, sub=sub, pst_=pst_: e.matmul(pst_[:, :], lhsT=win[:, nb, kc, sub * 128:(sub + 1) * 128], rhs=hT[:, kc, :], start=(kc == 0), stop=(kc == 7)),
                          r=[b_win, b_hT], w=[bp])
                if oc < 8:
                    si = oc % 2
                    fw.op("act", lambda e, si=si, pst_=pst_: e.copy(stg_b[si], pst_[:, :]), r=[bp], w=[b_stg_b[si]])
                    dst = s_QT0 if oc < 4 else s_KT0
                    nm = "QT0" if oc < 4 else "KT0"
                    r0 = (oc % 4) * 128
                    fw.dma("act", dst[r0:r0 + 128, blk * 512:(blk + 1) * 512], stg_b[si], r=[b_stg_b[si]], w=[b_scr[nm]])
                else:
                    si = oc % 2
                    fw.op("dve", lambda e, si=si, pst_=pst_: e.tensor_copy(stg_f[si], pst_[:, :]), r=[bp], w=[b_stg_f[si]])
                    dst = s_XR if oc < 16 else s_GT
                    nm = "XR" if oc < 16 else "GT"
                    r0 = (oc % 4) * 128
                    fw.dma("act", dst[r0:r0 + 128, blk * 512:(blk + 1) * 512], stg_f[si], r=[b_stg_f[si]], w=[b_scr[nm]])
            for j in range(4):
                t0 = blk * 512 + j * 128
                i = j % 2
                for half in range(2):
                    pi = 2 + (j * 2 + half) % 2
                    pst_, bp = self.ps[pi], self.bps[pi]
                    for q2 in range(2):
                        nb = 2 + half * 2 + q2
                        for kc in range(8):
                            fw.op("pe", lambda e, kc=kc, nb=nb, q2=q2, pst_=pst_, j=j: e.matmul(pst_[:, q2 * 256:(q2 + 1) * 256], lhsT=hT[:, kc, j * 128:(j + 1) * 128], rhs=win[:, nb, kc, :], start=(kc == 0), stop=(kc == 7)),
                                  r=[b_win, b_hT], w=[bp])
                    if half == 0:
                        fw.op("act", lambda e, i=i, pst_=pst_: e.copy(kv_f[i][:, 0:512], pst_[:, :]), r=[bp], w=[b_kv_f[i]])
                    else:
                        fw.op("dve", lambda e, i=i, pst_=pst_: e.tensor_copy(kv_f[i][:, 512:1024], pst_[:, :]), r=[bp], w=[b_kv_f[i]])
                fw.op("pool", lambda e, i=i: e.tensor_copy(kv_b[i], kv_f[i][:, 512:1024]), r=[b_kv_f[i]], w=[b_kv_b[i]])
                fw.dma("act", s_V0[t0:t0 + 128, :], kv_b[i], r=[b_kv_b[i]], w=[b_scr["V0"]])
                if t0 >= 2048:
                    fw.dma("act", o_dil_kv_p[t0 - 2048:t0 - 2048 + 128, :], kv_f[i], r=[b_kv_f[i]])
        self.sp_off = mark
        fw.barrier()
        if S < 2:
            return

        mark = self.sp_off
        T = SEQ
        rp = self.salloc(36, F32)
        b_rp = fw.buf("rp", dma=True)
        fw.dma("sp", rp, rnnp, w=[b_rp])
        bd_f = self.salloc(256, F32)
        bd_b = self.salloc(256, BF16)
        b_bdf = fw.buf("bdf", dma=True)
        b_bdb = fw.buf("bdb")
        xr = self.salloc(T + 4, F32)
        xc = self.salloc(T, F32)
        xcb = self.salloc(T, BF16)
        t1 = self.salloc(T, F32)
        t2 = self.salloc(T, F32)
        t3 = self.salloc(T, F32)
        ob = self.salloc(T, BF16)
        sp_t = self.salloc(4, F32)
        b_xr = fw.buf("xr", dma=True)
        b_xc, b_xcb, b_t1, b_t3, b_spt = [fw.buf(n) for n in ["xc", "xcb", "t1", "t3", "spt"]]
        b_t2 = fw.buf("t2", dma=True)
        b_ob = fw.buf("ob", dma=True)
        b_gt = fw.buf("gt_in", dma=True)
        for c in range(4):
            pr = rp[:, c * 9:(c + 1) * 9]
            fw.op("dve", lambda e: e.memset(bd_f, 0.0), w=[b_bdf])
            for gi, gw in enumerate((gaw, gxw)):
                fw.dma("sp", bd_f[0:64, gi * 128:gi * 128 + 64], gw[2 * c], w=[b_bdf])
                fw.dma("sp", bd_f[64:128, gi * 128 + 64:gi * 128 + 128], gw[2 * c + 1], w=[b_bdf])
            fw.op("dve", lambda e: e.tensor_copy(bd_b, bd_f), r=[b_bdf], w=[b_bdb])
            fw.op("dve", lambda e: e.memset(xr[:, 0:4], 0.0), w=[b_xr])
            fw.dma("sp", xr[:, 4:4 + T], s_XR[c * 128:(c + 1) * 128, :], r=[b_scr["XR"]], w=[b_xr])
            fw.dma("act", o_conv_p[:, c * 128:(c + 1) * 128].rearrange("k p -> p k"), xr[:, 4 + T - 3:4 + T], r=[b_xr], allow_slow_non_contiguous=True)
            fw.op("dve", lambda e, pr=pr: e.tensor_scalar(out=xc, in0=xr[:, 1:1 + T], scalar1=pr[:, 0:1], scalar2=pr[:, 4:5], op0=ALU.mult, op1=ALU.add), r=[b_xr, b_rp], w=[b_xc])
            for k in range(1, 4):
                fw.op("dve", lambda e, pr=pr, k=k: e.scalar_tensor_tensor(out=xc, in0=xr[:, 1 + k:1 + k + T], scalar=pr[:, k:k + 1], in1=xc, op0=ALU.mult, op1=ALU.add), r=[b_xr, b_rp, b_xc], w=[b_xc])
            fw.op("pool", lambda e: e.tensor_copy(xcb, xc), r=[b_xc], w=[b_xcb])
            fw.op("act", lambda e, pr=pr: e.activation(out=sp_t[:, 0:1], in_=pr[:, 7:8], func=AF.Exp, scale=-1.0), r=[b_rp], w=[b_spt])
            fw.op("act", lambda e: e.activation(out=sp_t[:, 1:2], in_=sp_t[:, 0:1], func=AF.Ln, bias=1.0), r=[b_spt], w=[b_spt])
            fw.op("dve", lambda e: e.tensor_scalar(out=sp_t[:, 2:3], in0=sp_t[:, 1:2], scalar1=-8.0, scalar2=None, op0=ALU.mult), r=[b_spt], w=[b_spt])
            for tb in range(T // 512):
                for gi, (tt, bt_) in enumerate(((t1, b_t1), (t2, b_t2))):
                    pi = (tb * 2 + gi) % 4
                    fw.op("pe", lambda e, gi=gi, tb=tb, pi=pi: e.matmul(self.ps[pi][:, :], lhsT=bd_b[:, gi * 128:(gi + 1) * 128], rhs=xcb[:, tb * 512:(tb + 1) * 512], start=True, stop=True),
                          r=[b_bdb, b_xcb], w=[self.bps[pi]])
                    fw.op("act", lambda e, gi=gi, tb=tb, pi=pi, tt=tt, pr=pr: e.activation(out=tt[:, tb * 512:(tb + 1) * 512], in_=self.ps[pi][:, :], func=AF.Sigmoid, bias=pr[:, 5 + gi:6 + gi]),
                          r=[self.bps[pi], b_rp], w=[bt_])
            fw.op("dve", lambda e: e.tensor_scalar(out=sp_t[:, 3:4], in0=sp_t[:, 2:3], scalar1=2.0, scalar2=None, op0=ALU.mult), r=[b_spt], w=[b_spt])
            fw.op("act", lambda e: e.activation(out=t3, in_=t1, func=AF.Exp, scale=sp_t[:, 3:4]), r=[b_t1, b_spt], w=[b_t3])
            fw.op("act", lambda e: e.activation(out=t1, in_=t1, func=AF.Exp, scale=sp_t[:, 2:3]), r=[b_t1, b_spt], w=[b_t1])
            fw.op("dve", lambda e: e.tensor_scalar(out=t3, in0=t3, scalar1=-1.0, scalar2=1.0, op0=ALU.mult, op1=ALU.add), r=[b_t3], w=[b_t3])
            fw.op("dve", lambda e: e.tensor_scalar(out=t3, in0=t3, scalar1=0.0, scalar2=None, op0=ALU.max), r=[b_t3], w=[b_t3])
            fw.op("act", lambda e: e.activation(out=t3, in_=t3, func=AF.Sqrt), r=[b_t3], w=[b_t3])
            fw.op("dve", lambda e: e.tensor_tensor(out=t3, in0=t3, in1=t2, op=ALU.mult), r=[b_t3, b_t2], w=[b_t3])
            fw.op("dve", lambda e: e.tensor_tensor(out=t3, in0=t3, in1=xc, op=ALU.mult), r=[b_t3, b_xc], w=[b_t3])
            for sg in range(T // 512):
                init = 0.0 if sg == 0 else t2[:, sg * 512 - 1:sg * 512]
                fw.op("dve", lambda e, sg=sg, init=init: e.tensor_tensor_scan(out=t2[:, sg * 512:(sg + 1) * 512], data0=t1[:, sg * 512:(sg + 1) * 512], data1=t3[:, sg * 512:(sg + 1) * 512], initial=init, op0=ALU.mult, op1=ALU.add),
                      r=[b_t1, b_t3, b_t2], w=[b_t2])
            fw.dma("act", o_rnn_p[c * 128:(c + 1) * 128].rearrange("(p o) -> p o", o=1), t2[:, T - 1:T], r=[b_t2])
            fw.dma("sp", t1, s_GT[c * 128:(c + 1) * 128, :], r=[b_scr["GT"], b_t1], w=[b_gt, b_t1])
            fw.op("act", lambda e: e.activation(out=t1, in_=t1, func=AF.Gelu), r=[b_gt, b_t1], w=[b_t1])
            fw.op("dve", lambda e: e.tensor_tensor(out=ob, in0=t2, in1=t1, op=ALU.mult), r=[b_t1, b_t2], w=[b_ob])
            fw.dma("act", s_OT0[512 + c * 128:512 + (c + 1) * 128, :], ob, r=[b_ob], w=[b_scr["OT0"]])
        self.sp_off = mark
        fw.barrier()
        if S < 2.5:
            return

        mark = self.sp_off
        Qh = self.salloc(SEQ, BF16, parts=KR)
        Kh = self.salloc(SEQ, BF16, parts=KR)
        b_Qh = fw.buf("Qh", dma=True)
        b_Kh = fw.buf("Kh", dma=True)
        accL = self.salloc(2 * SEQ, F32, parts=64).rearrange("p (a t) -> p a t", a=2)
        b_acc = fw.buf("accL")
        obh = self.salloc(SEQ, BF16, parts=64)
        b_obh = fw.buf("obh", dma=True)
        NV = 4
        vts = [self.salloc(64, BF16) for _ in range(NV)]
        b_vts = [fw.buf("vt%d" % i, dma=True) for i in range(NV)]
        pts = [self.salloc(256, BF16) for _ in range(2)]
        b_pts = [fw.buf("pt%d" % i) for i in range(2)]
        unit = 0
        vcnt = 0
        for h in range(AH):
            fw.dma("sp", Qh[0:64, :], s_QT0[h * 64:(h + 1) * 64, :], r=[b_scr["QT0"]], w=[b_Qh])
            fw.dma("sp", Qh[64:KR, :], c_qaug0[h], w=[b_Qh])
            fw.dma("sp", Kh[0:64, :], s_KT0[h * 64:(h + 1) * 64, :], r=[b_scr["KT0"]], w=[b_Kh])
            fw.dma("sp", Kh[64:KR, :], c_kaug, w=[b_Kh])
            for dil in (1, 4, 16):
                ntile = SEQ // dil // 128
                for r_ in range(dil):
                    prev_v = None
                    for qi in range(ntile):
                        def tok(t):
                            a0 = r_ + dil * 128 * t
                            return slice(a0, a0 + dil * 127 + 1, dil)
                        vi = vcnt % NV
                        vcnt += 1
                        fw.dma("sp", vts[vi], s_V0[tok(qi), h * 64:(h + 1) * 64], r=[b_scr["V0"]], w=[b_vts[vi]])
                        si = unit % 2
                        unit += 1
                        Sp, bS = self.ps[si], self.bps[si]
                        Op_, bO = self.ps[2 + si], self.bps[2 + si]
                        pt, bpt = pts[si], b_pts[si]
                        kts = ([(qi - 1, 0, prev_v)] if qi >= 1 else []) + [(qi, 1, vi)]
                        for (kt, ci, _) in kts:
                            cs = slice(ci * 128, (ci + 1) * 128)
                            fw.op("pe", lambda e, kt=kt, cs=cs, Sp=Sp, tk=tok(kt), tq=tok(qi): e.matmul(Sp[:, cs], lhsT=Kh[:, tk], rhs=Qh[:, tq], start=True, stop=False), r=[b_Qh, b_Kh], w=[bS])
                            fw.op("pe", lambda e, cs=cs, Sp=Sp: e.matmul(Sp[:, cs], lhsT=ident_b, rhs=cmask[:, cs], start=False, stop=True), r=[b_const, b_const2], w=[bS])
                        c0 = 0 if qi >= 1 else 128
                        fw.op("act", lambda e, Sp=Sp, pt=pt, c0=c0: e.activation(out=pt[:, c0:256], in_=Sp[:, c0:256], func=AF.Exp, scale=0.125), r=[bS], w=[bpt])
                        for n_, (kt, ci, vslot) in enumerate(kts):
                            cs = slice(ci * 128, (ci + 1) * 128)
                            first, last_ = (n_ == 0), (n_ == len(kts) - 1)
                            fw.op("pe", lambda e, cs=cs, Op_=Op_, pt=pt, vslot=vslot, first=first, last_=last_: e.matmul(Op_[0:64, 0:128], lhsT=vts[vslot], rhs=pt[:, cs], start=first, stop=last_), r=[bpt, b_vts[vslot]], w=[bO])
                        for n_, (kt, ci, vslot) in enumerate(kts):
                            cs = slice(ci * 128, (ci + 1) * 128)
                            first, last_ = (n_ == 0), (n_ == len(kts) - 1)
                            fw.op("pe", lambda e, cs=cs, Op_=Op_, pt=pt, first=first, last_=last_: e.matmul(Op_[0:64, 128:256], lhsT=ones_b[:, 0:64], rhs=pt[:, cs], start=first, stop=last_), r=[bpt, b_const2], w=[bO])
                        dst = accL[:, :, tok(qi)]
                        src = Op_[0:64, 0:256].rearrange("p (a t) -> p a t", a=2)
                        if dil == 1:
                            fw.op("dve", lambda e, dst=dst, src=src: e.tensor_copy(dst, src), r=[bO], w=[b_acc])
                        else:
                            fw.op("dve", lambda e, dst=dst, src=src: e.tensor_tensor(out=dst, in0=src, in1=dst, op=ALU.add), r=[bO, b_acc], w=[b_acc])
                        prev_v = vi
            for ch in range(4):
                cs = slice(ch * 1024, (ch + 1) * 1024)
                fw.op("dve", lambda e, cs=cs: e.reciprocal(out=accL[:, 1, cs], in_=accL[:, 1, cs]), r=[b_acc], w=[b_acc])
                fw.op("dve", lambda e, cs=cs: e.tensor_tensor(out=obh[:, cs], in0=accL[:, 0, cs], in1=accL[:, 1, cs], op=ALU.mult), r=[b_acc], w=[b_obh])
            fw.dma("act", s_OT0[h * 64:(h + 1) * 64, :], obh, r=[b_obh], w=[b_scr["OT0"]])
        self.sp_off = mark
        fw.barrier()

        def phaseD(l):
            mark = self.sp_off
            s_OT = s_OT0 if l == 0 else s_OT1
            nm_OT = "OT0" if l == 0 else "OT1"
            wb_out = wb_out_ab if l == 0 else wb_out_c
            wout, b_wout = load_w(wb_out, D, 4, "wout")
            w2t, b_w2 = load_w(wb_w2[l], FFN, 4, "w2")
            ot = self.salloc(8 * 512, BF16).rearrange("p (kc t) -> p kc t", kc=8)
            b_ot = fw.buf("ot", dma=True)
            x1 = [self.salloc(D, F32) for _ in range(4)]
            b_x1 = [fw.buf("x1_%d" % j, dma=True) for j in range(4)]
            tmp = (self.salloc(D, F32), self.salloc(4, F32), self.salloc(D, BF16), fw.buf("nrm_tmp"))
            hT = self.salloc(8 * 512, BF16).rearrange("p (kc t) -> p kc t", kc=8)
            b_hT = fw.buf("hT")
            gT = self.salloc(22 * 512, BF16).rearrange("p (hc t) -> p hc t", hc=22)
            b_gT = fw.buf("gT")
            wst = [self.salloc(2 * 8 * 256, BF16).rearrange("p (w kc n) -> p w kc n", w=2, kc=8) for _ in range(2)]
            b_wst = [fw.buf("wst%d" % i, dma=True) for i in range(2)]
            stmp = [self.salloc(512, F32) for _ in range(2)]
            b_stmp = [fw.buf("stmp%d" % i) for i in range(2)]
            stg_b = [self.salloc(512, BF16) for _ in range(2)]
            b_stg_b = [fw.buf("stgb%d" % i, dma=True) for i in range(2)]
            tm_f = [self.salloc(512, F32) for _ in range(2)]
            b_tm_f = [fw.buf("tmf%d" % i, dma=True) for i in range(2)]
            tm_b = [self.salloc(256, BF16) for _ in range(2)]
            b_tm_b = [fw.buf("tmb%d" % i, dma=True) for i in range(2)]
            if l == 1:
                gout = self.salloc(D, F32)
                b_gout = fw.buf("gout", dma=True)
                fw.dma("sp", gout, norm_out.partition_broadcast(128), w=[b_gout])
                yt = [self.salloc(D, F32) for _ in range(2)]
                b_yt = [fw.buf("yt%d" % i, dma=True) for i in range(2)]
            wcnt = 0
            pcnt = 0
            for blk in range(8):
                fw.dma("sp", ot, s_OT[:, blk * 512:(blk + 1) * 512].rearrange("(kc p) t -> p kc t", p=128), r=[b_scr[nm_OT]], w=[b_ot])
                for j in range(4):
                    t0 = blk * 512 + j * 128
                    if l == 0:
                        fw.dma("sp", x1[j], xp[t0:t0 + 128, :], w=[b_x1[j]])
                    else:
                        fw.dma("sp", x1[j], s_X1[t0:t0 + 128, :], r=[b_scr["X1"]], w=[b_x1[j]])
                    for half in range(2):
                        pi = pcnt % 2
                        pcnt += 1
                        for q2 in range(2):
                            nb = half * 2 + q2
                            for kc in range(8):
                                fw.op("pe", lambda e, kc=kc, nb=nb, q2=q2, pi=pi, j=j: e.matmul(self.ps[pi][:, q2 * 256:(q2 + 1) * 256], lhsT=ot[:, kc, j * 128:(j + 1) * 128], rhs=wout[:, nb, kc, :], start=(kc == 0), stop=(kc == 7)),
                                      r=[b_ot, b_wout], w=[self.bps[pi]])
                        fw.op("dve", lambda e, pi=pi, j=j, half=half: e.tensor_tensor(out=x1[j][:, half * 512:(half + 1) * 512], in0=self.ps[pi][:, :], in1=x1[j][:, half * 512:(half + 1) * 512], op=ALU.add),
                              r=[self.bps[pi], b_x1[j]], w=[b_x1[j]])
                    rmsnorm_T(x1[j], b_x1[j], 128, 1 + 2 * l, hT, b_hT, j * 128, tmp)
                if S < 2.85:
                    continue
                for nb in range(11):
                    wi = wcnt % 2
                    wcnt += 1
                    fw.dma("sp", wst[wi][:, 0], wb_w1[l][nb].rearrange("p (kc n) -> p kc n", kc=8), r=[b_scr["w"]], w=[b_wst[wi]])
                    fw.dma("sp", wst[wi][:, 1], wb_w3[l][nb].rearrange("p (kc n) -> p kc n", kc=8), r=[b_scr["w"]], w=[b_wst[wi]])
                    for sub in range(2):
                        hc = nb * 2 + sub
                        p1, p3 = 2 + (hc % 2) * 2, 3 + (hc % 2) * 2
                        for (pp, wsel) in ((p1, 0), (p3, 1)):
                            for kc in range(8):
                                fw.op("pe", lambda e, kc=kc, pp=pp, wsel=wsel, wi=wi, sub=sub: e.matmul(self.ps[pp][:, :], lhsT=wst[wi][:, wsel, kc, sub * 128:(sub + 1) * 128], rhs=hT[:, kc, :], start=(kc == 0), stop=(kc == 7)),
                                      r=[b_wst[wi], b_hT], w=[self.bps[pp]])
                        si = hc % 2
                        fw.op("act", lambda e, si=si, p1=p1: e.activation(out=stmp[si], in_=self.ps[p1][:, :], func=AF.Silu), r=[self.bps[p1]], w=[b_stmp[si]])
                        fw.op("dve", lambda e, si=si, p3=p3, hc=hc: e.tensor_tensor(out=gT[:, hc, :], in0=self.ps[p3][:, :], in1=stmp[si], op=ALU.mult), r=[self.bps[p3], b_stmp[si]], w=[b_gT])
                if S < 2.95:
                    continue
                for j in range(4):
                    t0 = blk * 512 + j * 128
                    for half in range(2):
                        pi = pcnt % 2
                        pcnt += 1
                        for q2 in range(2):
                            nb = half * 2 + q2
                            for hc in range(22):
                                fw.op("pe", lambda e, hc=hc, nb=nb, q2=q2, pi=pi, j=j: e.matmul(self.ps[pi][:, q2 * 256:(q2 + 1) * 256], lhsT=gT[:, hc, j * 128:(j + 1) * 128], rhs=w2t[:, nb, hc, :], start=(hc == 0), stop=(hc == 21)),
                                      r=[b_gT, b_w2], w=[self.bps[pi]])
                        fw.op("dve", lambda e, pi=pi, j=j, half=half: e.tensor_tensor(out=x1[j][:, half * 512:(half + 1) * 512], in0=self.ps[pi][:, :], in1=x1[j][:, half * 512:(half + 1) * 512], op=ALU.add),
                              r=[self.bps[pi], b_x1[j]], w=[b_x1[j]])
                    if l == 0:
                        fw.dma("act", s_X1[t0:t0 + 128, :], x1[j], r=[b_x1[j]], w=[b_scr["X1"]])
                        rmsnorm_T(x1[j], b_x1[j], 128, 2, hT, b_hT, j * 128, tmp)
                    else:
                        sq, ss, xn, bt = tmp
                        yi = j % 2
                        fw.op("act", lambda e, j=j: e.activation(out=sq, in_=x1[j], func=AF.Square, accum_out=ss[:, 0:1]), r=[b_x1[j]], w=[bt])
                        fw.op("dve", lambda e: e.tensor_scalar(out=ss[:, 1:2], in0=ss[:, 0:1], scalar1=1.0 / D, scalar2=EPS, op0=ALU.mult, op1=ALU.add), r=[bt], w=[bt])
                        fw.op("act", lambda e: e.activation(out=ss[:, 2:3], in_=ss[:, 1:2], func=AF.Sqrt), r=[bt], w=[bt])
                        fw.op("dve", lambda e: e.reciprocal(out=ss[:, 3:4], in_=ss[:, 2:3]), r=[bt], w=[bt])
                        fw.op("dve", lambda e, j=j, yi=yi: e.scalar_tensor_tensor(out=yt[yi], in0=x1[j], scalar=ss[:, 3:4], in1=gout, op0=ALU.mult, op1=ALU.mult), r=[bt, b_x1[j], b_gout], w=[b_yt[yi]])
                        fw.dma("act", o_y_p[t0:t0 + 128, :], yt[yi], r=[b_yt[yi]])
                if l == 0 and S >= 3:
                    for nb in range(8):
                        wi = wcnt % 2
                        wcnt += 1
                        fw.dma("sp", wst[wi][:, 0], wb_in_c[nb].rearrange("p (kc n) -> p kc n", kc=8), r=[b_scr["w"]], w=[b_wst[wi]])
                        for sub in range(2):
                            oc = nb * 2 + sub
                            pp = 2 + oc % 2
                            for kc in range(8):
                                fw.op("pe", lambda e, kc=kc, pp=pp, wi=wi, sub=sub: e.matmul(self.ps[pp][:, :], lhsT=wst[wi][:, 0, kc, sub * 128:(sub + 1) * 128], rhs=hT[:, kc, :], start=(kc == 0), stop=(kc == 7)),
                                      r=[b_wst[wi], b_hT], w=[self.bps[pp]])
                            si = oc % 2
                            fw.op("act", lambda e, si=si, pp=pp: e.copy(stg_b[si], self.ps[pp][:, :]), r=[self.bps[pp]], w=[b_stg_b[si]])
                            if oc < 8:
                                dst, nm, r0 = s_QT1, "QT1", oc * 128
                            elif oc < 12:
                                dst, nm, r0 = s_KC, "KC", (oc - 8) * 128
                            elif oc < 14:
                                dst, nm, r0 = s_KS, "KS", (oc - 12) * 128
                            else:
                                dst, nm, r0 = s_KW, "KW", (oc - 14) * 128
                            fw.dma("act", dst[r0:r0 + 128, blk * 512:(blk + 1) * 512], stg_b[si], r=[b_stg_b[si]], w=[b_scr[nm]])
                    for nb2 in range(4 if S >= 3.2 else (3 if S >= 3.1 else 0)):
                        wi = wcnt % 2
                        wcnt += 1
                        nq = 2 if nb2 < 3 else 1
                        for q2 in range(nq):
                            fw.dma("sp", wst[wi][:, q2], wb_in_c_tm[nb2 * 2 + q2].rearrange("p (kc n) -> p kc n", kc=8), r=[b_scr["w"]], w=[b_wst[wi]])
                        for j in range(4):
                            t0 = blk * 512 + j * 128
                            pp = 2 + j % 2
                            for q2 in range(nq):
                                for kc in range(8):
                                    fw.op("pe", lambda e, kc=kc, pp=pp, wi=wi, q2=q2, j=j: e.matmul(self.ps[pp][:, q2 * 256:(q2 + 1) * 256], lhsT=hT[:, kc, j * 128:(j + 1) * 128], rhs=wst[wi][:, q2, kc, :], start=(kc == 0), stop=(kc == 7)),
                                          r=[b_wst[wi], b_hT], w=[self.bps[pp]])
                            fi = j % 2
                            if nb2 < 3:
                                fw.op("act", lambda e, fi=fi, pp=pp: e.copy(tm_f[fi], self.ps[pp][:, :]), r=[self.bps[pp]], w=[b_tm_f[fi]])
                                if nb2 == 0:
                                    fw.dma("act", o_cmp_kv_p[t0:t0 + 128, :], tm_f[fi], r=[b_tm_f[fi]])
                                elif nb2 == 1:
                                    fw.dma("act", o_sel_kv_p[t0:t0 + 128, :], tm_f[fi], r=[b_tm_f[fi]])
                                elif t0 >= SEQ - 512:
                                    fw.dma("act", o_win_kv_p[t0 - (SEQ - 512):t0 - (SEQ - 512) + 128, :], tm_f[fi], r=[b_tm_f[fi]])
                                if nb2 >= 1:
                                    fw.op("dve", lambda e, fi=fi: e.tensor_copy(tm_b[fi], tm_f[fi][:, 256:512]), r=[b_tm_f[fi]], w=[b_tm_b[fi]])
                                    dst, nm = (s_VS, "VS") if nb2 == 1 else (s_VW, "VW")
                                    fw.dma("act", dst[t0:t0 + 128, :], tm_b[fi], r=[b_tm_b[fi]], w=[b_scr[nm]])
                            else:
                                fw.op("act", lambda e, fi=fi, pp=pp: e.activation(out=tm_f[fi][:, 0:48], in_=self.ps[pp][:, 0:48], func=AF.Sigmoid), r=[self.bps[pp]], w=[b_tm_f[fi]])
                                fw.dma("act", s_G[t0:t0 + 128, :], tm_f[fi][:, 0:48], r=[b_tm_f[fi]], w=[b_scr["G"]])
            self.sp_off = mark
            fw.barrier()

        if S < 2.8:
            return
        phaseD(0)
        if S < 4:
            return

        mark = self.sp_off
        mc_t = self.salloc(33 * 128, BF16).rearrange("p (m q) -> p m q", m=33)
        tsel_t = self.salloc(32 * 64, F32).rearrange("p (a j) -> p a j", a=32)
        e30_t = self.salloc(SEQ, BF16, parts=64)
        wc_f = self.salloc(4096, F32, parts=64)
        wc_b = self.salloc(4096, BF16, parts=64).rearrange("p (l c e) -> p l c e", l=32, c=2)
        pe_f = self.salloc(64, F32, parts=64)
        pe_b = self.salloc(64, BF16, parts=64).rearrange("p (l c) -> p l c", c=2)
        b_fc = fw.buf("fconst", dma=True)
        b_fc2 = fw.buf("fconst2")
        fw.dma("sp", mc_t, c_mc, w=[b_fc])
        fw.dma("sp", tsel_t, c_tsel, w=[b_fc])
        fw.dma("sp", e30_t, c_e30, w=[b_fc])
        fw.dma("sp", wc_f, w_cmp_t, w=[b_fc])
        fw.dma("sp", pe_f, pe_cmp_t, w=[b_fc])
        fw.op("dve", lambda e: e.tensor_copy(wc_b.rearrange("p l c e -> p (l c e)"), wc_f), r=[b_fc], w=[b_fc2])
        fw.op("dve", lambda e: e.tensor_copy(pe_b.rearrange("p l c -> p (l c)"), pe_f), r=[b_fc], w=[b_fc2])
        VS = self.salloc(32 * 4 * 65, BF16).rearrange("p (k g d) -> p k g d", k=32, g=4)
        VW = self.salloc(32 * 4 * 65, BF16).rearrange("p (k g d) -> p k g d", k=32, g=4)
        b_VS = fw.buf("VSa", dma=True)
        b_VW = fw.buf("VWa", dma=True)
        for (Vt, bV, src, nm) in ((VS, b_VS, s_VS, "VS"), (VW, b_VW, s_VW, "VW")):
            fw.op("dve", lambda e, Vt=Vt: e.memset(Vt[:, :, :, 64:65], 1.0), w=[bV])
            for kt in range(32):
                fw.dma("sp", Vt[:, kt, :, 0:64], src[kt * 128:(kt + 1) * 128, :].rearrange("p (g d) -> p g d", g=4), r=[b_scr[nm]], w=[bV])
        XK = self.salloc(SEQ, BF16, parts=64)
        XV = self.salloc(SEQ, BF16, parts=64)
        b_XK = fw.buf("XK", dma=True)
        b_XV = fw.buf("XV", dma=True)
        Ksa = self.salloc(SEQ, BF16, parts=KR)
        Kwa = self.salloc(SEQ, BF16, parts=KR)
        Kca = self.salloc(256, BF16, parts=KR)
        b_Ksa = fw.buf("Ksa", dma=True)
        b_Kwa = fw.buf("Kwa", dma=True)
        b_Kca = fw.buf("Kca", dma=True)
        VCX = self.salloc(2 * 129, BF16).rearrange("p (t x) -> p t x", t=2)
        b_VCX = fw.buf("VCX", dma=True)
        pek = self.salloc(4, F32, parts=64)
        pev = self.salloc(64, BF16, parts=1)
        b_pe = fw.buf("pekv")
        qas = [self.salloc(4 * 128, BF16, parts=KR).rearrange("p (r q) -> p r q", r=4) for _ in range(2)]
        b_qas = [fw.buf("qa%d" % i, dma=True) for i in range(2)]
        pcs = [self.salloc(512, BF16) for _ in range(2)]
        b_pcs = [fw.buf("pc%d" % i) for i in range(2)]
        pss = [self.salloc(512, BF16) for _ in range(3)]
        b_pss = [fw.buf("ps_%d" % i) for i in range(3)]
        sc = self.salloc(64, F32)
        sc2 = self.salloc(64, F32)
        m8 = self.salloc(16, F32)
        small = self.salloc(64, F32)
        madd = self.salloc(64, BF16)
        msel = self.salloc(512, BF16, parts=64).rearrange("p (r q) -> p r q", r=4)
        b_sel = fw.buf("seltmp")
        b_msel = fw.buf("msel")
        gts = [self.salloc(48, F32) for _ in range(2)]
        b_gts = [fw.buf("gt%d" % i, dma=True) for i in range(2)]
        otile = self.salloc(256, F32)
        otb = self.salloc(256, BF16)
        b_ot_ = fw.buf("otile")
        ot1s = [self.salloc(256, BF16).rearrange("p (c t) -> p c t", c=2) for _ in range(2)]
        b_ot1s = [fw.buf("ot1_%d" % i, dma=True) for i in range(2)]
        if DBG:
            dbg_br = self.dout("dbg_br", [3, SEQ, 1024])
            dbgt = self.salloc(3 * 256, F32).rearrange("p (b f) -> p b f", b=3)
            b_dbgt = fw.buf("dbgt", dma=True)
        PS_S = (0, 1)
        PS_CA, PS_CB, PS_OS, PS_OW, PS_X, PS_T = 2, 3, 4, 5, 6, 7
        ucnt = 0
        for g in range(4):
            fw.dma("sp", XK, s_KC[g * 64:(g + 1) * 64, :], r=[b_scr["KC"]], w=[b_XK])
            fw.dma("sp", XV, s_KC[256 + g * 64:256 + (g + 1) * 64, :], r=[b_scr["KC"]], w=[b_XV])
            fw.dma("sp", Ksa[0:64, :], s_KS[g * 64:(g + 1) * 64, :], r=[b_scr["KS"]], w=[b_Ksa])
            fw.dma("sp", Ksa[64:KR, :], c_kaug, w=[b_Ksa])
            fw.dma("sp", Kwa[0:64, :], s_KW[g * 64:(g + 1) * 64, :], r=[b_scr["KW"]], w=[b_Kwa])
            fw.dma("sp", Kwa[64:KR, :], c_kaug, w=[b_Kwa])
            fw.dma("sp", Kca[64:KR, :], c_kaugc, w=[b_Kca])
            fw.dma("sp", VCX[:, :, 65:129], c_c2s, w=[b_VCX])
            fw.op("dve", lambda e: e.memset(VCX[:, :, 64:65], 1.0), w=[b_VCX])
            for l in range(32):
                fw.op("pe", lambda e, l=l: e.matmul(self.ps[PS_X][0:64, 0:1], lhsT=wc_b[:, l, 0, :], rhs=pe_b[:, l, 0:1], start=(l == 0), stop=(l == 31)), r=[b_fc2], w=[self.bps[PS_X]])
            for l in range(32):
                fw.op("pe", lambda e, l=l: e.matmul(self.ps[PS_X][0:1, 64:128], lhsT=pe_b[:, l, 1:2], rhs=wc_b[:, l, 1, :], start=(l == 0), stop=(l == 31)), r=[b_fc2], w=[self.bps[PS_X]])
            fw.op("dve", lambda e: e.tensor_copy(pek[:, 0:1], self.ps[PS_X][0:64, 0:1]), r=[self.bps[PS_X]], w=[b_pe])
            fw.op("dve", lambda e: e.tensor_copy(pev, self.ps[PS_X][0:1, 64:128]), r=[self.bps[PS_X]], w=[b_pe])
            for l in range(32):
                fw.op("pe", lambda e, l=l: e.matmul(self.ps[PS_X][0:64, 0:255], lhsT=wc_b[:, l, 0, :], rhs=XK[:, l:l + 16 * 254 + 1:16], start=(l == 0), stop=(l == 31)), r=[b_fc2, b_XK], w=[self.bps[PS_X]])
            fw.op("dve", lambda e: e.memset(Kca[0:64, 255:256], 0.0), w=[b_Kca])
            fw.op("dve", lambda e: e.tensor_scalar(out=Kca[0:64, 0:255], in0=self.ps[PS_X][0:64, 0:255], scalar1=pek[:, 0:1], scalar2=None, op0=ALU.add), r=[self.bps[PS_X], b_pe], w=[b_Kca])
            for nt_, nrows in ((0, 128), (1, 127)):
                for l in range(32):
                    a0 = 16 * 128 * nt_ + l
                    fw.op("pe", lambda e, l=l, a0=a0, nrows=nrows: e.matmul(self.ps[PS_X][0:nrows, 256:320], lhsT=XV[:, a0:a0 + 16 * (nrows - 1) + 1:16], rhs=wc_b[:, l, 1, :], start=(l == 0), stop=False), r=[b_fc2, b_XV], w=[self.bps[PS_X]])
                fw.op("pe", lambda e, nrows=nrows: e.matmul(self.ps[PS_X][0:nrows, 256:320], lhsT=ones_b[0:1, 0:nrows], rhs=pev, start=False, stop=True), r=[b_const2, b_pe], w=[self.bps[PS_X]])
                fw.op("dve", lambda e, nt_=nt_, nrows=nrows: e.tensor_copy(VCX[0:nrows, nt_, 0:64], self.ps[PS_X][0:nrows, 256:320]), r=[self.bps[PS_X]], w=[b_VCX])
            for qt in range(32):
                tq = slice(qt * 128, (qt + 1) * 128)
                qi_ = (g * 32 + qt) % 2
                qa, b_qa = qas[qi_], b_qas[qi_]
                fw.dma("sp", qa[0:64, :, :], s_QT1[g * 256:(g + 1) * 256, tq].rearrange("(r d) t -> d r t", d=64), r=[b_scr["QT1"]], w=[b_qa])
                fw.dma("sp", qa[64:KR, :, :], c_qaug1[g][:, :, tq], w=[b_qa])
                gt, b_gt = gts[qi_], b_gts[qi_]
                fw.dma("sp", gt, s_G[tq, :], r=[b_scr["G"]], w=[b_gt])
                ntl = [(0, 128)] + ([(1, 127)] if qt >= 16 else [])
                pcl = []
                for (nt_, nrows) in ntl:
                    si = ucnt % 2
                    ucnt += 1
                    Sb, bS = self.ps[PS_S[si]], self.bps[PS_S[si]]
                    mi = qt if nt_ == 0 else 17 + qt - 16
                    need_mask = (nt_ == 1) or (qt <= 16)
                    fw.op("pe", lambda e, Sb=Sb, nt_=nt_, nrows=nrows, qa=qa, need_mask=need_mask: e.matmul(Sb[0:nrows, :], lhsT=Kca[:, nt_ * 128:nt_ * 128 + nrows], rhs=qa.rearrange("p r q -> p (r q)"), start=True, stop=not need_mask, skip_group_check=True), r=[b_Kca, b_qa], w=[bS])
                    if need_mask:
                        for r_ in range(4):
                            cs = slice(r_ * 128, (r_ + 1) * 128)
                            fw.op("pe", lambda e, Sb=Sb, cs=cs, nrows=nrows, mi=mi, r_=r_: e.matmul(Sb[0:nrows, cs], lhsT=ident_b[0:nrows, 0:nrows], rhs=mc_t[0:nrows, mi, :], start=False, stop=(r_ == 3), skip_group_check=True), r=[b_const2, b_fc], w=[bS])
                    pc, b_pc = pcs[nt_], b_pcs[nt_]
                    fw.op("act", lambda e, Sb=Sb, pc=pc, nrows=nrows: e.activation(out=pc[0:nrows, :], in_=Sb[0:nrows, :], func=AF.Exp, scale=0.125), r=[bS], w=[b_pc])
                    pcl.append((nt_, nrows, pc, b_pc))
                for r_ in range(4):
                    bank = PS_CA if r_ < 2 else PS_CB
                    c0 = (r_ % 2) * 129
                    for n_, (nt_, nrows, pc, b_pc) in enumerate(pcl):
                        fw.op("pe", lambda e, bank=bank, c0=c0, nt_=nt_, nrows=nrows, pc=pc, r_=r_, n_=n_: e.matmul(self.ps[bank][:, c0:c0 + 129], lhsT=pc[0:nrows, r_ * 128:(r_ + 1) * 128], rhs=VCX[0:nrows, nt_, :], start=(n_ == 0), stop=(n_ == len(pcl) - 1)), r=[b_pc, b_VCX], w=[self.bps[bank]])
                for r_ in range(4):
                    bank = PS_CA if r_ < 2 else PS_CB
                    c0 = (r_ % 2) * 129
                    fw.op("dve", lambda e, bank=bank, c0=c0, r_=r_: e.tensor_scalar(out=small[:, r_:r_ + 1], in0=self.ps[bank][:, c0 + 64:c0 + 65], scalar1=1e-30, scalar2=None, op0=ALU.max), r=[self.bps[bank]], w=[b_sel])
                fw.op("dve", lambda e: e.reciprocal(out=small[:, 16:20], in_=small[:, 0:4]), r=[b_sel], w=[b_sel])
                for r_ in range(4):
                    bank = PS_CA if r_ < 2 else PS_CB
                    c0 = (r_ % 2) * 129
                    if r_ == 0:
                        fw.op("dve", lambda e, bank=bank, c0=c0: e.tensor_scalar(out=sc, in0=self.ps[bank][:, c0 + 65:c0 + 129], scalar1=small[:, 16:17], scalar2=None, op0=ALU.mult), r=[self.bps[bank], b_sel], w=[b_sel])
                    else:
                        fw.op("dve", lambda e, bank=bank, c0=c0, r_=r_: e.scalar_tensor_tensor(out=sc, in0=self.ps[bank][:, c0 + 65:c0 + 129], scalar=small[:, 16 + r_:17 + r_], in1=sc, op0=ALU.mult, op1=ALU.add), r=[self.bps[bank], b_sel], w=[b_sel])
                fw.op("dve", lambda e, qt=qt: e.tensor_tensor(out=sc, in0=sc, in1=tsel_t[:, qt, :], op=ALU.add), r=[b_sel, b_fc], w=[b_sel])
                fw.op("dve", lambda e: e.max(out=m8[:, 0:8], in_=sc), r=[b_sel], w=[b_sel])
                fw.op("dve", lambda e: e.match_replace(out=sc2, in_to_replace=m8[:, 0:8], in_values=sc, imm_value=-1e30), r=[b_sel], w=[b_sel])
                fw.op("dve", lambda e: e.max(out=m8[:, 8:16], in_=sc2), r=[b_sel], w=[b_sel])
                fw.op("dve", lambda e: e.tensor_scalar(out=m8[:, 0:1], in0=m8[:, 15:16], scalar1=-1e29, scalar2=None, op0=ALU.max), r=[b_sel], w=[b_sel])
                fw.op("dve", lambda e: e.tensor_scalar(out=madd, in0=sc, scalar1=m8[:, 0:1], scalar2=1.0, op0=ALU.is_ge, op1=ALU.subtract), r=[b_sel], w=[b_sel])
                pstb = self.ps[PS_T][:, :].bitcast(BF16)
                fw.op("pe", lambda e: e.transpose(pstb[0:64, 0:128], madd, ident_b), r=[b_sel, b_const2], w=[self.bps[PS_T]])
                fw.op("dve", lambda e: e.tensor_copy(msel, pstb[0:64, 0:128].rearrange("p (o q) -> p o q", o=1).to_broadcast([64, 4, 128])), r=[self.bps[PS_T]], w=[b_msel])
                for (br, Ka, b_Ka, Vt, bV, bank, k0) in ((2, Kwa, b_Kwa, VW, b_VW, PS_OW, max(0, qt - 4)), (1, Ksa, b_Ksa, VS, b_VS, PS_OS, 0)):
                    for kt in range(k0, qt + 1):
                        si = ucnt % 2
                        ucnt += 1
                        Sb, bS = self.ps[PS_S[si]], self.bps[PS_S[si]]
                        tk = slice(kt * 128, (kt + 1) * 128)
                        extra = []
                        if br == 1:
                            extra.append((e30_t[:, tk], msel.rearrange("p r q -> p (r q)"), [b_fc, b_msel]))
                        if kt == qt:
                            extra.append((ident_b, cmask4[:, 512:1024], [b_const, b_const2]))
                        if br == 2 and kt == qt - 4:
                            extra.append((ident_b, cmask4[:, 0:512], [b_const, b_const2]))
                        fw.op("pe", lambda e, Sb=Sb, Ka=Ka, tk=tk, qa=qa, ne=len(extra): e.matmul(Sb[:, :], lhsT=Ka[:, tk], rhs=qa.rearrange("p r q -> p (r q)"), start=True, stop=(ne == 0)), r=[b_Ka, b_qa], w=[bS])
                        for xi, (lh, rh, rb) in enumerate(extra):
                            fw.op("pe", lambda e, Sb=Sb, lh=lh, rh=rh, last_=(xi == len(extra) - 1): e.matmul(Sb[:, :], lhsT=lh, rhs=rh, start=False, stop=last_), r=rb, w=[bS])
                        pi_ = ucnt % 3
                        pp_, b_pp = pss[pi_], b_pss[pi_]
                        fw.op("act", lambda e, Sb=Sb, pp_=pp_: e.activation(out=pp_, in_=Sb[:, :], func=AF.Exp, scale=0.125), r=[bS], w=[b_pp])
                        for r_ in range(4):
                            fw.op("pe", lambda e, bank=bank, r_=r_, pp_=pp_, Vt=Vt, kt=kt, k0=k0, g=g, qt=qt: e.matmul(self.ps[bank][:, r_ * 65:(r_ + 1) * 65], lhsT=pp_[:, r_ * 128:(r_ + 1) * 128], rhs=Vt[:, kt, g, :], start=(kt == k0 and r_ == 0), stop=(kt == qt), skip_group_check=True), r=[b_pp, bV], w=[self.bps[bank]])
                    fw.op("dve", lambda e, bank=bank, br=br: e.tensor_scalar(out=small[:, br * 4:(br + 1) * 4], in0=self.ps[bank][:, 0:260].rearrange("p (r x) -> p r x", r=4)[:, :, 64], scalar1=1e-30, scalar2=None, op0=ALU.max), r=[self.bps[bank]], w=[b_sel])
                fw.op("dve", lambda e: e.reciprocal(out=small[:, 16:28], in_=small[:, 0:12]), r=[b_sel], w=[b_sel])
                fw.op("dve", lambda e, gt=gt, g=g: e.tensor_tensor(out=small[:, 32:44].rearrange("p (b r) -> p b r", b=3), in0=small[:, 16:28].rearrange("p (b r) -> p b r", b=3), in1=gt[:, g * 12:(g + 1) * 12].rearrange("p (r b) -> p b r", b=3), op=ALU.mult), r=[b_sel, b_gt], w=[b_sel])
                for r_ in range(4):
                    bankc = PS_CA if r_ < 2 else PS_CB
                    c0 = (r_ % 2) * 129
                    od = otile[:, r_ * 64:(r_ + 1) * 64]
                    fw.op("dve", lambda e, bankc=bankc, c0=c0, od=od, r_=r_: e.tensor_scalar(out=od, in0=self.ps[bankc][:, c0:c0 + 64], scalar1=small[:, 32 + r_:33 + r_], scalar2=None, op0=ALU.mult), r=[self.bps[bankc], b_sel], w=[b_ot_])
                    fw.op("dve", lambda e, od=od, r_=r_: e.scalar_tensor_tensor(out=od, in0=self.ps[PS_OS][:, r_ * 65:r_ * 65 + 64], scalar=small[:, 36 + r_:37 + r_], in1=od, op0=ALU.mult, op1=ALU.add), r=[self.bps[PS_OS], b_sel, b_ot_], w=[b_ot_])
                    fw.op("dve", lambda e, od=od, r_=r_: e.scalar_tensor_tensor(out=od, in0=self.ps[PS_OW][:, r_ * 65:r_ * 65 + 64], scalar=small[:, 40 + r_:41 + r_], in1=od, op0=ALU.mult, op1=ALU.add), r=[self.bps[PS_OW], b_sel, b_ot_], w=[b_ot_])
                if DBG:
                    for b_i in range(3):
                        for r_ in range(4):
                            if b_i == 0:
                                bk = PS_CA if r_ < 2 else PS_CB
                                src = self.ps[bk][:, (r_ % 2) * 129:(r_ % 2) * 129 + 64]
                            else:
                                bk = PS_OS if b_i == 1 else PS_OW
                                src = self.ps[bk][:, r_ * 65:r_ * 65 + 64]
                            fw.op("dve", lambda e, src=src, b_i=b_i, r_=r_: e.tensor_scalar(out=dbgt[:, b_i, r_ * 64:(r_ + 1) * 64], in0=src, scalar1=small[:, 16 + b_i * 4 + r_:17 + b_i * 4 + r_], scalar2=None, op0=ALU.mult), r=[self.bps[bk], b_sel], w=[b_dbgt])
                    fw.dma("act", dbg_br[:, tq, g * 256:(g + 1) * 256].rearrange("b t f -> t b f"), dbgt, r=[b_dbgt])
                fw.op("act", lambda e: e.copy(otb, otile), r=[b_ot_], w=[b_ot_])
                for c_ in range(2):
                    fw.op("pe", lambda e, c_=c_: e.transpose(pstb[:, 256 + c_ * 128:256 + (c_ + 1) * 128], otb[:, c_ * 128:(c_ + 1) * 128], ident_b), r=[b_ot_, b_const2], w=[self.bps[PS_T]])
                oi = qt % 2
                fw.op("dve", lambda e, oi=oi: e.tensor_copy(ot1s[oi], pstb[:, 256:512].rearrange("p (c t) -> p c t", c=2)), r=[self.bps[PS_T]], w=[b_ot1s[oi]])
                fw.dma("act", s_OT1[g * 256:(g + 1) * 256, tq].rearrange("(c p) t -> p c t", p=128), ot1s[oi], r=[b_ot1s[oi]], w=[b_scr["OT1"]])
        self.sp_off = mark
        fw.barrier()
        phaseD(1)
        if S < 5:
            return
        self.sample_path(locals())

    def sample_path(self, L):
        fw = self.fw
        g_ = lambda n: L[n]
        xs, cache_dil, st_conv, st_rnn, cache_win = g_("xs"), g_("cache_dil"), g_("st_conv"), g_("st_rnn"), g_("cache_win")
        b_scr, load_w, rmsnorm_T = g_("b_scr"), g_("load_w"), g_("rmsnorm_T")
        ident_b, ones_b, b_const, b_const2 = g_("ident_b"), g_("ones_b"), g_("b_const"), g_("b_const2")
        NT = 32
        mark = self.sp_off
        win, b_win = load_w(g_("wb_in_ab"), D, 10, "win_ab_s")
        xt = self.salloc(D, F32)
        b_xt = fw.buf("xts", dma=True)
        tmp = (self.salloc(D, F32), self.salloc(4, F32), self.salloc(D, BF16), fw.buf("nrm_tmp_s"))
        hT = self.salloc(8 * NT, BF16).rearrange("p (kc t) -> p kc t", kc=8)
        b_hT = fw.buf("hTs")
        stg_b = [self.salloc(NT, BF16) for _ in range(2)]
        stg_f = [self.salloc(NT, F32) for _ in range(2)]
        b_stg_b = [fw.buf("sstgb%d" % i, dma=True) for i in range(2)]
        b_stg_f = [fw.buf("sstgf%d" % i, dma=True) for i in range(2)]
        kv_f = self.salloc(1024, F32)
        kv_b = self.salloc(1024, BF16)
        b_kv_f = fw.buf("skvf", dma=True)
        b_kv_b = fw.buf("skvb", dma=True)
        fw.dma("sp", xt[0:NT, :], xs, w=[b_xt])
        rmsnorm_T(xt, b_xt, NT, 0, hT, b_hT, 0, tmp)
        cnt = 0
        for oc in list(range(0, 4)) + list(range(12, 20)):
            pi = cnt % 2
            cnt += 1
            nb, sub = oc // 2, oc % 2
            for kc in range(8):
                fw.op("pe", lambda e, kc=kc, nb=nb, sub=sub, pi=pi: e.matmul(self.ps[pi][:, 0:NT], lhsT=win[:, nb, kc, sub * 128:(sub + 1) * 128], rhs=hT[:, kc, :], start=(kc == 0), stop=(kc == 7)), r=[b_win, b_hT], w=[self.bps[pi]])
            si = oc % 2
            r0 = (oc % 4) * 128
            if oc < 4:
                fw.op("act", lambda e, si=si, pi=pi: e.copy(stg_b[si], self.ps[pi][:, 0:NT]), r=[self.bps[pi]], w=[b_stg_b[si]])
                fw.dma("act", g_("s_sQT0")[r0:r0 + 128, :], stg_b[si], r=[b_stg_b[si]], w=[b_scr["sQT0"]])
            else:
                fw.op("act", lambda e, si=si, pi=pi: e.copy(stg_f[si], self.ps[pi][:, 0:NT]), r=[self.bps[pi]], w=[b_stg_f[si]])
                dst, nm = (g_("s_sXR"), "sXR") if oc < 16 else (g_("s_sGT"), "sGT")
                fw.dma("act", dst[r0:r0 + 128, :], stg_f[si], r=[b_stg_f[si]], w=[b_scr[nm]])
        for half in range(2):
            pi = 2 + half
            for q2 in range(2):
                nb = 2 + half * 2 + q2
                for kc in range(8):
                    fw.op("pe", lambda e, kc=kc, nb=nb, q2=q2, pi=pi: e.matmul(self.ps[pi][0:NT, q2 * 256:(q2 + 1) * 256], lhsT=hT[:, kc, :], rhs=win[:, nb, kc, :], start=(kc == 0), stop=(kc == 7)), r=[b_win, b_hT], w=[self.bps[pi]])
            fw.op("act", lambda e, half=half, pi=pi: e.copy(kv_f[0:NT, half * 512:(half + 1) * 512], self.ps[pi][0:NT, :]), r=[self.bps[pi]], w=[b_kv_f])
        fw.op("dve", lambda e: e.tensor_copy(kv_b[0:NT, :], kv_f[0:NT, :]), r=[b_kv_f], w=[b_kv_b])
        fw.dma("act", g_("s_sKV"), kv_b[0:NT, :], r=[b_kv_b], w=[b_scr["sKV"]])
        o_dil_kv_s = g_("o_dil_kv_s")
        b_d2d = fw.buf("d2d", dma=True)
        for sq in range(4):
            fw.dma("act", o_dil_kv_s[sq, 2040:2048, :], kv_f[sq * 8:(sq + 1) * 8, :], r=[b_kv_f])
            for c4 in range(4):
                fw.dma("sp", o_dil_kv_s[sq, c4 * 510:(c4 + 1) * 510, :], cache_dil[sq, 8 + c4 * 510:8 + (c4 + 1) * 510, :], w=[b_d2d])
        self.sp_off = mark
        fw.barrier()

        mark = self.sp_off
        rp = self.salloc(36, F32)
        stc = self.salloc(48, F32).rearrange("p (c s k) -> p c s k", c=4, s=4)
        strn = self.salloc(16, F32).rearrange("p (c s) -> p c s", c=4)
        b_rp = fw.buf("rps", dma=True)
        fw.dma("sp", rp, g_("rnnp"), w=[b_rp])
        fw.dma("sp", stc.rearrange("p c s k -> p (c s k)"), st_conv, w=[b_rp])
        fw.dma("sp", strn.rearrange("p c s -> p (c s)"), st_rnn, w=[b_rp])
        bd_f = self.salloc(256, F32)
        bd_b = self.salloc(256, BF16)
        b_bdf = fw.buf("bdfs", dma=True)
        b_bdb = fw.buf("bdbs")
        xr = self.salloc(4 * 11, F32).rearrange("p (s t) -> p s t", s=4)
        b_xr = fw.buf("xrs", dma=True)
        xc = self.salloc(NT, F32)
        xcb = self.salloc(NT, BF16)
        t1 = self.salloc(NT, F32)
        t2 = self.salloc(NT, F32)
        t3 = self.salloc(NT, F32)
        gtl = self.salloc(NT, F32)
        ob = self.salloc(NT, BF16)
        sp_t = self.salloc(4, F32)
        cvo = self.salloc(12, F32).rearrange("p (s k) -> p s k", s=4)
        hl = self.salloc(4, F32)
        b_w = fw.buf("rnn_s_work")
        b_gtl = fw.buf("gtls", dma=True)
        b_ob = fw.buf("obs", dma=True)
        b_cvo = fw.buf("cvos", dma=True)
        v3 = lambda a: a.rearrange("p (s t) -> p s t", s=4)
        for c in range(4):
            pr = rp[:, c * 9:(c + 1) * 9]
            fw.op("dve", lambda e: e.memset(bd_f, 0.0), w=[b_bdf])
            for gi, gw in enumerate((g_("gaw"), g_("gxw"))):
                fw.dma("sp", bd_f[0:64, gi * 128:gi * 128 + 64], gw[2 * c], w=[b_bdf])
                fw.dma("sp", bd_f[64:128, gi * 128 + 64:gi * 128 + 128], gw[2 * c + 1], w=[b_bdf])
            fw.op("dve", lambda e: e.tensor_copy(bd_b, bd_f), r=[b_bdf], w=[b_bdb])
            fw.dma("sp", xr[:, :, 3:11], g_("s_sXR")[c * 128:(c + 1) * 128, :].rearrange("p (s t) -> p s t", s=4), r=[b_scr["sXR"]], w=[b_xr])
            fw.op("dve", lambda e, c=c: e.tensor_copy(xr[:, :, 0:3], stc[:, c, :, :]), r=[b_rp, b_xr], w=[b_xr])
            fw.op("dve", lambda e: e.tensor_copy(cvo, xr[:, :, 8:11]), r=[b_xr], w=[b_cvo])
            fw.dma("act", g_("o_conv_s")[:, :, c * 128:(c + 1) * 128].rearrange("s k p -> p s k"), cvo, r=[b_cvo], allow_slow_non_contiguous=True)
            fw.op("dve", lambda e, pr=pr: e.tensor_scalar(out=v3(xc), in0=xr[:, :, 0:8], scalar1=pr[:, 0:1], scalar2=pr[:, 4:5], op0=ALU.mult, op1=ALU.add), r=[b_xr, b_rp], w=[b_w])
            for k in range(1, 4):
                fw.op("dve", lambda e, pr=pr, k=k: e.scalar_tensor_tensor(out=v3(xc), in0=xr[:, :, k:k + 8], scalar=pr[:, k:k + 1], in1=v3(xc), op0=ALU.mult, op1=ALU.add), r=[b_xr, b_rp, b_w], w=[b_w])
            fw.op("dve", lambda e: e.tensor_copy(xcb, xc), r=[b_w], w=[b_w])
            fw.op("act", lambda e, pr=pr: e.activation(out=sp_t[:, 0:1], in_=pr[:, 7:8], func=AF.Exp, scale=-1.0), r=[b_rp], w=[b_w])
            fw.op("act", lambda e: e.activation(out=sp_t[:, 1:2], in_=sp_t[:, 0:1], func=AF.Ln, bias=1.0), r=[b_w], w=[b_w])
            fw.op("dve", lambda e: e.tensor_scalar(out=sp_t[:, 2:3], in0=sp_t[:, 1:2], scalar1=-8.0, scalar2=None, op0=ALU.mult), r=[b_w], w=[b_w])
            fw.op("dve", lambda e: e.tensor_scalar(out=sp_t[:, 3:4], in0=sp_t[:, 1:2], scalar1=-16.0, scalar2=None, op0=ALU.mult), r=[b_w], w=[b_w])
            for gi, tt in enumerate((t1, t2)):
                fw.op("pe", lambda e, gi=gi: e.matmul(self.ps[gi][:, 0:NT], lhsT=bd_b[:, gi * 128:(gi + 1) * 128], rhs=xcb, start=True, stop=True), r=[b_bdb, b_w], w=[self.bps[gi]])
                fw.op("act", lambda e, gi=gi, tt=tt, pr=pr: e.activation(out=tt, in_=self.ps[gi][:, 0:NT], func=AF.Sigmoid, bias=pr[:, 5 + gi:6 + gi]), r=[self.bps[gi], b_rp], w=[b_w])
            fw.op("act", lambda e: e.activation(out=t3, in_=t1, func=AF.Exp, scale=sp_t[:, 3:4]), r=[b_w], w=[b_w])
            fw.op("act", lambda e: e.activation(out=t1, in_=t1, func=AF.Exp, scale=sp_t[:, 2:3]), r=[b_w], w=[b_w])
            fw.op("dve", lambda e: e.tensor_scalar(out=t3, in0=t3, scalar1=-1.0, scalar2=1.0, op0=ALU.mult, op1=ALU.add), r=[b_w], w=[b_w])
            fw.op("dve", lambda e: e.tensor_scalar(out=t3, in0=t3, scalar1=0.0, scalar2=None, op0=ALU.max), r=[b_w], w=[b_w])
            fw.op("act", lambda e: e.activation(out=t3, in_=t3, func=AF.Sqrt), r=[b_w], w=[b_w])
            fw.op("dve", lambda e: e.tensor_tensor(out=t3, in0=t3, in1=t2, op=ALU.mult), r=[b_w], w=[b_w])
            fw.op("dve", lambda e: e.tensor_tensor(out=t3, in0=t3, in1=xc, op=ALU.mult), r=[b_w], w=[b_w])
            for sq in range(4):
                fw.op("dve", lambda e, sq=sq, c=c: e.tensor_tensor_scan(out=t2[:, sq * 8:(sq + 1) * 8], data0=t1[:, sq * 8:(sq + 1) * 8], data1=t3[:, sq * 8:(sq + 1) * 8], initial=strn[:, c, sq:sq + 1], op0=ALU.mult, op1=ALU.add), r=[b_w, b_rp], w=[b_w])
            fw.op("dve", lambda e: e.tensor_copy(cvo[:, :, 0], v3(t2)[:, :, 7]), r=[b_w, b_cvo], w=[b_cvo])
            fw.dma("act", g_("o_rnn_s")[:, c * 128:(c + 1) * 128].rearrange("s p -> p s"), cvo[:, :, 0], r=[b_cvo], allow_slow_non_contiguous=True)
            fw.dma("sp", gtl, g_("s_sGT")[c * 128:(c + 1) * 128, :], r=[b_scr["sGT"]], w=[b_gtl])
            fw.op("act", lambda e: e.activation(out=gtl, in_=gtl, func=AF.Gelu), r=[b_gtl], w=[b_gtl])
            fw.op("dve", lambda e: e.tensor_tensor(out=ob, in0=t2, in1=gtl, op=ALU.mult), r=[b_w, b_gtl], w=[b_ob])
            fw.dma("act", g_("s_sOT0")[512 + c * 128:512 + (c + 1) * 128, :], ob, r=[b_ob], w=[b_scr["sOT0"]])
        self.sp_off = mark
        fw.barrier()

        mark = self.sp_off
        c_cnt_s, c_kaug_s, c_qaug0_s = g_("c_cnt_s"), g_("c_kaug_s"), g_("c_qaug0_s")
        cnt_t = self.salloc(17 * 8, BF16).rearrange("p (k q) -> p k q", k=17)
        kaug_t = self.salloc(17 * 128, BF16, parts=KR).rearrange("p (k t) -> p k t", k=17)
        b_cc = fw.buf("sc_const", dma=True)
        fw.dma("sp", cnt_t, c_cnt_s, w=[b_cc])
        fw.dma("sp", kaug_t[64:KR, :, :], c_kaug_s.rearrange("a (k t) -> a k t", k=17), w=[b_cc])
        cts = [self.salloc(1024, F32) for _ in range(2)]
        b_cts = [fw.buf("ct%d" % i, dma=True) for i in range(2)]
        ctbs = [self.salloc(1024, BF16) for _ in range(2)]
        b_ctbs = [fw.buf("ctb%d" % i, dma=True) for i in range(2)]
        KA = [self.salloc(8 * 128, BF16, parts=KR).rearrange("p (h t) -> p h t", h=8) for _ in range(2)]
        b_KA = [fw.buf("KA%d" % i) for i in range(2)]
        QA = self.salloc(64, BF16, parts=KR).rearrange("p (h q) -> p h q", h=8)
        b_QA = fw.buf("QAs", dma=True)
        Pf = [self.salloc(64, F32) for _ in range(2)]
        Pb = [self.salloc(64, BF16).rearrange("p (h q) -> p h q", h=8) for _ in range(2)]
        b_P = [fw.buf("Ps%d" % i) for i in range(2)]
        Va = [self.salloc(8 * 65, BF16).rearrange("p (h d) -> p h d", h=8) for _ in range(2)]
        b_Va = [fw.buf("Va%d" % i) for i in range(2)]
        for i in range(2):
            fw.op("dve", lambda e, i=i: e.memset(Va[i][:, :, 64:65], 1.0), w=[b_Va[i]])
        osm = self.salloc(16, F32)
        of = self.salloc(512, F32)
        obf = self.salloc(512, BF16)
        oT = self.salloc(32, BF16).rearrange("p (c q) -> p c q", c=4)
        b_o = fw.buf("so_work")
        b_oT = fw.buf("soT", dma=True)
        pstb = self.ps[7][:, :].bitcast(BF16)
        PA, PB_ = 4, 5
        u = 0
        for sq in range(4):
            fw.dma("sp", QA[0:64, :, :], g_("s_sQT0")[:, sq * 8:(sq + 1) * 8].rearrange("(h d) t -> d h t", d=64), r=[b_scr["sQT0"]], w=[b_QA])
            fw.dma("sp", QA[64:KR, :, :], c_qaug0_s, w=[b_QA])
            for kt in range(17):
                i = u % 2
                u += 1
                nk = 128 if kt < 16 else 8
                ctb, b_ctb = ctbs[i], b_ctbs[i]
                if kt < 16:
                    fw.dma("sp", cts[i], cache_dil[sq, kt * 128:(kt + 1) * 128, :], w=[b_cts[i]])
                    fw.op("dve", lambda e, i=i, ctb=ctb: e.tensor_copy(ctb[:, 0:512], cts[i][:, 0:512]), r=[b_cts[i]], w=[b_ctb])
                    fw.op("pool", lambda e, i=i, ctb=ctb: e.tensor_copy(ctb[:, 512:1024], cts[i][:, 512:1024]), r=[b_cts[i]], w=[b_ctb])
                else:
                    fw.dma("sp", ctb[0:8, :], g_("s_sKV")[sq * 8:(sq + 1) * 8, :], r=[b_scr["sKV"]], w=[b_ctb])
                for c4 in range(4):
                    fw.op("pe", lambda e, c4=c4, ctb=ctb, nk=nk: e.transpose(pstb[:, c4 * 128:c4 * 128 + nk], ctb[0:nk, c4 * 128:(c4 + 1) * 128], ident_b[0:nk, 0:nk]), r=[b_ctb, b_const2], w=[self.bps[7]])
                ka, b_ka = KA[i], b_KA[i]
                pv = pstb[:, 0:512].rearrange("p (c t) -> p c t", c=4)
                fw.op("dve", lambda e, ka=ka, pv=pv, nk=nk: e.tensor_copy(ka[0:64, 0:8:2, 0:nk], pv[0:64, :, 0:nk]), r=[self.bps[7]], w=[b_ka])
                fw.op("dve", lambda e, ka=ka, pv=pv, nk=nk: e.tensor_copy(ka[0:64, 1:8:2, 0:nk], pv[64:128, :, 0:nk]), r=[self.bps[7]], w=[b_ka])
                fw.op("dve", lambda e, ka=ka, kt=kt: e.tensor_copy(ka[64:KR, :, :], kaug_t[64:KR, kt:kt + 1, :].to_broadcast([NAUG, 8, 128])), r=[b_cc], w=[b_ka])
                va, b_va = Va[i], b_Va[i]
                fw.op("act", lambda e, va=va, ctb=ctb, nk=nk: e.copy(va[0:nk, :, 0:64], ctb[0:nk, 512:1024].rearrange("p (h d) -> p h d", h=8)), r=[b_ctb], w=[b_va])
                Sb, bS = self.ps[i], self.bps[i]
                for h in range(8):
                    fw.op("pe", lambda e, Sb=Sb, h=h, ka=ka, nk=nk: e.matmul(Sb[0:nk, h * 8:(h + 1) * 8], lhsT=ka[:, h, 0:nk], rhs=QA[:, h, :], start=True, stop=True), r=[b_ka, b_QA], w=[bS])
                fw.op("act", lambda e, Sb=Sb, i=i, nk=nk: e.activation(out=Pf[i][0:nk, :], in_=Sb[0:nk, 0:64], func=AF.Exp, scale=0.125), r=[bS], w=[b_P[i]])
                fw.op("dve", lambda e, i=i, kt=kt, nk=nk: e.tensor_tensor(out=Pb[i][0:nk, :, :], in0=Pf[i][0:nk, :].rearrange("p (h q) -> p h q", h=8), in1=cnt_t[0:nk, kt:kt + 1, :].to_broadcast([nk, 8, 8]), op=ALU.mult), r=[b_P[i], b_cc], w=[b_P[i]])
                for h in range(8):
                    bank = PA if h < 4 else PB_
                    hh = h % 4
                    fw.op("pe", lambda e, bank=bank, hh=hh, h=h, i=i, va=va, nk=nk, kt=kt: e.matmul(self.ps[bank][0:8, hh * 65:(hh + 1) * 65], lhsT=Pb[i][0:nk, h, :], rhs=va[0:nk, h, :], start=(kt == 0 and hh == 0), stop=(kt == 16), skip_group_check=True), r=[b_P[i], b_va], w=[self.bps[bank]])
            for half in range(2):
                bank = PA if half == 0 else PB_
                ov = self.ps[bank][0:8, 0:260].rearrange("p (h x) -> p h x", h=4)
                fw.op("dve", lambda e, ov=ov, half=half: e.reciprocal(out=osm[0:8, half * 4:(half + 1) * 4], in_=ov[:, :, 64]), r=[self.bps[bank]], w=[b_o])
                fw.op("dve", lambda e, ov=ov, half=half: e.tensor_tensor(out=of[0:8, half * 256:(half + 1) * 256].rearrange("p (h d) -> p h d", h=4), in0=ov[:, :, 0:64], in1=osm[0:8, half * 4:(half + 1) * 4].rearrange("p (h o) -> p h o", o=1).to_broadcast([8, 4, 64]), op=ALU.mult), r=[self.bps[bank], b_o], w=[b_o])
            fw.op("act", lambda e: e.copy(obf[0:8, :], of[0:8, :]), r=[b_o], w=[b_o])
            for c4 in range(4):
                fw.op("pe", lambda e, c4=c4: e.transpose(pstb[:, 512 + c4 * 8:512 + (c4 + 1) * 8], obf[0:8, c4 * 128:(c4 + 1) * 128], ident_b[0:8, 0:8]), r=[b_o, b_const2], w=[self.bps[7]])
            fw.op("dve", lambda e: e.tensor_copy(oT, pstb[:, 512:544].rearrange("p (c q) -> p c q", c=4)), r=[self.bps[7]], w=[b_oT])
            fw.dma("act", g_("s_sOT0")[0:512, sq * 8:(sq + 1) * 8].rearrange("(c p) t -> p c t", p=128), oT, r=[b_oT], w=[b_scr["sOT0"]])
        self.sp_off = mark
        fw.barrier()
        self.phaseD_s(L, 0)
        if L["S"] >= 6:
            self.sample_nsa(L)
            self.phaseD_s(L, 1)

    def sample_nsa(self, L):
        fw = self.fw
        g_ = lambda n: L[n]
        b_scr = g_("b_scr")
        ident_b, ones_b, b_const, b_const2 = g_("ident_b"), g_("ones_b"), g_("b_const"), g_("b_const2")
        pstb = self.ps[7][:, :].bitcast(BF16)

        def pool_gather(out, pool, idx_col, r, w, dsem):
            o = Op("pool", (lambda e: e.indirect_dma_start(out=out, out_offset=None, in_=pool, in_offset=bass.IndirectOffsetOnAxis(ap=idx_col, axis=0))), None, isdma=True, dsem=dsem)
            o.deps = fw._deps(o.key, r, w)
            dsem.count += 16
            o.dval = dsem.count
            fw.ops["pool"].append(o)
            fw._commit(o, r, w)

        mark = self.sp_off
        pti = self.salloc(256, I32)
        ptf = self.salloc(256, F32)
        iot = self.salloc(1, F32)
        idx = self.salloc(256, I32)
        b_pt = fw.buf("pt", dma=True)
        b_idx = fw.buf("idx")
        fw.dma("sp", pti, g_("ptab")[0:1, :].to_broadcast([128, 256]), w=[b_pt])
        fw.dma("sp", iot, g_("c_iota_p"), w=[b_pt])
        fw.op("dve", lambda e: e.tensor_copy(ptf, pti), r=[b_pt], w=[b_idx])
        fw.op("dve", lambda e: e.tensor_scalar(out=ptf, in0=ptf, scalar1=128.0, scalar2=iot[:, 0:1], op0=ALU.mult, op1=ALU.add), r=[b_idx, b_pt], w=[b_idx])
        fw.op("dve", lambda e: e.tensor_copy(idx, ptf), r=[b_idx], w=[b_idx])
        cts = [self.salloc(512, F32) for _ in range(2)]
        b_cts = [fw.buf("gct%d" % i, dma=True) for i in range(2)]
        gsems = [fw.new_dsem(sw=True) for _ in range(2)]
        ctbs = [self.salloc(512, BF16) for _ in range(2)]
        b_ctbs = [fw.buf("gctb%d" % i, dma=True) for i in range(2)]
        xts = [self.salloc(512, BF16).rearrange("p (c t) -> p c t", c=4) for _ in range(2)]
        b_xts = [fw.buf("gxt%d" % i, dma=True) for i in range(2)]
        u = 0
        for sq in range(4):
            for (cname, pool, nch, ntile, newsrc, nmnew) in (("cmp", g_("pool_cmp"), 4, 65, g_("o_cmp_kv_s"), "ocmp"), ("sel", g_("pool_sel"), 2, 65, g_("o_sel_kv_s"), "osel"), ("win", None, 2, 5, g_("o_win_kv_s"), "owin")):
                for kt in range(ntile):
                    i = u % 2
                    u += 1
                    last = (kt == ntile - 1)
                    nk = 8 if last else 128
                    if last:
                        src = newsrc[sq * 8:(sq + 1) * 8, :] if cname != "win" else newsrc[sq, 504:512, :]
                        fw.dma("sp", cts[i][0:8, :], src, r=[b_scr[nmnew]], w=[b_cts[i]])
                    elif cname == "win":
                        fw.dma("sp", cts[i], g_("cache_win")[sq, kt * 128:(kt + 1) * 128, :], w=[b_cts[i]])
                    else:
                        pool_gather(cts[i], pool, idx[:, sq * 64 + kt:sq * 64 + kt + 1], [b_idx], [b_cts[i]], gsems[i])
                    fw.op("dve", lambda e, i=i, nk=nk: e.tensor_copy(ctbs[i][0:nk, :], cts[i][0:nk, :]), r=[b_cts[i]], w=[b_ctbs[i]])
                    for c4 in range(nch):
                        fw.op("pe", lambda e, c4=c4, i=i, nk=nk: e.transpose(pstb[:, c4 * 128:c4 * 128 + nk], ctbs[i][0:nk, c4 * 128:(c4 + 1) * 128], ident_b[0:nk, 0:nk]), r=[b_ctbs[i], b_const2], w=[self.bps[7]])
                    fw.op("act", lambda e, i=i, nk=nk, nch=nch: e.copy(xts[i][:, 0:nch, 0:nk], pstb[:, 0:512].rearrange("p (c t) -> p c t", c=4)[:, 0:nch, 0:nk]), r=[self.bps[7]], w=[b_xts[i]])
                    dT = {"cmp": g_("s_cT_cmp"), "sel": g_("s_cT_sel"), "win": g_("s_cT_win")}[cname]
                    fw.dma("act", dT[sq, :, kt * 128:kt * 128 + nk].rearrange("(c p) t -> p c t", p=128), xts[i][:, 0:nch, 0:nk], r=[b_xts[i]], w=[b_scr["cT_" + cname]])
                    if cname != "cmp":
                        dV = g_("s_cV_sel") if cname == "sel" else g_("s_cV_win")
                        fw.dma("act", dV[sq, kt * 128:kt * 128 + nk, :], ctbs[i][0:nk, 256:512], r=[b_ctbs[i]], w=[b_scr["cV_" + cname]])
        self.sp_off = mark
        fw.barrier()

        mark = self.sp_off
        wc_f = self.salloc(4096, F32, parts=64)
        wc_b = self.salloc(4096, BF16, parts=64).rearrange("p (l c e) -> p l c e", l=32, c=2)
        pe_f = self.salloc(64, F32, parts=64)
        pe_b = self.salloc(64, BF16, parts=64).rearrange("p (l c) -> p l c", c=2)
        e30s = self.salloc(8192, BF16)
        c2s_t = self.salloc(4 * 129, BF16).rearrange("p (t x) -> p t x", t=4)
        tsel_t = self.salloc(129, F32)
        cm8 = self.salloc(32, BF16)
        wm0 = self.salloc(32, BF16)
        b_fc = fw.buf("sfconst", dma=True)
        b_fc2 = fw.buf("sfconst2")
        fw.dma("sp", wc_f, g_("w_cmp_t"), w=[b_fc])
        fw.dma("sp", pe_f, g_("pe_cmp_t"), w=[b_fc])
        fw.dma("sp", e30s, g_("c_e30s"), w=[b_fc])
        fw.dma("sp", c2s_t, g_("c_c2s_s"), w=[b_fc])
        fw.dma("sp", tsel_t[0:8, :], g_("c_tsel_s"), w=[b_fc])
        fw.dma("sp", cm8[0:8, :], g_("c_cm8"), w=[b_fc])
        fw.dma("sp", wm0, g_("c_wm0"), w=[b_fc])
        fw.op("dve", lambda e: e.tensor_copy(wc_b.rearrange("p l c e -> p (l c e)"), wc_f), r=[b_fc], w=[b_fc2])
        fw.op("dve", lambda e: e.tensor_copy(pe_b.rearrange("p l c -> p (l c)"), pe_f), r=[b_fc], w=[b_fc2])
        XK = self.salloc(8192, BF16, parts=64)
        XV = self.salloc(8192, BF16, parts=64)
        b_XK = fw.buf("sXK", dma=True)
        b_XV = fw.buf("sXV", dma=True)
        Ksa = self.salloc(8320, BF16, parts=KR)
        Kwa = self.salloc(640, BF16, parts=KR)
        Kca = self.salloc(512, BF16, parts=KR)
        b_Ksa, b_Kwa, b_Kca = fw.buf("sKsa", dma=True), fw.buf("sKwa", dma=True), fw.buf("sKca", dma=True)
        fw.dma("sp", Ksa[64:KR, :], g_("c_kaug_ss"), w=[b_Ksa])
        fw.dma("sp", Kwa[64:KR, :], g_("c_kaug_sw"), w=[b_Kwa])
        fw.dma("sp", Kca[64:KR, :], g_("c_kaugc_s"), w=[b_Kca])
        VSa = self.salloc(65 * 4 * 65, BF16).rearrange("p (k g d) -> p k g d", k=65, g=4)
        VWa = self.salloc(5 * 4 * 65, BF16).rearrange("p (k g d) -> p k g d", k=5, g=4)
        b_VSa, b_VWa = fw.buf("sVSa", dma=True), fw.buf("sVWa", dma=True)
        fw.op("dve", lambda e: e.memset(VSa[:, :, :, 64:65], 1.0), w=[b_VSa])
        fw.op("dve", lambda e: e.memset(VWa[:, :, :, 64:65], 1.0), w=[b_VWa])
        VCX = self.salloc(4 * 194, BF16).rearrange("p (t x) -> p t x", t=4)
        b_VCX = fw.buf("sVCX")
        fw.op("dve", lambda e: e.memset(VCX[:, :, 64:65], 1.0), w=[b_VCX])
        fw.op("dve", lambda e: e.tensor_copy(VCX[:, :, 65:194], c2s_t), r=[b_fc], w=[b_VCX])
        vcT = self.salloc(512, BF16, parts=64)
        pekv = self.salloc(4, F32, parts=64)
        b_pe = fw.buf("spekv")
        QA = self.salloc(32, BF16, parts=KR).rearrange("p (r q) -> p r q", r=4)
        b_QA = fw.buf("sQA1", dma=True)
        Pc = self.salloc(128, BF16).rearrange("p (t x) -> p t x", t=4)
        b_Pc = fw.buf("sPc")
        Pss = [self.salloc(32, BF16) for _ in range(3)]
        b_Pss = [fw.buf("sPs%d" % i) for i in range(3)]
        sc = self.salloc(129, F32)
        sc2 = self.salloc(129, F32)
        m8 = self.salloc(16, F32)
        small = self.salloc(64, F32)
        madd = self.salloc(128, BF16)
        msel = self.salloc(32, BF16).rearrange("p (r q) -> p r q", r=4)
        b_sel, b_msel = fw.buf("sseltmp"), fw.buf("smsel")
        gt = self.salloc(48, F32)
        b_gt = fw.buf("sgt", dma=True)
        otile = self.salloc(256, F32)
        otb = self.salloc(256, BF16)
        b_ot_ = fw.buf("sotile")
        ot1 = self.salloc(16, BF16).rearrange("p (c t) -> p c t", c=2)
        b_ot1 = fw.buf("sot1", dma=True)
        PS_S = (0, 1)
        PS_CA, PS_CB, PS_OS, PS_OW, PS_X, PS_T = 2, 3, 4, 5, 6, 7
        NROWS = (128, 128, 128, 127)
        ucnt = 0
        for sq in range(4):
            for kt in range(65):
                nk = 128 if kt < 64 else 8
                fw.dma("sp", VSa[0:nk, kt, :, 0:64], g_("s_cV_sel")[sq, kt * 128:kt * 128 + nk, :].rearrange("p (g d) -> p g d", g=4), r=[b_scr["cV_sel"]], w=[b_VSa])
            for kt in range(5):
                nk = 128 if kt < 4 else 8
                fw.dma("sp", VWa[0:nk, kt, :, 0:64], g_("s_cV_win")[sq, kt * 128:kt * 128 + nk, :].rearrange("p (g d) -> p g d", g=4), r=[b_scr["cV_win"]], w=[b_VWa])
            fw.dma("sp", gt[0:8, :], g_("s_sG")[sq * 8:(sq + 1) * 8, :], r=[b_scr["sG"]], w=[b_gt])
            for g in range(4):
                fw.dma("sp", XK, g_("s_cT_cmp")[sq, g * 64:(g + 1) * 64, 0:8192], r=[b_scr["cT_cmp"]], w=[b_XK])
                fw.dma("sp", XV, g_("s_cT_cmp")[sq, 256 + g * 64:256 + (g + 1) * 64, 0:8192], r=[b_scr["cT_cmp"]], w=[b_XV])
                fw.dma("sp", Ksa[0:64, 0:8200], g_("s_cT_sel")[sq, g * 64:(g + 1) * 64, 0:8200], r=[b_scr["cT_sel"]], w=[b_Ksa])
                fw.dma("sp", Kwa[0:64, 0:520], g_("s_cT_win")[sq, g * 64:(g + 1) * 64, 0:520], r=[b_scr["cT_win"]], w=[b_Kwa])
                fw.dma("sp", QA[0:64, :, :], g_("s_sQT1")[g * 256:(g + 1) * 256, sq * 8:(sq + 1) * 8].rearrange("(r d) t -> d r t", d=64), r=[b_scr["sQT1"]], w=[b_QA])
                fw.dma("sp", QA[64:KR, :, :], g_("c_qaug1_s")[g], w=[b_QA])
                QA32 = QA.rearrange("p r q -> p (r q)")
                for c_ in range(2):
                    for l in range(32):
                        fw.op("pe", lambda e, l=l, c_=c_: e.matmul(self.ps[PS_X][0:64, c_:c_ + 1], lhsT=wc_b[:, l, c_, :], rhs=pe_b[:, l, c_:c_ + 1], start=(l == 0), stop=(l == 31)), r=[b_fc2], w=[self.bps[PS_X]])
                fw.op("dve", lambda e: e.tensor_copy(pekv[:, 0:2], self.ps[PS_X][0:64, 0:2]), r=[self.bps[PS_X]], w=[b_pe])
                for c_, (X, bX, bank) in enumerate(((XK, b_XK, PS_X), (XV, b_XV, PS_T))):
                    for l in range(32):
                        fw.op("pe", lambda e, l=l, c_=c_, X=X, bank=bank: e.matmul(self.ps[bank][0:64, 0:511], lhsT=wc_b[:, l, c_, :], rhs=X[:, l:l + 16 * 510 + 1:16], start=(l == 0), stop=(l == 31)), r=[b_fc2, bX], w=[self.bps[bank]])
                fw.op("dve", lambda e: e.memset(Kca[0:64, 511:512], 0.0), w=[b_Kca])
                fw.op("dve", lambda e: e.tensor_scalar(out=Kca[0:64, 0:511], in0=self.ps[PS_X][0:64, 0:511], scalar1=pekv[:, 0:1], scalar2=None, op0=ALU.add), r=[self.bps[PS_X], b_pe], w=[b_Kca])
                fw.op("dve", lambda e: e.memset(vcT[:, 511:512], 0.0), w=[b_pe])
                fw.op("dve", lambda e: e.tensor_scalar(out=vcT[:, 0:511], in0=self.ps[PS_T][0:64, 0:511], scalar1=pekv[:, 1:2], scalar2=None, op0=ALU.add), r=[self.bps[PS_T], b_pe], w=[b_pe])
                for nt_ in range(4):
                    fw.op("pe", lambda e, nt_=nt_: e.transpose(pstb[:, nt_ * 64:(nt_ + 1) * 64], vcT[:, nt_ * 128:(nt_ + 1) * 128], ident_b[0:64, 0:64]), r=[b_pe, b_const2], w=[self.bps[PS_T]])
                fw.op("dve", lambda e: e.tensor_copy(VCX[:, :, 0:64], pstb[:, 0:256].rearrange("p (t d) -> p t d", t=4)), r=[self.bps[PS_T]], w=[b_VCX])
                Sb, bS = self.ps[PS_S[0]], self.bps[PS_S[0]]
                for nt_ in range(4):
                    nr = NROWS[nt_]
                    fw.op("pe", lambda e, nt_=nt_, nr=nr, Sb=Sb: e.matmul(Sb[0:nr, nt_ * 32:(nt_ + 1) * 32], lhsT=Kca[:, nt_ * 128:nt_ * 128 + nr], rhs=QA32, start=True, stop=True), r=[b_Kca, b_QA], w=[bS])
                fw.op("act", lambda e, Sb=Sb: e.activation(out=Pc[:, 0:3, :], in_=Sb[:, 0:96].rearrange("p (t x) -> p t x", t=3), func=AF.Exp, scale=0.125), r=[bS], w=[b_Pc])
                fw.op("act", lambda e, Sb=Sb: e.activation(out=Pc[0:127, 3, :], in_=Sb[0:127, 96:128], func=AF.Exp, scale=0.125), r=[bS], w=[b_Pc])
                for r_ in range(4):
                    bank = PS_CA if r_ < 2 else PS_CB
                    c0 = (r_ % 2) * 194
                    for nt_ in range(4):
                        nr = NROWS[nt_]
                        fw.op("pe", lambda e, bank=bank, c0=c0, nt_=nt_, nr=nr, r_=r_: e.matmul(self.ps[bank][0:8, c0:c0 + 194], lhsT=Pc[0:nr, nt_, r_ * 8:(r_ + 1) * 8], rhs=VCX[0:nr, nt_, :], start=(nt_ == 0), stop=(nt_ == 3)), r=[b_Pc, b_VCX], w=[self.bps[bank]])
                for r_ in range(4):
                    bank = PS_CA if r_ < 2 else PS_CB
                    c0 = (r_ % 2) * 194
                    fw.op("dve", lambda e, bank=bank, c0=c0, r_=r_: e.tensor_scalar(out=small[0:8, r_:r_ + 1], in0=self.ps[bank][0:8, c0 + 64:c0 + 65], scalar1=1e-30, scalar2=None, op0=ALU.max), r=[self.bps[bank]], w=[b_sel])
                fw.op("dve", lambda e: e.reciprocal(out=small[0:8, 16:20], in_=small[0:8, 0:4]), r=[b_sel], w=[b_sel])
                for r_ in range(4):
                    bank = PS_CA if r_ < 2 else PS_CB
                    c0 = (r_ % 2) * 194
                    if r_ == 0:
                        fw.op("dve", lambda e, bank=bank, c0=c0: e.tensor_scalar(out=sc[0:8, :], in0=self.ps[bank][0:8, c0 + 65:c0 + 194], scalar1=small[0:8, 16:17], scalar2=None, op0=ALU.mult), r=[self.bps[bank], b_sel], w=[b_sel])
                    else:
                        fw.op("dve", lambda e, bank=bank, c0=c0, r_=r_: e.scalar_tensor_tensor(out=sc[0:8, :], in0=self.ps[bank][0:8, c0 + 65:c0 + 194], scalar=small[0:8, 16 + r_:17 + r_], in1=sc[0:8, :], op0=ALU.mult, op1=ALU.add), r=[self.bps[bank], b_sel], w=[b_sel])
                fw.op("dve", lambda e: e.tensor_tensor(out=sc[0:8, :], in0=sc[0:8, :], in1=tsel_t[0:8, :], op=ALU.add), r=[b_sel, b_fc], w=[b_sel])
                fw.op("dve", lambda e: e.max(out=m8[0:8, 0:8], in_=sc[0:8, :]), r=[b_sel], w=[b_sel])
                fw.op("dve", lambda e: e.match_replace(out=sc2[0:8, :], in_to_replace=m8[0:8, 0:8], in_values=sc[0:8, :], imm_value=-1e30), r=[b_sel], w=[b_sel])
                fw.op("dve", lambda e: e.max(out=m8[0:8, 8:16], in_=sc2[0:8, :]), r=[b_sel], w=[b_sel])
                fw.op("dve", lambda e: e.tensor_scalar(out=madd[0:8, :], in0=sc[0:8, 0:128], scalar1=m8[0:8, 15:16], scalar2=1.0, op0=ALU.is_ge, op1=ALU.subtract), r=[b_sel], w=[b_sel])
                fw.op("pe", lambda e: e.transpose(pstb[:, 512:520], madd[0:8, :], ident_b[0:8, 0:8]), r=[b_sel, b_const2], w=[self.bps[PS_T]])
                fw.op("dve", lambda e: e.tensor_copy(msel, pstb[:, 512:520].rearrange("p (o q) -> p o q", o=1).to_broadcast([128, 4, 8])), r=[self.bps[PS_T]], w=[b_msel])
                msel32 = msel.rearrange("p r q -> p (r q)")
                for (br, Ka, b_Ka, Vt, bV, bank, ntile) in ((2, Kwa, b_Kwa, VWa, b_VWa, PS_OW, 5), (1, Ksa, b_Ksa, VSa, b_VSa, PS_OS, 65)):
                    for kt in range(ntile):
                        si = ucnt % 2
                        ucnt += 1
                        last = (kt == ntile - 1)
                        nk = 8 if last else 128
                        Sb, bS = self.ps[PS_S[si]], self.bps[PS_S[si]]
                        extra = []
                        if last:
                            extra.append((ident_b[0:8, 0:8], cm8[0:8, :], [b_const2, b_fc]))
                        elif br == 1:
                            extra.append((e30s[:, kt * 128:(kt + 1) * 128], msel32, [b_fc, b_msel]))
                        elif kt == 0:
                            extra.append((ident_b, wm0, [b_const2, b_fc]))
                        fw.op("pe", lambda e, Sb=Sb, Ka=Ka, kt=kt, nk=nk, ne=len(extra): e.matmul(Sb[0:nk, 0:32], lhsT=Ka[:, kt * 128:kt * 128 + nk], rhs=QA32, start=True, stop=(ne == 0)), r=[b_Ka, b_QA], w=[bS])
                        for (lh, rh, rb) in extra:
                            fw.op("pe", lambda e, Sb=Sb, lh=lh, rh=rh, nk=nk: e.matmul(Sb[0:nk, 0:32], lhsT=lh, rhs=rh, start=False, stop=True), r=rb, w=[bS])
                        pi_ = ucnt % 3
                        pp_, b_pp = Pss[pi_], b_Pss[pi_]
                        fw.op("act", lambda e, Sb=Sb, pp_=pp_, nk=nk: e.activation(out=pp_[0:nk, :], in_=Sb[0:nk, 0:32], func=AF.Exp, scale=0.125), r=[bS], w=[b_pp])
                        for r_ in range(4):
                            fw.op("pe", lambda e, bank=bank, r_=r_, pp_=pp_, Vt=Vt, kt=kt, g=g, nk=nk, ntile=ntile: e.matmul(self.ps[bank][0:8, r_ * 65:(r_ + 1) * 65], lhsT=pp_[0:nk, r_ * 8:(r_ + 1) * 8], rhs=Vt[0:nk, kt, g, :], start=(kt == 0 and r_ == 0), stop=(kt == ntile - 1), skip_group_check=True), r=[b_pp, bV], w=[self.bps[bank]])
                    fw.op("dve", lambda e, bank=bank, br=br: e.tensor_scalar(out=small[0:8, br * 4:(br + 1) * 4], in0=self.ps[bank][0:8, 0:260].rearrange("p (r x) -> p r x", r=4)[:, :, 64], scalar1=1e-30, scalar2=None, op0=ALU.max), r=[self.bps[bank]], w=[b_sel])
                fw.op("dve", lambda e: e.reciprocal(out=small[0:8, 16:28], in_=small[0:8, 0:12]), r=[b_sel], w=[b_sel])
                fw.op("dve", lambda e, g=g: e.tensor_tensor(out=small[0:8, 32:44].rearrange("p (b r) -> p b r", b=3), in0=small[0:8, 16:28].rearrange("p (b r) -> p b r", b=3), in1=gt[0:8, g * 12:(g + 1) * 12].rearrange("p (r b) -> p b r", b=3), op=ALU.mult), r=[b_sel, b_gt], w=[b_sel])
                for r_ in range(4):
                    bankc = PS_CA if r_ < 2 else PS_CB
                    c0 = (r_ % 2) * 194
                    od = otile[0:8, r_ * 64:(r_ + 1) * 64]
                    fw.op("dve", lambda e, bankc=bankc, c0=c0, od=od, r_=r_: e.tensor_scalar(out=od, in0=self.ps[bankc][0:8, c0:c0 + 64], scalar1=small[0:8, 32 + r_:33 + r_], scalar2=None, op0=ALU.mult), r=[self.bps[bankc], b_sel], w=[b_ot_])
                    fw.op("dve", lambda e, od=od, r_=r_: e.scalar_tensor_tensor(out=od, in0=self.ps[PS_OS][0:8, r_ * 65:r_ * 65 + 64], scalar=small[0:8, 36 + r_:37 + r_], in1=od, op0=ALU.mult, op1=ALU.add), r=[self.bps[PS_OS], b_sel, b_ot_], w=[b_ot_])
                    fw.op("dve", lambda e, od=od, r_=r_: e.scalar_tensor_tensor(out=od, in0=self.ps[PS_OW][0:8, r_ * 65:r_ * 65 + 64], scalar=small[0:8, 40 + r_:41 + r_], in1=od, op0=ALU.mult, op1=ALU.add), r=[self.bps[PS_OW], b_sel, b_ot_], w=[b_ot_])
                fw.op("act", lambda e: e.copy(otb[0:8, :], otile[0:8, :]), r=[b_ot_], w=[b_ot_])
                for c_ in range(2):
                    fw.op("pe", lambda e, c_=c_: e.transpose(pstb[:, 528 + c_ * 8:528 + (c_ + 1) * 8], otb[0:8, c_ * 128:(c_ + 1) * 128], ident_b[0:8, 0:8]), r=[b_ot_, b_const2], w=[self.bps[PS_T]])
                fw.op("dve", lambda e: e.tensor_copy(ot1, pstb[:, 528:544].rearrange("p (c t) -> p c t", c=2)), r=[self.bps[PS_T]], w=[b_ot1])
                fw.dma("act", g_("s_sOT1")[g * 256:(g + 1) * 256, sq * 8:(sq + 1) * 8].rearrange("(c p) t -> p c t", p=128), ot1, r=[b_ot1], w=[b_scr["sOT1"]])
        self.sp_off = mark
        fw.barrier()

    def phaseD_s(self, L, l):
        fw = self.fw
        g_ = lambda n: L[n]
        b_scr, load_w, rmsnorm_T = g_("b_scr"), g_("load_w"), g_("rmsnorm_T")
        NT = 32
        mark = self.sp_off
        s_OT = g_("s_sOT0") if l == 0 else g_("s_sOT1")
        nm_OT = "sOT0" if l == 0 else "sOT1"
        wout, b_wout = load_w(g_("wb_out_ab") if l == 0 else g_("wb_out_c"), D, 4, "wout_s")
        w2t, b_w2 = load_w(g_("wb_w2")[l], FFN, 4, "w2_s")
        ot = self.salloc(8 * NT, BF16).rearrange("p (kc t) -> p kc t", kc=8)
        b_ot = fw.buf("ot_s", dma=True)
        x1 = self.salloc(D, F32)
        b_x1 = fw.buf("x1_s", dma=True)
        tmp = (self.salloc(D, F32), self.salloc(4, F32), self.salloc(D, BF16), fw.buf("nrm_tmp_s2"))
        hT = self.salloc(8 * NT, BF16).rearrange("p (kc t) -> p kc t", kc=8)
        b_hT = fw.buf("hT_s2")
        gT = self.salloc(22 * NT, BF16).rearrange("p (hc t) -> p hc t", hc=22)
        b_gT = fw.buf("gT_s")
        wst = [self.salloc(2 * 8 * 256, BF16).rearrange("p (w kc n) -> p w kc n", w=2, kc=8) for _ in range(2)]
        b_wst = [fw.buf("wst_s%d" % i, dma=True) for i in range(2)]
        stmp = [self.salloc(NT, F32) for _ in range(2)]
        b_stmp = [fw.buf("stmp_s%d" % i) for i in range(2)]
        tm_f = [self.salloc(512, F32) for _ in range(2)]
        b_tm_f = [fw.buf("tmf_s%d" % i, dma=True) for i in range(2)]
        fw.dma("sp", ot, s_OT.rearrange("(kc p) t -> p kc t", p=128), r=[b_scr[nm_OT]], w=[b_ot])
        if l == 0:
            fw.dma("sp", x1[0:NT, :], g_("xs"), w=[b_x1])
        else:
            fw.dma("sp", x1[0:NT, :], g_("s_sX1"), r=[b_scr["sX1"]], w=[b_x1])
        for half in range(2):
            pi = half
            for q2 in range(2):
                nb = half * 2 + q2
                for kc in range(8):
                    fw.op("pe", lambda e, kc=kc, nb=nb, q2=q2, pi=pi: e.matmul(self.ps[pi][0:NT, q2 * 256:(q2 + 1) * 256], lhsT=ot[:, kc, :], rhs=wout[:, nb, kc, :], start=(kc == 0), stop=(kc == 7)), r=[b_ot, b_wout], w=[self.bps[pi]])
            fw.op("dve", lambda e, pi=pi, half=half: e.tensor_tensor(out=x1[0:NT, half * 512:(half + 1) * 512], in0=self.ps[pi][0:NT, :], in1=x1[0:NT, half * 512:(half + 1) * 512], op=ALU.add), r=[self.bps[pi], b_x1], w=[b_x1])
        rmsnorm_T(x1, b_x1, NT, 1 + 2 * l, hT, b_hT, 0, tmp)
        wcnt = 0
        for nb in range(11):
            wi = wcnt % 2
            wcnt += 1
            fw.dma("sp", wst[wi][:, 0], g_("wb_w1")[l][nb].rearrange("p (kc n) -> p kc n", kc=8), r=[b_scr["w"]], w=[b_wst[wi]])
            fw.dma("sp", wst[wi][:, 1], g_("wb_w3")[l][nb].rearrange("p (kc n) -> p kc n", kc=8), r=[b_scr["w"]], w=[b_wst[wi]])
            for sub in range(2):
                hc = nb * 2 + sub
                p1, p3 = 2 + (hc % 2) * 2, 3 + (hc % 2) * 2
                for (pp, wsel) in ((p1, 0), (p3, 1)):
                    for kc in range(8):
                        fw.op("pe", lambda e, kc=kc, pp=pp, wsel=wsel, wi=wi, sub=sub: e.matmul(self.ps[pp][:, 0:NT], lhsT=wst[wi][:, wsel, kc, sub * 128:(sub + 1) * 128], rhs=hT[:, kc, :], start=(kc == 0), stop=(kc == 7)), r=[b_wst[wi], b_hT], w=[self.bps[pp]])
                si = hc % 2
                fw.op("act", lambda e, si=si, p1=p1: e.activation(out=stmp[si], in_=self.ps[p1][:, 0:NT], func=AF.Silu), r=[self.bps[p1]], w=[b_stmp[si]])
                fw.op("dve", lambda e, si=si, p3=p3, hc=hc: e.tensor_tensor(out=gT[:, hc, :], in0=self.ps[p3][:, 0:NT], in1=stmp[si], op=ALU.mult), r=[self.bps[p3], b_stmp[si]], w=[b_gT])
        for half in range(2):
            pi = half
            for q2 in range(2):
                nb = half * 2 + q2
                for hc in range(22):
                    fw.op("pe", lambda e, hc=hc, nb=nb, q2=q2, pi=pi: e.matmul(self.ps[pi][0:NT, q2 * 256:(q2 + 1) * 256], lhsT=gT[:, hc, :], rhs=w2t[:, nb, hc, :], start=(hc == 0), stop=(hc == 21)), r=[b_gT, b_w2], w=[self.bps[pi]])
            fw.op("dve", lambda e, pi=pi, half=half: e.tensor_tensor(out=x1[0:NT, half * 512:(half + 1) * 512], in0=self.ps[pi][0:NT, :], in1=x1[0:NT, half * 512:(half + 1) * 512], op=ALU.add), r=[self.bps[pi], b_x1], w=[b_x1])
        if l == 1:
            gout = self.salloc(D, F32)
            b_gout = fw.buf("gout_s", dma=True)
            fw.dma("sp", gout[0:NT, :], g_("norm_out").partition_broadcast(NT), w=[b_gout])
            sq_, ss, xn, bt = tmp
            yt = self.salloc(D, F32)
            b_yt = fw.buf("yt_s", dma=True)
            fw.op("act", lambda e: e.activation(out=sq_[0:NT, :], in_=x1[0:NT, :], func=AF.Square, accum_out=ss[0:NT, 0:1]), r=[b_x1], w=[bt])
            fw.op("dve", lambda e: e.tensor_scalar(out=ss[0:NT, 1:2], in0=ss[0:NT, 0:1], scalar1=1.0 / D, scalar2=EPS, op0=ALU.mult, op1=ALU.add), r=[bt], w=[bt])
            fw.op("act", lambda e: e.activation(out=ss[0:NT, 2:3], in_=ss[0:NT, 1:2], func=AF.Sqrt), r=[bt], w=[bt])
            fw.op("dve", lambda e: e.reciprocal(out=ss[0:NT, 3:4], in_=ss[0:NT, 2:3]), r=[bt], w=[bt])
            fw.op("dve", lambda e: e.scalar_tensor_tensor(out=yt[0:NT, :], in0=x1[0:NT, :], scalar=ss[0:NT, 3:4], in1=gout[0:NT, :], op0=ALU.mult, op1=ALU.mult), r=[bt, b_x1, b_gout], w=[b_yt])
            fw.dma("act", g_("o_y_s"), yt[0:NT, :], r=[b_yt])
            self.sp_off = mark
            fw.barrier()
            return
        fw.dma("act", g_("s_sX1"), x1[0:NT, :], r=[b_x1], w=[b_scr["sX1"]])
        rmsnorm_T(x1, b_x1, NT, 2, hT, b_hT, 0, tmp)
        stq = [self.salloc(NT, BF16) for _ in range(2)]
        b_stq = [fw.buf("stq%d" % i, dma=True) for i in range(2)]
        for nb in range(4):
            wi = wcnt % 2
            wcnt += 1
            fw.dma("sp", wst[wi][:, 0], g_("wb_in_c")[nb].rearrange("p (kc n) -> p kc n", kc=8), r=[b_scr["w"]], w=[b_wst[wi]])
            for sub in range(2):
                oc = nb * 2 + sub
                pp = 4 + oc % 2
                for kc in range(8):
                    fw.op("pe", lambda e, kc=kc, pp=pp, wi=wi, sub=sub: e.matmul(self.ps[pp][:, 0:NT], lhsT=wst[wi][:, 0, kc, sub * 128:(sub + 1) * 128], rhs=hT[:, kc, :], start=(kc == 0), stop=(kc == 7)), r=[b_wst[wi], b_hT], w=[self.bps[pp]])
                si = oc % 2
                fw.op("act", lambda e, si=si, pp=pp: e.copy(stq[si], self.ps[pp][:, 0:NT]), r=[self.bps[pp]], w=[b_stq[si]])
                fw.dma("act", g_("s_sQT1")[oc * 128:(oc + 1) * 128, :], stq[si], r=[b_stq[si]], w=[b_scr["sQT1"]])
        wi = wcnt % 2
        wcnt += 1
        fw.dma("sp", wst[wi][:, 0], g_("wb_in_c_tm")[6].rearrange("p (kc n) -> p kc n", kc=8), r=[b_scr["w"]], w=[b_wst[wi]])
        for kc in range(8):
            fw.op("pe", lambda e, kc=kc, wi=wi: e.matmul(self.ps[4][0:NT, 0:256], lhsT=hT[:, kc, :], rhs=wst[wi][:, 0, kc, :], start=(kc == 0), stop=(kc == 7)), r=[b_wst[wi], b_hT], w=[self.bps[4]])
        fw.op("act", lambda e: e.activation(out=tm_f[0][0:NT, 0:48], in_=self.ps[4][0:NT, 0:48], func=AF.Sigmoid), r=[self.bps[4]], w=[b_tm_f[0]])
        fw.dma("act", g_("s_sG"), tm_f[0][0:NT, 0:48], r=[b_tm_f[0]], w=[b_scr["sG"]])
        o_win_kv_s, cache_win = g_("o_win_kv_s"), g_("cache_win")
        b_d2d = fw.buf("d2d_s", dma=True)
        for sq in range(4):
            fw.dma("sp", o_win_kv_s[sq, 0:504, :], cache_win[sq, 8:512, :], w=[b_d2d])
        for nb2 in range(3):
            wi = wcnt % 2
            wcnt += 1
            for q2 in range(2):
                fw.dma("sp", wst[wi][:, q2], g_("wb_in_c_tm")[nb2 * 2 + q2].rearrange("p (kc n) -> p kc n", kc=8), r=[b_scr["w"]], w=[b_wst[wi]])
            pp = 2 + nb2 % 2
            for q2 in range(2):
                for kc in range(8):
                    fw.op("pe", lambda e, kc=kc, pp=pp, wi=wi, q2=q2: e.matmul(self.ps[pp][0:NT, q2 * 256:(q2 + 1) * 256], lhsT=hT[:, kc, :], rhs=wst[wi][:, q2, kc, :], start=(kc == 0), stop=(kc == 7)), r=[b_wst[wi], b_hT], w=[self.bps[pp]])
            fi = nb2 % 2
            fw.op("act", lambda e, fi=fi, pp=pp: e.copy(tm_f[fi][0:NT, :], self.ps[pp][0:NT, :]), r=[self.bps[pp]], w=[b_tm_f[fi]])
            if nb2 == 0:
                fw.dma("act", g_("o_cmp_kv_s"), tm_f[fi][0:NT, :], r=[b_tm_f[fi]], w=[b_scr["ocmp"]])
            elif nb2 == 1:
                fw.dma("act", g_("o_sel_kv_s"), tm_f[fi][0:NT, :], r=[b_tm_f[fi]], w=[b_scr["osel"]])
            else:
                for sq in range(4):
                    fw.dma("act", o_win_kv_s[sq, 504:512, :], tm_f[fi][sq * 8:(sq + 1) * 8, :], r=[b_tm_f[fi]], w=[b_scr["owin"]])
        self.sp_off = mark
        fw.barrier()


def make_inputs(inputs, core):
    f = lambda a: np.ascontiguousarray(a, dtype=np.float32)
    inputs = {k: np.asarray(v) for k, v in inputs.items()}
    m = {}
    m["xp"] = f(inputs["x_prompt"][core % 4])
    m["w_in_ab"] = f(inputs["w_in_ab"][0])
    m["w_out_ab"] = f(inputs["w_out_ab"][0])
    m["w_out_c"] = f(inputs["w_out_c"][0])
    m["ffn_w1"] = f(inputs["ffn_w1"])
    m["ffn_w3"] = f(inputs["ffn_w3"])
    m["ffn_w2"] = f(inputs["ffn_w2"])
    wc = np.asarray(inputs["w_in_c"][0], np.float32)
    m["w_in_c"] = f(np.concatenate([wc[:, 0:1536], wc[:, 1536:1792], wc[:, 2048:2304]], axis=1))
    wt = np.zeros((D, 1792), np.float32)
    wt[:, 0:1536] = wc[:, 1024:2560]
    wt[:, 1536:1584] = wc[:, 2560:2608]
    m["w_in_c_tm"] = f(wt)
    m["norm_out"] = f(inputs["norm_out"])
    g = np.stack([inputs["norm_mix"][0], inputs["norm_ffn"][0], inputs["norm_mix"][1], inputs["norm_ffn"][1], inputs["norm_out"]])
    m["gains"] = f(np.asarray(g, np.float32).reshape(5, 8, 128).transpose(2, 0, 1).reshape(128, 40))
    rp = np.zeros((9, 512), np.float32)
    rp[0:4] = inputs["conv_w"][0]
    rp[4] = inputs["conv_b"][0]
    rp[5] = inputs["gate_a_b"][0]
    rp[6] = inputs["gate_x_b"][0]
    rp[7] = inputs["lru_lambda"][0]
    m["rnnp"] = f(rp.reshape(9, 4, 128).transpose(2, 1, 0).reshape(128, 36))
    m["gate_a_w"] = f(inputs["gate_a_w"][0])
    m["gate_x_w"] = f(inputs["gate_x_w"][0])
    sl = slice(4 * core, 4 * core + 4)
    m["xs"] = f(inputs["x_sample"][sl].reshape(32, D))
    m["cache_dil"] = f(inputs["cache_dil_kv"][0, sl].reshape(4, 2048, 1024))
    m["st_conv"] = f(np.asarray(inputs["state_conv"][0, sl], np.float32).reshape(4, 3, 4, 128).transpose(3, 2, 0, 1).reshape(128, 48))
    m["st_rnn"] = f(np.asarray(inputs["state_rnn"][0, sl], np.float32).reshape(4, 4, 128).transpose(2, 1, 0).reshape(128, 16))
    m["cache_win"] = f(inputs["cache_win_kv"][0, sl].reshape(4, 512, 512))
    m["pool_cmp"] = np.asarray(inputs["cache_cmp_kv"][0], np.float32).reshape(-1, 512)
    m["pool_sel"] = np.asarray(inputs["cache_sel_kv"][0], np.float32).reshape(-1, 512)
    m["ptab"] = np.ascontiguousarray(np.asarray(inputs["page_table"][sl], np.int32).reshape(1, 256))
    m["w_cmp_t"] = f(np.asarray(inputs["w_cmp"][0], np.float32).transpose(2, 0, 1, 3).reshape(64, 4096))
    m["pe_cmp_t"] = f(np.asarray(inputs["pe_cmp"][0], np.float32).transpose(2, 0, 1).reshape(64, 64))
    for k, v in host_consts().items():
        m[k] = v
    return m


_CACHE = {}


def run(inputs, stage):
    if stage not in _CACHE:
        kb = KB(stage)
        kb.build()
        _CACHE[stage] = kb
    kb = _CACHE[stage]
    in_maps = []
    for c in range(8):
        m = make_inputs(inputs, c)
        in_maps.append({k: m[k] for k in kb.ins})
    res = run_bass_kernel_spmd(kb.nc, in_maps, core_ids=list(range(8)))
    return kb, res


STAGE = 6


def kernel(**inputs):
    inputs = {k: np.asarray(v) for k, v in inputs.items()}
    kb, res = run(inputs, STAGE)
    R = res.results
    z = lambda *sh: np.zeros(sh, np.float32)

    def getp(name, shape):
        if name in R[0]:
            return np.stack([np.asarray(R[b][name], np.float32).reshape(shape) for b in range(4)])
        return np.zeros((4,) + tuple(shape), np.float32)
    y_prompt = getp("y_p", (SEQ, D))
    def gets(name, shape):
        if name in R[0]:
            return np.concatenate([np.asarray(R[c][name], np.float32).reshape(shape) for c in range(8)], axis=0)
        return None
    y_sample = gets("y_s", (4, 8, D))
    dil_kv_p = getp("dil_kv_p", (2048, 2, 8, 64))[None]
    dil_kv_s = gets("dil_kv_s", (4, 2048, 2, 8, 64))[None]
    conv_p = getp("conv_p", (3, 512))[None]
    conv_s = gets("conv_s", (4, 3, 512))[None]
    rnn_p = getp("rnn_p", (512,))[None]
    rnn_s = gets("rnn_s", (4, 512))[None]
    win_kv_p = getp("win_kv_p", (512, 2, 4, 64))[None]
    win_kv_s = gets("win_kv_s", (4, 512, 2, 4, 64))[None]
    cmp_kv_p = getp("cmp_kv_p", (SEQ, 2, 4, 64))[None]
    cmp_kv_s = gets("cmp_kv_s", (4, 8, 2, 4, 64))[None]
    sel_kv_p = getp("sel_kv_p", (SEQ, 2, 4, 64))[None]
    sel_kv_s = gets("sel_kv_s", (4, 8, 2, 4, 64))[None]
    return (y_prompt, y_sample, dil_kv_p, dil_kv_s, conv_p, conv_s, rnn_p, rnn_s, win_kv_p, win_kv_s, cmp_kv_p, cmp_kv_s, sel_kv_p, sel_kv_s)
```
